# Optimizing a Trainium2 kernel written in Bass

```python
import math
import jax, jax.numpy as jnp
from jax import lax
import numpy as np

D_MODEL = 1024
BATCH = 8
SEQ = 2048
DEPTH = 1
DEC_BATCH = 128
DEC_SEQ = 1
PAST_LEN = 2048
PAGE_SIZE = 128

C_CONV = D_MODEL // 2
CONV_WIDTH = 31
N_HEADS = 8
HEAD_DIM = (D_MODEL - C_CONV) // N_HEADS
N_KV_HEADS = 2
GROUP = N_HEADS // N_KV_HEADS
KV_W = N_KV_HEADS * HEAD_DIM
N_KV_SLOTS = 6
N_PAGED_SLOTS = 4
MIX_WIDTH = C_CONV + N_HEADS * HEAD_DIM
IN_COLS = 2 * C_CONV + N_HEADS * HEAD_DIM + N_KV_SLOTS * KV_W + 3 * N_HEADS
D_FF = 4 * D_MODEL
CMP_BLOCK = 32
SEL_BLOCK = 64
TOP_N = 16
N_LOCAL = 2
WINDOW = 512
WIN_Q_BLOCK = 128
SEL_Q_BLOCK = 64
FORCE_BONUS = float(GROUP + 1)
NEG_INF = -1e30
RMS_EPS = 1e-6
LN_EPS = 1e-5
ATTN_SCALE = HEAD_DIM ** -0.5

kernel_name = 'hymba_conformer_nsa_decoder_step'


def rms_norm(x, g):
    xf = x.astype(jnp.float32)
    y = xf * lax.rsqrt(jnp.mean(xf * xf, axis=-1, keepdims=True) + RMS_EPS)
    return (y * g.astype(jnp.float32)).astype(x.dtype)


def layer_norm(x, g, b):
    xf = x.astype(jnp.float32)
    mu = jnp.mean(xf, axis=-1, keepdims=True)
    var = jnp.mean(jnp.square(xf - mu), axis=-1, keepdims=True)
    y = (xf - mu) * lax.rsqrt(var + LN_EPS) * g.astype(jnp.float32) + b.astype(jnp.float32)
    return y.astype(x.dtype)


def masked_softmax(s, mask):
    p = jax.nn.softmax(jnp.where(mask, s, NEG_INF), axis=-1)
    return jnp.where(mask, p, 0.0)


def project(x, g, w_in):
    B, L, _ = x.shape
    z = rms_norm(x, g) @ w_in
    c0 = 2 * C_CONV
    c1 = c0 + N_HEADS * HEAD_DIM
    c2 = c1 + N_KV_SLOTS * KV_W
    u = z[..., :c0]
    q = z[..., c0:c1].reshape(B, L, N_KV_HEADS, GROUP, HEAD_DIM)
    kv = z[..., c1:c2].reshape(B, L, N_KV_SLOTS, N_KV_HEADS, HEAD_DIM)
    gl = z[..., c2:].reshape(B, L, N_HEADS, 3)
    return u, q, kv, gl


def conv_mixer(u, prev, w_dw, b_dw, ln_g, ln_b):
    a, b = jnp.split(u, 2, axis=-1)
    glu = a * jax.nn.sigmoid(b)
    ext = jnp.concatenate([prev.astype(glu.dtype), glu], axis=1)
    y = lax.conv_general_dilated(ext, w_dw[:, None, :].astype(glu.dtype), window_strides=(1,),
                                 padding='VALID', dimension_numbers=('NWC', 'WIO', 'NWC'),
                                 feature_group_count=C_CONV)
    y = layer_norm(y + b_dw, ln_g, ln_b)
    return jax.nn.silu(y), ext[:, -(CONV_WIDTH - 1):]


def compressed_selected(q, q_pos, k_cmp, v_cmp, k_sel, v_sel, w_ck, w_cv):
    B, Q = q.shape[0], q.shape[1]
    T = k_cmp.shape[1]
    t_pad = -(-T // SEL_BLOCK) * SEL_BLOCK
    padw = ((0, 0), (0, t_pad - T), (0, 0), (0, 0))
    k_cmp, v_cmp, k_sel, v_sel = [jnp.pad(a, padw) for a in (k_cmp, v_cmp, k_sel, v_sel)]
    n_cmp = t_pad // CMP_BLOCK
    n_blk = t_pad // SEL_BLOCK
    kc = jnp.einsum('bcjhd,jh->bchd', k_cmp.reshape(B, n_cmp, CMP_BLOCK, N_KV_HEADS, HEAD_DIM), w_ck)
    vc = jnp.einsum('bcjhd,jh->bchd', v_cmp.reshape(B, n_cmp, CMP_BLOCK, N_KV_HEADS, HEAD_DIM), w_cv)
    s = jnp.einsum('bqhgd,bchd->bqhgc', q, kc).astype(jnp.float32) * ATTN_SCALE
    c_end = (jnp.arange(n_cmp, dtype=jnp.int32) + 1) * CMP_BLOCK - 1
    cmask = c_end[None, :] <= q_pos[:, None]
    p = masked_softmax(s, cmask[None, :, None, None, :])
    o_cmp = jnp.einsum('bqhgc,bchd->bqhgd', p.astype(vc.dtype), vc)
    p_blk = p.sum(axis=3).reshape(B, Q, N_KV_HEADS, n_blk, SEL_BLOCK // CMP_BLOCK).sum(-1)
    blk = jnp.arange(n_blk, dtype=jnp.int32)
    back = (q_pos // SEL_BLOCK)[:, None] - blk[None, :]
    forced = (blk[None, :] == 0) | ((back >= 0) & (back < N_LOCAL))
    valid = back >= 0
    score = jnp.where(valid[None, :, None, :],
                      p_blk + jnp.where(forced, FORCE_BONUS, 0.0)[None, :, None, :], -jnp.inf)
    n_top = min(TOP_N, n_blk)
    _, idx = lax.top_k(score, n_top)
    kb = k_sel.reshape(B, n_blk, SEL_BLOCK, N_KV_HEADS, HEAD_DIM).transpose(0, 3, 1, 2, 4)
    vb = v_sel.reshape(B, n_blk, SEL_BLOCK, N_KV_HEADS, HEAD_DIM).transpose(0, 3, 1, 2, 4)
    qb = Q if Q <= SEL_Q_BLOCK else math.gcd(Q, SEL_Q_BLOCK)
    nqb = Q // qb
    take = jax.vmap(jax.vmap(lambda a, i: a[i]))
    n_keys = n_top * SEL_BLOCK

    def sel_block(args):
        q_b, i_b, pos_b = args
        it = jnp.swapaxes(i_b, 1, 2)
        kg = take(kb, it)
        vg = take(vb, it).reshape(B, N_KV_HEADS, qb, n_keys, HEAD_DIM)
        s_b = jnp.einsum('bqhgd,bhqnsd->bqhgns', q_b, kg).astype(jnp.float32) * ATTN_SCALE
        kpos = i_b[..., None] * SEL_BLOCK + jnp.arange(SEL_BLOCK, dtype=jnp.int32)
        m = (kpos <= pos_b[None, :, None, None, None]).reshape(B, qb, N_KV_HEADS, 1, n_keys)
        p_b = masked_softmax(s_b.reshape(B, qb, N_KV_HEADS, GROUP, n_keys), m)
        return jnp.einsum('bqhgk,bhqkd->bqhgd', p_b.astype(vg.dtype), vg)

    xs = (jnp.swapaxes(q.reshape(B, nqb, qb, N_KV_HEADS, GROUP, HEAD_DIM), 0, 1),
          jnp.swapaxes(idx.reshape(B, nqb, qb, N_KV_HEADS, n_top), 0, 1),
          q_pos.reshape(nqb, qb))
    o_sel = lax.map(sel_block, xs)
    o_sel = jnp.swapaxes(o_sel, 0, 1).reshape(B, Q, N_KV_HEADS, GROUP, HEAD_DIM)
    return o_cmp, o_sel


def window_attend(q, q_pos, k, v, k_pos):
    rel = q_pos[:, None] - k_pos[None, :]
    m = (rel >= 0) & (rel <= WINDOW) & (k_pos[None, :] >= 0)
    s = jnp.einsum('bqhgd,bkhd->bqhgk', q, k).astype(jnp.float32) * ATTN_SCALE
    p = masked_softmax(s, m[None, :, None, None, :])
    return jnp.einsum('bqhgk,bkhd->bqhgd', p.astype(v.dtype), v)


def window_prompt(q, k, v):
    B, S = q.shape[0], q.shape[1]
    qb = math.gcd(S, WIN_Q_BLOCK)
    padw = ((0, 0), (WINDOW, 0), (0, 0), (0, 0))
    kp, vp = jnp.pad(k, padw), jnp.pad(v, padw)
    r = jnp.arange(qb + WINDOW, dtype=jnp.int32)
    a = jnp.arange(qb, dtype=jnp.int32)

    def blk(start):
        return window_attend(lax.dynamic_slice_in_dim(q, start, qb, axis=1), start + a,
                             lax.dynamic_slice_in_dim(kp, start, qb + WINDOW, axis=1),
                             lax.dynamic_slice_in_dim(vp, start, qb + WINDOW, axis=1),
                             start - WINDOW + r)

    o = lax.map(blk, jnp.arange(S // qb, dtype=jnp.int32) * qb)
    return jnp.swapaxes(o, 0, 1).reshape(B, S, N_KV_HEADS, GROUP, HEAD_DIM)


def gate_merge(o_cmp, o_sel, o_win, gl):
    B, L = o_cmp.shape[0], o_cmp.shape[1]
    g = jax.nn.sigmoid(gl.astype(jnp.float32)).astype(o_cmp.dtype)
    hd = lambda o: o.reshape(B, L, N_HEADS, HEAD_DIM)
    o = hd(o_cmp) * g[..., 0:1] + hd(o_sel) * g[..., 1:2] + hd(o_win) * g[..., 2:3]
    return o.reshape(B, L, N_HEADS * HEAD_DIM)


def residual_block(x, conv_y, attn_y, w_out, g_mlp, w_up, w_down):
    h = x + jnp.concatenate([conv_y, attn_y], axis=-1) @ w_out
    m = rms_norm(h, g_mlp) @ w_up
    return h + jnp.square(jax.nn.relu(m)) @ w_down


def setup_inputs(seed: int = 0) -> dict:
    key = jax.random.key(seed)
    ks = jax.random.split(key, 24)
    f32 = jnp.float32
    nrm = lambda k, shape, s=1.0: jax.random.normal(k, shape, f32) * s
    n_pages = PAST_LEN // PAGE_SIZE
    n_used = DEC_BATCH * n_pages
    n_phys = n_used + (n_used + 3) // 4
    w_buf = min(WINDOW, PAST_LEN)
    page_table = jax.random.permutation(ks[5], n_phys)[:n_used].reshape(DEC_BATCH, n_pages).astype(jnp.int32)
    return {
        'x_prompt': nrm(ks[0], (BATCH, SEQ, D_MODEL)),
        'x_sample': nrm(ks[1], (DEC_BATCH, DEC_SEQ, D_MODEL)),
        'cache_kv': nrm(ks[2], (DEPTH, n_phys, PAGE_SIZE, N_PAGED_SLOTS, N_KV_HEADS, HEAD_DIM)),
        'cache_win': nrm(ks[3], (DEPTH, DEC_BATCH, w_buf, 2, N_KV_HEADS, HEAD_DIM)),
        'state_conv': nrm(ks[4], (DEPTH, DEC_BATCH, CONV_WIDTH - 1, C_CONV), 0.5),
        'page_table': page_table,
        'g_attn_norm': 1.0 + nrm(ks[6], (DEPTH, D_MODEL), 0.05),
        'w_in': nrm(ks[7], (DEPTH, D_MODEL, IN_COLS), D_MODEL ** -0.5),
        'w_dw': nrm(ks[8], (DEPTH, CONV_WIDTH, C_CONV), CONV_WIDTH ** -0.5),
        'b_dw': nrm(ks[9], (DEPTH, C_CONV), 0.02),
        'conv_ln_g': 1.0 + nrm(ks[10], (DEPTH, C_CONV), 0.05),
        'conv_ln_b': nrm(ks[11], (DEPTH, C_CONV), 0.02),
        'w_cmp_k': (1.0 + nrm(ks[12], (DEPTH, CMP_BLOCK, N_KV_HEADS), 0.3)) * CMP_BLOCK ** -0.5,
        'w_cmp_v': (1.0 + nrm(ks[13], (DEPTH, CMP_BLOCK, N_KV_HEADS), 0.3)) * CMP_BLOCK ** -0.5,
        'w_out': nrm(ks[14], (DEPTH, MIX_WIDTH, D_MODEL), MIX_WIDTH ** -0.5),
        'g_mlp_norm': 1.0 + nrm(ks[15], (DEPTH, D_MODEL), 0.05),
        'w_up': nrm(ks[16], (DEPTH, D_MODEL, D_FF), D_MODEL ** -0.5),
        'w_down': nrm(ks[17], (DEPTH, D_FF, D_MODEL), D_FF ** -0.5),
        'g_final': 1.0 + nrm(ks[18], (D_MODEL,), 0.05),
    }


def reference(x_prompt, x_sample, cache_kv, cache_win, state_conv, page_table, g_attn_norm, w_in, w_dw, b_dw,
              conv_ln_g, conv_ln_b, w_cmp_k, w_cmp_v, w_out, g_mlp_norm, w_up, w_down, g_final):
    hp, hs = x_prompt, x_sample
    w_buf = cache_win.shape[2]
    q_pos_p = jnp.arange(SEQ, dtype=jnp.int32)
    q_pos_s = PAST_LEN + jnp.arange(DEC_SEQ, dtype=jnp.int32)
    k_pos_win_s = PAST_LEN - w_buf + jnp.arange(w_buf + DEC_SEQ, dtype=jnp.int32)
    kv_p, win_p, conv_p, kv_s, win_s, conv_s = [], [], [], [], [], []
    for l in range(DEPTH):
        u, q, kv, gl = project(hp, g_attn_norm[l], w_in[l])
        zeros = jnp.zeros((BATCH, CONV_WIDTH - 1, C_CONV), u.dtype)
        conv_y, conv_st = conv_mixer(u, zeros, w_dw[l], b_dw[l], conv_ln_g[l], conv_ln_b[l])
        o_cmp, o_sel = compressed_selected(q, q_pos_p, kv[:, :, 0], kv[:, :, 1], kv[:, :, 2], kv[:, :, 3],
                                           w_cmp_k[l], w_cmp_v[l])
        o_win = window_prompt(q, kv[:, :, 4], kv[:, :, 5])
        hp = residual_block(hp, conv_y, gate_merge(o_cmp, o_sel, o_win, gl), w_out[l], g_mlp_norm[l], w_up[l], w_down[l])
        kv_p.append(kv[:, :, :N_PAGED_SLOTS])
        win_p.append(kv[:, -min(WINDOW, SEQ):, N_PAGED_SLOTS:])
        conv_p.append(conv_st)
        u, q, kv, gl = project(hs, g_attn_norm[l], w_in[l])
        conv_y, conv_st = conv_mixer(u, state_conv[l], w_dw[l], b_dw[l], conv_ln_g[l], conv_ln_b[l])
        past = jnp.take(cache_kv[l], page_table, axis=0).reshape(DEC_BATCH, PAST_LEN, N_PAGED_SLOTS, N_KV_HEADS, HEAD_DIM)
        full = jnp.concatenate([past, kv[:, :, :N_PAGED_SLOTS].astype(past.dtype)], axis=1)
        o_cmp, o_sel = compressed_selected(q, q_pos_s, full[:, :, 0], full[:, :, 1], full[:, :, 2], full[:, :, 3],
                                           w_cmp_k[l], w_cmp_v[l])
        win = jnp.concatenate([cache_win[l].astype(kv.dtype), kv[:, :, N_PAGED_SLOTS:]], axis=1)
        o_win = window_attend(q, q_pos_s, win[:, :, 0], win[:, :, 1], k_pos_win_s)
        hs = residual_block(hs, conv_y, gate_merge(o_cmp, o_sel, o_win, gl), w_out[l], g_mlp_norm[l], w_up[l], w_down[l])
        kv_s.append(kv[:, :, :N_PAGED_SLOTS])
        win_s.append(win[:, -w_buf:])
        conv_s.append(conv_st)
    y_prompt = rms_norm(hp, g_final)
    y_sample = rms_norm(hs, g_final)
    return (y_prompt, y_sample, jnp.stack(kv_p), jnp.stack(win_p), jnp.stack(conv_p),
            jnp.stack(kv_s), jnp.stack(win_s), jnp.stack(conv_s))
```

```python
import contextlib
import os
import numpy as np
import concourse.bass as bass
import concourse.mybir as mybir
from concourse.bass_utils import run_bass_kernel_spmd

F32 = mybir.dt.float32
BF16 = mybir.dt.bfloat16
I32 = mybir.dt.int32
AF = mybir.ActivationFunctionType
ALU = mybir.AluOpType
AX = mybir.AxisListType

EPOCH = 16000
N_DMA_SEMS = 16

SEQ = 2048
DM = 1024
NT = 16
INC = 2328
SCALE = 0.125
BIG = 20000.0
NSB = 16
N_CORES = 8


class Sched:
    def __init__(self, nc):
        self.nc = nc
        self.ops = []
        self.last_writer = {}
        self.readers = {}
        self.cur_barrier = 0

    def add(self, eng, fn, r=(), w=(), dma=False):
        deps = set()
        for k in r:
            if k in self.last_writer:
                deps.add(self.last_writer[k])
        for k in w:
            if k in self.last_writer:
                deps.add(self.last_writer[k])
            deps.update(self.readers.get(k, ()))
        idx = len(self.ops)
        self.ops.append(dict(eng=eng, fn=fn, deps=sorted(deps), dma=dma, barrier=self.cur_barrier))
        for k in r:
            self.readers.setdefault(k, []).append(idx)
        for k in w:
            self.last_writer[k] = idx
            self.readers[k] = []
        return idx

    def barrier(self):
        self.cur_barrier = len(self.ops)

    def emit(self, final_wait_engine="sp"):
        nc = self.nc
        engs = ["pe", "act", "dve", "pool", "sp"]
        ops = self.ops
        cnt = {e: 0 for e in engs}
        dcnt = {e: 0 for e in engs}
        need = set()
        for op in ops:
            e = op["eng"]
            if op["dma"]:
                k = dcnt[e] % N_DMA_SEMS
                n = dcnt[e] // N_DMA_SEMS
                dcnt[e] += 1
                op["ticket"] = (("d", e, k), 16 * (n + 1))
                op["prev"] = (("d", e, k), 16 * n) if n > 0 else None
            else:
                ep = cnt[e] // EPOCH
                v = cnt[e] % EPOCH + 1
                cnt[e] += 1
                op["ticket"] = (("c", e, ep), v)
            need.add(op["ticket"][0])
        bounds = sorted(set(op["barrier"] for op in ops))
        btk = {}
        run = {}
        bi = 0
        for i, op in enumerate(ops):
            while bi < len(bounds) and bounds[bi] <= i:
                btk[bounds[bi]] = dict(run)
                bi += 1
            sn, v = op["ticket"]
            run[sn] = max(run.get(sn, 0), v)
        while bi < len(bounds):
            btk[bounds[bi]] = dict(run)
            bi += 1
        with contextlib.ExitStack() as st:
            sems = {}
            for sn in sorted(need):
                sems[sn] = st.enter_context(nc.semaphore("s_" + "_".join(map(str, sn))))
            block = st.enter_context(nc.Block())

            def make(e):
                def body(engine):
                    known = {}
                    seen_barrier = [0]

                    def wait(t):
                        sn, v = t
                        if sn[0] == "c":
                            for sn2 in known:
                                if sn2[0] == "c" and sn2[1] == sn[1] and sn2[2] > sn[2]:
                                    return
                        if known.get(sn, 0) >= v:
                            return
                        engine.wait_ge(sems[sn], v)
                        known[sn] = v

                    for op in ops:
                        if op["eng"] != e:
                            continue
                        if op["barrier"] > seen_barrier[0]:
                            seen_barrier[0] = op["barrier"]
                            for sn, v in sorted(btk[op["barrier"]].items()):
                                if sn == ("c", e, sn[2]) and e == "pe":
                                    continue
                                wait((sn, v))
                        for d in op["deps"]:
                            dop = ops[d]
                            if dop["eng"] == e and e == "pe" and not dop["dma"]:
                                continue
                            wait(dop["ticket"])
                        if op["dma"] and op["prev"] is not None:
                            wait(op["prev"])
                        ins = op["fn"](engine)
                        sn, v = op["ticket"]
                        ins.then_inc(sems[sn], 16 if op["dma"] else 1)
                    if e == final_wait_engine:
                        fin = {}
                        for op in ops:
                            sn, v = op["ticket"]
                            fin[sn] = max(fin.get(sn, 0), v)
                        for sn, v in sorted(fin.items()):
                            wait((sn, v))
                return body

            block.tensor(make("pe"))
            block.scalar(make("act"))
            block.vector(make("dve"))
            block.gpsimd(make("pool"))
            block.sync(make("sp"))


def build(stages=("p1", "p2", "p3", "samp"), debug=False, cache_rows=2560 * 128 * 4):
    nc = bass.Bass("TRN2", target_bir_lowering=False)

    def din(name, shape, dt=F32):
        return nc.dram_tensor(name, shape, dt, kind="ExternalInput").ap()

    def dout(name, shape, dt=F32):
        return nc.dram_tensor(name, shape, dt, kind="ExternalOutput").ap()

    xp = din("xp", [SEQ, DM])
    xs = din("xs", [NSB, DM])
    cache = din("cache", [cache_rows, 128])
    cwin = din("cwin", [NSB, 512, 256])
    sconv = din("sconv", [NSB * 30, 512])
    ptab = din("ptab", [1, NSB * 16], I32)
    g_attn = din("g_attn", [1, DM])
    w_in = din("w_in", [DM, INC])
    wdall = din("wdall", [34, 512])
    w_ck = din("w_ck", [32, 2])
    w_cv = din("w_cv", [32, 2])
    w_out = din("w_out", [DM, DM])
    g_mlp = din("g_mlp", [1, DM])
    w_up = din("w_up", [DM, 4096])
    w_down = din("w_down", [4096, DM])
    g_fin = din("g_fin", [1, DM])

    y_p = dout("y_p", [SEQ, DM])
    y_s = dout("y_s", [NSB, DM])
    kv_p = dout("kv_p", [SEQ, 512])
    win_p = dout("win_p", [512, 256])
    conv_p = dout("conv_p", [30, 512])
    kv_s = dout("kv_s", [NSB, 512])
    win_s = dout("win_s", [NSB, 512, 256])
    conv_s = dout("conv_s", [NSB * 30, 512])
    dbg = {}

    S = Sched(nc)
    A = S.add
    ES = contextlib.ExitStack()

    def sb(name, shape, dt, stack=None, side=None):
        return (stack or ES).enter_context(nc.sbuf_tensor(name, shape, dt, side=side))

    def pst(name, shape, dt, stack):
        return stack.enter_context(nc.psum_tensor(name, shape, dt))

    with ES:
        identf = sb("identf", [128, 128], F32)
        ident = sb("ident", [128, 128], BF16)
        A("pool", lambda e: e.memset(identf[:], 0.0), w=["identf"])
        A("pool", lambda e: e.affine_select(out=identf[:], in_=identf[:], pattern=[[-1, 128]],
                                            compare_op=ALU.not_equal, fill=1.0, base=0, channel_multiplier=1),
          r=["identf"], w=["identf"])
        A("dve", lambda e: e.tensor_copy(out=ident[:], in_=identf[:]), r=["identf"], w=["ident"])
        gbc_mlp = sb("gbc_mlp", [128, DM], F32)
        A("sp", lambda e: e.dma_start(out=gbc_mlp[:], in_=g_mlp.partition_broadcast(128)), w=["gbc_mlp"], dma=True)
        hs_acc = sb("hs_acc", [NSB, DM], F32)
        hnTs = sb("hnTs", [128, 8, NSB], BF16)
        w_out_bf = sb("w_out_bf", [128, 8, DM], BF16)
        cw = sb("cw", [128, 4, 34], F32)
        wckb = sb("wckb", [128, 2, 32], F32)
        PmB = [sb(f"PmB{h}", [128, 124], BF16) for h in range(2)]

        RS1 = contextlib.ExitStack()
        w_in_bf = sb("w_in_bf", [128, 8, INC], BF16, RS1, side="right")

        with contextlib.ExitStack() as W0:
            stg = [sb(f"stg{i}", [128, INC], F32, W0) for i in range(2)]
            ci = 0
            for (wsrc, wdst, wkey, ncol) in ((w_in, w_in_bf, "w_in_bf", INC), (w_out, w_out_bf, "w_out_bf", DM)):
                for k in range(8):
                    b = ci % 2
                    A("sp", lambda e, k=k, b=b, wsrc=wsrc, ncol=ncol: e.dma_start(out=stg[b][:, 0:ncol], in_=wsrc[k * 128:(k + 1) * 128, :]),
                      w=[("stg", b)], dma=True)
                    if ci % 2 == 0:
                        A("act", lambda e, k=k, b=b, wdst=wdst, ncol=ncol: e.copy(out=wdst[:, k, :], in_=stg[b][:, 0:ncol]),
                          r=[("stg", b)], w=[(wkey, k)])
                    else:
                        A("pool", lambda e, k=k, b=b, wdst=wdst, ncol=ncol: e.tensor_copy(out=wdst[:, k, :], in_=stg[b][:, 0:ncol]),
                          r=[("stg", b)], w=[(wkey, k)])
                    ci += 1
            wd_sb = sb("wd_sb", [34, 512], F32, W0)
            A("sp", lambda e: e.dma_start(out=wd_sb[:], in_=wdall), w=["wd_sb"], dma=True)
            with contextlib.ExitStack() as PW:
                pcw = pst("pcw", [128, 4, 34], F32, PW)
                for c4 in range(4):
                    A("pe", lambda e, c4=c4: e.transpose(out=pcw[:, c4, :], in_=wd_sb[:, c4 * 128:(c4 + 1) * 128],
                                                         identity=identf[0:34, 0:34]),
                      r=["wd_sb", "identf"], w=[("pcw", c4)])
                A("dve", lambda e: e.tensor_copy(out=cw[:], in_=pcw[:]), r=[("pcw", c4) for c4 in range(4)], w=["cw"])
                S.barrier()
            for h in range(2):
                A("sp", lambda e, h=h: e.dma_start(out=wckb[:, h, :], in_=w_ck[:, h:h + 1].rearrange("j o -> o j").partition_broadcast(128),
                                                   allow_slow_non_contiguous=True), w=["wckb"], dma=True)
            wcol = sb("wcol", [128, 2], F32, W0)
            for rr in range(4):
                A("sp", lambda e, rr=rr: e.dma_start(out=wcol[rr * 32:(rr + 1) * 32, :], in_=w_cv), w=["wcol"], dma=True)
            pmf = sb("pmf", [128, 4], F32, W0)
            A("pool", lambda e: e.memset(pmf[:], 1.0), w=["pmf"])
            A("pool", lambda e: e.affine_select(out=pmf[:], in_=pmf[:], pattern=[[-32, 4]], compare_op=ALU.is_ge,
                                                fill=0.0, base=0, channel_multiplier=1), r=["pmf"], w=["pmf"])
            A("pool", lambda e: e.affine_select(out=pmf[:], in_=pmf[:], pattern=[[32, 4]], compare_op=ALU.is_ge,
                                                fill=0.0, base=31, channel_multiplier=-1), r=["pmf"], w=["pmf"])
            for h in range(2):
                A("pool", lambda e, h=h: e.memset(PmB[h][:], 0.0), w=[("PmB", h)])
                A("dve", lambda e, h=h: e.tensor_scalar(out=PmB[h][:, 60:64], in0=pmf[:], scalar1=wcol[:, h:h + 1],
                                                        scalar2=None, op0=ALU.mult),
                  r=["pmf", "wcol", ("PmB", h)], w=[("PmB", h)])
            S.barrier()

        if "samp" in stages:
            with contextlib.ExitStack() as SP:
                build_samp(nc, S, SP, sb, pst, locals())
                S.barrier()

        ATT = contextlib.ExitStack()
        with ATT:
            QTs = [sb(f"QTs{h}", [96, 4, SEQ], BF16, ATT) for h in range(2)]
            KTs = [sb(f"KTs{h}", [96, SEQ], BF16, ATT) for h in range(2)]
            KTw = [sb(f"KTw{h}", [64, SEQ], BF16, ATT) for h in range(2)]
            kcT = [sb(f"kcT{h}", [64, 64], BF16, ATT) for h in range(2)]
            Vaug = sb("Vaug", [128, NT, 3, 2, 65], BF16, ATT)
            vcaug = sb("vcaug", [64, 2, 65], BF16, ATT)
            gates = sb("gates", [128, NT, 24], F32, ATT)
            convyT = sb("convyT", [128, 4, SEQ], BF16, ATT)
            for h in range(2):
                A("pool", lambda e, h=h: e.memset(KTs[h][64:96, :], 1.0), w=[("KTsaug", h)])
                A("pool", lambda e, h=h: e.affine_select(out=KTs[h][64:96, :], in_=KTs[h][64:96, :], pattern=[[1, SEQ]],
                                                         compare_op=ALU.is_ge, fill=0.0, base=0, channel_multiplier=-64),
                  r=[("KTsaug", h)], w=[("KTsaug", h)])
                A("pool", lambda e, h=h: e.affine_select(out=KTs[h][64:96, :], in_=KTs[h][64:96, :], pattern=[[-1, SEQ]],
                                                         compare_op=ALU.is_ge, fill=0.0, base=63, channel_multiplier=64),
                  r=[("KTsaug", h)], w=[("KTsaug", h)])
            A("pool", lambda e: e.memset(Vaug[:], 1.0), w=["Vaug_init"])
            A("pool", lambda e: e.memset(vcaug[:], 1.0), w=["vcaug_init"])

            with contextlib.ExitStack() as P1:
                build_p1(nc, S, P1, sb, pst, locals())
                S.barrier()
            RS1.close()
            RS2 = contextlib.ExitStack()
            h_acc = sb("h_acc", [128, NT, DM], F32, RS2, side="right")
            if "p2" in stages:
                with contextlib.ExitStack() as P2:
                    build_p2(nc, S, P2, sb, pst, locals())
                    S.barrier()
        if "p3" in stages:
            with contextlib.ExitStack() as P3:
                build_p3(nc, S, P3, sb, pst, locals())
        RS2.close()
        S.emit()
    return nc, dbg


def build_p1(nc, S, P1, sb, pst, L):
    A = S.add
    (QTs, KTs, KTw, kcT, Vaug, vcaug, gates, convyT, cw, wckb, PmB, w_in_bf, ident, identf, xp, g_attn, kv_p, win_p,
     conv_p) = (L[k] for k in ("QTs", "KTs", "KTw", "kcT", "Vaug", "vcaug", "gates", "convyT", "cw", "wckb", "PmB",
                               "w_in_bf", "ident", "identf", "xp", "g_attn", "kv_p", "win_p", "conv_p"))
    debug, dbg, dout = L["debug"], L["dbg"], L["dout"]
    onesf = sb("onesf", [128, 128], F32, P1)
    A("pool", lambda e: e.memset(onesf[:], 1.0 / 512.0), w=["onesf"])
    gbc_attn = sb("gbc_attn", [128, DM], F32, P1)
    A("sp", lambda e: e.dma_start(out=gbc_attn[:], in_=g_attn.partition_broadcast(128)), w=["gbc_attn"], dma=True)
    xt = sb("xt", [128, DM], F32, P1)
    ss = sb("ss", [128, 2], F32, P1)
    xn = sb("xn", [128, DM], BF16, P1)
    xnT = [sb("xnT0", [128, 8, 512], BF16, P1)] * 2
    glu = [sb(f"glu{i}", [128, 4, 542], F32, P1) for i in range(2)]
    sgt = sb("sgt", [128, 512], F32, P1)
    accA = sb("accA", [128, 512], F32, P1)
    accB = sb("accB", [128, 512], F32, P1)
    ych = sb("ych", [128, 4, 512], F32, P1)
    ysq = sb("ysq", [128, 512], F32, P1)
    mean_sb = sb("mean_sb", [128, 512], F32, P1)
    rstd_sb = sb("rstd_sb", [128, 512], F32, P1)
    zt = sb("zt", [128, 792], F32, P1)
    kcp = sb("kcp", [64, 16, 32], F32, P1)
    kcf = sb("kcf", [64, 16], F32, P1)
    cps = sb("cps", [30, 512], F32, P1)
    psF = [pst(f"psF{i}", [128, 512], F32, P1) for i in range(2)]
    psTM = pst("psTM", [128, 1024], F32, P1)
    psT = pst("psT", [128, 8, 128], BF16, P1)
    psVC = pst("psVC", [128, 512], F32, P1)
    psMean = pst("psMean", [128, 512], F32, P1)
    psMsq = pst("psMsq", [128, 512], F32, P1)

    A("pool", lambda e: e.memset(glu[0][:, :, 0:30], 0.0), w=[("gluhead", 0)])
    fcnt = [0]

    def fm_mm(xT, c0, M, Gk):
        b = fcnt[0] % 2
        fcnt[0] += 1
        for k in range(8):
            A("pe", lambda e, k=k, b=b: e.matmul(psF[b][0:M, :], lhsT=w_in_bf[:, k, c0:c0 + M], rhs=xT[:, k, :],
                                                 start=(k == 0), stop=(k == 7)),
              r=[("w_in_bf", k), Gk], w=[("psF", b)])
        return b

    for G in range(4):
        xTg = xnT[G % 2]
        Gk = "xnT"
        gl = glu[G % 2]
        gln = glu[(G + 1) % 2]
        for tt in range(4):
            t = 4 * G + tt
            A("sp", lambda e, t=t: e.dma_start(out=xt[:], in_=xp[t * 128:(t + 1) * 128, :]), w=["xt"], dma=True)
            A("pool", lambda e: e.memset(ss[:, 0:1], 0.0), w=["ss"])
            A("act", lambda e: e.activation(out=xn[:], in_=xt[:], func=AF.Square, accum_out=ss[:, 0:1]),
              r=["xt", "ss"], w=["xn", "ss"])
            A("dve", lambda e: e.tensor_scalar(out=ss[:, 1:2], in0=ss[:, 0:1], scalar1=1.0 / DM, scalar2=1e-6,
                                               op0=ALU.mult, op1=ALU.add), r=["ss"], w=["rs"])
            A("act", lambda e: e.activation(out=ss[:, 1:2], in_=ss[:, 1:2], func=AF.Sqrt), r=["rs"], w=["rs"])
            A("dve", lambda e: e.reciprocal(out=ss[:, 1:2], in_=ss[:, 1:2]), r=["rs"], w=["rs"])
            A("dve", lambda e: e.scalar_tensor_tensor(out=xn[:], in0=xt[:], scalar=ss[:, 1:2],
                                                      in1=gbc_attn[:], op0=ALU.mult, op1=ALU.mult),
              r=["xt", "rs", "gbc_attn"], w=["xn"])
            for k in range(8):
                A("pe", lambda e, k=k: e.transpose(out=psT[:, k, :], in_=xn[:, k * 128:(k + 1) * 128], identity=ident[:]),
                  r=["xn", "ident"], w=["psT"])
            A("act", lambda e, tt=tt, xTg=xTg: e.copy(out=xTg[:, :, tt * 128:(tt + 1) * 128], in_=psT[:]),
              r=["psT"], w=[Gk])
        tok = slice(G * 512, (G + 1) * 512)
        for c4 in range(4):
            ba = fm_mm(xTg, c4 * 128, 128, Gk)
            bb = fm_mm(xTg, 512 + c4 * 128, 128, Gk)
            A("act", lambda e, bb=bb: e.activation(out=sgt[:], in_=psF[bb][:], func=AF.Sigmoid), r=[("psF", bb)], w=["sgt"])
            A("dve", lambda e, ba=ba, c4=c4, gl=gl: e.tensor_tensor(out=gl[:, c4, 30:542], in0=psF[ba][:], in1=sgt[:], op=ALU.mult),
              r=[("psF", ba), "sgt"], w=[("glu", G % 2, c4)])
        for hd in range(8):
            h, g = hd // 4, hd % 4
            b = fm_mm(xTg, 1024 + hd * 64, 64, Gk)
            A("act", lambda e, b=b, h=h, g=g, tok=tok: e.copy(out=QTs[h][0:64, g, tok], in_=psF[b][0:64, :]),
              r=[("psF", b)], w=[("QT", h, G)])
        for h in range(2):
            b = fm_mm(xTg, 1536 + h * 64, 64, Gk)
            A("dve", lambda e, b=b, h=h: e.tensor_tensor(
                out=kcp[:], in0=psF[b][0:64, :].rearrange("p (c j) -> p c j", j=32),
                in1=wckb[0:64, h, :].unsqueeze(1).to_broadcast([64, 16, 32]), op=ALU.mult),
              r=[("psF", b), "wckb"], w=["kcp"])
            A("dve", lambda e: e.tensor_reduce(out=kcf[:], in_=kcp[:], axis=AX.X, op=ALU.add), r=["kcp"], w=["kcf"])
            A("dve", lambda e, h=h, G=G: e.tensor_copy(out=kcT[h][:, G * 16:(G + 1) * 16], in_=kcf[:]),
              r=["kcf"], w=[("kcT", h, G)])
            b = fm_mm(xTg, 1536 + 256 + h * 64, 64, Gk)
            A("act", lambda e, b=b, h=h, tok=tok: e.copy(out=KTs[h][0:64, tok], in_=psF[b][0:64, :]),
              r=[("psF", b)], w=[("KTs", h, G)])
            b = fm_mm(xTg, 1536 + 512 + h * 64, 64, Gk)
            A("act", lambda e, b=b, h=h, tok=tok: e.copy(out=KTw[h][:, tok], in_=psF[b][0:64, :]),
              r=[("psF", b)], w=[("KTw", h, G)])
        for tt in range(4):
            t = 4 * G + tt
            for (c0, n, o0) in ((1536, 512, 0), (2048, 280, 512)):
                for k in range(8):
                    A("pe", lambda e, k=k, tt=tt, c0=c0, n=n, o0=o0, xTg=xTg: e.matmul(
                        psTM[:, o0:o0 + n], lhsT=xTg[:, k, tt * 128:(tt + 1) * 128], rhs=w_in_bf[:, k, c0:c0 + n],
                        start=(k == 0), stop=(k == 7)),
                      r=[("w_in_bf", k), Gk], w=["psTM"])
            A("act", lambda e: e.copy(out=zt[:], in_=psTM[:, 0:792]), r=["psTM"], w=["zt"])
            A("sp", lambda e, t=t: e.dma_start(out=kv_p[t * 128:(t + 1) * 128, :], in_=zt[:, 0:512]), r=["zt"], dma=True)
            if t >= 12:
                A("sp", lambda e, t=t: e.dma_start(out=win_p[(t - 12) * 128:(t - 11) * 128, :], in_=zt[:, 512:768]),
                  r=["zt"], dma=True)
            A("act", lambda e, t=t: e.activation(out=gates[:, t, :], in_=zt[:, 768:792], func=AF.Sigmoid),
              r=["zt"], w=[("gates", t)])
            for s3 in range(3):
                A("pool", lambda e, t=t, s3=s3: e.tensor_copy(
                    out=Vaug[:, t, s3, :, 0:64],
                    in_=zt[:, 128 + 256 * s3:256 + 256 * s3].rearrange("p (h d) -> p h d", d=64)),
                  r=["zt", "Vaug_init"], w=[("Vaug", t)])
            for h in range(2):
                A("pe", lambda e, t=t, h=h: e.matmul(psVC[0:64, h * 64:(h + 1) * 64], lhsT=PmB[h][:, 60 - 4 * t:124 - 4 * t],
                                                     rhs=Vaug[:, t, 0, h, 0:64], start=(t == 0 and h == 0), stop=(t == NT - 1),
                                                     skip_group_check=True),
                  r=[("PmB", h), ("Vaug", t)], w=["psVC"])
        for c4 in range(4):
            rk = [("glu", G % 2, c4), ("gluhead", G % 2), "cw"]

            def gsl(j, c4=c4, gl=gl):
                return gl[:, c4, j:j + 512]
            A("dve", lambda e, c4=c4, gsl=gsl: e.tensor_scalar(out=accA[:], in0=gsl(0), scalar1=cw[:, c4, 0:1],
                                                               scalar2=cw[:, c4, 31:32], op0=ALU.mult, op1=ALU.add),
              r=rk, w=["accA"])
            for j in range(1, 31):
                dst = accA[:] if j < 30 else ych[:, c4, :]
                A("dve", lambda e, c4=c4, j=j, gsl=gsl, dst=dst: e.scalar_tensor_tensor(
                    out=dst, in0=gsl(j), scalar=cw[:, c4, j:j + 1], in1=accA[:], op0=ALU.mult, op1=ALU.add),
                  r=rk + ["accA"], w=(["accA"] if j < 30 else [("ych", c4)]))
            A("act", lambda e, c4=c4: e.activation(out=ysq[:], in_=ych[:, c4, :], func=AF.Square),
              r=[("ych", c4)], w=["ysq"])
            A("pe", lambda e, c4=c4: e.matmul(psMean[:], lhsT=onesf[:], rhs=ych[:, c4, :], start=(c4 == 0), stop=(c4 == 3)),
              r=["onesf", ("ych", c4)], w=["psMean"])
            A("pe", lambda e, c4=c4: e.matmul(psMsq[:], lhsT=onesf[:], rhs=ysq[:], start=(c4 == 0), stop=(c4 == 3)),
              r=["onesf", "ysq"], w=["psMsq"])
        A("pool", lambda e, gl=gl, gln=gln: e.tensor_copy(out=gln[:, :, 0:30], in_=gl[:, :, 512:542]),
          r=[("glu", G % 2, c4) for c4 in range(4)], w=[("gluhead", (G + 1) % 2)])
        A("act", lambda e: e.copy(out=mean_sb[:], in_=psMean[:]), r=["psMean"], w=["mean_sb"])
        A("pool", lambda e: e.tensor_tensor(out=rstd_sb[:], in0=mean_sb[:], in1=mean_sb[:], op=ALU.mult),
          r=["mean_sb"], w=["rstd_sb"])
        A("dve", lambda e: e.tensor_tensor(out=rstd_sb[:], in0=psMsq[:], in1=rstd_sb[:], op=ALU.subtract),
          r=["psMsq", "rstd_sb"], w=["rstd_sb"])
        A("dve", lambda e: e.tensor_scalar(out=rstd_sb[:], in0=rstd_sb[:], scalar1=1e-5, scalar2=None,
                                           op0=ALU.add), r=["rstd_sb"], w=["rstd_sb"])
        A("act", lambda e: e.activation(out=rstd_sb[:], in_=rstd_sb[:], func=AF.Sqrt), r=["rstd_sb"], w=["rstd_sb"])
        A("dve", lambda e: e.reciprocal(out=rstd_sb[:], in_=rstd_sb[:]), r=["rstd_sb"], w=["rstd_sb"])
        for c4 in range(4):
            eng = "dve" if c4 % 2 == 0 else "pool"
            A(eng, lambda e, c4=c4: e.tensor_tensor(out=ych[:, c4, :], in0=ych[:, c4, :], in1=mean_sb[:], op=ALU.subtract),
              r=[("ych", c4), "mean_sb"], w=[("ych", c4)])
            A(eng, lambda e, c4=c4: e.tensor_tensor(out=ych[:, c4, :], in0=ych[:, c4, :], in1=rstd_sb[:], op=ALU.mult),
              r=[("ych", c4), "rstd_sb"], w=[("ych", c4)])
            A("act", lambda e, c4=c4, tok=tok: e.activation(out=convyT[:, c4, tok], in_=ych[:, c4, :], func=AF.Silu,
                                                            bias=cw[:, c4, 33:34], scale=cw[:, c4, 32:33]),
              r=[("ych", c4), "cw"], w=[("convyT", G)])
    A("act", lambda e: e.copy(out=vcaug[:, :, 0:64], in_=psVC[0:64, 0:128].rearrange("p (h d) -> p h d", d=64)),
      r=["psVC", "vcaug_init"], w=["vcaug"])
    for c4 in range(4):
        A("pe", lambda e, c4=c4: e.transpose(out=psF[0][0:30, c4 * 128:(c4 + 1) * 128], in_=glu[0][:, c4, 0:30], identity=identf[:]),
          r=[("gluhead", 0), "identf"], w=[("psF", 0)])
    A("act", lambda e: e.copy(out=cps[:], in_=psF[0][0:30, :]), r=[("psF", 0)], w=["cps"])
    A("sp", lambda e: e.dma_start(out=conv_p, in_=cps[:]), r=["cps"], dma=True)
    if debug:
        dbg["QT0"] = dout("d_QT0", [96, 4 * SEQ], BF16)
        A("sp", lambda e: e.dma_start(out=dbg["QT0"], in_=QTs[0][:]), r=[("QT", 0, G) for G in range(4)], dma=True)
        dbg["convyT"] = dout("d_convyT", [128, 4 * SEQ], BF16)
        A("sp", lambda e: e.dma_start(out=dbg["convyT"], in_=convyT[:]), r=[("convyT", G) for G in range(4)], dma=True)
        dbg["kcT0"] = dout("d_kcT0", [64, 64], BF16)
        A("sp", lambda e: e.dma_start(out=dbg["kcT0"], in_=kcT[0][:]), r=[("kcT", 0, G) for G in range(4)], dma=True)
        dbg["vcaug"] = dout("d_vcaug", [64, 130], BF16)
        A("sp", lambda e: e.dma_start(out=dbg["vcaug"], in_=vcaug[:]), r=["vcaug"], dma=True)


def build_p2(nc, S, P2, sb, pst, L):
    A = S.add
    QTs, KTs, KTw, kcT, Vaug, vcaug, gates, convyT = (L[k] for k in
                                                       ("QTs", "KTs", "KTw", "kcT", "Vaug", "vcaug", "gates", "convyT"))
    ident, identf, w_out_bf, h_acc, gbc_mlp, xp = (L[k] for k in
                                                   ("ident", "identf", "w_out_bf", "h_acc", "gbc_mlp", "xp"))
    debug, dbg, dout = L["debug"], L["dbg"], L["dout"]
    sbias = sb("sbias", [128, NT, 32], F32, P2)
    A("pool", lambda e: e.memset(sbias[:], 0.0), w=["sbias"])
    for qt in range(NT):
        for half in range(2):
            qb = 2 * qt + half
            ps_ = slice(64 * half, 64 * half + 64)
            if qb + 1 < 32:
                A("pool", lambda e, qt=qt, ps_=ps_, qb=qb: e.memset(sbias[ps_, qt, qb + 1:32], -1e30), r=["sbias"], w=["sbias"])
            A("pool", lambda e, qt=qt, ps_=ps_: e.memset(sbias[ps_, qt, 0:1], 5.0), r=["sbias"], w=["sbias"])
            A("pool", lambda e, qt=qt, ps_=ps_, qb=qb: e.memset(sbias[ps_, qt, max(qb - 1, 0):qb + 1], 5.0),
              r=["sbias"], w=["sbias"])
    pS = [pst(f"pS{i}", [128, 512], F32, P2) for i in range(2)]
    pOb = [pst(f"pO{i}", [128, 512], F32, P2) for i in range(3)]
    pO = [p[:, 0:260].rearrange("p (g d) -> p g d", d=65) for p in pOb]
    pM = pst("pM", [128, 512], F32, P2)
    pH = pst("pH", [128, 512], F32, P2)
    pTb = pst("pTb", [128, 8, 128], BF16, P2)
    PT = [sb(f"PT{i}", [128, 512], BF16, P2) for i in range(3)]
    NSEL = 3
    Ecmp = [sb(f"Ecmp{i}", [128, 8, 64], F32, P2) for i in range(NSEL)]
    zc = [sb(f"zc{i}", [128, 16], F32, P2) for i in range(NSEL)]
    pblk = [sb(f"pblk{i}", [128, 2, 32], F32, P2) for i in range(NSEL)]
    pg4 = [sb(f"pg4{i}", [128, 2, 64], F32, P2) for i in range(NSEL)]
    m8 = [sb(f"m8{i}", [128, 2, 8], F32, P2) for i in range(NSEL)]
    wk32 = [sb(f"wk32{i}", [128, 2, 32], F32, P2) for i in range(NSEL)]
    selT_in = [sb(f"selT_in{i}", [128, 2, 96], BF16, P2) for i in range(NSEL)]
    for i in range(NSEL):
        A("pool", lambda e, i=i: e.memset(selT_in[i][:], 0.0), w=[("selT_in", i)])
    coef = sb("coef", [128, 3, 4], F32, P2)
    zr = sb("zr", [128, 3, 4], F32, P2)
    osb = sb("osb", [128, 4, 64], F32, P2)
    otmp = sb("otmp", [128, 4, 64], F32, P2)
    attn = sb("attn", [128, 512], BF16, P2)
    attnT = sb("attnT", [128, 4, 128], BF16, P2)
    xt2 = [sb("xt2_0", [128, DM], F32, P2)] * 2
    pcnt = [0]
    scnt = [0]

    def score_exp(h, qt, lhsT, K, rhs_rows, masks, rkeys):
        sbuf_i = scnt[0] % 2
        scnt[0] += 1
        pb = pcnt[0] % 3
        pcnt[0] += 1
        M = lhsT.shape[1]
        A("pe", lambda e: e.matmul(pS[sbuf_i][0:M, :], lhsT=lhsT, rhs=QTs[h][0:K, :, qt * 128:(qt + 1) * 128],
                                   start=True, stop=True),
          r=rkeys + [("QT", h, qt // 4)] + ([("QTaug", h, qt)] if K == 96 else []), w=[("pS", sbuf_i)])
        A("act", lambda e: e.activation(out=PT[pb][0:M, :], in_=pS[sbuf_i][0:M, :], func=AF.Exp, scale=SCALE),
          r=[("pS", sbuf_i)], w=[("PT", pb)])
        for (cm, qs, base) in masks:
            A("pool", lambda e, cm=cm, qs=qs, base=base: e.affine_select(
                out=PT[pb][0:M, :].rearrange("p (g q) -> p g q", g=4), in_=PT[pb][0:M, :].rearrange("p (g q) -> p g q", g=4),
                pattern=[[0, 4], [qs, 128]], compare_op=ALU.is_ge, fill=0.0, base=base, channel_multiplier=cm),
              r=[("PT", pb)], w=[("PT", pb)])
        return pb

    pM2 = [pM, pH]
    for qt in range(NT):
        r = qt % NSEL
        pm = pM2[qt % 2]
        pmk = ("pM2", qt % 2)
        for hd in range(8):
            h, g = hd // 4, hd % 4
            A("pe", lambda e, h=h, g=g, hd=hd, qt=qt, pm=pm: e.matmul(pm[:, hd * 64:(hd + 1) * 64],
                                                                      lhsT=QTs[h][0:64, g, qt * 128:(qt + 1) * 128], rhs=kcT[h][:],
                                                                      start=True, stop=True),
              r=[("QT", h, qt // 4)] + [("kcT", h, G) for G in range(4)], w=[pmk])
        E = Ecmp[r]
        A("act", lambda e, E=E, pm=pm: e.activation(out=E[:], in_=pm[:].rearrange("p (g c) -> p g c", g=8), func=AF.Exp, scale=SCALE),
          r=[pmk], w=[("Ecmp", r)])
        A("pool", lambda e, qt=qt, E=E: e.affine_select(out=E[:], in_=E[:], pattern=[[0, 8], [-32, 64]], compare_op=ALU.is_ge,
                                                        fill=0.0, base=qt * 128 - 31, channel_multiplier=1),
          r=[("Ecmp", r)], w=[("Ecmp", r)])
        Z = zc[r]
        A("dve", lambda e, E=E, Z=Z: e.tensor_reduce(out=Z[:, 0:8], in_=E[:], axis=AX.X, op=ALU.add), r=[("Ecmp", r)], w=[("zc", r)])
        A("dve", lambda e, Z=Z: e.tensor_scalar(out=Z[:, 0:8], in0=Z[:, 0:8], scalar1=1e-30, scalar2=None, op0=ALU.add),
          r=[("zc", r)], w=[("zc", r)])
        A("dve", lambda e, Z=Z: e.reciprocal(out=Z[:, 8:16], in_=Z[:, 0:8]), r=[("zc", r)], w=[("zc", r)])
        A("dve", lambda e, E=E, Z=Z: e.tensor_tensor(out=E[:], in0=E[:], in1=Z[:, 8:16].unsqueeze(2).to_broadcast([128, 8, 64]),
                                                     op=ALU.mult), r=[("Ecmp", r), ("zc", r)], w=[("Ecmp", r)])
        for h in range(2):
            A("dve", lambda e, E=E, h=h, r=r: e.tensor_reduce(out=pg4[r][:, h, :], in_=E[:, h * 4:(h + 1) * 4, :].rearrange("p g c -> p c g"),
                                                              axis=AX.X, op=ALU.add), r=[("Ecmp", r)], w=[("pg4", r)])
        A("dve", lambda e, r=r: e.tensor_reduce(out=pblk[r][:], in_=pg4[r][:].rearrange("p h (b t) -> p h b t", t=2),
                                                axis=AX.X, op=ALU.add), r=[("pg4", r)], w=[("pblk", r)])
        A("dve", lambda e, r=r, qt=qt: e.tensor_tensor(out=pblk[r][:], in0=pblk[r][:],
                                                       in1=sbias[:, qt, :].unsqueeze(1).to_broadcast([128, 2, 32]), op=ALU.add),
          r=[("pblk", r), "sbias"], w=[("pblk", r)])
        for h in range(2):
            A("dve", lambda e, h=h, r=r: e.max(out=m8[r][:, h, :], in_=pblk[r][:, h, :]), r=[("pblk", r)], w=[("m8", r, h)])
            A("dve", lambda e, h=h, r=r: e.match_replace(out=wk32[r][:, h, :], in_to_replace=m8[r][:, h, :],
                                                         in_values=pblk[r][:, h, :], imm_value=-3e38),
              r=[("m8", r, h), ("pblk", r)], w=[("wk32", r, h)])
            A("dve", lambda e, h=h, r=r: e.max(out=m8[r][:, h, :], in_=wk32[r][:, h, :]), r=[("wk32", r, h)], w=[("m8", r, h)])
            A("dve", lambda e, h=h, r=r: e.tensor_scalar(out=wk32[r][:, h, :], in0=pblk[r][:, h, :], scalar1=m8[r][:, h, 7:8],
                                                         scalar2=-1.0, op0=ALU.is_ge, op1=ALU.add),
              r=[("pblk", r), ("m8", r, h)], w=[("wk32", r, h)])
        A("dve", lambda e, r=r: e.tensor_scalar(out=selT_in[r][:, :, 64:96], in0=wk32[r][:], scalar1=BIG, scalar2=None, op0=ALU.mult),
          r=[("wk32", r, 0), ("wk32", r, 1), ("selT_in", r)], w=[("selT_in", r)])
        for h in range(2):
            A("pe", lambda e, h=h, r=r: e.transpose(out=pTb[0:96, (qt % 2) * 2 + h if False else h, :], in_=selT_in[r][:, h, :], identity=ident[:]),
              r=[("selT_in", r), "ident"], w=[("pTbs", h)])
            A("act", lambda e, h=h, qt=qt: e.copy(out=QTs[h][64:96, :, qt * 128:(qt + 1) * 128],
                                                  in_=pTb[64:96, h, :].unsqueeze(1).to_broadcast([32, 4, 128])),
              r=[("pTbs", h)], w=[("QTaug", h, qt)])
    LOOK = 2
    pSx = [pS[0], pS[1], pM]
    Osb = sb("Osb", [128, 3, 4, 65], F32, P2)
    tiles = []
    for qt in range(NT):
        for h in range(2):
            tl = [dict(br=0, kt=0, lhsT=kcT[h][:], K=64, masks=[(-32, 1, qt * 128 - 31)],
                       rk=[("kcT", h, G) for G in range(4)], rhs=vcaug[:, h, :], rhsk="vcaug", first=True, last=True)]
            for kt in range(qt + 1):
                tl.append(dict(br=1, kt=kt, lhsT=KTs[h][:, kt * 128:(kt + 1) * 128], K=96, masks=[(-1, 1, 0)] if kt == qt else [],
                               rk=[("KTs", h, kt // 4), ("KTsaug", h)], rhs=Vaug[:, kt, 1, h, :], rhsk=("Vaug", kt),
                               first=(kt == 0), last=(kt == qt)))
            k0 = max(0, qt - 4)
            for kt in range(k0, qt + 1):
                masks = []
                if kt == qt:
                    masks.append((-1, 1, 0))
                if kt == qt - 4:
                    masks.append((1, -1, 0))
                tl.append(dict(br=2, kt=kt, lhsT=KTw[h][:, kt * 128:(kt + 1) * 128], K=64, masks=masks,
                               rk=[("KTw", h, kt // 4)], rhs=Vaug[:, kt, 2, h, :], rhsk=("Vaug", kt),
                               first=(kt == k0), last=(kt == qt)))
            for t_ in tl:
                t_["qt"], t_["h"] = qt, h
            tl[-1]["end"] = True
            tiles += tl

    def emit_qk(t_, i):
        si = i % 3
        pb = i % 3
        t_["pb"] = pb
        h, qt, K, lhsT = t_["h"], t_["qt"], t_["K"], t_["lhsT"]
        M = lhsT.shape[1]
        A("pe", lambda e: e.matmul(pSx[si][0:M, :], lhsT=lhsT, rhs=QTs[h][0:K, :, qt * 128:(qt + 1) * 128], start=True, stop=True),
          r=t_["rk"] + [("QT", h, qt // 4)] + ([("QTaug", h, qt)] if K == 96 else []), w=[("pSx", si)])
        A("act", lambda e: e.activation(out=PT[pb][0:M, :], in_=pSx[si][0:M, :], func=AF.Exp, scale=SCALE),
          r=[("pSx", si)], w=[("PT", pb)])
        for (cm, qs, base) in t_["masks"]:
            A("pool", lambda e, cm=cm, qs=qs, base=base: e.affine_select(
                out=PT[pb][0:M, :].rearrange("p (g q) -> p g q", g=4), in_=PT[pb][0:M, :].rearrange("p (g q) -> p g q", g=4),
                pattern=[[0, 4], [qs, 128]], compare_op=ALU.is_ge, fill=0.0, base=base, channel_multiplier=cm),
              r=[("PT", pb)], w=[("PT", pb)])

    def emit_pv(t_):
        pb, br = t_["pb"], t_["br"]
        M = t_["lhsT"].shape[1]
        for g in range(4):
            A("pe", lambda e, g=g: e.matmul(pO[br][:, g, :], lhsT=PT[pb][0:M, g * 128:(g + 1) * 128], rhs=t_["rhs"],
                                            start=(t_["first"] and g == 0), stop=t_["last"], skip_group_check=True),
              r=[("PT", pb), t_["rhsk"]], w=[("pO", br)])
        if t_["last"]:
            A("act", lambda e: e.copy(out=Osb[:, br, :, :], in_=pO[br][:]), r=[("pO", br)], w=[("Osb", br)])

    def emit_end(qt, h):
        A("dve", lambda e: e.tensor_scalar(out=zr[:], in0=Osb[:, :, :, 64], scalar1=1e-30, scalar2=None, op0=ALU.add),
          r=[("Osb", br) for br in range(3)], w=["zr"])
        A("dve", lambda e: e.reciprocal(out=zr[:], in_=zr[:]), r=["zr"], w=["zr"])
        A("dve", lambda e: e.tensor_tensor(
            out=coef[:], in0=zr[:], in1=gates[:, qt, h * 12:(h + 1) * 12].rearrange("p (g b) -> p b g", b=3), op=ALU.mult),
          r=["zr", ("gates", qt)], w=["coef"])
        for br in range(3):
            dst = osb if br == 0 else otmp
            A("dve", lambda e, br=br, dst=dst: e.tensor_tensor(
                out=dst[:], in0=Osb[:, br, :, 0:64], in1=coef[:, br, :].unsqueeze(2).to_broadcast([128, 4, 64]), op=ALU.mult),
              r=[("Osb", br), "coef"], w=["osb" if br == 0 else "otmp"])
            if br > 0:
                A("dve", lambda e: e.tensor_tensor(out=osb[:], in0=osb[:], in1=otmp[:], op=ALU.add),
                  r=["osb", "otmp"], w=["osb"])
        A("pool", lambda e: e.tensor_copy(out=attn[:, h * 256:(h + 1) * 256], in_=osb[:].rearrange("p g d -> p (g d)")),
          r=["osb"], w=[("attn", h)])
        if h == 0:
            return
        for c4 in range(4):
            A("pe", lambda e, c4=c4: e.transpose(out=pTb[:, c4, :], in_=attn[:, c4 * 128:(c4 + 1) * 128], identity=ident[:]),
              r=[("attn", 0), ("attn", 1), "ident"], w=["pTb", ("pTbs", 0), ("pTbs", 1)])
        A("act", lambda e: e.copy(out=attnT[:], in_=pTb[:, 0:4, :]), r=["pTb", ("pTbs", 0), ("pTbs", 1)], w=["attnT"])
        A("sp", lambda e: e.dma_start(out=xt2[0][:], in_=xp[qt * 128:(qt + 1) * 128, :]), w=["xt2"], dma=True)
        for half in range(2):
            for k in range(8):
                lhs = (convyT[:, k, qt * 128:(qt + 1) * 128] if k < 4 else attnT[:, k - 4, :])
                A("pe", lambda e, k=k, half=half, lhs=lhs: e.matmul(pH[:], lhsT=lhs, rhs=w_out_bf[:, k, half * 512:(half + 1) * 512],
                                                                    start=(k == 0), stop=(k == 7)),
                  r=[("w_out_bf", k), ("convyT", qt // 4), "attnT"], w=["pH", ("pM2", 1)])
            A("dve", lambda e, half=half: e.tensor_tensor(
                out=h_acc[:, qt, half * 512:(half + 1) * 512], in0=pH[:], in1=xt2[0][:, half * 512:(half + 1) * 512], op=ALU.add),
              r=["pH", "xt2"], w=[("h_acc", qt)])

    S.add("pe", lambda e: e.matmul(pM[0:64, 0:64], lhsT=kcT[0][:], rhs=kcT[0][:], start=True, stop=True),
          r=[("pM2", 0)], w=[("pSx", 2), ("pM2", 0)])
    for i in range(len(tiles) + LOOK):
        if i < len(tiles):
            emit_qk(tiles[i], i)
        j = i - LOOK
        if j >= 0:
            emit_pv(tiles[j])
            if tiles[j].get("end"):
                emit_end(tiles[j]["qt"], tiles[j]["h"])
    if debug:
        dbg["attn"] = dout("d_attn", [128, 512], BF16)
        A("sp", lambda e: e.dma_start(out=dbg["attn"], in_=attn[:]), r=[("attn", 0), ("attn", 1)], dma=True)
        dbg["h"] = dout("d_h", [128, NT * DM], F32)
        A("sp", lambda e: e.dma_start(out=dbg["h"], in_=h_acc[:]), r=[("h_acc", t) for t in range(NT)], dma=True)


def emit_norm_T(S, src, srckey, junk, ss, hn, gbc, gkey, ident, psT8, dst_k, dst_all, dstkey):
    A = S.add
    P = src.shape[0]
    A("pool", lambda e: e.memset(ss[0:P, 0:1], 0.0), w=[("ss", id(ss))])
    A("act", lambda e: e.activation(out=junk[0:P, :], in_=src, func=AF.Square, accum_out=ss[0:P, 0:1]),
      r=[srckey, ("ss", id(ss))], w=[("junk", id(junk)), ("ss", id(ss))])
    A("dve", lambda e: e.tensor_scalar(out=ss[0:P, 1:2], in0=ss[0:P, 0:1], scalar1=1.0 / DM, scalar2=1e-6,
                                       op0=ALU.mult, op1=ALU.add), r=[("ss", id(ss))], w=[("rs", id(ss))])
    A("act", lambda e: e.activation(out=ss[0:P, 1:2], in_=ss[0:P, 1:2], func=AF.Sqrt), r=[("rs", id(ss))], w=[("rs", id(ss))])
    A("dve", lambda e: e.reciprocal(out=ss[0:P, 1:2], in_=ss[0:P, 1:2]), r=[("rs", id(ss))], w=[("rs", id(ss))])
    A("dve", lambda e: e.scalar_tensor_tensor(out=hn[0:P, :], in0=src, scalar=ss[0:P, 1:2], in1=gbc[0:P, :],
                                              op0=ALU.mult, op1=ALU.mult), r=[srckey, ("rs", id(ss)), gkey], w=[("hn", id(hn))])
    for k in range(8):
        A("pe", lambda e, k=k: e.transpose(out=psT8[:, k, 0:P], in_=hn[0:P, k * 128:(k + 1) * 128], identity=ident[0:P, 0:P]),
          r=[("hn", id(hn)), "ident"], w=["pTb"])
    A("act", lambda e: e.copy(out=dst_all, in_=psT8[:, :, 0:P]), r=["pTb"], w=[dstkey])


def build_p3(nc, S, P3, sb, pst, L):
    A = S.add
    h_acc, hs_acc, hnTs, w_up, w_down, y_p, y_s, gbc_mlp, ident = (L[k] for k in (
        "h_acc", "hs_acc", "hnTs", "w_up", "w_down", "y_p", "y_s", "gbc_mlp", "ident"))
    hnT = sb("hnT", [128, 8, SEQ], BF16, P3)
    junk3 = sb("junk3", [128, DM], BF16, P3)
    ss3 = sb("ss3", [128, 2], F32, P3)
    hn = sb("hn", [128, DM], BF16, P3)
    pTb = pst("pTb3", [128, 8, 128], BF16, P3)
    for qt in range(NT):
        emit_norm_T(S, h_acc[:, qt, :], ("h_acc", qt), junk3, ss3, hn, gbc_mlp, "gbc_mlp", ident, pTb,
                    None, hnT[:, :, qt * 128:(qt + 1) * 128], ("hnT", qt))
    with_s = "samp" in L["stages"]
    stgU = [sb("stgU0", [128, 8, 512], F32, P3)] * 2
    stgD = [sb("stgD0", [128, 4, DM], F32, P3)] * 2
    gbc_fin = sb("gbc_fin", [128, DM], F32, P3)
    A("sp", lambda e: e.dma_start(out=gbc_fin[:], in_=L["g_fin"].partition_broadcast(128)), w=["gbc_fin"], dma=True)
    wu = [sb(f"wu{i}", [128, 8, 512], BF16, P3) for i in range(2)]
    wd = [sb(f"wd{i}", [128, 4, DM], BF16, P3) for i in range(2)]
    aT = [sb(f"aT{i}", [128, 4, 512], BF16, P3) for i in range(2)]
    aTs = sb("aTs", [128, 4, NSB], BF16, P3)
    rl = [sb(f"rl{i}", [128, 512], F32, P3) for i in range(2)]
    pU = [pst(f"pU{i}", [128, 512], F32, P3) for i in range(2)]
    pD = [pst(f"pD{i}", [128, 512], F32, P3) for i in range(2)]
    yo = [stgD[0][:, 0, :]] * 2
    ucnt = [0]
    dcnt = [0]
    acnt = [0]
    groups = list(range(4)) + (["s"] if with_s else [])

    def load_w(fg):
        b = fg % 2
        A("sp", lambda e, fg=fg, b=b: e.dma_start(out=stgU[b][:], in_=w_up[:, fg * 512:(fg + 1) * 512].rearrange(
            "(k p) f -> p k f", p=128)), w=["stgU"], dma=True)
        A("sp", lambda e, fg=fg, b=b: e.dma_start(out=stgD[b][:], in_=w_down[fg * 512:(fg + 1) * 512, :].rearrange(
            "(c p) d -> p c d", p=128)), w=["stgD"], dma=True)
        A("act", lambda e, b=b: e.copy(out=wu[b][:], in_=stgU[b][:]), r=["stgU"], w=[("wu", b)])
        A("pool", lambda e, b=b: e.tensor_copy(out=wd[b][:], in_=stgD[b][:]), r=["stgD"], w=[("wd", b)])

    def emit_up(fg, TG):
        b = fg % 2
        ntok = 512 if TG != "s" else NSB
        if TG == "s":
            rhs_of = lambda k: hnTs[:, k, :]
            rk = ["hnTs"]
            adst = aTs
            akey = "aTs"
        else:
            rhs_of = lambda k, TG=TG: hnT[:, k, TG * 512:(TG + 1) * 512]
            rk = [("hnT", t) for t in range(4 * TG, 4 * TG + 4)]
            ai = acnt[0] % 2
            acnt[0] += 1
            adst = aT[ai]
            akey = ("aT", ai)
        for fc in range(4):
            ui = ucnt[0] % 2
            ucnt[0] += 1
            for k in range(8):
                A("pe", lambda e, k=k, fc=fc, ui=ui, b=b, rhs_of=rhs_of, ntok=ntok: e.matmul(
                    pU[ui][:, 0:ntok], lhsT=wu[b][:, k, fc * 128:(fc + 1) * 128], rhs=rhs_of(k),
                    start=(k == 0), stop=(k == 7)), r=[("wu", b)] + rk, w=[("pU", ui)])
            A("act", lambda e, ui=ui, ntok=ntok: e.activation(out=rl[ui][:, 0:ntok], in_=pU[ui][:, 0:ntok], func=AF.Relu),
              r=[("pU", ui)], w=[("rl", ui)])
            A("pool" if fc % 2 else "dve", lambda e, ui=ui, fc=fc, adst=adst, ntok=ntok: e.tensor_tensor(
                out=adst[:, fc, 0:ntok], in0=rl[ui][:, 0:ntok], in1=rl[ui][:, 0:ntok], op=ALU.mult),
              r=[("rl", ui)], w=[(akey, fc)])
        return adst, akey

    def emit_down(fg, TG, adst, akey):
        b = fg % 2
        tiles = range(4) if TG != "s" else [0]
        for tt in tiles:
            for half in range(2):
                di = dcnt[0] % 2
                dcnt[0] += 1
                mrows = 128 if TG != "s" else NSB
                for fc in range(4):
                    lhs = adst[:, fc, tt * 128:(tt + 1) * 128] if TG != "s" else adst[:, fc, :]
                    A("pe", lambda e, fc=fc, half=half, di=di, lhs=lhs, b=b, mrows=mrows: e.matmul(
                        pD[di][0:mrows, :], lhsT=lhs, rhs=wd[b][:, fc, half * 512:(half + 1) * 512],
                        start=(fc == 0), stop=(fc == 3)), r=[(akey, fc), ("wd", b)], w=[("pD", di)])
                if TG != "s":
                    t = 4 * TG + tt
                    A("dve", lambda e, t=t, half=half, di=di: e.tensor_tensor(
                        out=h_acc[:, t, half * 512:(half + 1) * 512], in0=pD[di][:], in1=h_acc[:, t, half * 512:(half + 1) * 512],
                        op=ALU.add), r=[("pD", di), ("h_acc", t)], w=[("h_acc", t)])
                else:
                    A("dve", lambda e, half=half, di=di: e.tensor_tensor(
                        out=hs_acc[:, half * 512:(half + 1) * 512], in0=pD[di][0:NSB, :],
                        in1=hs_acc[:, half * 512:(half + 1) * 512], op=ALU.add), r=[("pD", di), "hs_acc"], w=["hs_acc"])

    load_w(0)
    pending = None
    for fg in range(8):
        for gi, TG in enumerate(groups):
            cur = emit_up(fg, TG)
            if gi == 1 and fg + 1 < 8:
                load_w(fg + 1)
            if pending is not None:
                emit_down(*pending)
            pending = (fg, TG) + cur
    emit_down(*pending)
    outs = [(h_acc[:, t, :], ("h_acc", t), y_p[t * 128:(t + 1) * 128, :], 128) for t in range(NT)]
    if with_s:
        outs.append((hs_acc[:], "hs_acc", y_s, NSB))
    for i, (src, skey, dst, P) in enumerate(outs):
        yb = i % 2
        A("pool", lambda e, P=P: e.memset(ss3[0:P, 0:1], 0.0), w=["ss3"])
        A("act", lambda e, src=src, P=P: e.activation(out=junk3[0:P, :], in_=src, func=AF.Square, accum_out=ss3[0:P, 0:1]),
          r=[skey, "ss3"], w=["junk3", "ss3"])
        A("dve", lambda e, P=P: e.tensor_scalar(out=ss3[0:P, 1:2], in0=ss3[0:P, 0:1], scalar1=1.0 / DM, scalar2=1e-6,
                                                op0=ALU.mult, op1=ALU.add), r=["ss3"], w=["rs3"])
        A("act", lambda e, P=P: e.activation(out=ss3[0:P, 1:2], in_=ss3[0:P, 1:2], func=AF.Sqrt), r=["rs3"], w=["rs3"])
        A("dve", lambda e, P=P: e.reciprocal(out=ss3[0:P, 1:2], in_=ss3[0:P, 1:2]), r=["rs3"], w=["rs3"])
        A("dve", lambda e, src=src, P=P, yb=yb: e.scalar_tensor_tensor(
            out=yo[yb][0:P], in0=src, scalar=ss3[0:P, 1:2], in1=gbc_fin[0:P, :], op0=ALU.mult, op1=ALU.mult),
          r=[skey, "rs3", "gbc_fin"], w=["stgD"])
        A("sp", lambda e, dst=dst, P=P, yb=yb: e.dma_start(out=dst, in_=yo[yb][0:P]), r=["stgD"], dma=True)


_NC_CACHE = {}


def _get_nc():
    if "nc" not in _NC_CACHE:
        _NC_CACHE["nc"] = build()[0]
    return _NC_CACHE["nc"]


def make_in_maps(inp, cores):
    f = lambda a: np.ascontiguousarray(np.asarray(a, dtype=np.float32))
    cache = f(inp["cache_kv"]).reshape(2560 * 128 * 4, 128)
    wdall = np.ascontiguousarray(np.concatenate(
        [f(inp["w_dw"])[0], f(inp["b_dw"]), f(inp["conv_ln_g"]), f(inp["conv_ln_b"])], axis=0))
    shared = dict(
        cache=cache, g_attn=f(inp["g_attn_norm"]), w_in=f(inp["w_in"])[0], wdall=wdall,
        w_ck=f(inp["w_cmp_k"])[0], w_cv=f(inp["w_cmp_v"])[0], w_out=f(inp["w_out"])[0], g_mlp=f(inp["g_mlp_norm"]),
        w_up=f(inp["w_up"])[0], w_down=f(inp["w_down"])[0], g_fin=f(inp["g_final"]).reshape(1, DM))
    maps = []
    for c in cores:
        sl = slice(c * NSB, (c + 1) * NSB)
        m = dict(shared)
        m["xp"] = f(inp["x_prompt"])[c]
        m["xs"] = f(inp["x_sample"])[sl, 0]
        m["cwin"] = f(inp["cache_win"])[0, sl].reshape(NSB, 512, 256)
        m["sconv"] = f(inp["state_conv"])[0, sl].reshape(NSB * 30, 512)
        m["ptab"] = np.ascontiguousarray(np.asarray(inp["page_table"], dtype=np.int32)[sl].reshape(1, NSB * 16))
        maps.append(m)
    return maps


def kernel(**inp):
    nc = _get_nc()
    cores = list(range(N_CORES))
    res = run_bass_kernel_spmd(nc, make_in_maps(inp, cores), core_ids=cores)
    R = res.results
    cat = lambda k: np.stack([np.asarray(r[k], dtype=np.float32) for r in R], axis=0)
    y_p = cat("y_p")
    y_s = cat("y_s").reshape(128, 1, DM)
    kv_p = cat("kv_p").reshape(1, 8, SEQ, 4, 2, 64)
    win_p = cat("win_p").reshape(1, 8, 512, 2, 2, 64)
    conv_p = cat("conv_p").reshape(1, 8, 30, 512)
    kv_s = cat("kv_s").reshape(1, 128, 1, 4, 2, 64)
    win_s = cat("win_s").reshape(1, 128, 512, 2, 2, 64)
    conv_s = cat("conv_s").reshape(1, 128, 30, 512)
    return (y_p, y_s, kv_p, win_p, conv_p, kv_s, win_s, conv_s)


def _dap(ap, offset, dims):
    return bass.AP(ap.tensor, offset, [list(d) for d in dims])


def build_samp(nc, S, SP, sb, pst, L):
    A = S.add
    (xs, cache, cwin, sconv, ptab, wdall, w_cv, kv_s, win_s, conv_s, w_in_bf, w_out_bf, ident, identf, gbc_mlp, hs_acc,
     hnTs, wckb, g_attn) = (L[k] for k in ("xs", "cache", "cwin", "sconv", "ptab", "wdall", "w_cv", "kv_s", "win_s",
                                            "conv_s", "w_in_bf", "w_out_bf", "ident", "identf", "gbc_mlp", "hs_acc",
                                            "hnTs", "wckb", "g_attn"))
    debug, dbg, dout = L["debug"], L["dbg"], L["dout"]
    zs_d = nc.dram_tensor("zs_d", [NSB, INC], F32, kind="Internal").ap()

    xs_sb = sb("xs_sb", [NSB, DM], F32, SP)
    gbc_a = sb("gbc_a", [NSB, DM], F32, SP)
    xnTs = sb("xnTs", [128, 8, NSB], BF16, SP)
    junk = sb("s_junk", [NSB, DM], BF16, SP)
    ssx = sb("s_ss", [128, 2], F32, SP)
    hn = sb("s_hn", [NSB, DM], BF16, SP)
    zs = sb("zs", [NSB, INC], F32, SP)
    pA = pst("s_pA", [128, 512], F32, SP)
    pB = pst("s_pB", [128, 512], F32, SP)
    pTb = pst("s_pTb", [128, 8, 128], BF16, SP)
    pKT = pst("s_pKT", [128, 8, 128], BF16, SP)
    pS = [pst(f"s_pS{i}", [128, 512], F32, SP) for i in range(3)]
    A("sp", lambda e: e.dma_start(out=xs_sb[:], in_=xs), w=["xs_sb"], dma=True)
    A("sp", lambda e: e.dma_start(out=gbc_a[:], in_=g_attn.partition_broadcast(NSB)), w=["gbc_a"], dma=True)
    emit_norm_T(S, xs_sb[:], "xs_sb", junk, ssx, hn, gbc_a, "gbc_a", ident, pTb, None, xnTs[:], "xnTs")
    for ci, c0 in enumerate(range(0, INC, 512)):
        n = min(512, INC - c0)
        for k in range(8):
            A("pe", lambda e, k=k, c0=c0, n=n: e.matmul(pA[0:NSB, 0:n], lhsT=xnTs[:, k, :], rhs=w_in_bf[:, k, c0:c0 + n],
                                                        start=(k == 0), stop=(k == 7)), r=["xnTs", ("w_in_bf", k)], w=["s_pA"])
        A("act", lambda e, c0=c0, n=n: e.copy(out=zs[:, c0:c0 + n], in_=pA[0:NSB, 0:n]), r=["s_pA"], w=["zs"])
    A("sp", lambda e: e.dma_start(out=zs_d, in_=zs[:]), r=["zs"], w=["zs_d"], dma=True)
    A("sp", lambda e: e.dma_start(out=kv_s, in_=zs[:, 1536:2048]), r=["zs"], dma=True)
    A("sp", lambda e: e.dma_start(out=win_s[:, 511, :], in_=zs[:, 2048:2304]), r=["zs"], dma=True)
    for b in range(NSB):
        A("sp", lambda e, b=b: e.dma_start(out=win_s[b, 0:511, :], in_=cwin[b, 1:512, :]), dma=True)
    A("sp", lambda e: e.dma_start(out=conv_s.rearrange("(b j) c -> b (j c)", j=30)[:, 0:29 * 512],
                                  in_=sconv.rearrange("(b j) c -> b (j c)", j=30)[:, 512:30 * 512]), dma=True)
    if int(os.environ.get("SAMP_STOP", "99")) <= 1:
        return
    Qrows = sb("Qrows", [128, 64], F32, SP)
    Kn = sb("Kn", [128, 2, 64], F32, SP)
    Vn = sb("Vn", [128, 2, 64], F32, SP)
    Grows = sb("Grows", [128, 3], F32, SP)
    for b in range(NSB):
        rows = slice(8 * b, 8 * b + 8)
        A("sp", lambda e, b=b, rows=rows: e.dma_start(out=Qrows[rows, :], in_=_dap(zs_d, b * INC + 1024, [[64, 8], [1, 64]])),
          r=["zs_d"], w=["Qrows"], dma=True)
        for j, col in enumerate((1536 + 256, 1536 + 512)):
            A("sp", lambda e, b=b, rows=rows, j=j, col=col: e.dma_start(
                out=Kn[rows, j, :], in_=_dap(zs_d, b * INC + col, [[64, 2], [0, 4], [1, 64]])), r=["zs_d"], w=["Kn"], dma=True)
        for j, col in enumerate((1536 + 384, 1536 + 640)):
            A("sp", lambda e, b=b, rows=rows, j=j, col=col: e.dma_start(
                out=Vn[rows, j, :], in_=_dap(zs_d, b * INC + col, [[64, 2], [0, 4], [1, 64]])), r=["zs_d"], w=["Vn"], dma=True)
        A("sp", lambda e, b=b, rows=rows: e.dma_start(out=Grows[rows, :], in_=_dap(zs_d, b * INC + 2304, [[3, 8], [1, 3]])),
          r=["zs_d"], w=["Grows"], dma=True)
    QTpad = sb("QTpad", [128, NSB, 128], BF16, SP)
    A("pool", lambda e: e.memset(QTpad[:], 0.0), w=["QTpad"])
    qsrc = sb("qsrc", [NSB, 4, 2, 64], F32, SP)
    A("dve", lambda e: e.tensor_copy(out=qsrc[:], in_=zs[:, 1024:1536].rearrange("p (h g d) -> p g h d", h=2, g=4)),
      r=["zs"], w=["qsrc"])
    for g in range(4):
        A("pe", lambda e, g=g: e.transpose(out=pB[:, g * 16:(g + 1) * 16], in_=qsrc[:, g, :, :].rearrange("p h d -> p (h d)"),
                                           identity=identf[0:NSB, 0:NSB]), r=["qsrc", "identf"], w=["s_pB"])
    QTflat = QTpad[:].rearrange("p b c -> p (b c)")
    for h in range(2):
        for g in range(4):
            c0 = 4 * h + g
            A("act", lambda e, h=h, g=g, c0=c0: e.copy(out=QTflat[64 * h:64 * h + 64, c0:c0 + 136 * 15 + 1:136],
                                                       in_=pB[64 * h:64 * h + 64, g * 16:(g + 1) * 16]),
              r=["s_pB", "QTpad"], w=["QTpad"])
    if int(os.environ.get("SAMP_STOP", "99")) <= 2:
        return
    ptb = sb("ptb", [128, NSB * 16], I32, SP)
    iop = sb("iop", [128, 1], I32, SP)
    idx = sb("idx", [128, NSB * 16], I32, SP)
    A("sp", lambda e: e.dma_start(out=ptb[:], in_=ptab.partition_broadcast(128)), w=["ptb"], dma=True)
    A("pool", lambda e: e.iota(iop[:], pattern=[[0, 1]], base=0, channel_multiplier=1), w=["iop"])
    A("dve", lambda e: e.tensor_scalar(out=idx[:], in0=ptb[:], scalar1=128, scalar2=iop[:, 0:1], op0=ALU.mult, op1=ALU.add),
      r=["ptb", "iop"], w=["idx"])
    cacheR = cache.rearrange("(r s) c -> r (s c)", s=4)
    pidx = sb("pidx", [128, 4], I32, SP)
    pf = sb("pf", [128, 4], F32, SP)
    A("pool", lambda e: e.iota(pidx[:, 0:1], pattern=[[0, 1]], base=0, channel_multiplier=1), w=["pidx"])
    A("dve", lambda e: e.tensor_single_scalar(out=pidx[:, 1:2], in_=pidx[:, 0:1], scalar=2, op=ALU.arith_shift_right),
      r=["pidx"], w=["pidx"])
    A("dve", lambda e: e.tensor_single_scalar(out=pidx[:, 2:3], in_=pidx[:, 1:2], scalar=1, op=ALU.bitwise_and),
      r=["pidx"], w=["pidx"])
    A("dve", lambda e: e.tensor_copy(out=pf[:, 0:3], in_=pidx[:, 0:3]), r=["pidx"], w=["pf"])
    coli = sb("coli", [128, 128], I32, SP)
    colf = sb("colf", [128, 128], F32, SP)
    GG = sb("GG", [128, 128], F32, SP)
    A("pool", lambda e: e.iota(coli[:], pattern=[[1, 128]], base=0, channel_multiplier=0), w=["coli"])
    A("dve", lambda e: e.tensor_single_scalar(out=coli[:], in_=coli[:], scalar=2, op=ALU.arith_shift_right), r=["coli"], w=["coli"])
    A("dve", lambda e: e.tensor_copy(out=colf[:], in_=coli[:]), r=["coli"], w=["colf"])
    A("dve", lambda e: e.tensor_scalar(out=GG[:], in0=colf[:], scalar1=pf[:, 1:2], scalar2=None, op0=ALU.is_equal),
      r=["colf", "pf"], w=["GG"])
    Mh = sb("Mh", [128, 2, 16], F32, SP)
    for half in range(2):
        A("dve", lambda e, half=half: e.tensor_scalar(out=Mh[:, half, :], in0=colf[:, 0:64:4], scalar1=float(16 * half),
                                                      scalar2=pf[:, 1:2], op0=ALU.add, op1=ALU.is_equal),
          r=["colf", "pf"], w=["Mh"])
    wcvb = sb("wcvb", [128, 2, 32], F32, SP)
    for h in range(2):
        A("sp", lambda e, h=h: e.dma_start(out=wcvb[:, h, :], in_=w_cv[:, h:h + 1].rearrange("j o -> o j").partition_broadcast(128),
                                           allow_slow_non_contiguous=True), w=["wcvb"], dma=True)
    Wk = sb("Wk", [128, 32], F32, SP)
    Wv = sb("Wv", [128, 32], F32, SP)
    wtmp = sb("wtmp", [128, 32], F32, SP)
    for (src, dst, key) in ((wckb, Wk, "Wk"), (wcvb, Wv, "Wv")):
        A("dve", lambda e, src=src: e.tensor_tensor(out=wtmp[:], in0=src[:, 1, :], in1=src[:, 0, :], op=ALU.subtract),
          r=["wckb", "wcvb"], w=["wtmp"])
        A("dve", lambda e, src=src, dst=dst: e.scalar_tensor_tensor(out=dst[:], in0=wtmp[:], scalar=pf[:, 2:3], in1=src[:, 0, :],
                                                                    op0=ALU.mult, op1=ALU.add), r=["wtmp", "pf", "wckb", "wcvb"], w=[key])
    if int(os.environ.get("SAMP_STOP", "99")) <= 3:
        return
    SS_ = sb("SS_", [128, 2, SEQ], F32, SP)
    Scmp = SS_[:, 0, :]
    Ssel = SS_[:, 1, :]
    Swin = sb("Swin", [128, 512], F32, SP)
    NKB = 3
    kst = [sb(f"kst{i}", [128, 4, 4, 128], F32, SP) for i in range(NKB)]
    kbf = [sb(f"kbf{i}", [128, 4, 2, 128], BF16, SP) for i in range(2)]
    KTsb = [sb(f"KTsb{i}", [128, 2, 512], BF16, SP) for i in range(2)]
    it = 0
    for pg in range(5):
        for b in range(NSB):
            bi = it % NKB
            b2 = it % 2
            it += 1
            if pg < 4:
                for i in range(4):
                    col = b * 16 + pg * 4 + i
                    A("pool", lambda e, bi=bi, i=i, col=col: e.indirect_dma_start(
                        out=kst[bi][:, i, :, :].rearrange("p s c -> p (s c)"), out_offset=None, in_=cacheR,
                        in_offset=bass.IndirectOffsetOnAxis(ap=idx[:, col:col + 1], axis=0)),
                      r=["idx"], w=[("kst", bi, i)], dma=True)
                A("act", lambda e, bi=bi, b2=b2: e.copy(out=kbf[b2][:], in_=kst[bi][:, :, 0:4:2, :]),
                  r=[("kst", bi, i) for i in range(4)], w=[("kbf", b2)])
                for i in range(4):
                    for si in range(2):
                        A("pe", lambda e, b2=b2, i=i, si=si: e.transpose(out=pKT[:, si * 4 + i, :], in_=kbf[b2][:, i, si, :],
                                                                         identity=ident[:]), r=[("kbf", b2), "ident"], w=["s_pKT"])
                A("dve", lambda e, b2=b2: e.tensor_copy(out=KTsb[b2][:], in_=pKT[:].rearrange("p (s i) t -> p s (i t)", s=2)),
                  r=["s_pKT"], w=[("KTsb", b2)])
                for si in range(2):
                    A("pe", lambda e, b2=b2, si=si, b=b: e.matmul(pS[si][:], lhsT=QTpad[:, b, :], rhs=KTsb[b2][:, si, :],
                                                                  start=(b == 0), stop=(b == NSB - 1)),
                      r=["QTpad", ("KTsb", b2)], w=[("s_pS", si)])
            else:
                A("sp", lambda e, bi=bi, b=b: e.dma_start(
                    out=kst[bi][:, :, 0, :], in_=cwin[b].rearrange("(i p) (s c) -> p i s c", p=128, c=128)[:, :, 0, :]),
                  w=[("kst", bi, i) for i in range(4)], dma=True)
                A("act", lambda e, bi=bi, b2=b2: e.copy(out=kbf[b2][:, :, 0, :], in_=kst[bi][:, :, 0, :]),
                  r=[("kst", bi, i) for i in range(4)], w=[("kbf", b2)])
                for i in range(4):
                    A("pe", lambda e, b2=b2, i=i: e.transpose(out=pKT[:, i, :], in_=kbf[b2][:, i, 0, :], identity=ident[:]),
                      r=[("kbf", b2), "ident"], w=["s_pKT"])
                A("dve", lambda e, b2=b2: e.tensor_copy(out=KTsb[b2][:, 0, :], in_=pKT[:, 0:4, :].rearrange("p i t -> p (i t)")),
                  r=["s_pKT"], w=[("KTsb", b2)])
                A("pe", lambda e, b2=b2, b=b: e.matmul(pS[2][:], lhsT=QTpad[:, b, :], rhs=KTsb[b2][:, 0, :],
                                                       start=(b == 0), stop=(b == NSB - 1)), r=["QTpad", ("KTsb", b2)], w=[("s_pS", 2)])
        if pg < 4:
            A("act", lambda e, pg=pg: e.copy(out=SS_[:, 0, pg * 512:(pg + 1) * 512], in_=pS[0][:]), r=[("s_pS", 0)], w=["Scmp"])
            A("dve", lambda e, pg=pg: e.tensor_copy(out=SS_[:, 1, pg * 512:(pg + 1) * 512], in_=pS[1][:]), r=[("s_pS", 1)], w=["Ssel"])
        else:
            A("act", lambda e: e.copy(out=Swin[:], in_=pS[2][:]), r=[("s_pS", 2)], w=["Swin"])
    if int(os.environ.get("SAMP_STOP", "99")) <= 4:
        return
    big = sb("s_big", [128, SEQ], F32, SP)
    sc = sb("s_sc", [128, 64], F32, SP)
    st = sb("s_st", [128, 16], F32, SP)
    pb32 = sb("s_pb32", [128, 32], F32, SP)
    m8 = sb("s_m8", [128, 8], F32, SP)
    wk32 = sb("s_wk32", [128, 32], F32, SP)
    selm = sb("s_selm", [128, 32], F32, SP)
    Pc = sb("Pc", [128, SEQ], BF16, SP)
    Ps = sb("Ps", [128, SEQ], BF16, SP)
    Pw = sb("Pw", [128, 512], BF16, SP)
    A("dve", lambda e: e.tensor_tensor(out=big[:].rearrange("p (c j) -> p c j", j=32), in0=Scmp.rearrange("p (c j) -> p c j", j=32),
                                       in1=Wk[:].unsqueeze(1).to_broadcast([128, 64, 32]), op=ALU.mult), r=["Scmp", "Wk"], w=["s_big"])
    A("dve", lambda e: e.tensor_reduce(out=sc[:], in_=big[:].rearrange("p (c j) -> p c j", j=32), axis=AX.X, op=ALU.add),
      r=["s_big"], w=["s_sc"])
    A("pool", lambda e: e.memset(st[:], 0.0), w=["s_st"])
    A("act", lambda e: e.activation(out=sc[:], in_=sc[:], func=AF.Exp, scale=SCALE, accum_out=st[:, 0:1]), r=["s_sc", "s_st"], w=["s_sc", "s_st"])
    A("dve", lambda e: e.reciprocal(out=st[:, 1:2], in_=st[:, 0:1]), r=["s_st"], w=["s_st"])
    A("dve", lambda e: e.tensor_scalar(out=sc[:], in0=sc[:], scalar1=st[:, 1:2], scalar2=None, op0=ALU.mult), r=["s_sc", "s_st"], w=["s_sc"])
    A("pe", lambda e: e.matmul(pA[:, 0:64], lhsT=GG[:], rhs=sc[:], start=True, stop=True), r=["GG", "s_sc"], w=["s_pA"])
    A("dve", lambda e: e.tensor_reduce(out=pb32[:], in_=pA[:, 0:64].rearrange("p (b t) -> p b t", t=2), axis=AX.X, op=ALU.add),
      r=["s_pA"], w=["s_pb32"])
    A("dve", lambda e: e.tensor_scalar(out=pb32[:, 0:1], in0=pb32[:, 0:1], scalar1=5.0, scalar2=None, op0=ALU.add), r=["s_pb32"], w=["s_pb32"])
    A("dve", lambda e: e.tensor_scalar(out=pb32[:, 31:32], in0=pb32[:, 31:32], scalar1=5.0, scalar2=None, op0=ALU.add), r=["s_pb32"], w=["s_pb32"])
    A("dve", lambda e: e.max(out=m8[:], in_=pb32[:]), r=["s_pb32"], w=["s_m8"])
    A("dve", lambda e: e.match_replace(out=wk32[:], in_to_replace=m8[:], in_values=pb32[:], imm_value=-3e38), r=["s_m8", "s_pb32"], w=["s_wk32"])
    A("dve", lambda e: e.max(out=m8[:], in_=wk32[:]), r=["s_wk32"], w=["s_m8"])
    A("dve", lambda e: e.tensor_scalar(out=selm[:], in0=pb32[:], scalar1=m8[:, 6:7], scalar2=None, op0=ALU.is_ge), r=["s_pb32", "s_m8"], w=["s_selm"])
    A("dve", lambda e: e.tensor_tensor(out=big[:].rearrange("p (c j) -> p c j", j=32), in0=sc[:].unsqueeze(2).to_broadcast([128, 64, 32]),
                                       in1=Wv[:].unsqueeze(1).to_broadcast([128, 64, 32]), op=ALU.mult), r=["s_sc", "Wv", "s_big"], w=["s_big"])
    A("act", lambda e: e.copy(out=Pc[:], in_=big[:]), r=["s_big"], w=["Pc"])
    for j in range(2):
        A("dve", lambda e, j=j: e.tensor_tensor(out=big[:, 0:64], in0=Qrows[:], in1=Kn[:, j, :], op=ALU.mult), r=["Qrows", "Kn", "s_big", "Pc"], w=["s_big"])
        A("dve", lambda e, j=j: e.tensor_reduce(out=st[:, 2 + 2 * j:3 + 2 * j], in_=big[:, 0:64], axis=AX.X, op=ALU.add), r=["s_big"], w=["s_st"])
        A("act", lambda e, j=j: e.activation(out=st[:, 3 + 2 * j:4 + 2 * j], in_=st[:, 2 + 2 * j:3 + 2 * j], func=AF.Exp, scale=SCALE),
          r=["s_st"], w=["s_st"])
    A("act", lambda e: e.activation(out=big[:], in_=Ssel, func=AF.Exp, scale=SCALE), r=["Ssel", "s_big"], w=["s_big"])
    A("dve", lambda e: e.tensor_tensor(out=big[:].rearrange("p (b t) -> p b t", t=64), in0=big[:].rearrange("p (b t) -> p b t", t=64),
                                       in1=selm[:].unsqueeze(2).to_broadcast([128, 32, 64]), op=ALU.mult), r=["s_big", "s_selm"], w=["s_big"])
    A("dve", lambda e: e.tensor_reduce(out=st[:, 6:7], in_=big[:], axis=AX.X, op=ALU.add), r=["s_big"], w=["s_st"])
    A("pool", lambda e: e.tensor_copy(out=Ps[:], in_=big[:]), r=["s_big"], w=["Ps"])
    A("act", lambda e: e.activation(out=big[:, 0:512], in_=Swin[:], func=AF.Exp, scale=SCALE), r=["Swin", "s_big", "Ps"], w=["s_big"])
    A("dve", lambda e: e.tensor_reduce(out=st[:, 7:8], in_=big[:, 0:512], axis=AX.X, op=ALU.add), r=["s_big"], w=["s_st"])
    A("pool", lambda e: e.tensor_copy(out=Pw[:], in_=big[:, 0:512]), r=["s_big"], w=["Pw"])
    A("dve", lambda e: e.tensor_tensor(out=st[:, 8:9], in0=st[:, 6:7], in1=st[:, 3:4], op=ALU.add), r=["s_st"], w=["s_st"])
    A("dve", lambda e: e.tensor_tensor(out=st[:, 9:10], in0=st[:, 7:8], in1=st[:, 5:6], op=ALU.add), r=["s_st"], w=["s_st"])
    A("dve", lambda e: e.reciprocal(out=st[:, 8:10], in_=st[:, 8:10]), r=["s_st"], w=["s_st"])
    if int(os.environ.get("SAMP_STOP", "99")) <= 5:
        return
    PTc = sb("PTc", [128, 16, 128], BF16, SP)
    PTs = sb("PTs", [128, 16, 128], BF16, SP)
    PTw = sb("PTw", [128, 4, 128], BF16, SP)
    for (src, dst, key, n) in ((Pc, PTc, "PTc", 16), (Ps, PTs, "PTs", 16), (Pw, PTw, "PTw", 4)):
        for i0 in range(0, n, 8):
            m = min(8, n - i0)
            for i in range(m):
                A("pe", lambda e, src=src, i=i, i0=i0: e.transpose(out=pKT[:, i, :], in_=src[:, (i0 + i) * 128:(i0 + i + 1) * 128],
                                                                   identity=ident[:]), r=["ident", "Pc", "Ps", "Pw"], w=["s_pKT"])
            A("dve", lambda e, dst=dst, i0=i0, m=m: e.tensor_copy(out=dst[:, i0:i0 + m, :], in_=pKT[:, 0:m, :]), r=["s_pKT"], w=[key])
    S.barrier()
    vst = [SS_[:, vi, :].rearrange("p (i s c) -> p i s c", i=4, s=4) for vi in range(2)]
    vbf = [sb(f"vbf{i}", [128, 4, 2, 128], BF16, SP) for i in range(2)]
    vwst = [sb(f"vwst{i}", [128, 4, 128], F32, SP) for i in range(2)]
    vwbf = [sb(f"vwbf{i}", [128, 4, 128], BF16, SP) for i in range(2)]
    Oacc = sb("Oacc", [128, 3, 64], F32, SP)
    otmp = big[:, 0:1024].rearrange("p (b h d) -> p b h d", b=8, h=2)
    ored = sb("s_ored", [128, 64], F32, SP)
    A("pool", lambda e: e.memset(Oacc[:], 0.0), w=["Oacc"])
    pW2 = pst("s_pW2", [128, 512], F32, SP)
    pO = [[pA, pB], [pS[0], pS[1]], [pS[2], pW2]]
    pkeys = [["s_pA", "s_pB"], [("s_pS", 0), ("s_pS", 1)], [("s_pS", 2), "s_pW2"]]
    vcnt = 0
    for half in range(2):
        for bl in range(8):
            b = half * 8 + bl
            bank, cb = bl // 4, (bl % 4) * 128
            wi = b % 2
            A("sp", lambda e, wi=wi, b=b: e.dma_start(
                out=vwst[wi][:], in_=cwin[b].rearrange("(i p) (s c) -> p i s c", p=128, c=128)[:, :, 1, :]), w=[("vwst", wi)], dma=True)
            A("act", lambda e, wi=wi: e.copy(out=vwbf[wi][:], in_=vwst[wi][:]), r=[("vwst", wi)], w=[("vwbf", wi)])
            for q4 in range(4):
                vi = vcnt % 2
                vcnt += 1
                for i in range(4):
                    col = b * 16 + q4 * 4 + i
                    A("pool", lambda e, vi=vi, i=i, col=col: e.indirect_dma_start(
                        out=vst[vi][:, i, :, :].rearrange("p s c -> p (s c)"), out_offset=None, in_=cacheR,
                        in_offset=bass.IndirectOffsetOnAxis(ap=idx[:, col:col + 1], axis=0)), r=["idx"], w=[("vst", vi, i)], dma=True)
                A("act", lambda e, vi=vi: e.copy(out=vbf[vi][:], in_=vst[vi][:, :, 1:4:2, :]),
                  r=[("vst", vi, i) for i in range(4)], w=[("vbf", vi)])
                for br, (PT_, ptk) in enumerate(((PTc, "PTc"), (PTs, "PTs"))):
                    for i in range(4):
                        pgi = q4 * 4 + i
                        A("pe", lambda e, br=br, bank=bank, cb=cb, PT_=PT_, i=i, pgi=pgi, vi=vi: e.matmul(
                            pO[br][bank][:, cb:cb + 128], lhsT=PT_[:, pgi, :], rhs=vbf[vi][:, i, br, :],
                            start=(pgi == 0), stop=(pgi == 15)), r=[ptk, ("vbf", vi)], w=[pkeys[br][bank]])
            for i in range(4):
                A("pe", lambda e, bank=bank, cb=cb, i=i, wi=wi: e.matmul(
                    pO[2][bank][:, cb:cb + 128], lhsT=PTw[:, i, :], rhs=vwbf[wi][:, i, :], start=(i == 0), stop=(i == 3)),
                  r=["PTw", ("vwbf", wi)], w=[pkeys[2][bank]])
        for br in range(3):
            for bank in range(2):
                A("dve", lambda e, br=br, bank=bank, half=half: e.tensor_tensor(
                    out=otmp[:, bank * 4:(bank + 1) * 4, :, :].rearrange("p b h d -> p (b h) d"),
                    in0=pO[br][bank][:].rearrange("p (j d) -> p j d", d=64),
                    in1=Mh[:, half, bank * 8:(bank + 1) * 8].unsqueeze(2).to_broadcast([128, 8, 64]), op=ALU.mult),
                  r=[pkeys[br][bank], "Mh"], w=["s_otmp"])
            A("dve", lambda e: e.tensor_reduce(out=ored[:], in_=otmp.rearrange("p b h d -> p d (b h)"), axis=AX.X, op=ALU.add),
              r=["s_otmp"], w=["s_ored"])
            A("dve", lambda e, br=br: e.tensor_tensor(out=Oacc[:, br, :], in0=Oacc[:, br, :], in1=ored[:], op=ALU.add),
              r=["s_ored", "Oacc"], w=["Oacc"])
    if int(os.environ.get("SAMP_STOP", "99")) <= 6:
        return
    G3 = sb("G3", [128, 3], F32, SP)
    A("act", lambda e: e.activation(out=G3[:], in_=Grows[:], func=AF.Sigmoid), r=["Grows"], w=["G3"])
    arow = sb("arow", [128, 128], F32, SP)
    tmp64 = sb("tmp64", [128, 64], F32, SP)
    A("dve", lambda e: e.tensor_scalar(out=arow[:, 0:64], in0=Oacc[:, 0, :], scalar1=G3[:, 0:1], scalar2=None, op0=ALU.mult),
      r=["Oacc", "G3"], w=["arow"])
    for j, br in ((0, 1), (1, 2)):
        A("dve", lambda e, j=j, br=br: e.scalar_tensor_tensor(out=tmp64[:], in0=Vn[:, j, :], scalar=st[:, 3 + 2 * j:4 + 2 * j],
                                                              in1=Oacc[:, br, :], op0=ALU.mult, op1=ALU.add),
          r=["Vn", "s_st", "Oacc"], w=["tmp64"])
        A("dve", lambda e, j=j, br=br: e.tensor_scalar(out=tmp64[:], in0=tmp64[:], scalar1=st[:, 8 + j:9 + j], scalar2=G3[:, br:br + 1],
                                                       op0=ALU.mult, op1=ALU.mult), r=["tmp64", "s_st", "G3"], w=["tmp64"])
        A("dve", lambda e: e.tensor_tensor(out=arow[:, 0:64], in0=arow[:, 0:64], in1=tmp64[:], op=ALU.add), r=["arow", "tmp64"], w=["arow"])
    if int(os.environ.get("SAMP_STOP", "99")) <= 7:
        return
    glus = sb("glus", [NSB, 512], F32, SP)
    cvb = sb("cvb", [NSB, 4, 512], F32, SP)
    A("sp", lambda e: e.dma_start(out=cvb[:], in_=wdall[30:34, :].partition_broadcast(NSB)), w=["cvb"], dma=True)
    A("act", lambda e: e.activation(out=glus[:], in_=zs[:, 512:1024], func=AF.Sigmoid), r=["zs"], w=["glus"])
    A("dve", lambda e: e.tensor_tensor(out=glus[:], in0=glus[:], in1=zs[:, 0:512], op=ALU.mult), r=["glus", "zs"], w=["glus"])
    A("sp", lambda e: e.dma_start(out=conv_s.rearrange("(b j) c -> b j c", j=30)[:, 29, :], in_=glus[:]), r=["glus"], dma=True)
    Xc = sb("Xc", [120, 512], F32, SP)
    Wrep = sb("Wrep", [120, 512], F32, SP)
    sel4 = sb("sel4", [120, 4, NSB], F32, SP)
    for r4 in range(4):
        A("sp", lambda e, r4=r4: e.dma_start(out=Wrep[r4 * 30:(r4 + 1) * 30, :], in_=wdall[0:30, :]), w=["Wrep"], dma=True)
    A("pool", lambda e: e.memset(sel4[:], 1.0), w=["sel4"])
    for i4 in range(4):
        A("pool", lambda e, i4=i4: e.affine_select(out=sel4[:, i4, :], in_=sel4[:, i4, :], pattern=[[-30, NSB]], compare_op=ALU.is_ge,
                                                   fill=0.0, base=120 * i4, channel_multiplier=1), r=["sel4"], w=["sel4"])
        A("pool", lambda e, i4=i4: e.affine_select(out=sel4[:, i4, :], in_=sel4[:, i4, :], pattern=[[30, NSB]], compare_op=ALU.is_ge,
                                                   fill=0.0, base=29 - 120 * i4, channel_multiplier=-1), r=["sel4"], w=["sel4"])
    for i4 in range(4):
        A("sp", lambda e, i4=i4: e.dma_start(out=Xc[:], in_=sconv[i4 * 120:(i4 + 1) * 120, :]), w=["Xc"], dma=True)
        A("dve", lambda e: e.tensor_tensor(out=Xc[:], in0=Xc[:], in1=Wrep[:], op=ALU.mult), r=["Xc", "Wrep"], w=["Xc"])
        A("pe", lambda e, i4=i4: e.matmul(pB[0:NSB, :], lhsT=sel4[:, i4, :], rhs=Xc[:], start=(i4 == 0), stop=(i4 == 3)),
          r=["sel4", "Xc"], w=["s_pB"])
    yc = sb("yc", [NSB, 512], F32, SP)
    ycs = sb("ycs", [NSB, 4], F32, SP)
    A("dve", lambda e: e.tensor_tensor(out=yc[:], in0=glus[:], in1=cvb[:, 0, :], op=ALU.mult), r=["glus", "cvb"], w=["yc"])
    A("dve", lambda e: e.tensor_tensor(out=yc[:], in0=yc[:], in1=pB[0:NSB, :], op=ALU.add), r=["yc", "s_pB"], w=["yc"])
    A("dve", lambda e: e.tensor_tensor(out=yc[:], in0=yc[:], in1=cvb[:, 1, :], op=ALU.add), r=["yc", "cvb"], w=["yc"])
    A("dve", lambda e: e.tensor_reduce(out=ycs[:, 0:1], in_=yc[:], axis=AX.X, op=ALU.add), r=["yc"], w=["ycs"])
    A("dve", lambda e: e.tensor_scalar(out=ycs[:, 0:1], in0=ycs[:, 0:1], scalar1=1.0 / 512.0, scalar2=None, op0=ALU.mult), r=["ycs"], w=["ycs"])
    A("dve", lambda e: e.tensor_scalar(out=yc[:], in0=yc[:], scalar1=ycs[:, 0:1], scalar2=None, op0=ALU.subtract), r=["yc", "ycs"], w=["yc"])
    ysq_ = attn_s_early = sb("ysq_", [NSB, 512], F32, SP)
    A("dve", lambda e: e.tensor_tensor(out=ysq_[:], in0=yc[:], in1=yc[:], op=ALU.mult), r=["yc"], w=["ysq_"])
    A("dve", lambda e: e.tensor_reduce(out=ycs[:, 1:2], in_=ysq_[:], axis=AX.X, op=ALU.add), r=["ysq_"], w=["ycs"])
    A("dve", lambda e: e.tensor_scalar(out=ycs[:, 1:2], in0=ycs[:, 1:2], scalar1=1.0 / 512.0, scalar2=1e-5, op0=ALU.mult, op1=ALU.add),
      r=["ycs"], w=["ycs"])
    A("act", lambda e: e.activation(out=ycs[:, 1:2], in_=ycs[:, 1:2], func=AF.Sqrt), r=["ycs"], w=["ycs"])
    A("dve", lambda e: e.reciprocal(out=ycs[:, 1:2], in_=ycs[:, 1:2]), r=["ycs"], w=["ycs"])
    A("dve", lambda e: e.scalar_tensor_tensor(out=yc[:], in0=yc[:], scalar=ycs[:, 1:2], in1=cvb[:, 2, :], op0=ALU.mult, op1=ALU.mult),
      r=["yc", "ycs", "cvb"], w=["yc"])
    A("dve", lambda e: e.tensor_tensor(out=yc[:], in0=yc[:], in1=cvb[:, 3, :], op=ALU.add), r=["yc", "cvb"], w=["yc"])
    ycb = sb("ycb", [NSB, 512], BF16, SP)
    A("act", lambda e: e.activation(out=ycb[:], in_=yc[:], func=AF.Silu), r=["yc"], w=["ycb"])
    cyT = sb("cyT", [128, 4, NSB], BF16, SP)
    for c4 in range(4):
        A("pe", lambda e, c4=c4: e.transpose(out=pTb[:, c4, 0:NSB], in_=ycb[:, c4 * 128:(c4 + 1) * 128], identity=ident[0:NSB, 0:NSB]),
          r=["ycb", "ident"], w=["pTb"])
    A("act", lambda e: e.copy(out=cyT[:], in_=pTb[:, 0:4, 0:NSB]), r=["pTb"], w=["cyT"])
    if int(os.environ.get("SAMP_STOP", "99")) <= 8:
        return
    as_d = nc.dram_tensor("as_d", [128, 64], F32, kind="Internal").ap()
    attn_s = ysq_
    attn_sb = sb("attn_sb", [NSB, 512], BF16, SP)
    aT2 = sb("aT2", [128, 4, NSB], BF16, SP)
    A("sp", lambda e: e.dma_start(out=as_d, in_=arow[:, 0:64]), r=["arow"], w=["as_d"], dma=True)
    A("sp", lambda e: e.dma_start(out=attn_s[:], in_=as_d.rearrange("(b h) d -> b (h d)", h=8)), r=["as_d"], w=["ysq_"], dma=True)
    A("act", lambda e: e.copy(out=attn_sb[:], in_=attn_s[:]), r=["ysq_"], w=["attn_sb"])
    for c4 in range(4):
        A("pe", lambda e, c4=c4: e.transpose(out=pTb[:, 4 + c4, 0:NSB], in_=attn_sb[:, c4 * 128:(c4 + 1) * 128], identity=ident[0:NSB, 0:NSB]),
          r=["attn_sb", "ident"], w=["pTb"])
    A("act", lambda e: e.copy(out=aT2[:], in_=pTb[:, 4:8, 0:NSB]), r=["pTb"], w=["aT2"])
    for half in range(2):
        cs = slice(half * 512, (half + 1) * 512)
        for k in range(8):
            lhs = cyT[:, k, :] if k < 4 else aT2[:, k - 4, :]
            A("pe", lambda e, k=k, cs=cs, lhs=lhs: e.matmul(pA[0:NSB, :], lhsT=lhs, rhs=w_out_bf[:, k, cs], start=(k == 0), stop=(k == 7)),
              r=["cyT", "aT2", ("w_out_bf", k)], w=["s_pA"])
        A("dve", lambda e, cs=cs: e.tensor_tensor(out=hs_acc[:, cs], in0=pA[0:NSB, :], in1=xs_sb[:, cs], op=ALU.add),
          r=["s_pA", "xs_sb"], w=["hs_acc"])
    if int(os.environ.get("SAMP_STOP", "99")) <= 9:
        return
    emit_norm_T(S, hs_acc[:], "hs_acc", junk, ssx, hn, gbc_mlp, "gbc_mlp", ident, pTb, None, hnTs[:], "hnTs")
    if debug:
        dbg["arow"] = dout("d_arow", [128, 128], F32)
        A("sp", lambda e: e.dma_start(out=dbg["arow"], in_=arow[:]), r=["arow"], dma=True)
        dbg["hs"] = dout("d_hs", [NSB, DM], F32)
        A("sp", lambda e: e.dma_start(out=dbg["hs"], in_=hs_acc[:]), r=["hs_acc"], dma=True)
        dbg["ycb"] = dout("d_ycb", [NSB, 512], BF16)
        A("sp", lambda e: e.dma_start(out=dbg["ycb"], in_=ycb[:]), r=["ycb"], dma=True)
        dbg["Oacc"] = dout("d_Oacc", [128, 192], F32)
        A("sp", lambda e: e.dma_start(out=dbg["Oacc"], in_=Oacc[:]), r=["Oacc"], dma=True)
        dbg["st"] = dout("d_st", [128, 16], F32)
        A("sp", lambda e: e.dma_start(out=dbg["st"], in_=st[:]), r=["s_st"], dma=True)
        dbg["selm"] = dout("d_selm", [128, 32], F32)
        A("sp", lambda e: e.dma_start(out=dbg["selm"], in_=selm[:]), r=["s_selm"], dma=True)
```

```python
import contextlib
import os
import numpy as np
import concourse.bass as bass
import concourse.mybir as mybir
from concourse.bass_utils import run_bass_kernel_spmd

F32 = mybir.dt.float32
BF16 = mybir.dt.bfloat16
I32 = mybir.dt.int32
AF = mybir.ActivationFunctionType
ALU = mybir.AluOpType
AX = mybir.AxisListType

EPOCH = 4000
N_DMA_SEMS = 8

SEQ = 2048
DM = 1024
NT = 16
INC = 2328
SCALE = 0.125
BIG = 20000.0
NSB = 16
N_CORES = 8


class Sched:
    def __init__(self, nc):
        self.nc = nc
        self.ops = []
        self.last_writer = {}
        self.readers = {}
        self.cur_barrier = 0

    def add(self, eng, fn, r=(), w=(), dma=False):
        deps = set()
        for k in r:
            if k in self.last_writer:
                deps.add(self.last_writer[k])
        for k in w:
            if k in self.last_writer:
                deps.add(self.last_writer[k])
            deps.update(self.readers.get(k, ()))
        idx = len(self.ops)
        self.ops.append(dict(eng=eng, fn=fn, deps=sorted(deps), dma=dma, barrier=self.cur_barrier))
        for k in r:
            self.readers.setdefault(k, []).append(idx)
        for k in w:
            self.last_writer[k] = idx
            self.readers[k] = []
        return idx

    def barrier(self):
        self.cur_barrier = len(self.ops)

    def emit(self, final_wait_engine="sp"):
        nc = self.nc
        engs = ["pe", "act", "dve", "pool", "sp"]
        ops = self.ops
        cnt = {e: 0 for e in engs}
        dcnt = {e: 0 for e in engs}
        need = set()
        for op in ops:
            e = op["eng"]
            if op["dma"]:
                k = dcnt[e] % N_DMA_SEMS
                n = dcnt[e] // N_DMA_SEMS
                dcnt[e] += 1
                op["ticket"] = (("d", e, k), 16 * (n + 1))
                op["prev"] = (("d", e, k), 16 * n) if n > 0 else None
            else:
                ep = cnt[e] // EPOCH
                v = cnt[e] % EPOCH + 1
                cnt[e] += 1
                op["ticket"] = (("c", e, ep), v)
            need.add(op["ticket"][0])
        bounds = sorted(set(op["barrier"] for op in ops))
        btk = {}
        run = {}
        bi = 0
        for i, op in enumerate(ops):
            while bi < len(bounds) and bounds[bi] <= i:
                btk[bounds[bi]] = dict(run)
                bi += 1
            sn, v = op["ticket"]
            run[sn] = max(run.get(sn, 0), v)
        while bi < len(bounds):
            btk[bounds[bi]] = dict(run)
            bi += 1
        with contextlib.ExitStack() as st:
            sems = {}
            for sn in sorted(need):
                sems[sn] = st.enter_context(nc.semaphore("s_" + "_".join(map(str, sn))))
            block = st.enter_context(nc.Block())

            def make(e):
                def body(engine):
                    known = {}
                    seen_barrier = [0]

                    def wait(t):
                        sn, v = t
                        if sn[0] == "c":
                            for sn2 in known:
                                if sn2[0] == "c" and sn2[1] == sn[1] and sn2[2] > sn[2]:
                                    return
                        if known.get(sn, 0) >= v:
                            return
                        engine.wait_ge(sems[sn], v)
                        known[sn] = v

                    for op in ops:
                        if op["eng"] != e:
                            continue
                        if op["barrier"] > seen_barrier[0]:
                            seen_barrier[0] = op["barrier"]
                            for sn, v in sorted(btk[op["barrier"]].items()):
                                if sn == ("c", e, sn[2]) and e == "pe":
                                    continue
                                wait((sn, v))
                        for d in op["deps"]:
                            dop = ops[d]
                            if dop["eng"] == e and e == "pe" and not dop["dma"]:
                                continue
                            wait(dop["ticket"])
                        if op["dma"] and op["prev"] is not None:
                            wait(op["prev"])
                        ins = op["fn"](engine)
                        sn, v = op["ticket"]
                        ins.then_inc(sems[sn], 16 if op["dma"] else 1)
                    if e == final_wait_engine:
                        fin = {}
                        for op in ops:
                            sn, v = op["ticket"]
                            fin[sn] = max(fin.get(sn, 0), v)
                        for sn, v in sorted(fin.items()):
                            wait((sn, v))
                return body

            block.tensor(make("pe"))
            block.scalar(make("act"))
            block.vector(make("dve"))
            block.gpsimd(make("pool"))
            block.sync(make("sp"))


def build(stages=("p1", "p2", "p3", "samp"), debug=False, cache_rows=2560 * 128 * 4):
    nc = bass.Bass("TRN2", target_bir_lowering=False)

    def din(name, shape, dt=F32):
        return nc.dram_tensor(name, shape, dt, kind="ExternalInput").ap()

    def dout(name, shape, dt=F32):
        return nc.dram_tensor(name, shape, dt, kind="ExternalOutput").ap()

    xp = din("xp", [SEQ, DM])
    xs = din("xs", [NSB, DM])
    cache = din("cache", [cache_rows, 128])
    cwin = din("cwin", [NSB, 512, 256])
    sconv = din("sconv", [NSB * 30, 512])
    ptab = din("ptab", [1, NSB * 16], I32)
    g_attn = din("g_attn", [1, DM])
    w_in = din("w_in", [DM, INC])
    wdall = din("wdall", [34, 512])
    w_ck = din("w_ck", [32, 2])
    w_cv = din("w_cv", [32, 2])
    w_out = din("w_out", [DM, DM])
    g_mlp = din("g_mlp", [1, DM])
    w_up = din("w_up", [DM, 4096])
    w_down = din("w_down", [4096, DM])
    g_fin = din("g_fin", [1, DM])

    y_p = dout("y_p", [SEQ, DM])
    y_s = dout("y_s", [NSB, DM])
    kv_p = dout("kv_p", [SEQ, 512])
    win_p = dout("win_p", [512, 256])
    conv_p = dout("conv_p", [30, 512])
    kv_s = dout("kv_s", [NSB, 512])
    win_s = dout("win_s", [NSB, 512, 256])
    conv_s = dout("conv_s", [NSB * 30, 512])
    dbg = {}

    S = Sched(nc)
    A = S.add
    ES = contextlib.ExitStack()

    def sb(name, shape, dt, stack=None, side=None):
        return (stack or ES).enter_context(nc.sbuf_tensor(name, shape, dt, side=side))

    def pst(name, shape, dt, stack):
        return stack.enter_context(nc.psum_tensor(name, shape, dt))

    with ES:
        identf = sb("identf", [128, 128], F32)
        ident = sb("ident", [128, 128], BF16)
        A("pool", lambda e: e.memset(identf[:], 0.0), w=["identf"])
        A("pool", lambda e: e.affine_select(out=identf[:], in_=identf[:], pattern=[[-1, 128]],
                                            compare_op=ALU.not_equal, fill=1.0, base=0, channel_multiplier=1),
          r=["identf"], w=["identf"])
        A("dve", lambda e: e.tensor_copy(out=ident[:], in_=identf[:]), r=["identf"], w=["ident"])
        gbc_mlp = sb("gbc_mlp", [128, DM], F32)
        A("sp", lambda e: e.dma_start(out=gbc_mlp[:], in_=g_mlp.partition_broadcast(128)), w=["gbc_mlp"], dma=True)
        hs_acc = sb("hs_acc", [NSB, DM], F32)
        hss = sb("hss", [128, NT], F32)
        A("pool", lambda e: e.memset(hss[:], 0.0), w=["hss"])
        hnTs = sb("hnTs", [128, 8, NSB], BF16)
        w_out_bf = sb("w_out_bf", [128, 8, DM], BF16)
        cw = sb("cw", [128, 4, 34], F32)
        wckb = sb("wckb", [128, 2, 32], F32)
        wcvb = sb("wcvb", [128, 2, 32], F32)
        PmB = [sb(f"PmB{h}", [128, 124], BF16) for h in range(2)]

        RS1 = contextlib.ExitStack()
        w_in_bf = sb("w_in_bf", [128, 8, INC], BF16, RS1, side="right")

        with contextlib.ExitStack() as W0:
            stg = [sb(f"stg{i}", [128, INC], F32, W0) for i in range(2)]
            ci = 0
            for (wsrc, wdst, wkey, ncol) in ((w_in, w_in_bf, "w_in_bf", INC), (w_out, w_out_bf, "w_out_bf", DM)):
                for k in range(8):
                    b = ci % 2
                    A("sp", lambda e, k=k, b=b, wsrc=wsrc, ncol=ncol: e.dma_start(out=stg[b][:, 0:ncol], in_=wsrc[k * 128:(k + 1) * 128, :]),
                      w=[("stg", b)], dma=True)
                    if ci % 2 == 0:
                        A("act", lambda e, k=k, b=b, wdst=wdst, ncol=ncol: e.copy(out=wdst[:, k, :], in_=stg[b][:, 0:ncol]),
                          r=[("stg", b)], w=[(wkey, k)])
                    else:
                        A("dve", lambda e, k=k, b=b, wdst=wdst, ncol=ncol: e.tensor_copy(out=wdst[:, k, :], in_=stg[b][:, 0:ncol]),
                          r=[("stg", b)], w=[(wkey, k)])
                    ci += 1
            wd_sb = sb("wd_sb", [34, 512], F32, W0)
            A("sp", lambda e: e.dma_start(out=wd_sb[:], in_=wdall), w=["wd_sb"], dma=True)
            with contextlib.ExitStack() as PW:
                pcw = pst("pcw", [128, 4, 34], F32, PW)
                for c4 in range(4):
                    A("pe", lambda e, c4=c4: e.transpose(out=pcw[:, c4, :], in_=wd_sb[:, c4 * 128:(c4 + 1) * 128],
                                                         identity=identf[0:34, 0:34]),
                      r=["wd_sb", "identf"], w=[("pcw", c4)])
                A("dve", lambda e: e.tensor_copy(out=cw[:], in_=pcw[:]), r=[("pcw", c4) for c4 in range(4)], w=["cw"])
                S.barrier()
            wraw = sb("wraw", [128, 2, 64], F32, W0)
            A("sp", lambda e: e.dma_start(out=wraw[:, 0, :], in_=w_ck.rearrange("j h -> (j h)").partition_broadcast(128)), w=["wraw"], dma=True)
            A("sp", lambda e: e.dma_start(out=wraw[:, 1, :], in_=w_cv.rearrange("j h -> (j h)").partition_broadcast(128)), w=["wraw"], dma=True)
            A("dve", lambda e: e.tensor_copy(out=wckb[:], in_=wraw[:, 0, :].rearrange("p (j h) -> p h j", h=2)), r=["wraw"], w=["wckb"])
            A("dve", lambda e: e.tensor_copy(out=wcvb[:], in_=wraw[:, 1, :].rearrange("p (j h) -> p h j", h=2)), r=["wraw"], w=["wcvb"])
            wcol = sb("wcol", [128, 2], F32, W0)
            for rr in range(4):
                A("sp", lambda e, rr=rr: e.dma_start(out=wcol[rr * 32:(rr + 1) * 32, :], in_=w_cv), w=["wcol"], dma=True)
            pmf = sb("pmf", [128, 4], F32, W0)
            A("pool", lambda e: e.memset(pmf[:], 1.0), w=["pmf"])
            A("pool", lambda e: e.affine_select(out=pmf[:], in_=pmf[:], pattern=[[-32, 4]], compare_op=ALU.is_ge,
                                                fill=0.0, base=0, channel_multiplier=1), r=["pmf"], w=["pmf"])
            A("pool", lambda e: e.affine_select(out=pmf[:], in_=pmf[:], pattern=[[32, 4]], compare_op=ALU.is_ge,
                                                fill=0.0, base=31, channel_multiplier=-1), r=["pmf"], w=["pmf"])
            for h in range(2):
                A("pool", lambda e, h=h: e.memset(PmB[h][:], 0.0), w=[("PmB", h)])
                A("dve", lambda e, h=h: e.tensor_scalar(out=PmB[h][:, 60:64], in0=pmf[:], scalar1=wcol[:, h:h + 1],
                                                        scalar2=None, op0=ALU.mult),
                  r=["pmf", "wcol", ("PmB", h)], w=[("PmB", h)])
            S.barrier()

        if "samp" in stages:
            with contextlib.ExitStack() as SP:
                build_samp(nc, S, SP, sb, pst, locals())
                S.barrier()

        ATT = contextlib.ExitStack()
        with ATT:
            QTs = [sb(f"QTs{h}", [96, 4, SEQ], BF16, ATT) for h in range(2)]
            KTs = [sb(f"KTs{h}", [96, SEQ], BF16, ATT) for h in range(2)]
            KTw = [sb(f"KTw{h}", [64, SEQ], BF16, ATT) for h in range(2)]
            kcT = [sb(f"kcT{h}", [64, 64], BF16, ATT) for h in range(2)]
            Vaug = sb("Vaug", [128, NT, 3, 2, 65], BF16, ATT)
            vcaug = sb("vcaug", [64, 2, 65], BF16, ATT)
            gates = sb("gates", [128, NT, 24], F32, ATT)
            convyT = sb("convyT", [128, 4, SEQ], BF16, ATT)
            for h in range(2):
                A("pool", lambda e, h=h: e.memset(KTs[h][64:96, :], 1.0), w=[("KTsaug", h)])
                A("pool", lambda e, h=h: e.affine_select(out=KTs[h][64:96, :], in_=KTs[h][64:96, :], pattern=[[1, SEQ]],
                                                         compare_op=ALU.is_ge, fill=0.0, base=0, channel_multiplier=-64),
                  r=[("KTsaug", h)], w=[("KTsaug", h)])
                A("pool", lambda e, h=h: e.affine_select(out=KTs[h][64:96, :], in_=KTs[h][64:96, :], pattern=[[-1, SEQ]],
                                                         compare_op=ALU.is_ge, fill=0.0, base=63, channel_multiplier=64),
                  r=[("KTsaug", h)], w=[("KTsaug", h)])
            A("pool", lambda e: e.memset(Vaug[:], 1.0), w=["Vaug_init"])
            A("pool", lambda e: e.memset(vcaug[:], 1.0), w=["vcaug_init"])

            with contextlib.ExitStack() as P1:
                build_p1(nc, S, P1, sb, pst, locals())
                S.barrier()
            RS1.close()
            RS2 = contextlib.ExitStack()
            h_acc = sb("h_acc", [128, NT, DM], F32, RS2, side="right")
            if "p2" in stages:
                with contextlib.ExitStack() as P2:
                    build_p2(nc, S, P2, sb, pst, locals())
                    S.barrier()
        if "p3" in stages:
            with contextlib.ExitStack() as P3:
                build_p3(nc, S, P3, sb, pst, locals())
        RS2.close()
        S.emit()
    return nc, dbg


def build_p1(nc, S, P1, sb, pst, L):
    A = S.add
    (QTs, KTs, KTw, kcT, Vaug, vcaug, gates, convyT, cw, wckb, PmB, w_in_bf, ident, identf, xp, g_attn, kv_p, win_p,
     conv_p) = (L[k] for k in ("QTs", "KTs", "KTw", "kcT", "Vaug", "vcaug", "gates", "convyT", "cw", "wckb", "PmB",
                               "w_in_bf", "ident", "identf", "xp", "g_attn", "kv_p", "win_p", "conv_p"))
    debug, dbg, dout = L["debug"], L["dbg"], L["dout"]
    onesf = sb("onesf", [128, 128], F32, P1)
    A("pool", lambda e: e.memset(onesf[:], 1.0 / 512.0), w=["onesf"])
    gbc_attn = sb("gbc_attn", [128, DM], F32, P1)
    A("sp", lambda e: e.dma_start(out=gbc_attn[:], in_=g_attn.partition_broadcast(128)), w=["gbc_attn"], dma=True)
    xt = [sb(f"xt{i}", [128, DM], F32, P1) for i in range(2)]
    ssall = sb("ssall", [128, NT], F32, P1)
    xn = [sb("xn0", [128, DM], BF16, P1)] * 2
    A("pool", lambda e: e.memset(ssall[:], 0.0), w=["ssall"])
    for t in range(NT):
        A("sp", lambda e, t=t: e.dma_start(out=xt[t % 2][:], in_=xp[t * 128:(t + 1) * 128, :]), w=[("xt", t % 2)], dma=True)
        A("act", lambda e, t=t: e.activation(out=xn[t % 2][:], in_=xt[t % 2][:], func=AF.Square, accum_out=ssall[:, t:t + 1]),
          r=[("xt", t % 2), "ssall"], w=["xn", "ssall"])
    A("dve", lambda e: e.tensor_scalar(out=ssall[:], in0=ssall[:], scalar1=1.0 / DM, scalar2=1e-6, op0=ALU.mult, op1=ALU.add),
      r=["ssall"], w=["ssall"])
    A("act", lambda e: e.activation(out=ssall[:], in_=ssall[:], func=AF.Sqrt), r=["ssall"], w=["ssall"])
    A("dve", lambda e: e.reciprocal(out=ssall[:], in_=ssall[:]), r=["ssall"], w=["ssall"])
    xnT = [sb("xnT0", [128, 8, 512], BF16, P1)] * 2
    glu = [sb(f"glu{i}", [128, 4, 542], BF16, P1) for i in range(2)]
    glutail = sb("glutail", [128, 4, 30], F32, P1)
    glo = [sb(f"glo{i}", [128, 4, 542], BF16, P1) for i in range(2)]
    Dg = [sb(f"Dg{i}", [128, 16, 128], BF16, P1) for i in range(2)]
    sgt = sb("sgt", [128, 512], F32, P1)
    ych = sb("ych", [128, 4, 512], F32, P1)
    mean_sb = sb("mean_sb", [128, 512], F32, P1)
    rstd_sb = sb("rstd_sb", [128, 512], F32, P1)
    ysq = rstd_sb
    kcp = mean_sb[0:64, :].rearrange("p (c j) -> p c j", j=32)
    zt = sb("zt", [128, 792], F32, P1)
    kcf = sb("kcf", [64, 16], F32, P1)
    psF = [pst(f"psF{i}", [128, 512], F32, P1) for i in range(2)]
    psTM = pst("psTM", [128, 1024], F32, P1)
    psT = pst("psT", [128, 8, 128], BF16, P1)
    psVC = pst("psVC", [128, 512], F32, P1)
    psMean = pst("psMean", [128, 512], F32, P1)
    psMsq = pst("psMsq", [128, 512], F32, P1)

    A("pool", lambda e: e.memset(glu[0][:, :, 0:30], 0.0), w=[("gluhead", 0)])
    A("pool", lambda e: e.memset(glo[0][:, :, 0:30], 0.0), w=[("gluhead", 0)])
    fcnt = [0]

    def fm_mm(xT, c0, M, Gk):
        b = fcnt[0] % 2
        fcnt[0] += 1
        for k in range(8):
            A("pe", lambda e, k=k, b=b: e.matmul(psF[b][0:M, :], lhsT=w_in_bf[:, k, c0:c0 + M], rhs=xT[:, k, :],
                                                 start=(k == 0), stop=(k == 7)),
              r=[("w_in_bf", k), Gk], w=[("psF", b)])
        return b

    def p1_front(G):
        xTg = xnT[G % 2]
        Gk = "xnT"
        gl = glu[G % 2]
        gln = glu[(G + 1) % 2]
        tok = slice(G * 512, (G + 1) * 512)
        for tt in range(4):
            t = 4 * G + tt
            xb = t % 2
            A("sp", lambda e, t=t, xb=xb: e.dma_start(out=xt[xb][:], in_=xp[t * 128:(t + 1) * 128, :]), w=[("xt", xb)], dma=True)
            A("dve", lambda e, t=t, xb=xb: e.scalar_tensor_tensor(out=xn[xb][:], in0=xt[xb][:], scalar=ssall[:, t:t + 1],
                                                                  in1=gbc_attn[:], op0=ALU.mult, op1=ALU.mult),
              r=[("xt", xb), "ssall", "gbc_attn"], w=["xn"])
            for k in range(8):
                A("pe", lambda e, k=k, xb=xb: e.transpose(out=psT[:, k, :], in_=xn[xb][:, k * 128:(k + 1) * 128], identity=ident[:]),
                  r=["xn", "ident"], w=["psT"])
            A("act", lambda e, tt=tt, xTg=xTg: e.copy(out=xTg[:, :, tt * 128:(tt + 1) * 128], in_=psT[:]),
              r=["psT"], w=[Gk])
            yield
        for c4 in range(4):
            ba = fm_mm(xTg, c4 * 128, 128, Gk)
            bb = fm_mm(xTg, 512 + c4 * 128, 128, Gk)
            A("act", lambda e, bb=bb: e.activation(out=sgt[:], in_=psF[bb][:], func=AF.Sigmoid), r=[("psF", bb)], w=["sgt"])
            A("dve", lambda e, ba=ba, c4=c4, gl=gl: e.tensor_tensor(out=gl[:, c4, 30:542], in0=psF[ba][:], in1=sgt[:], op=ALU.mult),
              r=[("psF", ba), "sgt"], w=[("glu", G % 2, c4)])
            A("dve", lambda e, ba=ba, c4=c4, G=G: e.tensor_tensor(out=glo[G % 2][:, c4, 29:541], in0=psF[ba][:], in1=sgt[:], op=ALU.mult),
              r=[("psF", ba), "sgt", ("gluhead", G % 2)], w=[("glu", G % 2, c4)])
            if G == 3:
                A("dve", lambda e, ba=ba, c4=c4: e.tensor_tensor(out=glutail[:, c4, :], in0=psF[ba][:, 482:512], in1=sgt[:, 482:512],
                                                                 op=ALU.mult), r=[("psF", ba), "sgt"], w=["glutail"])
            yield
        for hd in range(8):
            h, g = hd // 4, hd % 4
            b = fm_mm(xTg, 1024 + hd * 64, 64, Gk)
            A("act", lambda e, b=b, h=h, g=g, tok=tok: e.copy(out=QTs[h][0:64, g, tok], in_=psF[b][0:64, :]),
              r=[("psF", b)], w=[("QT", h, G)])
            if hd % 4 == 3:
                yield
        for h in range(2):
            b = fm_mm(xTg, 1536 + h * 64, 64, Gk)
            A("act", lambda e, b=b: e.copy(out=mean_sb[0:64, :], in_=psF[b][0:64, :]), r=[("psF", b)], w=["mean_sb"])
            A("pool", lambda e, h=h: e.tensor_tensor(
                out=kcp, in0=kcp, in1=wckb[0:64, h, :].unsqueeze(1).to_broadcast([64, 16, 32]), op=ALU.mult),
              r=["mean_sb", "wckb"], w=["mean_sb"])
            A("dve", lambda e: e.tensor_reduce(out=kcf[:], in_=kcp, axis=AX.X, op=ALU.add), r=["mean_sb"], w=["kcf"])
            A("pool", lambda e, h=h, G=G: e.tensor_copy(out=kcT[h][:, G * 16:(G + 1) * 16], in_=kcf[:]),
              r=["kcf"], w=[("kcT", h, G)])
            b = fm_mm(xTg, 1536 + 256 + h * 64, 64, Gk)
            A("act", lambda e, b=b, h=h, tok=tok: e.copy(out=KTs[h][0:64, tok], in_=psF[b][0:64, :]),
              r=[("psF", b)], w=[("KTs", h, G)])
            b = fm_mm(xTg, 1536 + 512 + h * 64, 64, Gk)
            A("act", lambda e, b=b, h=h, tok=tok: e.copy(out=KTw[h][:, tok], in_=psF[b][0:64, :]),
              r=[("psF", b)], w=[("KTw", h, G)])
            yield
        for tt in range(4):
            t = 4 * G + tt
            for (c0, n, o0) in ((1536, 512, 0), (2048, 280, 512)):
                for k in range(8):
                    A("pe", lambda e, k=k, tt=tt, c0=c0, n=n, o0=o0, xTg=xTg: e.matmul(
                        psTM[:, o0:o0 + n], lhsT=xTg[:, k, tt * 128:(tt + 1) * 128], rhs=w_in_bf[:, k, c0:c0 + n],
                        start=(k == 0), stop=(k == 7)),
                      r=[("w_in_bf", k), Gk], w=["psTM"])
            A("act", lambda e: e.copy(out=zt[:], in_=psTM[:, 0:792]), r=["psTM"], w=["zt"])
            A("sp", lambda e, t=t: e.dma_start(out=kv_p[t * 128:(t + 1) * 128, :], in_=zt[:, 0:512]), r=["zt"], dma=True)
            if t >= 12:
                A("sp", lambda e, t=t: e.dma_start(out=win_p[(t - 12) * 128:(t - 11) * 128, :], in_=zt[:, 512:768]),
                  r=["zt"], dma=True)
            A("act", lambda e, t=t: e.activation(out=gates[:, t, :], in_=zt[:, 768:792], func=AF.Sigmoid),
              r=["zt"], w=[("gates", t)])
            for s3 in range(3):
                A("pool", lambda e, t=t, s3=s3: e.tensor_copy(
                    out=Vaug[:, t, s3, :, 0:64],
                    in_=zt[:, 128 + 256 * s3:256 + 256 * s3].rearrange("p (h d) -> p h d", d=64)),
                  r=["zt", "Vaug_init"], w=[("Vaug", t)])
            for h in range(2):
                A("pe", lambda e, t=t, h=h: e.matmul(psVC[0:64, h * 64:(h + 1) * 64], lhsT=PmB[h][:, 60 - 4 * t:124 - 4 * t],
                                                     rhs=Vaug[:, t, 0, h, 0:64], start=(t == 0 and h == 0), stop=(t == NT - 1),
                                                     skip_group_check=True),
                  r=[("PmB", h), ("Vaug", t)], w=["psVC"])
            yield

    def p1_conv(G):
        xTg = xnT[G % 2]
        Gk = "xnT"
        gl = glu[G % 2]
        gln = glu[(G + 1) % 2]
        tok = slice(G * 512, (G + 1) * 512)
        for c4 in range(4):
            rk = [("glu", G % 2, c4), ("gluhead", G % 2)]
            for di, (j0, nj) in enumerate(((0, 16), (16, 15))):
                A("pool", lambda e, c4=c4, di=di, j0=j0, nj=nj: e.tensor_tensor(
                    out=Dg[di][:, 0:nj, :], in0=ident[:].unsqueeze(1).to_broadcast([128, nj, 128]),
                    in1=cw[:, c4, j0:j0 + nj].unsqueeze(2).to_broadcast([128, nj, 128]), op=ALU.mult), r=["ident", "cw"], w=[("Dg", di)])
            for j in range(31):
                di, jj = (0, j) if j < 16 else (1, j - 16)
                src = gl[:, c4, j:j + 512] if j % 2 == 0 else glo[G % 2][:, c4, j - 1:j - 1 + 512]
                A("pe", lambda e, j=j, src=src, di=di, jj=jj: e.matmul(psMsq[:], lhsT=Dg[di][:, jj, :], rhs=src,
                                                                       start=(j == 0), stop=(j == 30)),
                  r=rk + [("Dg", di)], w=["psMsq"])
            A("act", lambda e, c4=c4: e.activation(out=ych[:, c4, :], in_=psMsq[:], func=AF.Identity, bias=cw[:, c4, 31:32]),
              r=["psMsq", "cw"], w=[("ych", c4)])
            yield

    def p1_ln(G):
        xTg = xnT[G % 2]
        Gk = "xnT"
        gl = glu[G % 2]
        gln = glu[(G + 1) % 2]
        tok = slice(G * 512, (G + 1) * 512)
        for c4 in range(4):
            A("act", lambda e, c4=c4: e.activation(out=ysq[:], in_=ych[:, c4, :], func=AF.Square),
              r=[("ych", c4)], w=["rstd_sb"])
            A("pe", lambda e, c4=c4: e.matmul(psMean[:], lhsT=onesf[:], rhs=ych[:, c4, :], start=(c4 == 0), stop=(c4 == 3)),
              r=["onesf", ("ych", c4)], w=["psMean"])
            A("pe", lambda e, c4=c4: e.matmul(psMsq[:], lhsT=onesf[:], rhs=ysq[:], start=(c4 == 0), stop=(c4 == 3)),
              r=["onesf", "rstd_sb"], w=["psMsq"])
        A("pool", lambda e, gl=gl, gln=gln: e.tensor_copy(out=gln[:, :, 0:30], in_=gl[:, :, 512:542]),
          r=[("glu", G % 2, c4) for c4 in range(4)], w=[("gluhead", (G + 1) % 2)])
        A("pool", lambda e, gl=gl, G=G: e.tensor_copy(out=glo[(G + 1) % 2][:, :, 0:29], in_=gl[:, :, 513:542]),
          r=[("glu", G % 2, c4) for c4 in range(4)], w=[("gluhead", (G + 1) % 2)])
        A("act", lambda e: e.copy(out=mean_sb[:], in_=psMean[:]), r=["psMean"], w=["mean_sb"])
        A("pool", lambda e: e.tensor_tensor(out=rstd_sb[:], in0=mean_sb[:], in1=mean_sb[:], op=ALU.mult),
          r=["mean_sb"], w=["rstd_sb"])
        A("dve", lambda e: e.tensor_tensor(out=rstd_sb[:], in0=psMsq[:], in1=rstd_sb[:], op=ALU.subtract),
          r=["psMsq", "rstd_sb"], w=["rstd_sb"])
        A("dve", lambda e: e.tensor_scalar(out=rstd_sb[:], in0=rstd_sb[:], scalar1=1e-5, scalar2=None,
                                           op0=ALU.add), r=["rstd_sb"], w=["rstd_sb"])
        A("act", lambda e: e.activation(out=rstd_sb[:], in_=rstd_sb[:], func=AF.Sqrt), r=["rstd_sb"], w=["rstd_sb"])
        A("dve", lambda e: e.reciprocal(out=rstd_sb[:], in_=rstd_sb[:]), r=["rstd_sb"], w=["rstd_sb"])
        for c4 in range(4):
            eng = "dve" if c4 % 2 == 0 else "pool"
            A(eng, lambda e, c4=c4: e.tensor_tensor(out=ych[:, c4, :], in0=ych[:, c4, :], in1=mean_sb[:], op=ALU.subtract),
              r=[("ych", c4), "mean_sb"], w=[("ych", c4)])
            A(eng, lambda e, c4=c4: e.tensor_tensor(out=ych[:, c4, :], in0=ych[:, c4, :], in1=rstd_sb[:], op=ALU.mult),
              r=[("ych", c4), "rstd_sb"], w=[("ych", c4)])
            A("act", lambda e, c4=c4, tok=tok: e.activation(out=convyT[:, c4, tok], in_=ych[:, c4, :], func=AF.Silu,
                                                            bias=cw[:, c4, 33:34], scale=cw[:, c4, 32:33]),
              r=[("ych", c4), "cw"], w=[("convyT", G)])
            yield
    def run_interleaved(ga, gb):
        sentinel = object()
        for _ in range(4):
            if next(ga, sentinel) is sentinel:
                break
        a_alive, b_alive = True, True
        while a_alive or b_alive:
            for _ in range(2):
                if a_alive and next(ga, sentinel) is sentinel:
                    a_alive = False
            if b_alive and next(gb, sentinel) is sentinel:
                b_alive = False

    for _ in p1_front(0):
        pass
    for G in range(4):
        run_interleaved(p1_front(G + 1) if G + 1 < 4 else iter(()), p1_conv(G))
        for _ in p1_ln(G):
            pass
    A("act", lambda e: e.copy(out=vcaug[:, :, 0:64], in_=psVC[0:64, 0:128].rearrange("p (h d) -> p h d", d=64)),
      r=["psVC", "vcaug_init"], w=["vcaug"])
    for c4 in range(4):
        A("pe", lambda e, c4=c4: e.transpose(out=psF[0][0:30, c4 * 128:(c4 + 1) * 128], in_=glutail[:, c4, :], identity=identf[:]),
          r=["glutail", "identf"], w=[("psF", 0)])
    cps = ych[0:30, 0, :]
    A("act", lambda e: e.copy(out=cps, in_=psF[0][0:30, :]), r=[("psF", 0), ("ych", 0)], w=[("ych", 0)])
    A("sp", lambda e: e.dma_start(out=conv_p, in_=cps), r=[("ych", 0)], dma=True)
    if debug:
        dbg["QT0"] = dout("d_QT0", [96, 4 * SEQ], BF16)
        A("sp", lambda e: e.dma_start(out=dbg["QT0"], in_=QTs[0][:]), r=[("QT", 0, G) for G in range(4)], dma=True)
        dbg["convyT"] = dout("d_convyT", [128, 4 * SEQ], BF16)
        A("sp", lambda e: e.dma_start(out=dbg["convyT"], in_=convyT[:]), r=[("convyT", G) for G in range(4)], dma=True)
        dbg["kcT0"] = dout("d_kcT0", [64, 64], BF16)
        A("sp", lambda e: e.dma_start(out=dbg["kcT0"], in_=kcT[0][:]), r=[("kcT", 0, G) for G in range(4)], dma=True)
        dbg["vcaug"] = dout("d_vcaug", [64, 130], BF16)
        A("sp", lambda e: e.dma_start(out=dbg["vcaug"], in_=vcaug[:]), r=["vcaug"], dma=True)


def build_p2(nc, S, P2, sb, pst, L):
    A = S.add
    QTs, KTs, KTw, kcT, Vaug, vcaug, gates, convyT = (L[k] for k in
                                                       ("QTs", "KTs", "KTw", "kcT", "Vaug", "vcaug", "gates", "convyT"))
    ident, identf, w_out_bf, h_acc, gbc_mlp, xp = (L[k] for k in
                                                   ("ident", "identf", "w_out_bf", "h_acc", "gbc_mlp", "xp"))
    debug, dbg, dout = L["debug"], L["dbg"], L["dout"]
    sbias = sb("sbias", [128, NT, 32], F32, P2)
    A("pool", lambda e: e.memset(sbias[:], 0.0), w=["sbias"])
    for qt in range(NT):
        for half in range(2):
            qb = 2 * qt + half
            ps_ = slice(64 * half, 64 * half + 64)
            if qb + 1 < 32:
                A("pool", lambda e, qt=qt, ps_=ps_, qb=qb: e.memset(sbias[ps_, qt, qb + 1:32], -1e30), r=["sbias"], w=["sbias"])
            A("pool", lambda e, qt=qt, ps_=ps_: e.memset(sbias[ps_, qt, 0:1], 5.0), r=["sbias"], w=["sbias"])
            A("pool", lambda e, qt=qt, ps_=ps_, qb=qb: e.memset(sbias[ps_, qt, max(qb - 1, 0):qb + 1], 5.0),
              r=["sbias"], w=["sbias"])
    pS = [pst(f"pS{i}", [128, 512], F32, P2) for i in range(2)]
    pOb = [pst(f"pO{i}", [128, 512], F32, P2) for i in range(3)]
    pO = [p[:, 0:260].rearrange("p (g d) -> p g d", d=65) for p in pOb]
    pM = pst("pM", [128, 512], F32, P2)
    pH = pst("pH", [128, 512], F32, P2)
    pTb = pst("pTb", [128, 8, 128], BF16, P2)
    PT = [sb(f"PT{i}", [128, 512], BF16, P2) for i in range(3)]
    NSEL = 3
    Ecmp = [sb(f"Ecmp{i}", [128, 8, 64], F32, P2) for i in range(NSEL)]
    zc = [sb(f"zc{i}", [128, 16], F32, P2) for i in range(NSEL)]
    pblk = [sb(f"pblk{i}", [128, 2, 32], F32, P2) for i in range(NSEL)]
    pg4 = [sb(f"pg4{i}", [128, 2, 64], F32, P2) for i in range(NSEL)]
    m8 = [sb(f"m8{i}", [128, 2, 8], F32, P2) for i in range(NSEL)]
    wk32 = [sb(f"wk32{i}", [128, 2, 32], F32, P2) for i in range(NSEL)]
    selT_in = [sb(f"selT_in{i}", [128, 2, 96], BF16, P2) for i in range(NSEL)]
    for i in range(NSEL):
        A("pool", lambda e, i=i: e.memset(selT_in[i][:], 0.0), w=[("selT_in", i)])
    coef = sb("coef", [128, 3, 4], F32, P2)
    zr = sb("zr", [128, 3, 4], F32, P2)
    osb = sb("osb", [128, 4, 64], F32, P2)
    otmp = sb("otmp", [128, 4, 64], F32, P2)
    attn = sb("attn", [128, 512], BF16, P2)
    attnT = sb("attnT", [128, 4, 128], BF16, P2)
    xt2 = [sb("xt2_0", [128, DM], F32, P2)] * 2
    pcnt = [0]
    scnt = [0]

    def score_exp(h, qt, lhsT, K, rhs_rows, masks, rkeys):
        sbuf_i = scnt[0] % 2
        scnt[0] += 1
        pb = pcnt[0] % 3
        pcnt[0] += 1
        M = lhsT.shape[1]
        A("pe", lambda e: e.matmul(pS[sbuf_i][0:M, :], lhsT=lhsT, rhs=QTs[h][0:K, :, qt * 128:(qt + 1) * 128],
                                   start=True, stop=True),
          r=rkeys + [("QT", h, qt // 4)] + ([("QTaug", h, qt)] if K == 96 else []), w=[("pS", sbuf_i)])
        A("act", lambda e: e.activation(out=PT[pb][0:M, :], in_=pS[sbuf_i][0:M, :], func=AF.Exp, scale=SCALE),
          r=[("pS", sbuf_i)], w=[("PT", pb)])
        for (cm, qs, base) in masks:
            A("pool", lambda e, cm=cm, qs=qs, base=base: e.affine_select(
                out=PT[pb][0:M, :].rearrange("p (g q) -> p g q", g=4), in_=PT[pb][0:M, :].rearrange("p (g q) -> p g q", g=4),
                pattern=[[0, 4], [qs, 128]], compare_op=ALU.is_ge, fill=0.0, base=base, channel_multiplier=cm),
              r=[("PT", pb)], w=[("PT", pb)])
        return pb

    def emit_sel(qt):
        r = qt % NSEL
        pm = pH
        pmk = "pH"
        for hd in range(8):
            h, g = hd // 4, hd % 4
            A("pe", lambda e, h=h, g=g, hd=hd, qt=qt, pm=pm: e.matmul(pm[:, hd * 64:(hd + 1) * 64],
                                                                      lhsT=QTs[h][0:64, g, qt * 128:(qt + 1) * 128], rhs=kcT[h][:],
                                                                      start=True, stop=True),
              r=[("QT", h, qt // 4)] + [("kcT", h, G) for G in range(4)], w=[pmk])
        E = Ecmp[r]
        A("act", lambda e, E=E, pm=pm: e.activation(out=E[:], in_=pm[:].rearrange("p (g c) -> p g c", g=8), func=AF.Exp, scale=SCALE),
          r=[pmk], w=[("Ecmp", r)])
        A("pool", lambda e, qt=qt, E=E: e.affine_select(out=E[:], in_=E[:], pattern=[[0, 8], [-32, 64]], compare_op=ALU.is_ge,
                                                        fill=0.0, base=qt * 128 - 31, channel_multiplier=1),
          r=[("Ecmp", r)], w=[("Ecmp", r)])
        Z = zc[r]
        A("dve", lambda e, E=E, Z=Z: e.tensor_reduce(out=Z[:, 0:8], in_=E[:], axis=AX.X, op=ALU.add), r=[("Ecmp", r)], w=[("zc", r)])
        A("dve", lambda e, Z=Z: e.tensor_scalar(out=Z[:, 0:8], in0=Z[:, 0:8], scalar1=1e-30, scalar2=None, op0=ALU.add),
          r=[("zc", r)], w=[("zc", r)])
        A("dve", lambda e, Z=Z: e.reciprocal(out=Z[:, 8:16], in_=Z[:, 0:8]), r=[("zc", r)], w=[("zc", r)])
        A("dve", lambda e, E=E, Z=Z: e.tensor_tensor(out=E[:], in0=E[:], in1=Z[:, 8:16].unsqueeze(2).to_broadcast([128, 8, 64]),
                                                     op=ALU.mult), r=[("Ecmp", r), ("zc", r)], w=[("Ecmp", r)])
        for h in range(2):
            A("dve", lambda e, E=E, h=h, r=r: e.tensor_reduce(out=pg4[r][:, h, :], in_=E[:, h * 4:(h + 1) * 4, :].rearrange("p g c -> p c g"),
                                                              axis=AX.X, op=ALU.add), r=[("Ecmp", r)], w=[("pg4", r)])
        A("dve", lambda e, r=r: e.tensor_reduce(out=pblk[r][:], in_=pg4[r][:].rearrange("p h (b t) -> p h b t", t=2),
                                                axis=AX.X, op=ALU.add), r=[("pg4", r)], w=[("pblk", r)])
        A("dve", lambda e, r=r, qt=qt: e.tensor_tensor(out=pblk[r][:], in0=pblk[r][:],
                                                       in1=sbias[:, qt, :].unsqueeze(1).to_broadcast([128, 2, 32]), op=ALU.add),
          r=[("pblk", r), "sbias"], w=[("pblk", r)])
        for h in range(2):
            A("dve", lambda e, h=h, r=r: e.max(out=m8[r][:, h, :], in_=pblk[r][:, h, :]), r=[("pblk", r)], w=[("m8", r, h)])
            A("dve", lambda e, h=h, r=r: e.match_replace(out=wk32[r][:, h, :], in_to_replace=m8[r][:, h, :],
                                                         in_values=pblk[r][:, h, :], imm_value=-3e38),
              r=[("m8", r, h), ("pblk", r)], w=[("wk32", r, h)])
            A("dve", lambda e, h=h, r=r: e.max(out=m8[r][:, h, :], in_=wk32[r][:, h, :]), r=[("wk32", r, h)], w=[("m8", r, h)])
            A("dve", lambda e, h=h, r=r: e.tensor_scalar(out=wk32[r][:, h, :], in0=pblk[r][:, h, :], scalar1=m8[r][:, h, 7:8],
                                                         scalar2=-1.0, op0=ALU.is_ge, op1=ALU.add),
              r=[("pblk", r), ("m8", r, h)], w=[("wk32", r, h)])
        A("dve", lambda e, r=r: e.tensor_scalar(out=selT_in[r][:, :, 64:96], in0=wk32[r][:], scalar1=BIG, scalar2=None, op0=ALU.mult),
          r=[("wk32", r, 0), ("wk32", r, 1), ("selT_in", r)], w=[("selT_in", r)])
        for h in range(2):
            A("pe", lambda e, h=h, r=r: e.transpose(out=pTb[0:96, 4 + h, :], in_=selT_in[r][:, h, :], identity=ident[:]),
              r=[("selT_in", r), "ident"], w=[("pTbs", h)])
            A("act", lambda e, h=h, qt=qt: e.copy(out=QTs[h][64:96, :, qt * 128:(qt + 1) * 128],
                                                  in_=pTb[64:96, 4 + h, :].unsqueeze(1).to_broadcast([32, 4, 128])),
              r=[("pTbs", h)], w=[("QTaug", h, qt)])
    LOOK = 2
    pSx = [pS[0], pS[1], pM]
    Osb = sb("Osb", [128, 3, 4, 65], F32, P2)
    tiles = []
    for qt in range(NT):
        for h in range(2):
            tl = [dict(br=0, kt=0, lhsT=kcT[h][:], K=64, masks=[(-32, 1, qt * 128 - 31)],
                       rk=[("kcT", h, G) for G in range(4)], rhs=vcaug[:, h, :], rhsk="vcaug", first=True, last=True)]
            for kt in range(qt + 1):
                tl.append(dict(br=1, kt=kt, lhsT=KTs[h][:, kt * 128:(kt + 1) * 128], K=96, masks=[(-1, 1, 0)] if kt == qt else [],
                               rk=[("KTs", h, kt // 4), ("KTsaug", h)], rhs=Vaug[:, kt, 1, h, :], rhsk=("Vaug", kt),
                               first=(kt == 0), last=(kt == qt)))
            k0 = max(0, qt - 4)
            for kt in range(k0, qt + 1):
                masks = []
                if kt == qt:
                    masks.append((-1, 1, 0))
                if kt == qt - 4:
                    masks.append((1, -1, 0))
                tl.append(dict(br=2, kt=kt, lhsT=KTw[h][:, kt * 128:(kt + 1) * 128], K=64, masks=masks,
                               rk=[("KTw", h, kt // 4)], rhs=Vaug[:, kt, 2, h, :], rhsk=("Vaug", kt),
                               first=(kt == k0), last=(kt == qt)))
            for t_ in tl:
                t_["qt"], t_["h"] = qt, h
            tl[-1]["end"] = True
            tiles += tl

    def emit_qk(t_, i):
        si = i % 3
        pb = i % 3
        t_["pb"] = pb
        h, qt, K, lhsT = t_["h"], t_["qt"], t_["K"], t_["lhsT"]
        M = lhsT.shape[1]
        A("pe", lambda e: e.matmul(pSx[si][0:M, :], lhsT=lhsT, rhs=QTs[h][0:K, :, qt * 128:(qt + 1) * 128], start=True, stop=True),
          r=t_["rk"] + [("QT", h, qt // 4)] + ([("QTaug", h, qt)] if K == 96 else []), w=[("pSx", si)])
        A("act", lambda e: e.activation(out=PT[pb][0:M, :], in_=pSx[si][0:M, :], func=AF.Exp, scale=SCALE),
          r=[("pSx", si)], w=[("PT", pb)])
        for (cm, qs, base) in t_["masks"]:
            A("pool", lambda e, cm=cm, qs=qs, base=base: e.affine_select(
                out=PT[pb][0:M, :].rearrange("p (g q) -> p g q", g=4), in_=PT[pb][0:M, :].rearrange("p (g q) -> p g q", g=4),
                pattern=[[0, 4], [qs, 128]], compare_op=ALU.is_ge, fill=0.0, base=base, channel_multiplier=cm),
              r=[("PT", pb)], w=[("PT", pb)])

    def emit_pv(t_):
        pb, br = t_["pb"], t_["br"]
        M = t_["lhsT"].shape[1]
        for g in range(4):
            A("pe", lambda e, g=g: e.matmul(pO[br][:, g, :], lhsT=PT[pb][0:M, g * 128:(g + 1) * 128], rhs=t_["rhs"],
                                            start=(t_["first"] and g == 0), stop=t_["last"], skip_group_check=True),
              r=[("PT", pb), t_["rhsk"]], w=[("pO", br)])
        if t_["last"]:
            A("dve", lambda e: e.tensor_copy(out=Osb[:, br, :, :], in_=pO[br][:]), r=[("pO", br)], w=[("Osb", br)])

    def emit_end(qt, h):
        A("dve", lambda e: e.tensor_scalar(out=zr[:], in0=Osb[:, :, :, 64], scalar1=1e-30, scalar2=None, op0=ALU.add),
          r=[("Osb", br) for br in range(3)], w=["zr"])
        A("dve", lambda e: e.reciprocal(out=zr[:], in_=zr[:]), r=["zr"], w=["zr"])
        A("dve", lambda e: e.tensor_tensor(
            out=coef[:], in0=zr[:], in1=gates[:, qt, h * 12:(h + 1) * 12].rearrange("p (g b) -> p b g", b=3), op=ALU.mult),
          r=["zr", ("gates", qt)], w=["coef"])
        for br in range(3):
            dst = osb if br == 0 else otmp
            A("dve", lambda e, br=br, dst=dst: e.tensor_tensor(
                out=dst[:], in0=Osb[:, br, :, 0:64], in1=coef[:, br, :].unsqueeze(2).to_broadcast([128, 4, 64]), op=ALU.mult),
              r=[("Osb", br), "coef"], w=["osb" if br == 0 else "otmp"])
            if br > 0:
                A("dve", lambda e: e.tensor_tensor(out=osb[:], in0=osb[:], in1=otmp[:], op=ALU.add),
                  r=["osb", "otmp"], w=["osb"])
        A("pool", lambda e: e.tensor_copy(out=attn[:, h * 256:(h + 1) * 256], in_=osb[:].rearrange("p g d -> p (g d)")),
          r=["osb"], w=[("attn", h)])
        if h == 0:
            return
        for c4 in range(4):
            A("pe", lambda e, c4=c4: e.transpose(out=pTb[:, c4, :], in_=attn[:, c4 * 128:(c4 + 1) * 128], identity=ident[:]),
              r=[("attn", 0), ("attn", 1), "ident"], w=["pTb"])
        A("dve", lambda e: e.tensor_copy(out=attnT[:], in_=pTb[:, 0:4, :]), r=["pTb"], w=["attnT"])
        A("sp", lambda e: e.dma_start(out=xt2[0][:], in_=xp[qt * 128:(qt + 1) * 128, :]), w=["xt2"], dma=True)
        for half in range(2):
            for k in range(8):
                lhs = (convyT[:, k, qt * 128:(qt + 1) * 128] if k < 4 else attnT[:, k - 4, :])
                A("pe", lambda e, k=k, half=half, lhs=lhs: e.matmul(pH[:], lhsT=lhs, rhs=w_out_bf[:, k, half * 512:(half + 1) * 512],
                                                                    start=(k == 0), stop=(k == 7)),
                  r=[("w_out_bf", k), ("convyT", qt // 4), "attnT"], w=["pH"])
            A("dve", lambda e, half=half: e.tensor_tensor(
                out=h_acc[:, qt, half * 512:(half + 1) * 512], in0=pH[:], in1=xt2[0][:, half * 512:(half + 1) * 512], op=ALU.add),
              r=["pH", "xt2"], w=[("h_acc", qt)])
        A("act", lambda e: e.activation(out=xt2[0][:], in_=h_acc[:, qt, :], func=AF.Square, accum_out=L["hss"][:, qt:qt + 1]),
          r=[("h_acc", qt), "hss", "xt2"], w=["xt2", "hss"])
        if qt + 2 < NT:
            emit_sel(qt + 2)

    emit_sel(0)
    emit_sel(1)
    for i in range(len(tiles) + LOOK):
        if i < len(tiles):
            emit_qk(tiles[i], i)
        j = i - LOOK
        if j >= 0:
            emit_pv(tiles[j])
            if tiles[j].get("end"):
                emit_end(tiles[j]["qt"], tiles[j]["h"])
    if debug:
        dbg["attn"] = dout("d_attn", [128, 512], BF16)
        A("sp", lambda e: e.dma_start(out=dbg["attn"], in_=attn[:]), r=[("attn", 0), ("attn", 1)], dma=True)
        dbg["h"] = dout("d_h", [128, NT * DM], F32)
        A("sp", lambda e: e.dma_start(out=dbg["h"], in_=h_acc[:]), r=[("h_acc", t) for t in range(NT)], dma=True)


def emit_norm_T(S, src, srckey, junk, ss, hn, gbc, gkey, ident, psT8, dst_k, dst_all, dstkey):
    A = S.add
    P = src.shape[0]
    A("pool", lambda e: e.memset(ss[0:P, 0:1], 0.0), w=[("ss", id(ss))])
    A("act", lambda e: e.activation(out=junk[0:P, :], in_=src, func=AF.Square, accum_out=ss[0:P, 0:1]),
      r=[srckey, ("ss", id(ss))], w=[("junk", id(junk)), ("ss", id(ss))])
    A("dve", lambda e: e.tensor_scalar(out=ss[0:P, 1:2], in0=ss[0:P, 0:1], scalar1=1.0 / DM, scalar2=1e-6,
                                       op0=ALU.mult, op1=ALU.add), r=[("ss", id(ss))], w=[("rs", id(ss))])
    A("act", lambda e: e.activation(out=ss[0:P, 1:2], in_=ss[0:P, 1:2], func=AF.Sqrt), r=[("rs", id(ss))], w=[("rs", id(ss))])
    A("dve", lambda e: e.reciprocal(out=ss[0:P, 1:2], in_=ss[0:P, 1:2]), r=[("rs", id(ss))], w=[("rs", id(ss))])
    A("dve", lambda e: e.scalar_tensor_tensor(out=hn[0:P, :], in0=src, scalar=ss[0:P, 1:2], in1=gbc[0:P, :],
                                              op0=ALU.mult, op1=ALU.mult), r=[srckey, ("rs", id(ss)), gkey], w=[("hn", id(hn))])
    for k in range(8):
        A("pe", lambda e, k=k: e.transpose(out=psT8[:, k, 0:P], in_=hn[0:P, k * 128:(k + 1) * 128], identity=ident[0:P, 0:P]),
          r=[("hn", id(hn)), "ident"], w=["pTb"])
    A("act", lambda e: e.copy(out=dst_all, in_=psT8[:, :, 0:P]), r=["pTb"], w=[dstkey])


def build_p3(nc, S, P3, sb, pst, L):
    A = S.add
    h_acc, hs_acc, hnTs, w_up, w_down, y_p, y_s, gbc_mlp, ident = (L[k] for k in (
        "h_acc", "hs_acc", "hnTs", "w_up", "w_down", "y_p", "y_s", "gbc_mlp", "ident"))
    with_s = "samp" in L["stages"]
    hnT = sb("hnT", [128, 8, SEQ], BF16, P3)
    junk3 = sb("junk3", [128, DM], BF16, P3)
    ss3 = sb("ss3", [128, 2], F32, P3)
    hnb = [junk3, sb("hn", [128, DM], BF16, P3)]
    pTb = pst("pTb3", [128, 8, 128], BF16, P3)
    hss = L["hss"]
    A("dve", lambda e: e.tensor_scalar(out=hss[:], in0=hss[:], scalar1=1.0 / DM, scalar2=1e-6, op0=ALU.mult, op1=ALU.add),
      r=["hss"], w=["hss"])
    A("act", lambda e: e.activation(out=hss[:], in_=hss[:], func=AF.Sqrt), r=["hss"], w=["hss"])
    A("dve", lambda e: e.reciprocal(out=hss[:], in_=hss[:]), r=["hss"], w=["hss"])
    for qt in range(NT):
        hb = qt % 2
        A("dve", lambda e, qt=qt, hb=hb: e.scalar_tensor_tensor(out=hnb[hb][:], in0=h_acc[:, qt, :], scalar=hss[:, qt:qt + 1],
                                                                in1=gbc_mlp[:], op0=ALU.mult, op1=ALU.mult),
          r=[("h_acc", qt), "hss", "gbc_mlp"], w=[("hnb", hb)])
        for k in range(8):
            A("pe", lambda e, k=k, hb=hb: e.transpose(out=pTb[:, k, :], in_=hnb[hb][:, k * 128:(k + 1) * 128], identity=ident[:]),
              r=[("hnb", hb), "ident"], w=["pTb3"])
        A("act", lambda e, qt=qt: e.copy(out=hnT[:, :, qt * 128:(qt + 1) * 128], in_=pTb[:]), r=["pTb3"], w=[("hnT", qt)])
    stgU = [sb("stgU0", [128, 8, 512], F32, P3)] * 2
    stgD = [sb("stgD0", [128, 4, DM], F32, P3)] * 2
    gbc_fin = sb("gbc_fin", [128, DM], F32, P3)
    A("sp", lambda e: e.dma_start(out=gbc_fin[:], in_=L["g_fin"].partition_broadcast(128)), w=["gbc_fin"], dma=True)
    wu = [sb(f"wu{i}", [128, 8, 512], BF16, P3) for i in range(2)]
    wd = [sb(f"wd{i}", [128, 4, DM], BF16, P3) for i in range(2)]
    aT = [sb(f"aT{i}", [128, 4, 512], BF16, P3) for i in range(2)]
    aTs = sb("aTs", [128, 4, NSB], BF16, P3)
    rl = [sb(f"rl{i}", [128, 512], F32, P3) for i in range(2)]
    pU = [pst(f"pU{i}", [128, 512], F32, P3) for i in range(2)]
    pD = [pst(f"pD{i}", [128, 512], F32, P3) for i in range(2)]
    yo = [stgD[0][:, 0, :]] * 2
    ucnt = [0]
    dcnt = [0]
    acnt = [0]
    groups = list(range(4)) + (["s"] if with_s else [])

    def load_w(fg):
        b = fg % 2
        A("sp", lambda e, fg=fg, b=b: e.dma_start(out=stgU[b][:], in_=w_up[:, fg * 512:(fg + 1) * 512].rearrange(
            "(k p) f -> p k f", p=128)), w=["stgU"], dma=True)
        A("sp", lambda e, fg=fg, b=b: e.dma_start(out=stgD[b][:], in_=w_down[fg * 512:(fg + 1) * 512, :].rearrange(
            "(c p) d -> p c d", p=128)), w=["stgD"], dma=True)
        A("act", lambda e, b=b: e.copy(out=wu[b][:], in_=stgU[b][:]), r=["stgU"], w=[("wu", b)])
        A("pool", lambda e, b=b: e.tensor_copy(out=wd[b][:], in_=stgD[b][:]), r=["stgD"], w=[("wd", b)])

    def emit_up(fg, TG):
        b = fg % 2
        ntok = 512 if TG != "s" else NSB
        if TG == "s":
            rhs_of = lambda k: hnTs[:, k, :]
            rk = ["hnTs"]
            adst = aTs
            akey = "aTs"
        else:
            rhs_of = lambda k, TG=TG: hnT[:, k, TG * 512:(TG + 1) * 512]
            rk = [("hnT", t) for t in range(4 * TG, 4 * TG + 4)]
            ai = acnt[0] % 2
            acnt[0] += 1
            adst = aT[ai]
            akey = ("aT", ai)
        for fc in range(4):
            ui = ucnt[0] % 2
            ucnt[0] += 1
            for k in range(8):
                A("pe", lambda e, k=k, fc=fc, ui=ui, b=b, rhs_of=rhs_of, ntok=ntok: e.matmul(
                    pU[ui][:, 0:ntok], lhsT=wu[b][:, k, fc * 128:(fc + 1) * 128], rhs=rhs_of(k),
                    start=(k == 0), stop=(k == 7)), r=[("wu", b)] + rk, w=[("pU", ui)])
            A("act", lambda e, ui=ui, ntok=ntok: e.activation(out=rl[ui][:, 0:ntok], in_=pU[ui][:, 0:ntok], func=AF.Relu),
              r=[("pU", ui)], w=[("rl", ui)])
            A("pool" if fc % 2 else "dve", lambda e, ui=ui, fc=fc, adst=adst, ntok=ntok: e.tensor_tensor(
                out=adst[:, fc, 0:ntok], in0=rl[ui][:, 0:ntok], in1=rl[ui][:, 0:ntok], op=ALU.mult),
              r=[("rl", ui)], w=[(akey, fc)])
        return adst, akey

    def emit_down(fg, TG, adst, akey):
        b = fg % 2
        tiles = range(4) if TG != "s" else [0]
        for tt in tiles:
            for half in range(2):
                di = dcnt[0] % 2
                dcnt[0] += 1
                mrows = 128 if TG != "s" else NSB
                for fc in range(4):
                    lhs = adst[:, fc, tt * 128:(tt + 1) * 128] if TG != "s" else adst[:, fc, :]
                    A("pe", lambda e, fc=fc, half=half, di=di, lhs=lhs, b=b, mrows=mrows: e.matmul(
                        pD[di][0:mrows, :], lhsT=lhs, rhs=wd[b][:, fc, half * 512:(half + 1) * 512],
                        start=(fc == 0), stop=(fc == 3)), r=[(akey, fc), ("wd", b)], w=[("pD", di)])
                if TG != "s":
                    t = 4 * TG + tt
                    A("dve", lambda e, t=t, half=half, di=di: e.tensor_tensor(
                        out=h_acc[:, t, half * 512:(half + 1) * 512], in0=pD[di][:], in1=h_acc[:, t, half * 512:(half + 1) * 512],
                        op=ALU.add), r=[("pD", di), ("h_acc", t)], w=[("h_acc", t)])
                else:
                    A("dve", lambda e, half=half, di=di: e.tensor_tensor(
                        out=hs_acc[:, half * 512:(half + 1) * 512], in0=pD[di][0:NSB, :],
                        in1=hs_acc[:, half * 512:(half + 1) * 512], op=ALU.add), r=[("pD", di), "hs_acc"], w=["hs_acc"])

    load_w(0)
    pending = None
    for fg in range(8):
        for gi, TG in enumerate(groups):
            cur = emit_up(fg, TG)
            if gi == 1 and fg + 1 < 8:
                load_w(fg + 1)
            if pending is not None:
                emit_down(*pending)
            pending = (fg, TG) + cur
    emit_down(*pending)
    outs = [(h_acc[:, t, :], ("h_acc", t), y_p[t * 128:(t + 1) * 128, :], 128) for t in range(NT)]
    if with_s:
        outs.append((hs_acc[:], "hs_acc", y_s, NSB))
    for i, (src, skey, dst, P) in enumerate(outs):
        yb = i % 2
        A("pool", lambda e, P=P: e.memset(ss3[0:P, 0:1], 0.0), w=["ss3"])
        A("act", lambda e, src=src, P=P: e.activation(out=junk3[0:P, :], in_=src, func=AF.Square, accum_out=ss3[0:P, 0:1]),
          r=[skey, "ss3", ("hnb", 0)], w=[("hnb", 0), "ss3"])
        A("dve", lambda e, P=P: e.tensor_scalar(out=ss3[0:P, 1:2], in0=ss3[0:P, 0:1], scalar1=1.0 / DM, scalar2=1e-6,
                                                op0=ALU.mult, op1=ALU.add), r=["ss3"], w=["rs3"])
        A("act", lambda e, P=P: e.activation(out=ss3[0:P, 1:2], in_=ss3[0:P, 1:2], func=AF.Sqrt), r=["rs3"], w=["rs3"])
        A("dve", lambda e, P=P: e.reciprocal(out=ss3[0:P, 1:2], in_=ss3[0:P, 1:2]), r=["rs3"], w=["rs3"])
        A("dve", lambda e, src=src, P=P, yb=yb: e.scalar_tensor_tensor(
            out=yo[yb][0:P], in0=src, scalar=ss3[0:P, 1:2], in1=gbc_fin[0:P, :], op0=ALU.mult, op1=ALU.mult),
          r=[skey, "rs3", "gbc_fin"], w=["stgD"])
        A("sp", lambda e, dst=dst, P=P, yb=yb: e.dma_start(out=dst, in_=yo[yb][0:P]), r=["stgD"], dma=True)


_NC_CACHE = {}


def _get_nc():
    if "nc" not in _NC_CACHE:
        _NC_CACHE["nc"] = build()[0]
    return _NC_CACHE["nc"]


def make_in_maps(inp, cores):
    f = lambda a: np.ascontiguousarray(np.asarray(a, dtype=np.float32))
    cache = f(inp["cache_kv"]).reshape(2560 * 128 * 4, 128)
    wdall = np.ascontiguousarray(np.concatenate(
        [f(inp["w_dw"])[0], f(inp["b_dw"]), f(inp["conv_ln_g"]), f(inp["conv_ln_b"])], axis=0))
    shared = dict(
        cache=cache, g_attn=f(inp["g_attn_norm"]), w_in=f(inp["w_in"])[0], wdall=wdall,
        w_ck=f(inp["w_cmp_k"])[0], w_cv=f(inp["w_cmp_v"])[0], w_out=f(inp["w_out"])[0], g_mlp=f(inp["g_mlp_norm"]),
        w_up=f(inp["w_up"])[0], w_down=f(inp["w_down"])[0], g_fin=f(inp["g_final"]).reshape(1, DM))
    maps = []
    for c in cores:
        sl = slice(c * NSB, (c + 1) * NSB)
        m = dict(shared)
        m["xp"] = f(inp["x_prompt"])[c]
        m["xs"] = f(inp["x_sample"])[sl, 0]
        m["cwin"] = f(inp["cache_win"])[0, sl].reshape(NSB, 512, 256)
        m["sconv"] = f(inp["state_conv"])[0, sl].reshape(NSB * 30, 512)
        m["ptab"] = np.ascontiguousarray(np.asarray(inp["page_table"], dtype=np.int32)[sl].reshape(1, NSB * 16))
        maps.append(m)
    return maps


def kernel(**inp):
    nc = _get_nc()
    cores = list(range(N_CORES))
    res = run_bass_kernel_spmd(nc, make_in_maps(inp, cores), core_ids=cores)
    R = res.results
    cat = lambda k: np.stack([np.asarray(r[k], dtype=np.float32) for r in R], axis=0)
    y_p = cat("y_p")
    y_s = cat("y_s").reshape(128, 1, DM)
    kv_p = cat("kv_p").reshape(1, 8, SEQ, 4, 2, 64)
    win_p = cat("win_p").reshape(1, 8, 512, 2, 2, 64)
    conv_p = cat("conv_p").reshape(1, 8, 30, 512)
    kv_s = cat("kv_s").reshape(1, 128, 1, 4, 2, 64)
    win_s = cat("win_s").reshape(1, 128, 512, 2, 2, 64)
    conv_s = cat("conv_s").reshape(1, 128, 30, 512)
    return (y_p, y_s, kv_p, win_p, conv_p, kv_s, win_s, conv_s)


def _dap(ap, offset, dims):
    return bass.AP(ap.tensor, offset, [list(d) for d in dims])


def build_samp(nc, S, SP, sb, pst, L):
    A = S.add
    (xs, cache, cwin, sconv, ptab, wdall, w_cv, kv_s, win_s, conv_s, w_in_bf, w_out_bf, ident, identf, gbc_mlp, hs_acc,
     hnTs, wckb, g_attn) = (L[k] for k in ("xs", "cache", "cwin", "sconv", "ptab", "wdall", "w_cv", "kv_s", "win_s",
                                            "conv_s", "w_in_bf", "w_out_bf", "ident", "identf", "gbc_mlp", "hs_acc",
                                            "hnTs", "wckb", "g_attn"))
    debug, dbg, dout = L["debug"], L["dbg"], L["dout"]
    zs_d = nc.dram_tensor("zs_d", [NSB, INC], F32, kind="Internal").ap()
    ptb = sb("ptb", [128, NSB * 16], I32, SP)
    iop = sb("iop", [128, 1], I32, SP)
    idx = sb("idx", [128, NSB * 16], I32, SP)
    A("sp", lambda e: e.dma_start(out=ptb[:], in_=ptab.partition_broadcast(128)), w=["ptb"], dma=True)
    A("pool", lambda e: e.iota(iop[:], pattern=[[0, 1]], base=0, channel_multiplier=1), w=["iop"])
    A("dve", lambda e: e.tensor_scalar(out=idx[:], in0=ptb[:], scalar1=128, scalar2=iop[:, 0:1], op0=ALU.mult, op1=ALU.add),
      r=["ptb", "iop"], w=["idx"])
    cacheR = cache.rearrange("(r s) c -> r (s c)", s=4)

    xs_sb = sb("xs_sb", [NSB, DM], F32, SP)
    gbc_a = sb("gbc_a", [NSB, DM], F32, SP)
    xnTs = sb("xnTs", [128, 8, NSB], BF16, SP)
    junk = sb("s_junk", [NSB, DM], BF16, SP)
    ssx = sb("s_ss", [128, 2], F32, SP)
    hn = sb("s_hn", [NSB, DM], BF16, SP)
    zs = sb("zs", [NSB, INC], F32, SP)
    pA = pst("s_pA", [128, 512], F32, SP)
    pB = pst("s_pB", [128, 512], F32, SP)
    pTb = pst("s_pTb", [128, 8, 128], BF16, SP)
    pKT = pst("s_pKT", [128, 8, 128], BF16, SP)
    pS = [pst(f"s_pS{i}", [128, 512], F32, SP) for i in range(3)]
    A("sp", lambda e: e.dma_start(out=xs_sb[:], in_=xs), w=["xs_sb"], dma=True)
    A("sp", lambda e: e.dma_start(out=gbc_a[:], in_=g_attn.partition_broadcast(NSB)), w=["gbc_a"], dma=True)
    emit_norm_T(S, xs_sb[:], "xs_sb", junk, ssx, hn, gbc_a, "gbc_a", ident, pTb, None, xnTs[:], "xnTs")
    for ci, c0 in enumerate(range(0, INC, 512)):
        n = min(512, INC - c0)
        for k in range(8):
            A("pe", lambda e, k=k, c0=c0, n=n: e.matmul(pA[0:NSB, 0:n], lhsT=xnTs[:, k, :], rhs=w_in_bf[:, k, c0:c0 + n],
                                                        start=(k == 0), stop=(k == 7)), r=["xnTs", ("w_in_bf", k)], w=["s_pA"])
        A("act", lambda e, c0=c0, n=n: e.copy(out=zs[:, c0:c0 + n], in_=pA[0:NSB, 0:n]), r=["s_pA"], w=["zs"])
    A("sp", lambda e: e.dma_start(out=zs_d, in_=zs[:]), r=["zs"], w=["zs_d"], dma=True)
    A("sp", lambda e: e.dma_start(out=kv_s, in_=zs[:, 1536:2048]), r=["zs"], dma=True)
    A("sp", lambda e: e.dma_start(out=win_s[:, 511, :], in_=zs[:, 2048:2304]), r=["zs"], dma=True)
    for b in range(NSB):
        A("sp", lambda e, b=b: e.dma_start(out=win_s[b, 0:511, :], in_=cwin[b, 1:512, :]), dma=True)
    A("sp", lambda e: e.dma_start(out=conv_s.rearrange("(b j) c -> b (j c)", j=30)[:, 0:29 * 512],
                                  in_=sconv.rearrange("(b j) c -> b (j c)", j=30)[:, 512:30 * 512]), dma=True)
    if int(os.environ.get("SAMP_STOP", "99")) <= 1:
        return
    Qrows = sb("Qrows", [128, 64], F32, SP)
    Kn = sb("Kn", [128, 2, 64], F32, SP)
    Vn = sb("Vn", [128, 2, 64], F32, SP)
    Grows = sb("Grows", [128, 3], F32, SP)
    for b in range(NSB):
        rows = slice(8 * b, 8 * b + 8)
        A("sp", lambda e, b=b, rows=rows: e.dma_start(out=Qrows[rows, :], in_=_dap(zs_d, b * INC + 1024, [[64, 8], [1, 64]])),
          r=["zs_d"], w=["Qrows"], dma=True)
        for j, col in enumerate((1536 + 256, 1536 + 512)):
            A("sp", lambda e, b=b, rows=rows, j=j, col=col: e.dma_start(
                out=Kn[rows, j, :], in_=_dap(zs_d, b * INC + col, [[64, 2], [0, 4], [1, 64]])), r=["zs_d"], w=["Kn"], dma=True)
        for j, col in enumerate((1536 + 384, 1536 + 640)):
            A("sp", lambda e, b=b, rows=rows, j=j, col=col: e.dma_start(
                out=Vn[rows, j, :], in_=_dap(zs_d, b * INC + col, [[64, 2], [0, 4], [1, 64]])), r=["zs_d"], w=["Vn"], dma=True)
        A("sp", lambda e, b=b, rows=rows: e.dma_start(out=Grows[rows, :], in_=_dap(zs_d, b * INC + 2304, [[3, 8], [1, 3]])),
          r=["zs_d"], w=["Grows"], dma=True)
    QTpad = sb("QTpad", [128, NSB, 128], BF16, SP)
    A("pool", lambda e: e.memset(QTpad[:], 0.0), w=["QTpad"])
    qsrc = sb("qsrc", [NSB, 4, 2, 64], F32, SP)
    A("dve", lambda e: e.tensor_copy(out=qsrc[:], in_=zs[:, 1024:1536].rearrange("p (h g d) -> p g h d", h=2, g=4)),
      r=["zs"], w=["qsrc"])
    for g in range(4):
        A("pe", lambda e, g=g: e.transpose(out=pB[:, g * 16:(g + 1) * 16], in_=qsrc[:, g, :, :].rearrange("p h d -> p (h d)"),
                                           identity=identf[0:NSB, 0:NSB]), r=["qsrc", "identf"], w=["s_pB"])
    QTflat = QTpad[:].rearrange("p b c -> p (b c)")
    for h in range(2):
        for g in range(4):
            c0 = 4 * h + g
            A("act", lambda e, h=h, g=g, c0=c0: e.copy(out=QTflat[64 * h:64 * h + 64, c0:c0 + 136 * 15 + 1:136],
                                                       in_=pB[64 * h:64 * h + 64, g * 16:(g + 1) * 16]),
              r=["s_pB", "QTpad"], w=["QTpad"])
    if int(os.environ.get("SAMP_STOP", "99")) <= 2:
        return
    pidx = sb("pidx", [128, 4], I32, SP)
    pf = sb("pf", [128, 4], F32, SP)
    A("pool", lambda e: e.iota(pidx[:, 0:1], pattern=[[0, 1]], base=0, channel_multiplier=1), w=["pidx"])
    A("dve", lambda e: e.tensor_single_scalar(out=pidx[:, 1:2], in_=pidx[:, 0:1], scalar=2, op=ALU.arith_shift_right),
      r=["pidx"], w=["pidx"])
    A("dve", lambda e: e.tensor_single_scalar(out=pidx[:, 2:3], in_=pidx[:, 1:2], scalar=1, op=ALU.bitwise_and),
      r=["pidx"], w=["pidx"])
    A("dve", lambda e: e.tensor_copy(out=pf[:, 0:3], in_=pidx[:, 0:3]), r=["pidx"], w=["pf"])
    coli = sb("coli", [128, 128], I32, SP)
    colf = sb("colf", [128, 128], F32, SP)
    GG = sb("GG", [128, 128], F32, SP)
    A("pool", lambda e: e.iota(coli[:], pattern=[[1, 128]], base=0, channel_multiplier=0), w=["coli"])
    A("dve", lambda e: e.tensor_single_scalar(out=coli[:], in_=coli[:], scalar=2, op=ALU.arith_shift_right), r=["coli"], w=["coli"])
    A("dve", lambda e: e.tensor_copy(out=colf[:], in_=coli[:]), r=["coli"], w=["colf"])
    A("dve", lambda e: e.tensor_scalar(out=GG[:], in0=colf[:], scalar1=pf[:, 1:2], scalar2=None, op0=ALU.is_equal),
      r=["colf", "pf"], w=["GG"])
    Mh = sb("Mh", [128, 2, 16], F32, SP)
    for half in range(2):
        A("dve", lambda e, half=half: e.tensor_scalar(out=Mh[:, half, :], in0=colf[:, 0:64:4], scalar1=float(16 * half),
                                                      scalar2=pf[:, 1:2], op0=ALU.add, op1=ALU.is_equal),
          r=["colf", "pf"], w=["Mh"])
    wcvb = L["wcvb"]
    Wk = sb("Wk", [128, 32], F32, SP)
    Wv = sb("Wv", [128, 32], F32, SP)
    wtmp = sb("wtmp", [128, 32], F32, SP)
    for (src, dst, key) in ((wckb, Wk, "Wk"), (wcvb, Wv, "Wv")):
        A("dve", lambda e, src=src: e.tensor_tensor(out=wtmp[:], in0=src[:, 1, :], in1=src[:, 0, :], op=ALU.subtract),
          r=["wckb", "wcvb"], w=["wtmp"])
        A("dve", lambda e, src=src, dst=dst: e.scalar_tensor_tensor(out=dst[:], in0=wtmp[:], scalar=pf[:, 2:3], in1=src[:, 0, :],
                                                                    op0=ALU.mult, op1=ALU.add), r=["wtmp", "pf", "wckb", "wcvb"], w=[key])
    if int(os.environ.get("SAMP_STOP", "99")) <= 3:
        return
    SS_ = sb("SS_", [128, 2, SEQ], F32, SP)
    Scmp = SS_[:, 0, :]
    Ssel = SS_[:, 1, :]
    Swin = sb("Swin", [128, 512], F32, SP)
    NKB = 3
    kst = [sb(f"kst{i}", [128, 4, 4, 128], F32, SP) for i in range(NKB)]
    kbf = [sb(f"kbf{i}", [128, 4, 2, 128], BF16, SP) for i in range(2)]
    KTsb = [sb(f"KTsb{i}", [128, 2, 512], BF16, SP) for i in range(2)]
    it = 0
    for pg in range(5):
        for b in range(NSB):
            bi = it % NKB
            b2 = it % 2
            it += 1
            if pg < 4:
                for i in range(4):
                    col = b * 16 + pg * 4 + i
                    A("pool", lambda e, bi=bi, i=i, col=col: e.indirect_dma_start(
                        out=kst[bi][:, i, :, :].rearrange("p s c -> p (s c)"), out_offset=None, in_=cacheR,
                        in_offset=bass.IndirectOffsetOnAxis(ap=idx[:, col:col + 1], axis=0)),
                      r=["idx"], w=[("kst", bi, i)], dma=True)
                A("act", lambda e, bi=bi, b2=b2: e.copy(out=kbf[b2][:], in_=kst[bi][:, :, 0:4:2, :]),
                  r=[("kst", bi, i) for i in range(4)], w=[("kbf", b2)])
                for i in range(4):
                    for si in range(2):
                        A("pe", lambda e, b2=b2, i=i, si=si: e.transpose(out=pKT[:, si * 4 + i, :], in_=kbf[b2][:, i, si, :],
                                                                         identity=ident[:]), r=[("kbf", b2), "ident"], w=["s_pKT"])
                A("dve", lambda e, b2=b2: e.tensor_copy(out=KTsb[b2][:], in_=pKT[:].rearrange("p (s i) t -> p s (i t)", s=2)),
                  r=["s_pKT"], w=[("KTsb", b2)])
                for si in range(2):
                    A("pe", lambda e, b2=b2, si=si, b=b: e.matmul(pS[si][:], lhsT=QTpad[:, b, :], rhs=KTsb[b2][:, si, :],
                                                                  start=(b == 0), stop=(b == NSB - 1)),
                      r=["QTpad", ("KTsb", b2)], w=[("s_pS", si)])
            else:
                A("sp", lambda e, bi=bi, b=b: e.dma_start(
                    out=kst[bi][:, :, 0, :], in_=cwin[b].rearrange("(i p) (s c) -> p i s c", p=128, c=128)[:, :, 0, :]),
                  w=[("kst", bi, i) for i in range(4)], dma=True)
                A("act", lambda e, bi=bi, b2=b2: e.copy(out=kbf[b2][:, :, 0, :], in_=kst[bi][:, :, 0, :]),
                  r=[("kst", bi, i) for i in range(4)], w=[("kbf", b2)])
                for i in range(4):
                    A("pe", lambda e, b2=b2, i=i: e.transpose(out=pKT[:, i, :], in_=kbf[b2][:, i, 0, :], identity=ident[:]),
                      r=[("kbf", b2), "ident"], w=["s_pKT"])
                A("dve", lambda e, b2=b2: e.tensor_copy(out=KTsb[b2][:, 0, :], in_=pKT[:, 0:4, :].rearrange("p i t -> p (i t)")),
                  r=["s_pKT"], w=[("KTsb", b2)])
                A("pe", lambda e, b2=b2, b=b: e.matmul(pS[2][:], lhsT=QTpad[:, b, :], rhs=KTsb[b2][:, 0, :],
                                                       start=(b == 0), stop=(b == NSB - 1)), r=["QTpad", ("KTsb", b2)], w=[("s_pS", 2)])
        if pg < 4:
            A("act", lambda e, pg=pg: e.copy(out=SS_[:, 0, pg * 512:(pg + 1) * 512], in_=pS[0][:]), r=[("s_pS", 0)], w=["Scmp"])
            A("dve", lambda e, pg=pg: e.tensor_copy(out=SS_[:, 1, pg * 512:(pg + 1) * 512], in_=pS[1][:]), r=[("s_pS", 1)], w=["Ssel"])
        else:
            A("act", lambda e: e.copy(out=Swin[:], in_=pS[2][:]), r=[("s_pS", 2)], w=["Swin"])
    if int(os.environ.get("SAMP_STOP", "99")) <= 4:
        return
    big = sb("s_big", [128, SEQ], F32, SP)
    sc = sb("s_sc", [128, 64], F32, SP)
    st = sb("s_st", [128, 16], F32, SP)
    pb32 = sb("s_pb32", [128, 32], F32, SP)
    m8 = sb("s_m8", [128, 8], F32, SP)
    wk32 = sb("s_wk32", [128, 32], F32, SP)
    selm = sb("s_selm", [128, 32], F32, SP)
    Pc = sb("Pc", [128, SEQ], BF16, SP)
    Ps = sb("Ps", [128, SEQ], BF16, SP)
    Pw = sb("Pw", [128, 512], BF16, SP)
    A("dve", lambda e: e.tensor_tensor(out=big[:].rearrange("p (c j) -> p c j", j=32), in0=Scmp.rearrange("p (c j) -> p c j", j=32),
                                       in1=Wk[:].unsqueeze(1).to_broadcast([128, 64, 32]), op=ALU.mult), r=["Scmp", "Wk"], w=["s_big"])
    A("dve", lambda e: e.tensor_reduce(out=sc[:], in_=big[:].rearrange("p (c j) -> p c j", j=32), axis=AX.X, op=ALU.add),
      r=["s_big"], w=["s_sc"])
    A("pool", lambda e: e.memset(st[:], 0.0), w=["s_st"])
    A("act", lambda e: e.activation(out=sc[:], in_=sc[:], func=AF.Exp, scale=SCALE, accum_out=st[:, 0:1]), r=["s_sc", "s_st"], w=["s_sc", "s_st"])
    A("dve", lambda e: e.reciprocal(out=st[:, 1:2], in_=st[:, 0:1]), r=["s_st"], w=["s_st"])
    A("dve", lambda e: e.tensor_scalar(out=sc[:], in0=sc[:], scalar1=st[:, 1:2], scalar2=None, op0=ALU.mult), r=["s_sc", "s_st"], w=["s_sc"])
    A("pe", lambda e: e.matmul(pA[:, 0:64], lhsT=GG[:], rhs=sc[:], start=True, stop=True), r=["GG", "s_sc"], w=["s_pA"])
    A("dve", lambda e: e.tensor_reduce(out=pb32[:], in_=pA[:, 0:64].rearrange("p (b t) -> p b t", t=2), axis=AX.X, op=ALU.add),
      r=["s_pA"], w=["s_pb32"])
    A("dve", lambda e: e.tensor_scalar(out=pb32[:, 0:1], in0=pb32[:, 0:1], scalar1=5.0, scalar2=None, op0=ALU.add), r=["s_pb32"], w=["s_pb32"])
    A("dve", lambda e: e.tensor_scalar(out=pb32[:, 31:32], in0=pb32[:, 31:32], scalar1=5.0, scalar2=None, op0=ALU.add), r=["s_pb32"], w=["s_pb32"])
    A("dve", lambda e: e.max(out=m8[:], in_=pb32[:]), r=["s_pb32"], w=["s_m8"])
    A("dve", lambda e: e.match_replace(out=wk32[:], in_to_replace=m8[:], in_values=pb32[:], imm_value=-3e38), r=["s_m8", "s_pb32"], w=["s_wk32"])
    A("dve", lambda e: e.max(out=m8[:], in_=wk32[:]), r=["s_wk32"], w=["s_m8"])
    A("dve", lambda e: e.tensor_scalar(out=selm[:], in0=pb32[:], scalar1=m8[:, 6:7], scalar2=None, op0=ALU.is_ge), r=["s_pb32", "s_m8"], w=["s_selm"])
    A("dve", lambda e: e.tensor_tensor(out=big[:].rearrange("p (c j) -> p c j", j=32), in0=sc[:].unsqueeze(2).to_broadcast([128, 64, 32]),
                                       in1=Wv[:].unsqueeze(1).to_broadcast([128, 64, 32]), op=ALU.mult), r=["s_sc", "Wv", "s_big"], w=["s_big"])
    A("act", lambda e: e.copy(out=Pc[:], in_=big[:]), r=["s_big"], w=["Pc"])
    for j in range(2):
        A("dve", lambda e, j=j: e.tensor_tensor(out=big[:, 0:64], in0=Qrows[:], in1=Kn[:, j, :], op=ALU.mult), r=["Qrows", "Kn", "s_big", "Pc"], w=["s_big"])
        A("dve", lambda e, j=j: e.tensor_reduce(out=st[:, 2 + 2 * j:3 + 2 * j], in_=big[:, 0:64], axis=AX.X, op=ALU.add), r=["s_big"], w=["s_st"])
        A("act", lambda e, j=j: e.activation(out=st[:, 3 + 2 * j:4 + 2 * j], in_=st[:, 2 + 2 * j:3 + 2 * j], func=AF.Exp, scale=SCALE),
          r=["s_st"], w=["s_st"])
    A("act", lambda e: e.activation(out=big[:], in_=Ssel, func=AF.Exp, scale=SCALE), r=["Ssel", "s_big"], w=["s_big"])
    A("dve", lambda e: e.tensor_tensor(out=big[:].rearrange("p (b t) -> p b t", t=64), in0=big[:].rearrange("p (b t) -> p b t", t=64),
                                       in1=selm[:].unsqueeze(2).to_broadcast([128, 32, 64]), op=ALU.mult), r=["s_big", "s_selm"], w=["s_big"])
    A("dve", lambda e: e.tensor_reduce(out=st[:, 6:7], in_=big[:], axis=AX.X, op=ALU.add), r=["s_big"], w=["s_st"])
    A("pool", lambda e: e.tensor_copy(out=Ps[:], in_=big[:]), r=["s_big"], w=["Ps"])
    A("act", lambda e: e.activation(out=big[:, 0:512], in_=Swin[:], func=AF.Exp, scale=SCALE), r=["Swin", "s_big", "Ps"], w=["s_big"])
    A("dve", lambda e: e.tensor_reduce(out=st[:, 7:8], in_=big[:, 0:512], axis=AX.X, op=ALU.add), r=["s_big"], w=["s_st"])
    A("pool", lambda e: e.tensor_copy(out=Pw[:], in_=big[:, 0:512]), r=["s_big"], w=["Pw"])
    A("dve", lambda e: e.tensor_tensor(out=st[:, 8:9], in0=st[:, 6:7], in1=st[:, 3:4], op=ALU.add), r=["s_st"], w=["s_st"])
    A("dve", lambda e: e.tensor_tensor(out=st[:, 9:10], in0=st[:, 7:8], in1=st[:, 5:6], op=ALU.add), r=["s_st"], w=["s_st"])
    A("dve", lambda e: e.reciprocal(out=st[:, 8:10], in_=st[:, 8:10]), r=["s_st"], w=["s_st"])
    if int(os.environ.get("SAMP_STOP", "99")) <= 5:
        return
    PTc = sb("PTc", [128, 16, 128], BF16, SP)
    PTs = sb("PTs", [128, 16, 128], BF16, SP)
    PTw = sb("PTw", [128, 4, 128], BF16, SP)
    for (src, dst, key, n) in ((Pc, PTc, "PTc", 16), (Ps, PTs, "PTs", 16), (Pw, PTw, "PTw", 4)):
        for i0 in range(0, n, 8):
            m = min(8, n - i0)
            for i in range(m):
                A("pe", lambda e, src=src, i=i, i0=i0: e.transpose(out=pKT[:, i, :], in_=src[:, (i0 + i) * 128:(i0 + i + 1) * 128],
                                                                   identity=ident[:]), r=["ident", "Pc", "Ps", "Pw"], w=["s_pKT"])
            A("dve", lambda e, dst=dst, i0=i0, m=m: e.tensor_copy(out=dst[:, i0:i0 + m, :], in_=pKT[:, 0:m, :]), r=["s_pKT"], w=[key])
    S.barrier()
    vst = [SS_[:, vi, :].rearrange("p (i s c) -> p i s c", i=4, s=4) for vi in range(2)]
    vbf = [sb(f"vbf{i}", [128, 4, 2, 128], BF16, SP) for i in range(2)]
    vwst = [sb(f"vwst{i}", [128, 4, 128], F32, SP) for i in range(2)]
    vwbf = [sb(f"vwbf{i}", [128, 4, 128], BF16, SP) for i in range(2)]
    Oacc = sb("Oacc", [128, 3, 64], F32, SP)
    otmp = big[:, 0:1024].rearrange("p (b h d) -> p b h d", b=8, h=2)
    ored = sb("s_ored", [128, 64], F32, SP)
    A("pool", lambda e: e.memset(Oacc[:], 0.0), w=["Oacc"])
    pW2 = pst("s_pW2", [128, 512], F32, SP)
    pO = [[pA, pB], [pS[0], pS[1]], [pS[2], pW2]]
    pkeys = [["s_pA", "s_pB"], [("s_pS", 0), ("s_pS", 1)], [("s_pS", 2), "s_pW2"]]
    vcnt = 0
    for half in range(2):
        for bl in range(8):
            b = half * 8 + bl
            bank, cb = bl // 4, (bl % 4) * 128
            wi = b % 2
            A("sp", lambda e, wi=wi, b=b: e.dma_start(
                out=vwst[wi][:], in_=cwin[b].rearrange("(i p) (s c) -> p i s c", p=128, c=128)[:, :, 1, :]), w=[("vwst", wi)], dma=True)
            A("act", lambda e, wi=wi: e.copy(out=vwbf[wi][:], in_=vwst[wi][:]), r=[("vwst", wi)], w=[("vwbf", wi)])
            for q4 in range(4):
                vi = vcnt % 2
                vcnt += 1
                for i in range(4):
                    col = b * 16 + q4 * 4 + i
                    A("pool", lambda e, vi=vi, i=i, col=col: e.indirect_dma_start(
                        out=vst[vi][:, i, :, :].rearrange("p s c -> p (s c)"), out_offset=None, in_=cacheR,
                        in_offset=bass.IndirectOffsetOnAxis(ap=idx[:, col:col + 1], axis=0)), r=["idx"], w=[("vst", vi, i)], dma=True)
                A("act", lambda e, vi=vi: e.copy(out=vbf[vi][:], in_=vst[vi][:, :, 1:4:2, :]),
                  r=[("vst", vi, i) for i in range(4)], w=[("vbf", vi)])
                for br, (PT_, ptk) in enumerate(((PTc, "PTc"), (PTs, "PTs"))):
                    for i in range(4):
                        pgi = q4 * 4 + i
                        A("pe", lambda e, br=br, bank=bank, cb=cb, PT_=PT_, i=i, pgi=pgi, vi=vi: e.matmul(
                            pO[br][bank][:, cb:cb + 128], lhsT=PT_[:, pgi, :], rhs=vbf[vi][:, i, br, :],
                            start=(pgi == 0), stop=(pgi == 15)), r=[ptk, ("vbf", vi)], w=[pkeys[br][bank]])
            for i in range(4):
                A("pe", lambda e, bank=bank, cb=cb, i=i, wi=wi: e.matmul(
                    pO[2][bank][:, cb:cb + 128], lhsT=PTw[:, i, :], rhs=vwbf[wi][:, i, :], start=(i == 0), stop=(i == 3)),
                  r=["PTw", ("vwbf", wi)], w=[pkeys[2][bank]])
        for br in range(3):
            for bank in range(2):
                A("dve", lambda e, br=br, bank=bank, half=half: e.tensor_tensor(
                    out=otmp[:, bank * 4:(bank + 1) * 4, :, :].rearrange("p b h d -> p (b h) d"),
                    in0=pO[br][bank][:].rearrange("p (j d) -> p j d", d=64),
                    in1=Mh[:, half, bank * 8:(bank + 1) * 8].unsqueeze(2).to_broadcast([128, 8, 64]), op=ALU.mult),
                  r=[pkeys[br][bank], "Mh"], w=["s_otmp"])
            A("dve", lambda e: e.tensor_reduce(out=ored[:], in_=otmp.rearrange("p b h d -> p d (b h)"), axis=AX.X, op=ALU.add),
              r=["s_otmp"], w=["s_ored"])
            A("dve", lambda e, br=br: e.tensor_tensor(out=Oacc[:, br, :], in0=Oacc[:, br, :], in1=ored[:], op=ALU.add),
              r=["s_ored", "Oacc"], w=["Oacc"])
    if int(os.environ.get("SAMP_STOP", "99")) <= 6:
        return
    G3 = sb("G3", [128, 3], F32, SP)
    A("act", lambda e: e.activation(out=G3[:], in_=Grows[:], func=AF.Sigmoid), r=["Grows"], w=["G3"])
    arow = sb("arow", [128, 128], F32, SP)
    tmp64 = sb("tmp64", [128, 64], F32, SP)
    A("dve", lambda e: e.tensor_scalar(out=arow[:, 0:64], in0=Oacc[:, 0, :], scalar1=G3[:, 0:1], scalar2=None, op0=ALU.mult),
      r=["Oacc", "G3"], w=["arow"])
    for j, br in ((0, 1), (1, 2)):
        A("dve", lambda e, j=j, br=br: e.scalar_tensor_tensor(out=tmp64[:], in0=Vn[:, j, :], scalar=st[:, 3 + 2 * j:4 + 2 * j],
                                                              in1=Oacc[:, br, :], op0=ALU.mult, op1=ALU.add),
          r=["Vn", "s_st", "Oacc"], w=["tmp64"])
        A("dve", lambda e, j=j, br=br: e.tensor_scalar(out=tmp64[:], in0=tmp64[:], scalar1=st[:, 8 + j:9 + j], scalar2=G3[:, br:br + 1],
                                                       op0=ALU.mult, op1=ALU.mult), r=["tmp64", "s_st", "G3"], w=["tmp64"])
        A("dve", lambda e: e.tensor_tensor(out=arow[:, 0:64], in0=arow[:, 0:64], in1=tmp64[:], op=ALU.add), r=["arow", "tmp64"], w=["arow"])
    if int(os.environ.get("SAMP_STOP", "99")) <= 7:
        return
    glus = sb("glus", [NSB, 512], F32, SP)
    cvb = sb("cvb", [NSB, 4, 512], F32, SP)
    A("sp", lambda e: e.dma_start(out=cvb[:], in_=wdall[30:34, :].partition_broadcast(NSB)), w=["cvb"], dma=True)
    A("act", lambda e: e.activation(out=glus[:], in_=zs[:, 512:1024], func=AF.Sigmoid), r=["zs"], w=["glus"])
    A("dve", lambda e: e.tensor_tensor(out=glus[:], in0=glus[:], in1=zs[:, 0:512], op=ALU.mult), r=["glus", "zs"], w=["glus"])
    A("sp", lambda e: e.dma_start(out=conv_s.rearrange("(b j) c -> b j c", j=30)[:, 29, :], in_=glus[:]), r=["glus"], dma=True)
    Xc = sb("Xc", [120, 512], F32, SP)
    Wrep = sb("Wrep", [120, 512], F32, SP)
    sel4 = sb("sel4", [120, 4, NSB], F32, SP)
    for r4 in range(4):
        A("sp", lambda e, r4=r4: e.dma_start(out=Wrep[r4 * 30:(r4 + 1) * 30, :], in_=wdall[0:30, :]), w=["Wrep"], dma=True)
    A("pool", lambda e: e.memset(sel4[:], 1.0), w=["sel4"])
    for i4 in range(4):
        A("pool", lambda e, i4=i4: e.affine_select(out=sel4[:, i4, :], in_=sel4[:, i4, :], pattern=[[-30, NSB]], compare_op=ALU.is_ge,
                                                   fill=0.0, base=120 * i4, channel_multiplier=1), r=["sel4"], w=["sel4"])
        A("pool", lambda e, i4=i4: e.affine_select(out=sel4[:, i4, :], in_=sel4[:, i4, :], pattern=[[30, NSB]], compare_op=ALU.is_ge,
                                                   fill=0.0, base=29 - 120 * i4, channel_multiplier=-1), r=["sel4"], w=["sel4"])
    for i4 in range(4):
        A("sp", lambda e, i4=i4: e.dma_start(out=Xc[:], in_=sconv[i4 * 120:(i4 + 1) * 120, :]), w=["Xc"], dma=True)
        A("dve", lambda e: e.tensor_tensor(out=Xc[:], in0=Xc[:], in1=Wrep[:], op=ALU.mult), r=["Xc", "Wrep"], w=["Xc"])
        A("pe", lambda e, i4=i4: e.matmul(pB[0:NSB, :], lhsT=sel4[:, i4, :], rhs=Xc[:], start=(i4 == 0), stop=(i4 == 3)),
          r=["sel4", "Xc"], w=["s_pB"])
    yc = sb("yc", [NSB, 512], F32, SP)
    ycs = sb("ycs", [NSB, 4], F32, SP)
    A("dve", lambda e: e.tensor_tensor(out=yc[:], in0=glus[:], in1=cvb[:, 0, :], op=ALU.mult), r=["glus", "cvb"], w=["yc"])
    A("dve", lambda e: e.tensor_tensor(out=yc[:], in0=yc[:], in1=pB[0:NSB, :], op=ALU.add), r=["yc", "s_pB"], w=["yc"])
    A("dve", lambda e: e.tensor_tensor(out=yc[:], in0=yc[:], in1=cvb[:, 1, :], op=ALU.add), r=["yc", "cvb"], w=["yc"])
    A("dve", lambda e: e.tensor_reduce(out=ycs[:, 0:1], in_=yc[:], axis=AX.X, op=ALU.add), r=["yc"], w=["ycs"])
    A("dve", lambda e: e.tensor_scalar(out=ycs[:, 0:1], in0=ycs[:, 0:1], scalar1=1.0 / 512.0, scalar2=None, op0=ALU.mult), r=["ycs"], w=["ycs"])
    A("dve", lambda e: e.tensor_scalar(out=yc[:], in0=yc[:], scalar1=ycs[:, 0:1], scalar2=None, op0=ALU.subtract), r=["yc", "ycs"], w=["yc"])
    ysq_ = attn_s_early = sb("ysq_", [NSB, 512], F32, SP)
    A("dve", lambda e: e.tensor_tensor(out=ysq_[:], in0=yc[:], in1=yc[:], op=ALU.mult), r=["yc"], w=["ysq_"])
    A("dve", lambda e: e.tensor_reduce(out=ycs[:, 1:2], in_=ysq_[:], axis=AX.X, op=ALU.add), r=["ysq_"], w=["ycs"])
    A("dve", lambda e: e.tensor_scalar(out=ycs[:, 1:2], in0=ycs[:, 1:2], scalar1=1.0 / 512.0, scalar2=1e-5, op0=ALU.mult, op1=ALU.add),
      r=["ycs"], w=["ycs"])
    A("act", lambda e: e.activation(out=ycs[:, 1:2], in_=ycs[:, 1:2], func=AF.Sqrt), r=["ycs"], w=["ycs"])
    A("dve", lambda e: e.reciprocal(out=ycs[:, 1:2], in_=ycs[:, 1:2]), r=["ycs"], w=["ycs"])
    A("dve", lambda e: e.scalar_tensor_tensor(out=yc[:], in0=yc[:], scalar=ycs[:, 1:2], in1=cvb[:, 2, :], op0=ALU.mult, op1=ALU.mult),
      r=["yc", "ycs", "cvb"], w=["yc"])
    A("dve", lambda e: e.tensor_tensor(out=yc[:], in0=yc[:], in1=cvb[:, 3, :], op=ALU.add), r=["yc", "cvb"], w=["yc"])
    ycb = sb("ycb", [NSB, 512], BF16, SP)
    A("act", lambda e: e.activation(out=ycb[:], in_=yc[:], func=AF.Silu), r=["yc"], w=["ycb"])
    cyT = sb("cyT", [128, 4, NSB], BF16, SP)
    for c4 in range(4):
        A("pe", lambda e, c4=c4: e.transpose(out=pTb[:, c4, 0:NSB], in_=ycb[:, c4 * 128:(c4 + 1) * 128], identity=ident[0:NSB, 0:NSB]),
          r=["ycb", "ident"], w=["pTb"])
    A("act", lambda e: e.copy(out=cyT[:], in_=pTb[:, 0:4, 0:NSB]), r=["pTb"], w=["cyT"])
    if int(os.environ.get("SAMP_STOP", "99")) <= 8:
        return
    as_d = nc.dram_tensor("as_d", [128, 64], F32, kind="Internal").ap()
    attn_s = ysq_
    attn_sb = sb("attn_sb", [NSB, 512], BF16, SP)
    aT2 = sb("aT2", [128, 4, NSB], BF16, SP)
    A("sp", lambda e: e.dma_start(out=as_d, in_=arow[:, 0:64]), r=["arow"], w=["as_d"], dma=True)
    A("sp", lambda e: e.dma_start(out=attn_s[:], in_=as_d.rearrange("(b h) d -> b (h d)", h=8)), r=["as_d"], w=["ysq_"], dma=True)
    A("act", lambda e: e.copy(out=attn_sb[:], in_=attn_s[:]), r=["ysq_"], w=["attn_sb"])
    for c4 in range(4):
        A("pe", lambda e, c4=c4: e.transpose(out=pTb[:, 4 + c4, 0:NSB], in_=attn_sb[:, c4 * 128:(c4 + 1) * 128], identity=ident[0:NSB, 0:NSB]),
          r=["attn_sb", "ident"], w=["pTb"])
    A("act", lambda e: e.copy(out=aT2[:], in_=pTb[:, 4:8, 0:NSB]), r=["pTb"], w=["aT2"])
    for half in range(2):
        cs = slice(half * 512, (half + 1) * 512)
        for k in range(8):
            lhs = cyT[:, k, :] if k < 4 else aT2[:, k - 4, :]
            A("pe", lambda e, k=k, cs=cs, lhs=lhs: e.matmul(pA[0:NSB, :], lhsT=lhs, rhs=w_out_bf[:, k, cs], start=(k == 0), stop=(k == 7)),
              r=["cyT", "aT2", ("w_out_bf", k)], w=["s_pA"])
        A("dve", lambda e, cs=cs: e.tensor_tensor(out=hs_acc[:, cs], in0=pA[0:NSB, :], in1=xs_sb[:, cs], op=ALU.add),
          r=["s_pA", "xs_sb"], w=["hs_acc"])
    if int(os.environ.get("SAMP_STOP", "99")) <= 9:
        return
    emit_norm_T(S, hs_acc[:], "hs_acc", junk, ssx, hn, gbc_mlp, "gbc_mlp", ident, pTb, None, hnTs[:], "hnTs")
    if debug:
        dbg["arow"] = dout("d_arow", [128, 128], F32)
        A("sp", lambda e: e.dma_start(out=dbg["arow"], in_=arow[:]), r=["arow"], dma=True)
        dbg["hs"] = dout("d_hs", [NSB, DM], F32)
        A("sp", lambda e: e.dma_start(out=dbg["hs"], in_=hs_acc[:]), r=["hs_acc"], dma=True)
        dbg["ycb"] = dout("d_ycb", [NSB, 512], BF16)
        A("sp", lambda e: e.dma_start(out=dbg["ycb"], in_=ycb[:]), r=["ycb"], dma=True)
        dbg["Oacc"] = dout("d_Oacc", [128, 192], F32)
        A("sp", lambda e: e.dma_start(out=dbg["Oacc"], in_=Oacc[:]), r=["Oacc"], dma=True)
        dbg["st"] = dout("d_st", [128, 16], F32)
        A("sp", lambda e: e.dma_start(out=dbg["st"], in_=st[:]), r=["s_st"], dma=True)
        dbg["selm"] = dout("d_selm", [128, 32], F32)
        A("sp", lambda e: e.dma_start(out=dbg["selm"], in_=selm[:]), r=["s_selm"], dma=True)
```

```python
import contextlib
import os
import numpy as np
import concourse.bass as bass
import concourse.mybir as mybir
from concourse.bass_utils import run_bass_kernel_spmd

F32 = mybir.dt.float32
BF16 = mybir.dt.bfloat16
I32 = mybir.dt.int32
AF = mybir.ActivationFunctionType
ALU = mybir.AluOpType
AX = mybir.AxisListType

EPOCH = 4000
N_DMA_SEMS = 8

SEQ = 2048
DM = 1024
NT = 16
INC = 2328
SCALE = 0.125
BIG = 20000.0
NSB = 16
N_CORES = 8


class Sched:
    def __init__(self, nc):
        self.nc = nc
        self.ops = []
        self.last_writer = {}
        self.readers = {}
        self.cur_barrier = 0

    def add(self, eng, fn, r=(), w=(), dma=False):
        deps = set()
        for k in r:
            if k in self.last_writer:
                deps.add(self.last_writer[k])
        for k in w:
            if k in self.last_writer:
                deps.add(self.last_writer[k])
            deps.update(self.readers.get(k, ()))
        idx = len(self.ops)
        self.ops.append(dict(eng=eng, fn=fn, deps=sorted(deps), dma=dma, barrier=self.cur_barrier))
        for k in r:
            self.readers.setdefault(k, []).append(idx)
        for k in w:
            self.last_writer[k] = idx
            self.readers[k] = []
        return idx

    def barrier(self):
        self.cur_barrier = len(self.ops)

    def emit(self, final_wait_engine="sp"):
        nc = self.nc
        engs = ["pe", "act", "dve", "pool", "sp"]
        ops = self.ops
        cnt = {e: 0 for e in engs}
        dcnt = {e: 0 for e in engs}
        need = set()
        for op in ops:
            e = op["eng"]
            if op["dma"]:
                k = dcnt[e] % N_DMA_SEMS
                n = dcnt[e] // N_DMA_SEMS
                dcnt[e] += 1
                op["ticket"] = (("d", e, k), 16 * (n + 1))
                op["prev"] = (("d", e, k), 16 * n) if n > 0 else None
            else:
                ep = cnt[e] // EPOCH
                v = cnt[e] % EPOCH + 1
                cnt[e] += 1
                op["ticket"] = (("c", e, ep), v)
            need.add(op["ticket"][0])
        bounds = sorted(set(op["barrier"] for op in ops))
        btk = {}
        run = {}
        bi = 0
        for i, op in enumerate(ops):
            while bi < len(bounds) and bounds[bi] <= i:
                btk[bounds[bi]] = dict(run)
                bi += 1
            sn, v = op["ticket"]
            run[sn] = max(run.get(sn, 0), v)
        while bi < len(bounds):
            btk[bounds[bi]] = dict(run)
            bi += 1
        with contextlib.ExitStack() as st:
            sems = {}
            for sn in sorted(need):
                sems[sn] = st.enter_context(nc.semaphore("s_" + "_".join(map(str, sn))))
            block = st.enter_context(nc.Block())

            def make(e):
                def body(engine):
                    known = {}
                    seen_barrier = [0]

                    def wait(t):
                        sn, v = t
                        if sn[0] == "c":
                            for sn2 in known:
                                if sn2[0] == "c" and sn2[1] == sn[1] and sn2[2] > sn[2]:
                                    return
                        if known.get(sn, 0) >= v:
                            return
                        engine.wait_ge(sems[sn], v)
                        known[sn] = v

                    for op in ops:
                        if op["eng"] != e:
                            continue
                        if op["barrier"] > seen_barrier[0]:
                            seen_barrier[0] = op["barrier"]
                            for sn, v in sorted(btk[op["barrier"]].items()):
                                if sn == ("c", e, sn[2]) and e == "pe":
                                    continue
                                wait((sn, v))
                        for d in op["deps"]:
                            dop = ops[d]
                            if dop["eng"] == e and e == "pe" and not dop["dma"]:
                                continue
                            wait(dop["ticket"])
                        if op["dma"] and op["prev"] is not None:
                            wait(op["prev"])
                        ins = op["fn"](engine)
                        sn, v = op["ticket"]
                        ins.then_inc(sems[sn], 16 if op["dma"] else 1)
                    if e == final_wait_engine:
                        fin = {}
                        for op in ops:
                            sn, v = op["ticket"]
                            fin[sn] = max(fin.get(sn, 0), v)
                        for sn, v in sorted(fin.items()):
                            wait((sn, v))
                return body

            block.tensor(make("pe"))
            block.scalar(make("act"))
            block.vector(make("dve"))
            block.gpsimd(make("pool"))
            block.sync(make("sp"))


def build(stages=("p1", "p2", "p3", "samp"), debug=False, cache_rows=2560 * 128 * 4):
    nc = bass.Bass("TRN2", target_bir_lowering=False)

    def din(name, shape, dt=F32):
        return nc.dram_tensor(name, shape, dt, kind="ExternalInput").ap()

    def dout(name, shape, dt=F32):
        return nc.dram_tensor(name, shape, dt, kind="ExternalOutput").ap()

    xp = din("xp", [SEQ, DM])
    xs = din("xs", [NSB, DM])
    cache = din("cache", [cache_rows, 128])
    cwin = din("cwin", [NSB, 512, 256])
    sconv = din("sconv", [NSB * 30, 512])
    ptab = din("ptab", [1, NSB * 16], I32)
    g_attn = din("g_attn", [1, DM])
    w_in = din("w_in", [DM, INC])
    wdall = din("wdall", [34, 512])
    w_ck = din("w_ck", [32, 2])
    w_cv = din("w_cv", [32, 2])
    w_out = din("w_out", [DM, DM])
    g_mlp = din("g_mlp", [1, DM])
    w_up = din("w_up", [DM, 4096])
    w_down = din("w_down", [4096, DM])
    g_fin = din("g_fin", [1, DM])

    y_p = dout("y_p", [SEQ, DM])
    y_s = dout("y_s", [NSB, DM])
    kv_p = dout("kv_p", [SEQ, 512])
    win_p = dout("win_p", [512, 256])
    conv_p = dout("conv_p", [30, 512])
    kv_s = dout("kv_s", [NSB, 512])
    win_s = dout("win_s", [NSB, 512, 256])
    conv_s = dout("conv_s", [NSB * 30, 512])
    dbg = {}

    S = Sched(nc)
    A = S.add
    ES = contextlib.ExitStack()

    def sb(name, shape, dt, stack=None, side=None):
        return (stack or ES).enter_context(nc.sbuf_tensor(name, shape, dt, side=side))

    def pst(name, shape, dt, stack):
        return stack.enter_context(nc.psum_tensor(name, shape, dt))

    with ES:
        identf = sb("identf", [128, 128], F32)
        ident = sb("ident", [128, 128], BF16)
        A("pool", lambda e: e.memset(identf[:], 0.0), w=["identf"])
        A("pool", lambda e: e.affine_select(out=identf[:], in_=identf[:], pattern=[[-1, 128]],
                                            compare_op=ALU.not_equal, fill=1.0, base=0, channel_multiplier=1),
          r=["identf"], w=["identf"])
        A("dve", lambda e: e.tensor_copy(out=ident[:], in_=identf[:]), r=["identf"], w=["ident"])
        gbc_mlp = sb("gbc_mlp", [128, DM], F32)
        A("sp", lambda e: e.dma_start(out=gbc_mlp[:], in_=g_mlp.partition_broadcast(128)), w=["gbc_mlp"], dma=True)
        hs_acc = sb("hs_acc", [NSB, DM], F32)
        hss = sb("hss", [128, NT], F32)
        A("pool", lambda e: e.memset(hss[:], 0.0), w=["hss"])
        hnTs = sb("hnTs", [128, 8, NSB], BF16)
        w_out_bf = sb("w_out_bf", [128, 8, DM], BF16)
        cw = sb("cw", [128, 4, 34], F32)
        wckb = sb("wckb", [128, 2, 32], F32)
        wcvb = sb("wcvb", [128, 2, 32], F32)
        PmB = [sb(f"PmB{h}", [128, 124], BF16) for h in range(2)]

        RS1 = contextlib.ExitStack()
        w_in_bf = sb("w_in_bf", [128, 8, INC], BF16, RS1, side="right")

        with contextlib.ExitStack() as W0:
            stg = [sb(f"stg{i}", [128, INC], F32, W0) for i in range(2)]
            ci = 0
            for (wsrc, wdst, wkey, ncol) in ((w_in, w_in_bf, "w_in_bf", INC), (w_out, w_out_bf, "w_out_bf", DM)):
                for k in range(8):
                    b = ci % 2
                    A("sp", lambda e, k=k, b=b, wsrc=wsrc, ncol=ncol: e.dma_start(out=stg[b][:, 0:ncol], in_=wsrc[k * 128:(k + 1) * 128, :]),
                      w=[("stg", b)], dma=True)
                    if ci % 2 == 0:
                        A("act", lambda e, k=k, b=b, wdst=wdst, ncol=ncol: e.copy(out=wdst[:, k, :], in_=stg[b][:, 0:ncol]),
                          r=[("stg", b)], w=[(wkey, k)])
                    else:
                        A("dve", lambda e, k=k, b=b, wdst=wdst, ncol=ncol: e.tensor_copy(out=wdst[:, k, :], in_=stg[b][:, 0:ncol]),
                          r=[("stg", b)], w=[(wkey, k)])
                    ci += 1
            wd_sb = sb("wd_sb", [34, 512], F32, W0)
            A("sp", lambda e: e.dma_start(out=wd_sb[:], in_=wdall), w=["wd_sb"], dma=True)
            with contextlib.ExitStack() as PW:
                pcw = pst("pcw", [128, 4, 34], F32, PW)
                for c4 in range(4):
                    A("pe", lambda e, c4=c4: e.transpose(out=pcw[:, c4, :], in_=wd_sb[:, c4 * 128:(c4 + 1) * 128],
                                                         identity=identf[0:34, 0:34]),
                      r=["wd_sb", "identf"], w=[("pcw", c4)])
                A("dve", lambda e: e.tensor_copy(out=cw[:], in_=pcw[:]), r=[("pcw", c4) for c4 in range(4)], w=["cw"])
                S.barrier()
            wraw = sb("wraw", [128, 2, 64], F32, W0)
            A("sp", lambda e: e.dma_start(out=wraw[:, 0, :], in_=w_ck.rearrange("j h -> (j h)").partition_broadcast(128)), w=["wraw"], dma=True)
            A("sp", lambda e: e.dma_start(out=wraw[:, 1, :], in_=w_cv.rearrange("j h -> (j h)").partition_broadcast(128)), w=["wraw"], dma=True)
            A("dve", lambda e: e.tensor_copy(out=wckb[:], in_=wraw[:, 0, :].rearrange("p (j h) -> p h j", h=2)), r=["wraw"], w=["wckb"])
            A("dve", lambda e: e.tensor_copy(out=wcvb[:], in_=wraw[:, 1, :].rearrange("p (j h) -> p h j", h=2)), r=["wraw"], w=["wcvb"])
            wcol = sb("wcol", [128, 2], F32, W0)
            for rr in range(4):
                A("sp", lambda e, rr=rr: e.dma_start(out=wcol[rr * 32:(rr + 1) * 32, :], in_=w_cv), w=["wcol"], dma=True)
            pmf = sb("pmf", [128, 4], F32, W0)
            A("pool", lambda e: e.memset(pmf[:], 1.0), w=["pmf"])
            A("pool", lambda e: e.affine_select(out=pmf[:], in_=pmf[:], pattern=[[-32, 4]], compare_op=ALU.is_ge,
                                                fill=0.0, base=0, channel_multiplier=1), r=["pmf"], w=["pmf"])
            A("pool", lambda e: e.affine_select(out=pmf[:], in_=pmf[:], pattern=[[32, 4]], compare_op=ALU.is_ge,
                                                fill=0.0, base=31, channel_multiplier=-1), r=["pmf"], w=["pmf"])
            for h in range(2):
                A("pool", lambda e, h=h: e.memset(PmB[h][:], 0.0), w=[("PmB", h)])
                A("dve", lambda e, h=h: e.tensor_scalar(out=PmB[h][:, 60:64], in0=pmf[:], scalar1=wcol[:, h:h + 1],
                                                        scalar2=None, op0=ALU.mult),
                  r=["pmf", "wcol", ("PmB", h)], w=[("PmB", h)])
            S.barrier()

        if "samp" in stages:
            with contextlib.ExitStack() as SP:
                build_samp(nc, S, SP, sb, pst, locals())
                S.barrier()

        ATT = contextlib.ExitStack()
        with ATT:
            QTs = [sb(f"QTs{h}", [96, 4, SEQ], BF16, ATT) for h in range(2)]
            KTs = [sb(f"KTs{h}", [96, SEQ], BF16, ATT) for h in range(2)]
            KTw = [sb(f"KTw{h}", [64, SEQ], BF16, ATT) for h in range(2)]
            kcT = [sb(f"kcT{h}", [64, 64], BF16, ATT) for h in range(2)]
            Vaug = sb("Vaug", [128, NT, 3, 2, 65], BF16, ATT)
            vcaug = sb("vcaug", [64, 2, 65], BF16, ATT)
            gates = sb("gates", [128, NT, 24], F32, ATT)
            convyT = sb("convyT", [128, 4, SEQ], BF16, ATT)
            for h in range(2):
                A("pool", lambda e, h=h: e.memset(KTs[h][64:96, :], 1.0), w=[("KTsaug", h)])
                A("pool", lambda e, h=h: e.affine_select(out=KTs[h][64:96, :], in_=KTs[h][64:96, :], pattern=[[1, SEQ]],
                                                         compare_op=ALU.is_ge, fill=0.0, base=0, channel_multiplier=-64),
                  r=[("KTsaug", h)], w=[("KTsaug", h)])
                A("pool", lambda e, h=h: e.affine_select(out=KTs[h][64:96, :], in_=KTs[h][64:96, :], pattern=[[-1, SEQ]],
                                                         compare_op=ALU.is_ge, fill=0.0, base=63, channel_multiplier=64),
                  r=[("KTsaug", h)], w=[("KTsaug", h)])
            A("pool", lambda e: e.memset(Vaug[:], 1.0), w=["Vaug_init"])
            A("pool", lambda e: e.memset(vcaug[:], 1.0), w=["vcaug_init"])

            with contextlib.ExitStack() as P1:
                build_p1(nc, S, P1, sb, pst, locals())
                S.barrier()
            RS1.close()
            RS2 = contextlib.ExitStack()
            h_acc = sb("h_acc", [128, NT, DM], F32, RS2, side="right")
            if "p2" in stages:
                with contextlib.ExitStack() as P2:
                    build_p2(nc, S, P2, sb, pst, locals())
                    S.barrier()
        if "p3" in stages:
            with contextlib.ExitStack() as P3:
                build_p3(nc, S, P3, sb, pst, locals())
        RS2.close()
        S.emit()
    return nc, dbg


def build_p1(nc, S, P1, sb, pst, L):
    A = S.add
    (QTs, KTs, KTw, kcT, Vaug, vcaug, gates, convyT, cw, wckb, PmB, w_in_bf, ident, identf, xp, g_attn, kv_p, win_p,
     conv_p) = (L[k] for k in ("QTs", "KTs", "KTw", "kcT", "Vaug", "vcaug", "gates", "convyT", "cw", "wckb", "PmB",
                               "w_in_bf", "ident", "identf", "xp", "g_attn", "kv_p", "win_p", "conv_p"))
    debug, dbg, dout = L["debug"], L["dbg"], L["dout"]
    onesf = sb("onesf", [128, 128], F32, P1)
    A("pool", lambda e: e.memset(onesf[:], 1.0 / 512.0), w=["onesf"])
    gbc_attn = sb("gbc_attn", [128, DM], F32, P1)
    A("sp", lambda e: e.dma_start(out=gbc_attn[:], in_=g_attn.partition_broadcast(128)), w=["gbc_attn"], dma=True)
    xt = [sb(f"xt{i}", [128, DM], F32, P1) for i in range(2)]
    ssall = sb("ssall", [128, NT], F32, P1)
    xn = [sb("xn0", [128, DM], BF16, P1)] * 2
    A("pool", lambda e: e.memset(ssall[:], 0.0), w=["ssall"])
    for t in range(NT):
        A("sp", lambda e, t=t: e.dma_start(out=xt[t % 2][:], in_=xp[t * 128:(t + 1) * 128, :]), w=[("xt", t % 2)], dma=True)
        A("act", lambda e, t=t: e.activation(out=xn[t % 2][:], in_=xt[t % 2][:], func=AF.Square, accum_out=ssall[:, t:t + 1]),
          r=[("xt", t % 2), "ssall"], w=["xn", "ssall"])
    A("dve", lambda e: e.tensor_scalar(out=ssall[:], in0=ssall[:], scalar1=1.0 / DM, scalar2=1e-6, op0=ALU.mult, op1=ALU.add),
      r=["ssall"], w=["ssall"])
    A("act", lambda e: e.activation(out=ssall[:], in_=ssall[:], func=AF.Sqrt), r=["ssall"], w=["ssall"])
    A("dve", lambda e: e.reciprocal(out=ssall[:], in_=ssall[:]), r=["ssall"], w=["ssall"])
    xnT = [sb("xnT0", [128, 8, 512], BF16, P1)] * 2
    glu = [sb(f"glu{i}", [128, 4, 542], BF16, P1) for i in range(2)]
    glutail = sb("glutail", [128, 4, 30], F32, P1)
    glo = [sb(f"glo{i}", [128, 4, 542], BF16, P1) for i in range(2)]
    Dg = [sb(f"Dg{i}", [128, 16, 128], BF16, P1) for i in range(2)]
    sgt = sb("sgt", [128, 512], F32, P1)
    ych = sb("ych", [128, 4, 512], F32, P1)
    mean_sb = sb("mean_sb", [128, 512], F32, P1)
    rstd_sb = sb("rstd_sb", [128, 512], F32, P1)
    ysq = rstd_sb
    kcp = mean_sb[0:64, :].rearrange("p (c j) -> p c j", j=32)
    zt = sb("zt", [128, 792], F32, P1)
    kcf = sb("kcf", [64, 16], F32, P1)
    psF = [pst(f"psF{i}", [128, 512], F32, P1) for i in range(2)]
    psTM = pst("psTM", [128, 1024], F32, P1)
    psT = pst("psT", [128, 8, 128], BF16, P1)
    psVC = pst("psVC", [128, 512], F32, P1)
    psMean = pst("psMean", [128, 512], F32, P1)
    psMsq = pst("psMsq", [128, 512], F32, P1)

    A("pool", lambda e: e.memset(glu[0][:, :, 0:30], 0.0), w=[("gluhead", 0)])
    A("pool", lambda e: e.memset(glo[0][:, :, 0:30], 0.0), w=[("gluhead", 0)])
    fcnt = [0]

    def fm_mm(xT, c0, M, Gk):
        b = fcnt[0] % 2
        fcnt[0] += 1
        for k in range(8):
            A("pe", lambda e, k=k, b=b: e.matmul(psF[b][0:M, :], lhsT=w_in_bf[:, k, c0:c0 + M], rhs=xT[:, k, :],
                                                 start=(k == 0), stop=(k == 7)),
              r=[("w_in_bf", k), Gk], w=[("psF", b)])
        return b

    def p1_front(G):
        xTg = xnT[G % 2]
        Gk = "xnT"
        gl = glu[G % 2]
        gln = glu[(G + 1) % 2]
        tok = slice(G * 512, (G + 1) * 512)
        for tt in range(4):
            t = 4 * G + tt
            xb = t % 2
            A("sp", lambda e, t=t, xb=xb: e.dma_start(out=xt[xb][:], in_=xp[t * 128:(t + 1) * 128, :]), w=[("xt", xb)], dma=True)
            A("dve", lambda e, t=t, xb=xb: e.scalar_tensor_tensor(out=xn[xb][:], in0=xt[xb][:], scalar=ssall[:, t:t + 1],
                                                                  in1=gbc_attn[:], op0=ALU.mult, op1=ALU.mult),
              r=[("xt", xb), "ssall", "gbc_attn"], w=["xn"])
            for k in range(8):
                A("pe", lambda e, k=k, xb=xb: e.transpose(out=psT[:, k, :], in_=xn[xb][:, k * 128:(k + 1) * 128], identity=ident[:]),
                  r=["xn", "ident"], w=["psT"])
            A("act", lambda e, tt=tt, xTg=xTg: e.copy(out=xTg[:, :, tt * 128:(tt + 1) * 128], in_=psT[:]),
              r=["psT"], w=[Gk])
            yield
        for c4 in range(4):
            ba = fm_mm(xTg, c4 * 128, 128, Gk)
            bb = fm_mm(xTg, 512 + c4 * 128, 128, Gk)
            A("act", lambda e, bb=bb: e.activation(out=sgt[:], in_=psF[bb][:], func=AF.Sigmoid), r=[("psF", bb)], w=["sgt"])
            A("dve", lambda e, ba=ba, c4=c4, gl=gl: e.tensor_tensor(out=gl[:, c4, 30:542], in0=psF[ba][:], in1=sgt[:], op=ALU.mult),
              r=[("psF", ba), "sgt"], w=[("glu", G % 2, c4)])
            A("dve", lambda e, ba=ba, c4=c4, G=G: e.tensor_tensor(out=glo[G % 2][:, c4, 29:541], in0=psF[ba][:], in1=sgt[:], op=ALU.mult),
              r=[("psF", ba), "sgt", ("gluhead", G % 2)], w=[("glu", G % 2, c4)])
            if G == 3:
                A("dve", lambda e, ba=ba, c4=c4: e.tensor_tensor(out=glutail[:, c4, :], in0=psF[ba][:, 482:512], in1=sgt[:, 482:512],
                                                                 op=ALU.mult), r=[("psF", ba), "sgt"], w=["glutail"])
            yield
        for hd in range(8):
            h, g = hd // 4, hd % 4
            b = fm_mm(xTg, 1024 + hd * 64, 64, Gk)
            A("act", lambda e, b=b, h=h, g=g, tok=tok: e.copy(out=QTs[h][0:64, g, tok], in_=psF[b][0:64, :]),
              r=[("psF", b)], w=[("QT", h, G)])
            if hd % 4 == 3:
                yield
        for h in range(2):
            b = fm_mm(xTg, 1536 + h * 64, 64, Gk)
            A("act", lambda e, b=b: e.copy(out=mean_sb[0:64, :], in_=psF[b][0:64, :]), r=[("psF", b)], w=["mean_sb"])
            A("pool", lambda e, h=h: e.tensor_tensor(
                out=kcp, in0=kcp, in1=wckb[0:64, h, :].unsqueeze(1).to_broadcast([64, 16, 32]), op=ALU.mult),
              r=["mean_sb", "wckb"], w=["mean_sb"])
            A("dve", lambda e: e.tensor_reduce(out=kcf[:], in_=kcp, axis=AX.X, op=ALU.add), r=["mean_sb"], w=["kcf"])
            A("pool", lambda e, h=h, G=G: e.tensor_copy(out=kcT[h][:, G * 16:(G + 1) * 16], in_=kcf[:]),
              r=["kcf"], w=[("kcT", h, G)])
            b = fm_mm(xTg, 1536 + 256 + h * 64, 64, Gk)
            A("act", lambda e, b=b, h=h, tok=tok: e.copy(out=KTs[h][0:64, tok], in_=psF[b][0:64, :]),
              r=[("psF", b)], w=[("KTs", h, G)])
            b = fm_mm(xTg, 1536 + 512 + h * 64, 64, Gk)
            A("act", lambda e, b=b, h=h, tok=tok: e.copy(out=KTw[h][:, tok], in_=psF[b][0:64, :]),
              r=[("psF", b)], w=[("KTw", h, G)])
            yield
        for tt in range(4):
            t = 4 * G + tt
            for (c0, n, o0) in ((1536, 512, 0), (2048, 280, 512)):
                for k in range(8):
                    A("pe", lambda e, k=k, tt=tt, c0=c0, n=n, o0=o0, xTg=xTg: e.matmul(
                        psTM[:, o0:o0 + n], lhsT=xTg[:, k, tt * 128:(tt + 1) * 128], rhs=w_in_bf[:, k, c0:c0 + n],
                        start=(k == 0), stop=(k == 7)),
                      r=[("w_in_bf", k), Gk], w=["psTM"])
            A("act", lambda e: e.copy(out=zt[:], in_=psTM[:, 0:792]), r=["psTM"], w=["zt"])
            A("sp", lambda e, t=t: e.dma_start(out=kv_p[t * 128:(t + 1) * 128, :], in_=zt[:, 0:512]), r=["zt"], dma=True)
            if t >= 12:
                A("sp", lambda e, t=t: e.dma_start(out=win_p[(t - 12) * 128:(t - 11) * 128, :], in_=zt[:, 512:768]),
                  r=["zt"], dma=True)
            A("act", lambda e, t=t: e.activation(out=gates[:, t, :], in_=zt[:, 768:792], func=AF.Sigmoid),
              r=["zt"], w=[("gates", t)])
            for s3 in range(3):
                A("pool", lambda e, t=t, s3=s3: e.tensor_copy(
                    out=Vaug[:, t, s3, :, 0:64],
                    in_=zt[:, 128 + 256 * s3:256 + 256 * s3].rearrange("p (h d) -> p h d", d=64)),
                  r=["zt", "Vaug_init"], w=[("Vaug", t)])
            for h in range(2):
                A("pe", lambda e, t=t, h=h: e.matmul(psVC[0:64, h * 64:(h + 1) * 64], lhsT=PmB[h][:, 60 - 4 * t:124 - 4 * t],
                                                     rhs=Vaug[:, t, 0, h, 0:64], start=(t == 0 and h == 0), stop=(t == NT - 1),
                                                     skip_group_check=True),
                  r=[("PmB", h), ("Vaug", t)], w=["psVC"])
            yield

    def p1_conv(G):
        xTg = xnT[G % 2]
        Gk = "xnT"
        gl = glu[G % 2]
        gln = glu[(G + 1) % 2]
        tok = slice(G * 512, (G + 1) * 512)
        for c4 in range(4):
            rk = [("glu", G % 2, c4), ("gluhead", G % 2)]
            for di, (j0, nj) in enumerate(((0, 16), (16, 15))):
                A("pool", lambda e, c4=c4, di=di, j0=j0, nj=nj: e.tensor_tensor(
                    out=Dg[di][:, 0:nj, :], in0=ident[:].unsqueeze(1).to_broadcast([128, nj, 128]),
                    in1=cw[:, c4, j0:j0 + nj].unsqueeze(2).to_broadcast([128, nj, 128]), op=ALU.mult), r=["ident", "cw"], w=[("Dg", di)])
            for j in range(31):
                di, jj = (0, j) if j < 16 else (1, j - 16)
                src = gl[:, c4, j:j + 512] if j % 2 == 0 else glo[G % 2][:, c4, j - 1:j - 1 + 512]
                A("pe", lambda e, j=j, src=src, di=di, jj=jj: e.matmul(psMsq[:], lhsT=Dg[di][:, jj, :], rhs=src,
                                                                       start=(j == 0), stop=(j == 30)),
                  r=rk + [("Dg", di)], w=["psMsq"])
            A("act", lambda e, c4=c4: e.activation(out=ych[:, c4, :], in_=psMsq[:], func=AF.Identity, bias=cw[:, c4, 31:32]),
              r=["psMsq", "cw"], w=[("ych", c4)])
            yield

    def p1_ln(G):
        xTg = xnT[G % 2]
        Gk = "xnT"
        gl = glu[G % 2]
        gln = glu[(G + 1) % 2]
        tok = slice(G * 512, (G + 1) * 512)
        for c4 in range(4):
            A("act", lambda e, c4=c4: e.activation(out=ysq[:], in_=ych[:, c4, :], func=AF.Square),
              r=[("ych", c4)], w=["rstd_sb"])
            A("pe", lambda e, c4=c4: e.matmul(psMean[:], lhsT=onesf[:], rhs=ych[:, c4, :], start=(c4 == 0), stop=(c4 == 3)),
              r=["onesf", ("ych", c4)], w=["psMean"])
            A("pe", lambda e, c4=c4: e.matmul(psMsq[:], lhsT=onesf[:], rhs=ysq[:], start=(c4 == 0), stop=(c4 == 3)),
              r=["onesf", "rstd_sb"], w=["psMsq"])
        A("pool", lambda e, gl=gl, gln=gln: e.tensor_copy(out=gln[:, :, 0:30], in_=gl[:, :, 512:542]),
          r=[("glu", G % 2, c4) for c4 in range(4)], w=[("gluhead", (G + 1) % 2)])
        A("pool", lambda e, gl=gl, G=G: e.tensor_copy(out=glo[(G + 1) % 2][:, :, 0:29], in_=gl[:, :, 513:542]),
          r=[("glu", G % 2, c4) for c4 in range(4)], w=[("gluhead", (G + 1) % 2)])
        A("act", lambda e: e.copy(out=mean_sb[:], in_=psMean[:]), r=["psMean"], w=["mean_sb"])
        A("pool", lambda e: e.tensor_tensor(out=rstd_sb[:], in0=mean_sb[:], in1=mean_sb[:], op=ALU.mult),
          r=["mean_sb"], w=["rstd_sb"])
        A("dve", lambda e: e.tensor_tensor(out=rstd_sb[:], in0=psMsq[:], in1=rstd_sb[:], op=ALU.subtract),
          r=["psMsq", "rstd_sb"], w=["rstd_sb"])
        A("dve", lambda e: e.tensor_scalar(out=rstd_sb[:], in0=rstd_sb[:], scalar1=1e-5, scalar2=None,
                                           op0=ALU.add), r=["rstd_sb"], w=["rstd_sb"])
        A("act", lambda e: e.activation(out=rstd_sb[:], in_=rstd_sb[:], func=AF.Sqrt), r=["rstd_sb"], w=["rstd_sb"])
        A("dve", lambda e: e.reciprocal(out=rstd_sb[:], in_=rstd_sb[:]), r=["rstd_sb"], w=["rstd_sb"])
        for c4 in range(4):
            eng = "dve" if c4 % 2 == 0 else "pool"
            A(eng, lambda e, c4=c4: e.tensor_tensor(out=ych[:, c4, :], in0=ych[:, c4, :], in1=mean_sb[:], op=ALU.subtract),
              r=[("ych", c4), "mean_sb"], w=[("ych", c4)])
            A(eng, lambda e, c4=c4: e.tensor_tensor(out=ych[:, c4, :], in0=ych[:, c4, :], in1=rstd_sb[:], op=ALU.mult),
              r=[("ych", c4), "rstd_sb"], w=[("ych", c4)])
            A("act", lambda e, c4=c4, tok=tok: e.activation(out=convyT[:, c4, tok], in_=ych[:, c4, :], func=AF.Silu,
                                                            bias=cw[:, c4, 33:34], scale=cw[:, c4, 32:33]),
              r=[("ych", c4), "cw"], w=[("convyT", G)])
            yield
    def run_interleaved(ga, gb):
        sentinel = object()
        for _ in range(4):
            if next(ga, sentinel) is sentinel:
                break
        a_alive, b_alive = True, True
        while a_alive or b_alive:
            for _ in range(2):
                if a_alive and next(ga, sentinel) is sentinel:
                    a_alive = False
            if b_alive and next(gb, sentinel) is sentinel:
                b_alive = False

    for _ in p1_front(0):
        pass
    for G in range(4):
        run_interleaved(p1_front(G + 1) if G + 1 < 4 else iter(()), p1_conv(G))
        for _ in p1_ln(G):
            pass
    A("act", lambda e: e.copy(out=vcaug[:, :, 0:64], in_=psVC[0:64, 0:128].rearrange("p (h d) -> p h d", d=64)),
      r=["psVC", "vcaug_init"], w=["vcaug"])
    for c4 in range(4):
        A("pe", lambda e, c4=c4: e.transpose(out=psF[0][0:30, c4 * 128:(c4 + 1) * 128], in_=glutail[:, c4, :], identity=identf[:]),
          r=["glutail", "identf"], w=[("psF", 0)])
    cps = ych[0:30, 0, :]
    A("act", lambda e: e.copy(out=cps, in_=psF[0][0:30, :]), r=[("psF", 0), ("ych", 0)], w=[("ych", 0)])
    A("sp", lambda e: e.dma_start(out=conv_p, in_=cps), r=[("ych", 0)], dma=True)
    if debug:
        dbg["QT0"] = dout("d_QT0", [96, 4 * SEQ], BF16)
        A("sp", lambda e: e.dma_start(out=dbg["QT0"], in_=QTs[0][:]), r=[("QT", 0, G) for G in range(4)], dma=True)
        dbg["convyT"] = dout("d_convyT", [128, 4 * SEQ], BF16)
        A("sp", lambda e: e.dma_start(out=dbg["convyT"], in_=convyT[:]), r=[("convyT", G) for G in range(4)], dma=True)
        dbg["kcT0"] = dout("d_kcT0", [64, 64], BF16)
        A("sp", lambda e: e.dma_start(out=dbg["kcT0"], in_=kcT[0][:]), r=[("kcT", 0, G) for G in range(4)], dma=True)
        dbg["vcaug"] = dout("d_vcaug", [64, 130], BF16)
        A("sp", lambda e: e.dma_start(out=dbg["vcaug"], in_=vcaug[:]), r=["vcaug"], dma=True)


def build_p2(nc, S, P2, sb, pst, L):
    A = S.add
    QTs, KTs, KTw, kcT, Vaug, vcaug, gates, convyT = (L[k] for k in
                                                       ("QTs", "KTs", "KTw", "kcT", "Vaug", "vcaug", "gates", "convyT"))
    ident, identf, w_out_bf, h_acc, gbc_mlp, xp = (L[k] for k in
                                                   ("ident", "identf", "w_out_bf", "h_acc", "gbc_mlp", "xp"))
    debug, dbg, dout = L["debug"], L["dbg"], L["dout"]
    sbias = sb("sbias", [128, NT, 32], F32, P2)
    A("pool", lambda e: e.memset(sbias[:], 0.0), w=["sbias"])
    for qt in range(NT):
        for half in range(2):
            qb = 2 * qt + half
            ps_ = slice(64 * half, 64 * half + 64)
            if qb + 1 < 32:
                A("pool", lambda e, qt=qt, ps_=ps_, qb=qb: e.memset(sbias[ps_, qt, qb + 1:32], -1e30), r=["sbias"], w=["sbias"])
            A("pool", lambda e, qt=qt, ps_=ps_: e.memset(sbias[ps_, qt, 0:1], 5.0), r=["sbias"], w=["sbias"])
            A("pool", lambda e, qt=qt, ps_=ps_, qb=qb: e.memset(sbias[ps_, qt, max(qb - 1, 0):qb + 1], 5.0),
              r=["sbias"], w=["sbias"])
    pS = [pst(f"pS{i}", [128, 512], F32, P2) for i in range(2)]
    pOb = [pst(f"pO{i}", [128, 512], F32, P2) for i in range(3)]
    pO = [p[:, 0:260].rearrange("p (g d) -> p g d", d=65) for p in pOb]
    pM = pst("pM", [128, 512], F32, P2)
    pH = pst("pH", [128, 512], F32, P2)
    pTb = pst("pTb", [128, 8, 128], BF16, P2)
    PT = [sb(f"PT{i}", [128, 512], BF16, P2) for i in range(3)]
    NSEL = 3
    Ecmp = [sb(f"Ecmp{i}", [128, 8, 64], F32, P2) for i in range(NSEL)]
    zc = [sb(f"zc{i}", [128, 16], F32, P2) for i in range(NSEL)]
    pblk = [sb(f"pblk{i}", [128, 2, 32], F32, P2) for i in range(NSEL)]
    pg4 = [sb(f"pg4{i}", [128, 2, 64], F32, P2) for i in range(NSEL)]
    m8 = [sb(f"m8{i}", [128, 2, 8], F32, P2) for i in range(NSEL)]
    wk32 = [sb(f"wk32{i}", [128, 2, 32], F32, P2) for i in range(NSEL)]
    selT_in = [sb(f"selT_in{i}", [128, 2, 96], BF16, P2) for i in range(NSEL)]
    for i in range(NSEL):
        A("pool", lambda e, i=i: e.memset(selT_in[i][:], 0.0), w=[("selT_in", i)])
    coef = sb("coef", [128, 3, 4], F32, P2)
    zr = sb("zr", [128, 3, 4], F32, P2)
    osb = sb("osb", [128, 4, 64], F32, P2)
    otmp = sb("otmp", [128, 4, 64], F32, P2)
    attn = sb("attn", [128, 512], BF16, P2)
    attnT = sb("attnT", [128, 4, 128], BF16, P2)
    xt2 = [sb("xt2_0", [128, DM], F32, P2)] * 2
    pcnt = [0]
    scnt = [0]

    def score_exp(h, qt, lhsT, K, rhs_rows, masks, rkeys):
        sbuf_i = scnt[0] % 2
        scnt[0] += 1
        pb = pcnt[0] % 3
        pcnt[0] += 1
        M = lhsT.shape[1]
        A("pe", lambda e: e.matmul(pS[sbuf_i][0:M, :], lhsT=lhsT, rhs=QTs[h][0:K, :, qt * 128:(qt + 1) * 128],
                                   start=True, stop=True),
          r=rkeys + [("QT", h, qt // 4)] + ([("QTaug", h, qt)] if K == 96 else []), w=[("pS", sbuf_i)])
        A("act", lambda e: e.activation(out=PT[pb][0:M, :], in_=pS[sbuf_i][0:M, :], func=AF.Exp, scale=SCALE),
          r=[("pS", sbuf_i)], w=[("PT", pb)])
        for (cm, qs, base) in masks:
            A("pool", lambda e, cm=cm, qs=qs, base=base: e.affine_select(
                out=PT[pb][0:M, :].rearrange("p (g q) -> p g q", g=4), in_=PT[pb][0:M, :].rearrange("p (g q) -> p g q", g=4),
                pattern=[[0, 4], [qs, 128]], compare_op=ALU.is_ge, fill=0.0, base=base, channel_multiplier=cm),
              r=[("PT", pb)], w=[("PT", pb)])
        return pb

    def emit_sel(qt):
        r = qt % NSEL
        pm = pH
        pmk = "pH"
        for hd in range(8):
            h, g = hd // 4, hd % 4
            A("pe", lambda e, h=h, g=g, hd=hd, qt=qt, pm=pm: e.matmul(pm[:, hd * 64:(hd + 1) * 64],
                                                                      lhsT=QTs[h][0:64, g, qt * 128:(qt + 1) * 128], rhs=kcT[h][:],
                                                                      start=True, stop=True),
              r=[("QT", h, qt // 4)] + [("kcT", h, G) for G in range(4)], w=[pmk])
        E = Ecmp[r]
        A("act", lambda e, E=E, pm=pm: e.activation(out=E[:], in_=pm[:].rearrange("p (g c) -> p g c", g=8), func=AF.Exp, scale=SCALE),
          r=[pmk], w=[("Ecmp", r)])
        A("pool", lambda e, qt=qt, E=E: e.affine_select(out=E[:], in_=E[:], pattern=[[0, 8], [-32, 64]], compare_op=ALU.is_ge,
                                                        fill=0.0, base=qt * 128 - 31, channel_multiplier=1),
          r=[("Ecmp", r)], w=[("Ecmp", r)])
        Z = zc[r]
        A("dve", lambda e, E=E, Z=Z: e.tensor_reduce(out=Z[:, 0:8], in_=E[:], axis=AX.X, op=ALU.add), r=[("Ecmp", r)], w=[("zc", r)])
        A("dve", lambda e, Z=Z: e.tensor_scalar(out=Z[:, 0:8], in0=Z[:, 0:8], scalar1=1e-30, scalar2=None, op0=ALU.add),
          r=[("zc", r)], w=[("zc", r)])
        A("dve", lambda e, Z=Z: e.reciprocal(out=Z[:, 8:16], in_=Z[:, 0:8]), r=[("zc", r)], w=[("zc", r)])
        A("dve", lambda e, E=E, Z=Z: e.tensor_tensor(out=E[:], in0=E[:], in1=Z[:, 8:16].unsqueeze(2).to_broadcast([128, 8, 64]),
                                                     op=ALU.mult), r=[("Ecmp", r), ("zc", r)], w=[("Ecmp", r)])
        for h in range(2):
            A("dve", lambda e, E=E, h=h, r=r: e.tensor_reduce(out=pg4[r][:, h, :], in_=E[:, h * 4:(h + 1) * 4, :].rearrange("p g c -> p c g"),
                                                              axis=AX.X, op=ALU.add), r=[("Ecmp", r)], w=[("pg4", r)])
        A("dve", lambda e, r=r: e.tensor_reduce(out=pblk[r][:], in_=pg4[r][:].rearrange("p h (b t) -> p h b t", t=2),
                                                axis=AX.X, op=ALU.add), r=[("pg4", r)], w=[("pblk", r)])
        A("dve", lambda e, r=r, qt=qt: e.tensor_tensor(out=pblk[r][:], in0=pblk[r][:],
                                                       in1=sbias[:, qt, :].unsqueeze(1).to_broadcast([128, 2, 32]), op=ALU.add),
          r=[("pblk", r), "sbias"], w=[("pblk", r)])
        for h in range(2):
            A("dve", lambda e, h=h, r=r: e.max(out=m8[r][:, h, :], in_=pblk[r][:, h, :]), r=[("pblk", r)], w=[("m8", r, h)])
            A("dve", lambda e, h=h, r=r: e.match_replace(out=wk32[r][:, h, :], in_to_replace=m8[r][:, h, :],
                                                         in_values=pblk[r][:, h, :], imm_value=-3e38),
              r=[("m8", r, h), ("pblk", r)], w=[("wk32", r, h)])
            A("dve", lambda e, h=h, r=r: e.max(out=m8[r][:, h, :], in_=wk32[r][:, h, :]), r=[("wk32", r, h)], w=[("m8", r, h)])
            A("dve", lambda e, h=h, r=r: e.tensor_scalar(out=wk32[r][:, h, :], in0=pblk[r][:, h, :], scalar1=m8[r][:, h, 7:8],
                                                         scalar2=-1.0, op0=ALU.is_ge, op1=ALU.add),
              r=[("pblk", r), ("m8", r, h)], w=[("wk32", r, h)])
        A("dve", lambda e, r=r: e.tensor_scalar(out=selT_in[r][:, :, 64:96], in0=wk32[r][:], scalar1=BIG, scalar2=None, op0=ALU.mult),
          r=[("wk32", r, 0), ("wk32", r, 1), ("selT_in", r)], w=[("selT_in", r)])
        for h in range(2):
            A("pe", lambda e, h=h, r=r: e.transpose(out=pTb[0:96, 4 + h, :], in_=selT_in[r][:, h, :], identity=ident[:]),
              r=[("selT_in", r), "ident"], w=[("pTbs", h)])
            A("act", lambda e, h=h, qt=qt: e.copy(out=QTs[h][64:96, :, qt * 128:(qt + 1) * 128],
                                                  in_=pTb[64:96, 4 + h, :].unsqueeze(1).to_broadcast([32, 4, 128])),
              r=[("pTbs", h)], w=[("QTaug", h, qt)])
    LOOK = 2
    pSx = [pS[0], pS[1], pM]
    Osb = sb("Osb", [128, 3, 4, 65], F32, P2)
    tiles = []
    for qt in range(NT):
        for h in range(2):
            tl = [dict(br=0, kt=0, lhsT=kcT[h][:], K=64, masks=[(-32, 1, qt * 128 - 31)],
                       rk=[("kcT", h, G) for G in range(4)], rhs=vcaug[:, h, :], rhsk="vcaug", first=True, last=True)]
            for kt in range(qt + 1):
                tl.append(dict(br=1, kt=kt, lhsT=KTs[h][:, kt * 128:(kt + 1) * 128], K=96, masks=[(-1, 1, 0)] if kt == qt else [],
                               rk=[("KTs", h, kt // 4), ("KTsaug", h)], rhs=Vaug[:, kt, 1, h, :], rhsk=("Vaug", kt),
                               first=(kt == 0), last=(kt == qt)))
            k0 = max(0, qt - 4)
            for kt in range(k0, qt + 1):
                masks = []
                if kt == qt:
                    masks.append((-1, 1, 0))
                if kt == qt - 4:
                    masks.append((1, -1, 0))
                tl.append(dict(br=2, kt=kt, lhsT=KTw[h][:, kt * 128:(kt + 1) * 128], K=64, masks=masks,
                               rk=[("KTw", h, kt // 4)], rhs=Vaug[:, kt, 2, h, :], rhsk=("Vaug", kt),
                               first=(kt == k0), last=(kt == qt)))
            for t_ in tl:
                t_["qt"], t_["h"] = qt, h
            tl[-1]["end"] = True
            tiles += tl

    def emit_qk(t_, i):
        si = i % 3
        pb = i % 3
        t_["pb"] = pb
        h, qt, K, lhsT = t_["h"], t_["qt"], t_["K"], t_["lhsT"]
        M = lhsT.shape[1]
        A("pe", lambda e: e.matmul(pSx[si][0:M, :], lhsT=lhsT, rhs=QTs[h][0:K, :, qt * 128:(qt + 1) * 128], start=True, stop=True),
          r=t_["rk"] + [("QT", h, qt // 4)] + ([("QTaug", h, qt)] if K == 96 else []), w=[("pSx", si)])
        A("act", lambda e: e.activation(out=PT[pb][0:M, :], in_=pSx[si][0:M, :], func=AF.Exp, scale=SCALE),
          r=[("pSx", si)], w=[("PT", pb)])
        for (cm, qs, base) in t_["masks"]:
            A("pool", lambda e, cm=cm, qs=qs, base=base: e.affine_select(
                out=PT[pb][0:M, :].rearrange("p (g q) -> p g q", g=4), in_=PT[pb][0:M, :].rearrange("p (g q) -> p g q", g=4),
                pattern=[[0, 4], [qs, 128]], compare_op=ALU.is_ge, fill=0.0, base=base, channel_multiplier=cm),
              r=[("PT", pb)], w=[("PT", pb)])

    def emit_pv(t_):
        pb, br = t_["pb"], t_["br"]
        M = t_["lhsT"].shape[1]
        for g in range(4):
            A("pe", lambda e, g=g: e.matmul(pO[br][:, g, :], lhsT=PT[pb][0:M, g * 128:(g + 1) * 128], rhs=t_["rhs"],
                                            start=(t_["first"] and g == 0), stop=t_["last"], skip_group_check=True),
              r=[("PT", pb), t_["rhsk"]], w=[("pO", br)])
        if t_["last"]:
            A("dve", lambda e: e.tensor_copy(out=Osb[:, br, :, :], in_=pO[br][:]), r=[("pO", br)], w=[("Osb", br)])

    def emit_end(qt, h):
        A("dve", lambda e: e.tensor_scalar(out=zr[:], in0=Osb[:, :, :, 64], scalar1=1e-30, scalar2=None, op0=ALU.add),
          r=[("Osb", br) for br in range(3)], w=["zr"])
        A("dve", lambda e: e.reciprocal(out=zr[:], in_=zr[:]), r=["zr"], w=["zr"])
        A("dve", lambda e: e.tensor_tensor(
            out=coef[:], in0=zr[:], in1=gates[:, qt, h * 12:(h + 1) * 12].rearrange("p (g b) -> p b g", b=3), op=ALU.mult),
          r=["zr", ("gates", qt)], w=["coef"])
        for br in range(3):
            dst = osb if br == 0 else otmp
            A("dve", lambda e, br=br, dst=dst: e.tensor_tensor(
                out=dst[:], in0=Osb[:, br, :, 0:64], in1=coef[:, br, :].unsqueeze(2).to_broadcast([128, 4, 64]), op=ALU.mult),
              r=[("Osb", br), "coef"], w=["osb" if br == 0 else "otmp"])
            if br > 0:
                A("dve", lambda e: e.tensor_tensor(out=osb[:], in0=osb[:], in1=otmp[:], op=ALU.add),
                  r=["osb", "otmp"], w=["osb"])
        A("pool", lambda e: e.tensor_copy(out=attn[:, h * 256:(h + 1) * 256], in_=osb[:].rearrange("p g d -> p (g d)")),
          r=["osb"], w=[("attn", h)])
        if h == 0:
            return
        for c4 in range(4):
            A("pe", lambda e, c4=c4: e.transpose(out=pTb[:, c4, :], in_=attn[:, c4 * 128:(c4 + 1) * 128], identity=ident[:]),
              r=[("attn", 0), ("attn", 1), "ident"], w=["pTb"])
        A("dve", lambda e: e.tensor_copy(out=attnT[:], in_=pTb[:, 0:4, :]), r=["pTb"], w=["attnT"])
        A("sp", lambda e: e.dma_start(out=xt2[0][:], in_=xp[qt * 128:(qt + 1) * 128, :]), w=["xt2"], dma=True)
        for half in range(2):
            for k in range(8):
                lhs = (convyT[:, k, qt * 128:(qt + 1) * 128] if k < 4 else attnT[:, k - 4, :])
                A("pe", lambda e, k=k, half=half, lhs=lhs: e.matmul(pH[:], lhsT=lhs, rhs=w_out_bf[:, k, half * 512:(half + 1) * 512],
                                                                    start=(k == 0), stop=(k == 7)),
                  r=[("w_out_bf", k), ("convyT", qt // 4), "attnT"], w=["pH"])
            A("dve", lambda e, half=half: e.tensor_tensor(
                out=h_acc[:, qt, half * 512:(half + 1) * 512], in0=pH[:], in1=xt2[0][:, half * 512:(half + 1) * 512], op=ALU.add),
              r=["pH", "xt2"], w=[("h_acc", qt)])
        A("act", lambda e: e.activation(out=xt2[0][:], in_=h_acc[:, qt, :], func=AF.Square, accum_out=L["hss"][:, qt:qt + 1]),
          r=[("h_acc", qt), "hss", "xt2"], w=["xt2", "hss"])
        if qt + 2 < NT:
            emit_sel(qt + 2)

    emit_sel(0)
    emit_sel(1)
    for i in range(len(tiles) + LOOK):
        if i < len(tiles):
            emit_qk(tiles[i], i)
        j = i - LOOK
        if j >= 0:
            emit_pv(tiles[j])
            if tiles[j].get("end"):
                emit_end(tiles[j]["qt"], tiles[j]["h"])
    if debug:
        dbg["attn"] = dout("d_attn", [128, 512], BF16)
        A("sp", lambda e: e.dma_start(out=dbg["attn"], in_=attn[:]), r=[("attn", 0), ("attn", 1)], dma=True)
        dbg["h"] = dout("d_h", [128, NT * DM], F32)
        A("sp", lambda e: e.dma_start(out=dbg["h"], in_=h_acc[:]), r=[("h_acc", t) for t in range(NT)], dma=True)


def emit_norm_T(S, src, srckey, junk, ss, hn, gbc, gkey, ident, psT8, dst_k, dst_all, dstkey):
    A = S.add
    P = src.shape[0]
    A("pool", lambda e: e.memset(ss[0:P, 0:1], 0.0), w=[("ss", id(ss))])
    A("act", lambda e: e.activation(out=junk[0:P, :], in_=src, func=AF.Square, accum_out=ss[0:P, 0:1]),
      r=[srckey, ("ss", id(ss))], w=[("junk", id(junk)), ("ss", id(ss))])
    A("dve", lambda e: e.tensor_scalar(out=ss[0:P, 1:2], in0=ss[0:P, 0:1], scalar1=1.0 / DM, scalar2=1e-6,
                                       op0=ALU.mult, op1=ALU.add), r=[("ss", id(ss))], w=[("rs", id(ss))])
    A("act", lambda e: e.activation(out=ss[0:P, 1:2], in_=ss[0:P, 1:2], func=AF.Sqrt), r=[("rs", id(ss))], w=[("rs", id(ss))])
    A("dve", lambda e: e.reciprocal(out=ss[0:P, 1:2], in_=ss[0:P, 1:2]), r=[("rs", id(ss))], w=[("rs", id(ss))])
    A("dve", lambda e: e.scalar_tensor_tensor(out=hn[0:P, :], in0=src, scalar=ss[0:P, 1:2], in1=gbc[0:P, :],
                                              op0=ALU.mult, op1=ALU.mult), r=[srckey, ("rs", id(ss)), gkey], w=[("hn", id(hn))])
    for k in range(8):
        A("pe", lambda e, k=k: e.transpose(out=psT8[:, k, 0:P], in_=hn[0:P, k * 128:(k + 1) * 128], identity=ident[0:P, 0:P]),
          r=[("hn", id(hn)), "ident"], w=["pTb"])
    A("act", lambda e: e.copy(out=dst_all, in_=psT8[:, :, 0:P]), r=["pTb"], w=[dstkey])


def build_p3(nc, S, P3, sb, pst, L):
    A = S.add
    h_acc, hs_acc, hnTs, w_up, w_down, y_p, y_s, gbc_mlp, ident = (L[k] for k in (
        "h_acc", "hs_acc", "hnTs", "w_up", "w_down", "y_p", "y_s", "gbc_mlp", "ident"))
    with_s = "samp" in L["stages"]
    hnT = sb("hnT", [128, 8, SEQ], BF16, P3)
    junk3 = sb("junk3", [128, DM], BF16, P3)
    ss3 = sb("ss3", [128, 2], F32, P3)
    hnb = [junk3, sb("hn", [128, DM], BF16, P3)]
    pTb = pst("pTb3", [128, 8, 128], BF16, P3)
    hss = L["hss"]
    A("dve", lambda e: e.tensor_scalar(out=hss[:], in0=hss[:], scalar1=1.0 / DM, scalar2=1e-6, op0=ALU.mult, op1=ALU.add),
      r=["hss"], w=["hss"])
    A("act", lambda e: e.activation(out=hss[:], in_=hss[:], func=AF.Sqrt), r=["hss"], w=["hss"])
    A("dve", lambda e: e.reciprocal(out=hss[:], in_=hss[:]), r=["hss"], w=["hss"])
    for qt in range(NT):
        hb = qt % 2
        A("dve", lambda e, qt=qt, hb=hb: e.scalar_tensor_tensor(out=hnb[hb][:], in0=h_acc[:, qt, :], scalar=hss[:, qt:qt + 1],
                                                                in1=gbc_mlp[:], op0=ALU.mult, op1=ALU.mult),
          r=[("h_acc", qt), "hss", "gbc_mlp"], w=[("hnb", hb)])
        for k in range(8):
            A("pe", lambda e, k=k, hb=hb: e.transpose(out=pTb[:, k, :], in_=hnb[hb][:, k * 128:(k + 1) * 128], identity=ident[:]),
              r=[("hnb", hb), "ident"], w=["pTb3"])
        A("act", lambda e, qt=qt: e.copy(out=hnT[:, :, qt * 128:(qt + 1) * 128], in_=pTb[:]), r=["pTb3"], w=[("hnT", qt)])
    stgU = [sb("stgU0", [128, 8, 512], F32, P3)] * 2
    stgD = [sb("stgD0", [128, 4, DM], F32, P3)] * 2
    gbc_fin = sb("gbc_fin", [128, DM], F32, P3)
    A("sp", lambda e: e.dma_start(out=gbc_fin[:], in_=L["g_fin"].partition_broadcast(128)), w=["gbc_fin"], dma=True)
    wu = [sb(f"wu{i}", [128, 8, 512], BF16, P3) for i in range(2)]
    wd = [sb(f"wd{i}", [128, 4, DM], BF16, P3) for i in range(2)]
    aT = [sb(f"aT{i}", [128, 4, 512], BF16, P3) for i in range(2)]
    aTs = sb("aTs", [128, 4, NSB], BF16, P3)
    rl = [sb(f"rl{i}", [128, 512], F32, P3) for i in range(2)]
    pU = [pst(f"pU{i}", [128, 512], F32, P3) for i in range(2)]
    pD = [pst(f"pD{i}", [128, 512], F32, P3) for i in range(2)]
    yo = [stgD[0][:, 0, :]] * 2
    ucnt = [0]
    dcnt = [0]
    acnt = [0]
    groups = list(range(4)) + (["s"] if with_s else [])

    def load_w(fg):
        b = fg % 2
        A("sp", lambda e, fg=fg, b=b: e.dma_start(out=stgU[b][:], in_=w_up[:, fg * 512:(fg + 1) * 512].rearrange(
            "(k p) f -> p k f", p=128)), w=["stgU"], dma=True)
        A("sp", lambda e, fg=fg, b=b: e.dma_start(out=stgD[b][:], in_=w_down[fg * 512:(fg + 1) * 512, :].rearrange(
            "(c p) d -> p c d", p=128)), w=["stgD"], dma=True)
        A("act", lambda e, b=b: e.copy(out=wu[b][:], in_=stgU[b][:]), r=["stgU"], w=[("wu", b)])
        A("pool", lambda e, b=b: e.tensor_copy(out=wd[b][:], in_=stgD[b][:]), r=["stgD"], w=[("wd", b)])

    def emit_up(fg, TG):
        b = fg % 2
        ntok = 512 if TG != "s" else NSB
        if TG == "s":
            rhs_of = lambda k: hnTs[:, k, :]
            rk = ["hnTs"]
            adst = aTs
            akey = "aTs"
        else:
            rhs_of = lambda k, TG=TG: hnT[:, k, TG * 512:(TG + 1) * 512]
            rk = [("hnT", t) for t in range(4 * TG, 4 * TG + 4)]
            ai = acnt[0] % 2
            acnt[0] += 1
            adst = aT[ai]
            akey = ("aT", ai)
        for fc in range(4):
            ui = ucnt[0] % 2
            ucnt[0] += 1
            for k in range(8):
                A("pe", lambda e, k=k, fc=fc, ui=ui, b=b, rhs_of=rhs_of, ntok=ntok: e.matmul(
                    pU[ui][:, 0:ntok], lhsT=wu[b][:, k, fc * 128:(fc + 1) * 128], rhs=rhs_of(k),
                    start=(k == 0), stop=(k == 7)), r=[("wu", b)] + rk, w=[("pU", ui)])
            A("act", lambda e, ui=ui, ntok=ntok: e.activation(out=rl[ui][:, 0:ntok], in_=pU[ui][:, 0:ntok], func=AF.Relu),
              r=[("pU", ui)], w=[("rl", ui)])
            A("pool" if fc % 2 else "dve", lambda e, ui=ui, fc=fc, adst=adst, ntok=ntok: e.tensor_tensor(
                out=adst[:, fc, 0:ntok], in0=rl[ui][:, 0:ntok], in1=rl[ui][:, 0:ntok], op=ALU.mult),
              r=[("rl", ui)], w=[(akey, fc)])
        return adst, akey

    def emit_down(fg, TG, adst, akey):
        b = fg % 2
        tiles = range(4) if TG != "s" else [0]
        for tt in tiles:
            for half in range(2):
                di = dcnt[0] % 2
                dcnt[0] += 1
                mrows = 128 if TG != "s" else NSB
                for fc in range(4):
                    lhs = adst[:, fc, tt * 128:(tt + 1) * 128] if TG != "s" else adst[:, fc, :]
                    A("pe", lambda e, fc=fc, half=half, di=di, lhs=lhs, b=b, mrows=mrows: e.matmul(
                        pD[di][0:mrows, :], lhsT=lhs, rhs=wd[b][:, fc, half * 512:(half + 1) * 512],
                        start=(fc == 0), stop=(fc == 3)), r=[(akey, fc), ("wd", b)], w=[("pD", di)])
                if TG != "s":
                    t = 4 * TG + tt
                    A("dve", lambda e, t=t, half=half, di=di: e.tensor_tensor(
                        out=h_acc[:, t, half * 512:(half + 1) * 512], in0=pD[di][:], in1=h_acc[:, t, half * 512:(half + 1) * 512],
                        op=ALU.add), r=[("pD", di), ("h_acc", t)], w=[("h_acc", t)])
                else:
                    A("dve", lambda e, half=half, di=di: e.tensor_tensor(
                        out=hs_acc[:, half * 512:(half + 1) * 512], in0=pD[di][0:NSB, :],
                        in1=hs_acc[:, half * 512:(half + 1) * 512], op=ALU.add), r=[("pD", di), "hs_acc"], w=["hs_acc"])

    load_w(0)
    pending = None
    for fg in range(8):
        for gi, TG in enumerate(groups):
            cur = emit_up(fg, TG)
            if gi == 1 and fg + 1 < 8:
                load_w(fg + 1)
            if pending is not None:
                emit_down(*pending)
            pending = (fg, TG) + cur
    emit_down(*pending)
    outs = [(h_acc[:, t, :], ("h_acc", t), y_p[t * 128:(t + 1) * 128, :], 128) for t in range(NT)]
    if with_s:
        outs.append((hs_acc[:], "hs_acc", y_s, NSB))
    for i, (src, skey, dst, P) in enumerate(outs):
        yb = i % 2
        A("pool", lambda e, P=P: e.memset(ss3[0:P, 0:1], 0.0), w=["ss3"])
        A("act", lambda e, src=src, P=P: e.activation(out=junk3[0:P, :], in_=src, func=AF.Square, accum_out=ss3[0:P, 0:1]),
          r=[skey, "ss3", ("hnb", 0)], w=[("hnb", 0), "ss3"])
        A("dve", lambda e, P=P: e.tensor_scalar(out=ss3[0:P, 1:2], in0=ss3[0:P, 0:1], scalar1=1.0 / DM, scalar2=1e-6,
                                                op0=ALU.mult, op1=ALU.add), r=["ss3"], w=["rs3"])
        A("act", lambda e, P=P: e.activation(out=ss3[0:P, 1:2], in_=ss3[0:P, 1:2], func=AF.Sqrt), r=["rs3"], w=["rs3"])
        A("dve", lambda e, P=P: e.reciprocal(out=ss3[0:P, 1:2], in_=ss3[0:P, 1:2]), r=["rs3"], w=["rs3"])
        A("dve", lambda e, src=src, P=P, yb=yb: e.scalar_tensor_tensor(
            out=yo[yb][0:P], in0=src, scalar=ss3[0:P, 1:2], in1=gbc_fin[0:P, :], op0=ALU.mult, op1=ALU.mult),
          r=[skey, "rs3", "gbc_fin"], w=["stgD"])
        A("sp", lambda e, dst=dst, P=P, yb=yb: e.dma_start(out=dst, in_=yo[yb][0:P]), r=["stgD"], dma=True)


_NC_CACHE = {}


def _get_nc():
    if "nc" not in _NC_CACHE:
        _NC_CACHE["nc"] = build()[0]
    return _NC_CACHE["nc"]


def make_in_maps(inp, cores):
    f = lambda a: np.ascontiguousarray(np.asarray(a, dtype=np.float32))
    cache = f(inp["cache_kv"]).reshape(2560 * 128 * 4, 128)
    wdall = np.ascontiguousarray(np.concatenate(
        [f(inp["w_dw"])[0], f(inp["b_dw"]), f(inp["conv_ln_g"]), f(inp["conv_ln_b"])], axis=0))
    shared = dict(
        cache=cache, g_attn=f(inp["g_attn_norm"]), w_in=f(inp["w_in"])[0], wdall=wdall,
        w_ck=f(inp["w_cmp_k"])[0], w_cv=f(inp["w_cmp_v"])[0], w_out=f(inp["w_out"])[0], g_mlp=f(inp["g_mlp_norm"]),
        w_up=f(inp["w_up"])[0], w_down=f(inp["w_down"])[0], g_fin=f(inp["g_final"]).reshape(1, DM))
    maps = []
    for c in cores:
        sl = slice(c * NSB, (c + 1) * NSB)
        m = dict(shared)
        m["xp"] = f(inp["x_prompt"])[c]
        m["xs"] = f(inp["x_sample"])[sl, 0]
        m["cwin"] = f(inp["cache_win"])[0, sl].reshape(NSB, 512, 256)
        m["sconv"] = f(inp["state_conv"])[0, sl].reshape(NSB * 30, 512)
        m["ptab"] = np.ascontiguousarray(np.asarray(inp["page_table"], dtype=np.int32)[sl].reshape(1, NSB * 16))
        maps.append(m)
    return maps


def kernel(**inp):
    nc = _get_nc()
    cores = list(range(N_CORES))
    res = run_bass_kernel_spmd(nc, make_in_maps(inp, cores), core_ids=cores)
    R = res.results
    cat = lambda k: np.stack([np.asarray(r[k], dtype=np.float32) for r in R], axis=0)
    y_p = cat("y_p")
    y_s = cat("y_s").reshape(128, 1, DM)
    kv_p = cat("kv_p").reshape(1, 8, SEQ, 4, 2, 64)
    win_p = cat("win_p").reshape(1, 8, 512, 2, 2, 64)
    conv_p = cat("conv_p").reshape(1, 8, 30, 512)
    kv_s = cat("kv_s").reshape(1, 128, 1, 4, 2, 64)
    win_s = cat("win_s").reshape(1, 128, 512, 2, 2, 64)
    conv_s = cat("conv_s").reshape(1, 128, 30, 512)
    return (y_p, y_s, kv_p, win_p, conv_p, kv_s, win_s, conv_s)


def _dap(ap, offset, dims):
    return bass.AP(ap.tensor, offset, [list(d) for d in dims])


def build_samp(nc, S, SP, sb, pst, L):
    A = S.add
    (xs, cache, cwin, sconv, ptab, wdall, w_cv, kv_s, win_s, conv_s, w_in_bf, w_out_bf, ident, identf, gbc_mlp, hs_acc,
     hnTs, wckb, g_attn) = (L[k] for k in ("xs", "cache", "cwin", "sconv", "ptab", "wdall", "w_cv", "kv_s", "win_s",
                                            "conv_s", "w_in_bf", "w_out_bf", "ident", "identf", "gbc_mlp", "hs_acc",
                                            "hnTs", "wckb", "g_attn"))
    debug, dbg, dout = L["debug"], L["dbg"], L["dout"]
    zs_d = nc.dram_tensor("zs_d", [NSB, INC], F32, kind="Internal").ap()
    ptb = sb("ptb", [128, NSB * 16], I32, SP)
    iop = sb("iop", [128, 1], I32, SP)
    idx = sb("idx", [128, NSB * 16], I32, SP)
    A("sp", lambda e: e.dma_start(out=ptb[:], in_=ptab.partition_broadcast(128)), w=["ptb"], dma=True)
    A("pool", lambda e: e.iota(iop[:], pattern=[[0, 1]], base=0, channel_multiplier=1), w=["iop"])
    A("dve", lambda e: e.tensor_scalar(out=idx[:], in0=ptb[:], scalar1=128, scalar2=iop[:, 0:1], op0=ALU.mult, op1=ALU.add),
      r=["ptb", "iop"], w=["idx"])
    cacheR = cache.rearrange("(r s) c -> r (s c)", s=4)

    xs_sb = sb("xs_sb", [NSB, DM], F32, SP)
    gbc_a = sb("gbc_a", [NSB, DM], F32, SP)
    xnTs = sb("xnTs", [128, 8, NSB], BF16, SP)
    junk = sb("s_junk", [NSB, DM], BF16, SP)
    ssx = sb("s_ss", [128, 2], F32, SP)
    hn = sb("s_hn", [NSB, DM], BF16, SP)
    zs = sb("zs", [NSB, INC], F32, SP)
    pA = pst("s_pA", [128, 512], F32, SP)
    pB = pst("s_pB", [128, 512], F32, SP)
    pTb = pst("s_pTb", [128, 8, 128], BF16, SP)
    pKT = pst("s_pKT", [128, 8, 128], BF16, SP)
    pS = [pst(f"s_pS{i}", [128, 512], F32, SP) for i in range(3)]
    A("sp", lambda e: e.dma_start(out=xs_sb[:], in_=xs), w=["xs_sb"], dma=True)
    A("sp", lambda e: e.dma_start(out=gbc_a[:], in_=g_attn.partition_broadcast(NSB)), w=["gbc_a"], dma=True)
    emit_norm_T(S, xs_sb[:], "xs_sb", junk, ssx, hn, gbc_a, "gbc_a", ident, pTb, None, xnTs[:], "xnTs")
    for ci, c0 in enumerate(range(0, INC, 512)):
        n = min(512, INC - c0)
        for k in range(8):
            A("pe", lambda e, k=k, c0=c0, n=n: e.matmul(pA[0:NSB, 0:n], lhsT=xnTs[:, k, :], rhs=w_in_bf[:, k, c0:c0 + n],
                                                        start=(k == 0), stop=(k == 7)), r=["xnTs", ("w_in_bf", k)], w=["s_pA"])
        A("act", lambda e, c0=c0, n=n: e.copy(out=zs[:, c0:c0 + n], in_=pA[0:NSB, 0:n]), r=["s_pA"], w=["zs"])
    A("sp", lambda e: e.dma_start(out=zs_d, in_=zs[:]), r=["zs"], w=["zs_d"], dma=True)
    A("sp", lambda e: e.dma_start(out=kv_s, in_=zs[:, 1536:2048]), r=["zs"], dma=True)
    A("sp", lambda e: e.dma_start(out=win_s[:, 511, :], in_=zs[:, 2048:2304]), r=["zs"], dma=True)
    for b in range(NSB):
        A("sp", lambda e, b=b: e.dma_start(out=win_s[b, 0:511, :], in_=cwin[b, 1:512, :]), dma=True)
    A("sp", lambda e: e.dma_start(out=conv_s.rearrange("(b j) c -> b (j c)", j=30)[:, 0:29 * 512],
                                  in_=sconv.rearrange("(b j) c -> b (j c)", j=30)[:, 512:30 * 512]), dma=True)
    Qrows = sb("Qrows", [128, 64], F32, SP)
    Kn = sb("Kn", [128, 2, 64], F32, SP)
    Vn = sb("Vn", [128, 2, 64], F32, SP)
    Grows = sb("Grows", [128, 3], F32, SP)
    for b in range(NSB):
        rows = slice(8 * b, 8 * b + 8)
        A("sp", lambda e, b=b, rows=rows: e.dma_start(out=Qrows[rows, :], in_=_dap(zs_d, b * INC + 1024, [[64, 8], [1, 64]])),
          r=["zs_d"], w=["Qrows"], dma=True)
        for j, col in enumerate((1536 + 256, 1536 + 512)):
            A("sp", lambda e, b=b, rows=rows, j=j, col=col: e.dma_start(
                out=Kn[rows, j, :], in_=_dap(zs_d, b * INC + col, [[64, 2], [0, 4], [1, 64]])), r=["zs_d"], w=["Kn"], dma=True)
        for j, col in enumerate((1536 + 384, 1536 + 640)):
            A("sp", lambda e, b=b, rows=rows, j=j, col=col: e.dma_start(
                out=Vn[rows, j, :], in_=_dap(zs_d, b * INC + col, [[64, 2], [0, 4], [1, 64]])), r=["zs_d"], w=["Vn"], dma=True)
        A("sp", lambda e, b=b, rows=rows: e.dma_start(out=Grows[rows, :], in_=_dap(zs_d, b * INC + 2304, [[3, 8], [1, 3]])),
          r=["zs_d"], w=["Grows"], dma=True)
    QTpad = sb("QTpad", [128, NSB, 128], BF16, SP)
    A("pool", lambda e: e.memset(QTpad[:], 0.0), w=["QTpad"])
    qsrc = sb("qsrc", [NSB, 4, 2, 64], F32, SP)
    A("dve", lambda e: e.tensor_copy(out=qsrc[:], in_=zs[:, 1024:1536].rearrange("p (h g d) -> p g h d", h=2, g=4)),
      r=["zs"], w=["qsrc"])
    for g in range(4):
        A("pe", lambda e, g=g: e.transpose(out=pB[:, g * 16:(g + 1) * 16], in_=qsrc[:, g, :, :].rearrange("p h d -> p (h d)"),
                                           identity=identf[0:NSB, 0:NSB]), r=["qsrc", "identf"], w=["s_pB"])
    QTflat = QTpad[:].rearrange("p b c -> p (b c)")
    for h in range(2):
        for g in range(4):
            c0 = 4 * h + g
            A("act", lambda e, h=h, g=g, c0=c0: e.copy(out=QTflat[64 * h:64 * h + 64, c0:c0 + 136 * 15 + 1:136],
                                                       in_=pB[64 * h:64 * h + 64, g * 16:(g + 1) * 16]),
              r=["s_pB", "QTpad"], w=["QTpad"])
    pidx = sb("pidx", [128, 4], I32, SP)
    pf = sb("pf", [128, 4], F32, SP)
    A("pool", lambda e: e.iota(pidx[:, 0:1], pattern=[[0, 1]], base=0, channel_multiplier=1), w=["pidx"])
    A("dve", lambda e: e.tensor_single_scalar(out=pidx[:, 1:2], in_=pidx[:, 0:1], scalar=2, op=ALU.arith_shift_right),
      r=["pidx"], w=["pidx"])
    A("dve", lambda e: e.tensor_single_scalar(out=pidx[:, 2:3], in_=pidx[:, 1:2], scalar=1, op=ALU.bitwise_and),
      r=["pidx"], w=["pidx"])
    A("dve", lambda e: e.tensor_copy(out=pf[:, 0:3], in_=pidx[:, 0:3]), r=["pidx"], w=["pf"])
    coli = sb("coli", [128, 128], I32, SP)
    colf = sb("colf", [128, 128], F32, SP)
    GG = sb("GG", [128, 128], F32, SP)
    A("pool", lambda e: e.iota(coli[:], pattern=[[1, 128]], base=0, channel_multiplier=0), w=["coli"])
    A("dve", lambda e: e.tensor_single_scalar(out=coli[:], in_=coli[:], scalar=2, op=ALU.arith_shift_right), r=["coli"], w=["coli"])
    A("dve", lambda e: e.tensor_copy(out=colf[:], in_=coli[:]), r=["coli"], w=["colf"])
    A("dve", lambda e: e.tensor_scalar(out=GG[:], in0=colf[:], scalar1=pf[:, 1:2], scalar2=None, op0=ALU.is_equal),
      r=["colf", "pf"], w=["GG"])
    Mh = sb("Mh", [128, 2, 16], F32, SP)
    for half in range(2):
        A("dve", lambda e, half=half: e.tensor_scalar(out=Mh[:, half, :], in0=colf[:, 0:64:4], scalar1=float(16 * half),
                                                      scalar2=pf[:, 1:2], op0=ALU.add, op1=ALU.is_equal),
          r=["colf", "pf"], w=["Mh"])
    wcvb = L["wcvb"]
    Wk = sb("Wk", [128, 32], F32, SP)
    Wv = sb("Wv", [128, 32], F32, SP)
    wtmp = sb("wtmp", [128, 32], F32, SP)
    for (src, dst, key) in ((wckb, Wk, "Wk"), (wcvb, Wv, "Wv")):
        A("dve", lambda e, src=src: e.tensor_tensor(out=wtmp[:], in0=src[:, 1, :], in1=src[:, 0, :], op=ALU.subtract),
          r=["wckb", "wcvb"], w=["wtmp"])
        A("dve", lambda e, src=src, dst=dst: e.scalar_tensor_tensor(out=dst[:], in0=wtmp[:], scalar=pf[:, 2:3], in1=src[:, 0, :],
                                                                    op0=ALU.mult, op1=ALU.add), r=["wtmp", "pf", "wckb", "wcvb"], w=[key])
    SS_ = sb("SS_", [128, 2, SEQ], F32, SP)
    Scmp = SS_[:, 0, :]
    Ssel = SS_[:, 1, :]
    Swin = sb("Swin", [128, 512], F32, SP)
    NKB = 3
    kst = [sb(f"kst{i}", [128, 4, 4, 128], F32, SP) for i in range(NKB)]
    kbf = [sb(f"kbf{i}", [128, 4, 2, 128], BF16, SP) for i in range(2)]
    KTsb = [sb(f"KTsb{i}", [128, 2, 512], BF16, SP) for i in range(2)]
    it = 0
    for pg in range(5):
        for b in range(NSB):
            bi = it % NKB
            b2 = it % 2
            it += 1
            if pg < 4:
                for i in range(4):
                    col = b * 16 + pg * 4 + i
                    A("pool", lambda e, bi=bi, i=i, col=col: e.indirect_dma_start(
                        out=kst[bi][:, i, :, :].rearrange("p s c -> p (s c)"), out_offset=None, in_=cacheR,
                        in_offset=bass.IndirectOffsetOnAxis(ap=idx[:, col:col + 1], axis=0)),
                      r=["idx"], w=[("kst", bi, i)], dma=True)
                A("act", lambda e, bi=bi, b2=b2: e.copy(out=kbf[b2][:], in_=kst[bi][:, :, 0:4:2, :]),
                  r=[("kst", bi, i) for i in range(4)], w=[("kbf", b2)])
                for i in range(4):
                    for si in range(2):
                        A("pe", lambda e, b2=b2, i=i, si=si: e.transpose(out=pKT[:, si * 4 + i, :], in_=kbf[b2][:, i, si, :],
                                                                         identity=ident[:]), r=[("kbf", b2), "ident"], w=["s_pKT"])
                A("dve", lambda e, b2=b2: e.tensor_copy(out=KTsb[b2][:], in_=pKT[:].rearrange("p (s i) t -> p s (i t)", s=2)),
                  r=["s_pKT"], w=[("KTsb", b2)])
                for si in range(2):
                    A("pe", lambda e, b2=b2, si=si, b=b: e.matmul(pS[si][:], lhsT=QTpad[:, b, :], rhs=KTsb[b2][:, si, :],
                                                                  start=(b == 0), stop=(b == NSB - 1)),
                      r=["QTpad", ("KTsb", b2)], w=[("s_pS", si)])
            else:
                A("sp", lambda e, bi=bi, b=b: e.dma_start(
                    out=kst[bi][:, :, 0, :], in_=cwin[b].rearrange("(i p) (s c) -> p i s c", p=128, c=128)[:, :, 0, :]),
                  w=[("kst", bi, i) for i in range(4)], dma=True)
                A("act", lambda e, bi=bi, b2=b2: e.copy(out=kbf[b2][:, :, 0, :], in_=kst[bi][:, :, 0, :]),
                  r=[("kst", bi, i) for i in range(4)], w=[("kbf", b2)])
                for i in range(4):
                    A("pe", lambda e, b2=b2, i=i: e.transpose(out=pKT[:, i, :], in_=kbf[b2][:, i, 0, :], identity=ident[:]),
                      r=[("kbf", b2), "ident"], w=["s_pKT"])
                A("dve", lambda e, b2=b2: e.tensor_copy(out=KTsb[b2][:, 0, :], in_=pKT[:, 0:4, :].rearrange("p i t -> p (i t)")),
                  r=["s_pKT"], w=[("KTsb", b2)])
                A("pe", lambda e, b2=b2, b=b: e.matmul(pS[2][:], lhsT=QTpad[:, b, :], rhs=KTsb[b2][:, 0, :],
                                                       start=(b == 0), stop=(b == NSB - 1)), r=["QTpad", ("KTsb", b2)], w=[("s_pS", 2)])
        if pg < 4:
            A("act", lambda e, pg=pg: e.copy(out=SS_[:, 0, pg * 512:(pg + 1) * 512], in_=pS[0][:]), r=[("s_pS", 0)], w=["Scmp"])
            A("dve", lambda e, pg=pg: e.tensor_copy(out=SS_[:, 1, pg * 512:(pg + 1) * 512], in_=pS[1][:]), r=[("s_pS", 1)], w=["Ssel"])
        else:
            A("act", lambda e: e.copy(out=Swin[:], in_=pS[2][:]), r=[("s_pS", 2)], w=["Swin"])
    big = sb("s_big", [128, SEQ], F32, SP)
    sc = sb("s_sc", [128, 64], F32, SP)
    st = sb("s_st", [128, 16], F32, SP)
    pb32 = sb("s_pb32", [128, 32], F32, SP)
    m8 = sb("s_m8", [128, 8], F32, SP)
    wk32 = sb("s_wk32", [128, 32], F32, SP)
    selm = sb("s_selm", [128, 32], F32, SP)
    Pc = sb("Pc", [128, SEQ], BF16, SP)
    Ps = sb("Ps", [128, SEQ], BF16, SP)
    Pw = sb("Pw", [128, 512], BF16, SP)
    A("dve", lambda e: e.tensor_tensor(out=big[:].rearrange("p (c j) -> p c j", j=32), in0=Scmp.rearrange("p (c j) -> p c j", j=32),
                                       in1=Wk[:].unsqueeze(1).to_broadcast([128, 64, 32]), op=ALU.mult), r=["Scmp", "Wk"], w=["s_big"])
    A("dve", lambda e: e.tensor_reduce(out=sc[:], in_=big[:].rearrange("p (c j) -> p c j", j=32), axis=AX.X, op=ALU.add),
      r=["s_big"], w=["s_sc"])
    A("pool", lambda e: e.memset(st[:], 0.0), w=["s_st"])
    A("act", lambda e: e.activation(out=sc[:], in_=sc[:], func=AF.Exp, scale=SCALE, accum_out=st[:, 0:1]), r=["s_sc", "s_st"], w=["s_sc", "s_st"])
    A("dve", lambda e: e.reciprocal(out=st[:, 1:2], in_=st[:, 0:1]), r=["s_st"], w=["s_st"])
    A("dve", lambda e: e.tensor_scalar(out=sc[:], in0=sc[:], scalar1=st[:, 1:2], scalar2=None, op0=ALU.mult), r=["s_sc", "s_st"], w=["s_sc"])
    A("pe", lambda e: e.matmul(pA[:, 0:64], lhsT=GG[:], rhs=sc[:], start=True, stop=True), r=["GG", "s_sc"], w=["s_pA"])
    A("dve", lambda e: e.tensor_reduce(out=pb32[:], in_=pA[:, 0:64].rearrange("p (b t) -> p b t", t=2), axis=AX.X, op=ALU.add),
      r=["s_pA"], w=["s_pb32"])
    A("dve", lambda e: e.tensor_scalar(out=pb32[:, 0:1], in0=pb32[:, 0:1], scalar1=5.0, scalar2=None, op0=ALU.add), r=["s_pb32"], w=["s_pb32"])
    A("dve", lambda e: e.tensor_scalar(out=pb32[:, 31:32], in0=pb32[:, 31:32], scalar1=5.0, scalar2=None, op0=ALU.add), r=["s_pb32"], w=["s_pb32"])
    A("dve", lambda e: e.max(out=m8[:], in_=pb32[:]), r=["s_pb32"], w=["s_m8"])
    A("dve", lambda e: e.match_replace(out=wk32[:], in_to_replace=m8[:], in_values=pb32[:], imm_value=-3e38), r=["s_m8", "s_pb32"], w=["s_wk32"])
    A("dve", lambda e: e.max(out=m8[:], in_=wk32[:]), r=["s_wk32"], w=["s_m8"])
    A("dve", lambda e: e.tensor_scalar(out=selm[:], in0=pb32[:], scalar1=m8[:, 6:7], scalar2=None, op0=ALU.is_ge), r=["s_pb32", "s_m8"], w=["s_selm"])
    A("dve", lambda e: e.tensor_tensor(out=big[:].rearrange("p (c j) -> p c j", j=32), in0=sc[:].unsqueeze(2).to_broadcast([128, 64, 32]),
                                       in1=Wv[:].unsqueeze(1).to_broadcast([128, 64, 32]), op=ALU.mult), r=["s_sc", "Wv", "s_big"], w=["s_big"])
    A("act", lambda e: e.copy(out=Pc[:], in_=big[:]), r=["s_big"], w=["Pc"])
    for j in range(2):
        A("dve", lambda e, j=j: e.tensor_tensor(out=big[:, 0:64], in0=Qrows[:], in1=Kn[:, j, :], op=ALU.mult), r=["Qrows", "Kn", "s_big", "Pc"], w=["s_big"])
        A("dve", lambda e, j=j: e.tensor_reduce(out=st[:, 2 + 2 * j:3 + 2 * j], in_=big[:, 0:64], axis=AX.X, op=ALU.add), r=["s_big"], w=["s_st"])
        A("act", lambda e, j=j: e.activation(out=st[:, 3 + 2 * j:4 + 2 * j], in_=st[:, 2 + 2 * j:3 + 2 * j], func=AF.Exp, scale=SCALE),
          r=["s_st"], w=["s_st"])
    A("act", lambda e: e.activation(out=big[:], in_=Ssel, func=AF.Exp, scale=SCALE), r=["Ssel", "s_big"], w=["s_big"])
    A("dve", lambda e: e.tensor_tensor(out=big[:].rearrange("p (b t) -> p b t", t=64), in0=big[:].rearrange("p (b t) -> p b t", t=64),
                                       in1=selm[:].unsqueeze(2).to_broadcast([128, 32, 64]), op=ALU.mult), r=["s_big", "s_selm"], w=["s_big"])
    A("dve", lambda e: e.tensor_reduce(out=st[:, 6:7], in_=big[:], axis=AX.X, op=ALU.add), r=["s_big"], w=["s_st"])
    A("pool", lambda e: e.tensor_copy(out=Ps[:], in_=big[:]), r=["s_big"], w=["Ps"])
    A("act", lambda e: e.activation(out=big[:, 0:512], in_=Swin[:], func=AF.Exp, scale=SCALE), r=["Swin", "s_big", "Ps"], w=["s_big"])
    A("dve", lambda e: e.tensor_reduce(out=st[:, 7:8], in_=big[:, 0:512], axis=AX.X, op=ALU.add), r=["s_big"], w=["s_st"])
    A("pool", lambda e: e.tensor_copy(out=Pw[:], in_=big[:, 0:512]), r=["s_big"], w=["Pw"])
    A("dve", lambda e: e.tensor_tensor(out=st[:, 8:9], in0=st[:, 6:7], in1=st[:, 3:4], op=ALU.add), r=["s_st"], w=["s_st"])
    A("dve", lambda e: e.tensor_tensor(out=st[:, 9:10], in0=st[:, 7:8], in1=st[:, 5:6], op=ALU.add), r=["s_st"], w=["s_st"])
    A("dve", lambda e: e.reciprocal(out=st[:, 8:10], in_=st[:, 8:10]), r=["s_st"], w=["s_st"])
    PTc = sb("PTc", [128, 16, 128], BF16, SP)
    PTs = sb("PTs", [128, 16, 128], BF16, SP)
    PTw = sb("PTw", [128, 4, 128], BF16, SP)
    for (src, dst, key, n) in ((Pc, PTc, "PTc", 16), (Ps, PTs, "PTs", 16), (Pw, PTw, "PTw", 4)):
        for i0 in range(0, n, 8):
            m = min(8, n - i0)
            for i in range(m):
                A("pe", lambda e, src=src, i=i, i0=i0: e.transpose(out=pKT[:, i, :], in_=src[:, (i0 + i) * 128:(i0 + i + 1) * 128],
                                                                   identity=ident[:]), r=["ident", "Pc", "Ps", "Pw"], w=["s_pKT"])
            A("dve", lambda e, dst=dst, i0=i0, m=m: e.tensor_copy(out=dst[:, i0:i0 + m, :], in_=pKT[:, 0:m, :]), r=["s_pKT"], w=[key])
    S.barrier()
    vst = [SS_[:, vi, :].rearrange("p (i s c) -> p i s c", i=4, s=4) for vi in range(2)]
    vbf = [sb(f"vbf{i}", [128, 4, 2, 128], BF16, SP) for i in range(2)]
    vwst = [sb(f"vwst{i}", [128, 4, 128], F32, SP) for i in range(2)]
    vwbf = [sb(f"vwbf{i}", [128, 4, 128], BF16, SP) for i in range(2)]
    Oacc = sb("Oacc", [128, 3, 64], F32, SP)
    otmp = big[:, 0:1024].rearrange("p (b h d) -> p b h d", b=8, h=2)
    ored = sb("s_ored", [128, 64], F32, SP)
    A("pool", lambda e: e.memset(Oacc[:], 0.0), w=["Oacc"])
    pW2 = pst("s_pW2", [128, 512], F32, SP)
    pO = [[pA, pB], [pS[0], pS[1]], [pS[2], pW2]]
    pkeys = [["s_pA", "s_pB"], [("s_pS", 0), ("s_pS", 1)], [("s_pS", 2), "s_pW2"]]
    vcnt = 0
    for half in range(2):
        for bl in range(8):
            b = half * 8 + bl
            bank, cb = bl // 4, (bl % 4) * 128
            wi = b % 2
            A("sp", lambda e, wi=wi, b=b: e.dma_start(
                out=vwst[wi][:], in_=cwin[b].rearrange("(i p) (s c) -> p i s c", p=128, c=128)[:, :, 1, :]), w=[("vwst", wi)], dma=True)
            A("act", lambda e, wi=wi: e.copy(out=vwbf[wi][:], in_=vwst[wi][:]), r=[("vwst", wi)], w=[("vwbf", wi)])
            for q4 in range(4):
                vi = vcnt % 2
                vcnt += 1
                for i in range(4):
                    col = b * 16 + q4 * 4 + i
                    A("pool", lambda e, vi=vi, i=i, col=col: e.indirect_dma_start(
                        out=vst[vi][:, i, :, :].rearrange("p s c -> p (s c)"), out_offset=None, in_=cacheR,
                        in_offset=bass.IndirectOffsetOnAxis(ap=idx[:, col:col + 1], axis=0)), r=["idx"], w=[("vst", vi, i)], dma=True)
                A("act", lambda e, vi=vi: e.copy(out=vbf[vi][:], in_=vst[vi][:, :, 1:4:2, :]),
                  r=[("vst", vi, i) for i in range(4)], w=[("vbf", vi)])
                for br, (PT_, ptk) in enumerate(((PTc, "PTc"), (PTs, "PTs"))):
                    for i in range(4):
                        pgi = q4 * 4 + i
                        A("pe", lambda e, br=br, bank=bank, cb=cb, PT_=PT_, i=i, pgi=pgi, vi=vi: e.matmul(
                            pO[br][bank][:, cb:cb + 128], lhsT=PT_[:, pgi, :], rhs=vbf[vi][:, i, br, :],
                            start=(pgi == 0), stop=(pgi == 15)), r=[ptk, ("vbf", vi)], w=[pkeys[br][bank]])
            for i in range(4):
                A("pe", lambda e, bank=bank, cb=cb, i=i, wi=wi: e.matmul(
                    pO[2][bank][:, cb:cb + 128], lhsT=PTw[:, i, :], rhs=vwbf[wi][:, i, :], start=(i == 0), stop=(i == 3)),
                  r=["PTw", ("vwbf", wi)], w=[pkeys[2][bank]])
        for br in range(3):
            for bank in range(2):
                A("dve", lambda e, br=br, bank=bank, half=half: e.tensor_tensor(
                    out=otmp[:, bank * 4:(bank + 1) * 4, :, :].rearrange("p b h d -> p (b h) d"),
                    in0=pO[br][bank][:].rearrange("p (j d) -> p j d", d=64),
                    in1=Mh[:, half, bank * 8:(bank + 1) * 8].unsqueeze(2).to_broadcast([128, 8, 64]), op=ALU.mult),
                  r=[pkeys[br][bank], "Mh"], w=["s_otmp"])
            A("dve", lambda e: e.tensor_reduce(out=ored[:], in_=otmp.rearrange("p b h d -> p d (b h)"), axis=AX.X, op=ALU.add),
              r=["s_otmp"], w=["s_ored"])
            A("dve", lambda e, br=br: e.tensor_tensor(out=Oacc[:, br, :], in0=Oacc[:, br, :], in1=ored[:], op=ALU.add),
              r=["s_ored", "Oacc"], w=["Oacc"])
    G3 = sb("G3", [128, 3], F32, SP)
    A("act", lambda e: e.activation(out=G3[:], in_=Grows[:], func=AF.Sigmoid), r=["Grows"], w=["G3"])
    arow = sb("arow", [128, 128], F32, SP)
    tmp64 = sb("tmp64", [128, 64], F32, SP)
    A("dve", lambda e: e.tensor_scalar(out=arow[:, 0:64], in0=Oacc[:, 0, :], scalar1=G3[:, 0:1], scalar2=None, op0=ALU.mult),
      r=["Oacc", "G3"], w=["arow"])
    for j, br in ((0, 1), (1, 2)):
        A("dve", lambda e, j=j, br=br: e.scalar_tensor_tensor(out=tmp64[:], in0=Vn[:, j, :], scalar=st[:, 3 + 2 * j:4 + 2 * j],
                                                              in1=Oacc[:, br, :], op0=ALU.mult, op1=ALU.add),
          r=["Vn", "s_st", "Oacc"], w=["tmp64"])
        A("dve", lambda e, j=j, br=br: e.tensor_scalar(out=tmp64[:], in0=tmp64[:], scalar1=st[:, 8 + j:9 + j], scalar2=G3[:, br:br + 1],
                                                       op0=ALU.mult, op1=ALU.mult), r=["tmp64", "s_st", "G3"], w=["tmp64"])
        A("dve", lambda e: e.tensor_tensor(out=arow[:, 0:64], in0=arow[:, 0:64], in1=tmp64[:], op=ALU.add), r=["arow", "tmp64"], w=["arow"])
    glus = sb("glus", [NSB, 512], F32, SP)
    cvb = sb("cvb", [NSB, 4, 512], F32, SP)
    A("sp", lambda e: e.dma_start(out=cvb[:], in_=wdall[30:34, :].partition_broadcast(NSB)), w=["cvb"], dma=True)
    A("act", lambda e: e.activation(out=glus[:], in_=zs[:, 512:1024], func=AF.Sigmoid), r=["zs"], w=["glus"])
    A("dve", lambda e: e.tensor_tensor(out=glus[:], in0=glus[:], in1=zs[:, 0:512], op=ALU.mult), r=["glus", "zs"], w=["glus"])
    A("sp", lambda e: e.dma_start(out=conv_s.rearrange("(b j) c -> b j c", j=30)[:, 29, :], in_=glus[:]), r=["glus"], dma=True)
    Xc = sb("Xc", [120, 512], F32, SP)
    Wrep = sb("Wrep", [120, 512], F32, SP)
    sel4 = sb("sel4", [120, 4, NSB], F32, SP)
    for r4 in range(4):
        A("sp", lambda e, r4=r4: e.dma_start(out=Wrep[r4 * 30:(r4 + 1) * 30, :], in_=wdall[0:30, :]), w=["Wrep"], dma=True)
    A("pool", lambda e: e.memset(sel4[:], 1.0), w=["sel4"])
    for i4 in range(4):
        A("pool", lambda e, i4=i4: e.affine_select(out=sel4[:, i4, :], in_=sel4[:, i4, :], pattern=[[-30, NSB]], compare_op=ALU.is_ge,
                                                   fill=0.0, base=120 * i4, channel_multiplier=1), r=["sel4"], w=["sel4"])
        A("pool", lambda e, i4=i4: e.affine_select(out=sel4[:, i4, :], in_=sel4[:, i4, :], pattern=[[30, NSB]], compare_op=ALU.is_ge,
                                                   fill=0.0, base=29 - 120 * i4, channel_multiplier=-1), r=["sel4"], w=["sel4"])
    for i4 in range(4):
        A("sp", lambda e, i4=i4: e.dma_start(out=Xc[:], in_=sconv[i4 * 120:(i4 + 1) * 120, :]), w=["Xc"], dma=True)
        A("dve", lambda e: e.tensor_tensor(out=Xc[:], in0=Xc[:], in1=Wrep[:], op=ALU.mult), r=["Xc", "Wrep"], w=["Xc"])
        A("pe", lambda e, i4=i4: e.matmul(pB[0:NSB, :], lhsT=sel4[:, i4, :], rhs=Xc[:], start=(i4 == 0), stop=(i4 == 3)),
          r=["sel4", "Xc"], w=["s_pB"])
    yc = sb("yc", [NSB, 512], F32, SP)
    ycs = sb("ycs", [NSB, 4], F32, SP)
    A("dve", lambda e: e.tensor_tensor(out=yc[:], in0=glus[:], in1=cvb[:, 0, :], op=ALU.mult), r=["glus", "cvb"], w=["yc"])
    A("dve", lambda e: e.tensor_tensor(out=yc[:], in0=yc[:], in1=pB[0:NSB, :], op=ALU.add), r=["yc", "s_pB"], w=["yc"])
    A("dve", lambda e: e.tensor_tensor(out=yc[:], in0=yc[:], in1=cvb[:, 1, :], op=ALU.add), r=["yc", "cvb"], w=["yc"])
    A("dve", lambda e: e.tensor_reduce(out=ycs[:, 0:1], in_=yc[:], axis=AX.X, op=ALU.add), r=["yc"], w=["ycs"])
    A("dve", lambda e: e.tensor_scalar(out=ycs[:, 0:1], in0=ycs[:, 0:1], scalar1=1.0 / 512.0, scalar2=None, op0=ALU.mult), r=["ycs"], w=["ycs"])
    A("dve", lambda e: e.tensor_scalar(out=yc[:], in0=yc[:], scalar1=ycs[:, 0:1], scalar2=None, op0=ALU.subtract), r=["yc", "ycs"], w=["yc"])
    ysq_ = attn_s_early = sb("ysq_", [NSB, 512], F32, SP)
    A("dve", lambda e: e.tensor_tensor(out=ysq_[:], in0=yc[:], in1=yc[:], op=ALU.mult), r=["yc"], w=["ysq_"])
    A("dve", lambda e: e.tensor_reduce(out=ycs[:, 1:2], in_=ysq_[:], axis=AX.X, op=ALU.add), r=["ysq_"], w=["ycs"])
    A("dve", lambda e: e.tensor_scalar(out=ycs[:, 1:2], in0=ycs[:, 1:2], scalar1=1.0 / 512.0, scalar2=1e-5, op0=ALU.mult, op1=ALU.add),
      r=["ycs"], w=["ycs"])
    A("act", lambda e: e.activation(out=ycs[:, 1:2], in_=ycs[:, 1:2], func=AF.Sqrt), r=["ycs"], w=["ycs"])
    A("dve", lambda e: e.reciprocal(out=ycs[:, 1:2], in_=ycs[:, 1:2]), r=["ycs"], w=["ycs"])
    A("dve", lambda e: e.scalar_tensor_tensor(out=yc[:], in0=yc[:], scalar=ycs[:, 1:2], in1=cvb[:, 2, :], op0=ALU.mult, op1=ALU.mult),
      r=["yc", "ycs", "cvb"], w=["yc"])
    A("dve", lambda e: e.tensor_tensor(out=yc[:], in0=yc[:], in1=cvb[:, 3, :], op=ALU.add), r=["yc", "cvb"], w=["yc"])
    ycb = sb("ycb", [NSB, 512], BF16, SP)
    A("act", lambda e: e.activation(out=ycb[:], in_=yc[:], func=AF.Silu), r=["yc"], w=["ycb"])
    cyT = sb("cyT", [128, 4, NSB], BF16, SP)
    for c4 in range(4):
        A("pe", lambda e, c4=c4: e.transpose(out=pTb[:, c4, 0:NSB], in_=ycb[:, c4 * 128:(c4 + 1) * 128], identity=ident[0:NSB, 0:NSB]),
          r=["ycb", "ident"], w=["pTb"])
    A("act", lambda e: e.copy(out=cyT[:], in_=pTb[:, 0:4, 0:NSB]), r=["pTb"], w=["cyT"])
    as_d = nc.dram_tensor("as_d", [128, 64], F32, kind="Internal").ap()
    attn_s = ysq_
    attn_sb = sb("attn_sb", [NSB, 512], BF16, SP)
    aT2 = sb("aT2", [128, 4, NSB], BF16, SP)
    A("sp", lambda e: e.dma_start(out=as_d, in_=arow[:, 0:64]), r=["arow"], w=["as_d"], dma=True)
    A("sp", lambda e: e.dma_start(out=attn_s[:], in_=as_d.rearrange("(b h) d -> b (h d)", h=8)), r=["as_d"], w=["ysq_"], dma=True)
    A("act", lambda e: e.copy(out=attn_sb[:], in_=attn_s[:]), r=["ysq_"], w=["attn_sb"])
    for c4 in range(4):
        A("pe", lambda e, c4=c4: e.transpose(out=pTb[:, 4 + c4, 0:NSB], in_=attn_sb[:, c4 * 128:(c4 + 1) * 128], identity=ident[0:NSB, 0:NSB]),
          r=["attn_sb", "ident"], w=["pTb"])
    A("act", lambda e: e.copy(out=aT2[:], in_=pTb[:, 4:8, 0:NSB]), r=["pTb"], w=["aT2"])
    for half in range(2):
        cs = slice(half * 512, (half + 1) * 512)
        for k in range(8):
            lhs = cyT[:, k, :] if k < 4 else aT2[:, k - 4, :]
            A("pe", lambda e, k=k, cs=cs, lhs=lhs: e.matmul(pA[0:NSB, :], lhsT=lhs, rhs=w_out_bf[:, k, cs], start=(k == 0), stop=(k == 7)),
              r=["cyT", "aT2", ("w_out_bf", k)], w=["s_pA"])
        A("dve", lambda e, cs=cs: e.tensor_tensor(out=hs_acc[:, cs], in0=pA[0:NSB, :], in1=xs_sb[:, cs], op=ALU.add),
          r=["s_pA", "xs_sb"], w=["hs_acc"])
    emit_norm_T(S, hs_acc[:], "hs_acc", junk, ssx, hn, gbc_mlp, "gbc_mlp", ident, pTb, None, hnTs[:], "hnTs")
    if debug:
        dbg["arow"] = dout("d_arow", [128, 128], F32)
        A("sp", lambda e: e.dma_start(out=dbg["arow"], in_=arow[:]), r=["arow"], dma=True)
        dbg["hs"] = dout("d_hs", [NSB, DM], F32)
        A("sp", lambda e: e.dma_start(out=dbg["hs"], in_=hs_acc[:]), r=["hs_acc"], dma=True)
        dbg["ycb"] = dout("d_ycb", [NSB, 512], BF16)
        A("sp", lambda e: e.dma_start(out=dbg["ycb"], in_=ycb[:]), r=["ycb"], dma=True)
        dbg["Oacc"] = dout("d_Oacc", [128, 192], F32)
        A("sp", lambda e: e.dma_start(out=dbg["Oacc"], in_=Oacc[:]), r=["Oacc"], dma=True)
        dbg["st"] = dout("d_st", [128, 16], F32)
        A("sp", lambda e: e.dma_start(out=dbg["st"], in_=st[:]), r=["s_st"], dma=True)
        dbg["selm"] = dout("d_selm", [128, 32], F32)
        A("sp", lambda e: e.dma_start(out=dbg["selm"], in_=selm[:]), r=["s_selm"], dma=True)
```

```python
import contextlib
import os
import numpy as np
import concourse.bass as bass
import concourse.mybir as mybir
from concourse.bass_utils import run_bass_kernel_spmd

F32 = mybir.dt.float32
BF16 = mybir.dt.bfloat16
I32 = mybir.dt.int32
AF = mybir.ActivationFunctionType
ALU = mybir.AluOpType
AX = mybir.AxisListType

EPOCH = 4000
N_DMA_SEMS = 8

SEQ = 2048
DM = 1024
NT = 16
INC = 2328
SCALE = 0.125
BIG = 20000.0
NSB = 16
N_CORES = 8


class Sched:
    def __init__(self, nc):
        self.nc = nc
        self.ops = []
        self.last_writer = {}
        self.readers = {}
        self.cur_barrier = 0

    def add(self, eng, fn, r=(), w=(), dma=False):
        deps = set()
        for k in r:
            if k in self.last_writer:
                deps.add(self.last_writer[k])
        for k in w:
            if k in self.last_writer:
                deps.add(self.last_writer[k])
            deps.update(self.readers.get(k, ()))
        idx = len(self.ops)
        self.ops.append(dict(eng=eng, fn=fn, deps=sorted(deps), dma=dma, barrier=self.cur_barrier))
        for k in r:
            self.readers.setdefault(k, []).append(idx)
        for k in w:
            self.last_writer[k] = idx
            self.readers[k] = []
        return idx

    def barrier(self):
        self.cur_barrier = len(self.ops)

    def emit(self, final_wait_engine="sp"):
        nc = self.nc
        engs = ["pe", "act", "dve", "pool", "sp"]
        ops = self.ops
        cnt = {e: 0 for e in engs}
        dcnt = {e: 0 for e in engs}
        need = set()
        for op in ops:
            e = op["eng"]
            if op["dma"]:
                k = dcnt[e] % N_DMA_SEMS
                n = dcnt[e] // N_DMA_SEMS
                dcnt[e] += 1
                op["ticket"] = (("d", e, k), 16 * (n + 1))
                op["prev"] = (("d", e, k), 16 * n) if n > 0 else None
            else:
                ep = cnt[e] // EPOCH
                v = cnt[e] % EPOCH + 1
                cnt[e] += 1
                op["ticket"] = (("c", e, ep), v)
            need.add(op["ticket"][0])
        bounds = sorted(set(op["barrier"] for op in ops))
        btk = {}
        run = {}
        bi = 0
        for i, op in enumerate(ops):
            while bi < len(bounds) and bounds[bi] <= i:
                btk[bounds[bi]] = dict(run)
                bi += 1
            sn, v = op["ticket"]
            run[sn] = max(run.get(sn, 0), v)
        while bi < len(bounds):
            btk[bounds[bi]] = dict(run)
            bi += 1
        with contextlib.ExitStack() as st:
            sems = {}
            for sn in sorted(need):
                sems[sn] = st.enter_context(nc.semaphore("s_" + "_".join(map(str, sn))))
            block = st.enter_context(nc.Block())

            def make(e):
                def body(engine):
                    known = {}
                    seen_barrier = [0]

                    def wait(t):
                        sn, v = t
                        if sn[0] == "c":
                            for sn2 in known:
                                if sn2[0] == "c" and sn2[1] == sn[1] and sn2[2] > sn[2]:
                                    return
                        if known.get(sn, 0) >= v:
                            return
                        engine.wait_ge(sems[sn], v)
                        known[sn] = v

                    for op in ops:
                        if op["eng"] != e:
                            continue
                        if op["barrier"] > seen_barrier[0]:
                            seen_barrier[0] = op["barrier"]
                            for sn, v in sorted(btk[op["barrier"]].items()):
                                if sn == ("c", e, sn[2]) and e == "pe":
                                    continue
                                wait((sn, v))
                        for d in op["deps"]:
                            dop = ops[d]
                            if dop["eng"] == e and e == "pe" and not dop["dma"]:
                                continue
                            wait(dop["ticket"])
                        if op["dma"] and op["prev"] is not None:
                            wait(op["prev"])
                        ins = op["fn"](engine)
                        sn, v = op["ticket"]
                        ins.then_inc(sems[sn], 16 if op["dma"] else 1)
                    if e == final_wait_engine:
                        fin = {}
                        for op in ops:
                            sn, v = op["ticket"]
                            fin[sn] = max(fin.get(sn, 0), v)
                        for sn, v in sorted(fin.items()):
                            wait((sn, v))
                return body

            block.tensor(make("pe"))
            block.scalar(make("act"))
            block.vector(make("dve"))
            block.gpsimd(make("pool"))
            block.sync(make("sp"))


def build(stages=("p1", "p2", "p3", "samp"), debug=False, cache_rows=2560 * 128 * 4):
    nc = bass.Bass("TRN2", target_bir_lowering=False)

    def din(name, shape, dt=F32):
        return nc.dram_tensor(name, shape, dt, kind="ExternalInput").ap()

    def dout(name, shape, dt=F32):
        return nc.dram_tensor(name, shape, dt, kind="ExternalOutput").ap()

    xp = din("xp", [SEQ, DM])
    xs = din("xs", [NSB, DM])
    cache = din("cache", [cache_rows, 128])
    cwin = din("cwin", [NSB, 512, 256])
    sconv = din("sconv", [NSB * 30, 512])
    ptab = din("ptab", [1, NSB * 16], I32)
    g_attn = din("g_attn", [1, DM])
    w_in = din("w_in", [DM, INC])
    wdall = din("wdall", [34, 512])
    w_ck = din("w_ck", [32, 2])
    w_cv = din("w_cv", [32, 2])
    w_out = din("w_out", [DM, DM])
    g_mlp = din("g_mlp", [1, DM])
    w_up = din("w_up", [DM, 4096])
    w_down = din("w_down", [4096, DM])
    g_fin = din("g_fin", [1, DM])

    y_p = dout("y_p", [SEQ, DM])
    y_s = dout("y_s", [NSB, DM])
    kv_p = dout("kv_p", [SEQ, 512])
    win_p = dout("win_p", [512, 256])
    conv_p = dout("conv_p", [30, 512])
    kv_s = dout("kv_s", [NSB, 512])
    win_s = dout("win_s", [NSB, 512, 256])
    conv_s = dout("conv_s", [NSB * 30, 512])
    dbg = {}

    S = Sched(nc)
    A = S.add
    ES = contextlib.ExitStack()

    def sb(name, shape, dt, stack=None, side=None):
        return (stack or ES).enter_context(nc.sbuf_tensor(name, shape, dt, side=side))

    def pst(name, shape, dt, stack):
        return stack.enter_context(nc.psum_tensor(name, shape, dt))

    with ES:
        identf = sb("identf", [128, 128], F32)
        ident = sb("ident", [128, 128], BF16)
        A("pool", lambda e: e.memset(identf[:], 0.0), w=["identf"])
        A("pool", lambda e: e.affine_select(out=identf[:], in_=identf[:], pattern=[[-1, 128]],
                                            compare_op=ALU.not_equal, fill=1.0, base=0, channel_multiplier=1),
          r=["identf"], w=["identf"])
        A("dve", lambda e: e.tensor_copy(out=ident[:], in_=identf[:]), r=["identf"], w=["ident"])
        gbc_mlp = sb("gbc_mlp", [128, DM], F32)
        A("sp", lambda e: e.dma_start(out=gbc_mlp[:], in_=g_mlp.partition_broadcast(128)), w=["gbc_mlp"], dma=True)
        hs_acc = sb("hs_acc", [NSB, DM], F32)
        hss = sb("hss", [128, NT], F32)
        A("pool", lambda e: e.memset(hss[:], 0.0), w=["hss"])
        hnTs = sb("hnTs", [128, 8, NSB], BF16)
        w_out_bf = sb("w_out_bf", [128, 8, DM], BF16)
        cw = sb("cw", [128, 4, 34], F32)
        wckb = sb("wckb", [128, 2, 32], F32)
        wcvb = sb("wcvb", [128, 2, 32], F32)
        PmB = [sb(f"PmB{h}", [128, 124], BF16) for h in range(2)]

        RS1 = contextlib.ExitStack()
        w_in_bf = sb("w_in_bf", [128, 8, INC], BF16, RS1, side="right")

        with contextlib.ExitStack() as W0:
            stg = [sb(f"stg{i}", [128, INC], F32, W0) for i in range(2)]
            ci = 0
            for (wsrc, wdst, wkey, ncol) in ((w_in, w_in_bf, "w_in_bf", INC), (w_out, w_out_bf, "w_out_bf", DM)):
                for k in range(8):
                    b = ci % 2
                    A("sp", lambda e, k=k, b=b, wsrc=wsrc, ncol=ncol: e.dma_start(out=stg[b][:, 0:ncol], in_=wsrc[k * 128:(k + 1) * 128, :]),
                      w=[("stg", b)], dma=True)
                    if ci % 2 == 0:
                        A("act", lambda e, k=k, b=b, wdst=wdst, ncol=ncol: e.copy(out=wdst[:, k, :], in_=stg[b][:, 0:ncol]),
                          r=[("stg", b)], w=[(wkey, k)])
                    else:
                        A("dve", lambda e, k=k, b=b, wdst=wdst, ncol=ncol: e.tensor_copy(out=wdst[:, k, :], in_=stg[b][:, 0:ncol]),
                          r=[("stg", b)], w=[(wkey, k)])
                    ci += 1
            wd_sb = sb("wd_sb", [34, 512], F32, W0)
            A("sp", lambda e: e.dma_start(out=wd_sb[:], in_=wdall), w=["wd_sb"], dma=True)
            with contextlib.ExitStack() as PW:
                pcw = pst("pcw", [128, 4, 34], F32, PW)
                for c4 in range(4):
                    A("pe", lambda e, c4=c4: e.transpose(out=pcw[:, c4, :], in_=wd_sb[:, c4 * 128:(c4 + 1) * 128],
                                                         identity=identf[0:34, 0:34]),
                      r=["wd_sb", "identf"], w=[("pcw", c4)])
                A("dve", lambda e: e.tensor_copy(out=cw[:], in_=pcw[:]), r=[("pcw", c4) for c4 in range(4)], w=["cw"])
                S.barrier()
            wraw = sb("wraw", [128, 2, 64], F32, W0)
            A("sp", lambda e: e.dma_start(out=wraw[:, 0, :], in_=w_ck.rearrange("j h -> (j h)").partition_broadcast(128)), w=["wraw"], dma=True)
            A("sp", lambda e: e.dma_start(out=wraw[:, 1, :], in_=w_cv.rearrange("j h -> (j h)").partition_broadcast(128)), w=["wraw"], dma=True)
            A("dve", lambda e: e.tensor_copy(out=wckb[:], in_=wraw[:, 0, :].rearrange("p (j h) -> p h j", h=2)), r=["wraw"], w=["wckb"])
            A("dve", lambda e: e.tensor_copy(out=wcvb[:], in_=wraw[:, 1, :].rearrange("p (j h) -> p h j", h=2)), r=["wraw"], w=["wcvb"])
            wcol = sb("wcol", [128, 2], F32, W0)
            for rr in range(4):
                A("sp", lambda e, rr=rr: e.dma_start(out=wcol[rr * 32:(rr + 1) * 32, :], in_=w_cv), w=["wcol"], dma=True)
            pmf = sb("pmf", [128, 4], F32, W0)
            A("pool", lambda e: e.memset(pmf[:], 1.0), w=["pmf"])
            A("pool", lambda e: e.affine_select(out=pmf[:], in_=pmf[:], pattern=[[-32, 4]], compare_op=ALU.is_ge,
                                                fill=0.0, base=0, channel_multiplier=1), r=["pmf"], w=["pmf"])
            A("pool", lambda e: e.affine_select(out=pmf[:], in_=pmf[:], pattern=[[32, 4]], compare_op=ALU.is_ge,
                                                fill=0.0, base=31, channel_multiplier=-1), r=["pmf"], w=["pmf"])
            for h in range(2):
                A("pool", lambda e, h=h: e.memset(PmB[h][:], 0.0), w=[("PmB", h)])
                A("dve", lambda e, h=h: e.tensor_scalar(out=PmB[h][:, 60:64], in0=pmf[:], scalar1=wcol[:, h:h + 1],
                                                        scalar2=None, op0=ALU.mult),
                  r=["pmf", "wcol", ("PmB", h)], w=[("PmB", h)])
            S.barrier()

        if "samp" in stages:
            with contextlib.ExitStack() as SP:
                build_samp(nc, S, SP, sb, pst, locals())
                S.barrier()

        ATT = contextlib.ExitStack()
        with ATT:
            QTs = [sb(f"QTs{h}", [96, 4, SEQ], BF16, ATT) for h in range(2)]
            KTs = [sb(f"KTs{h}", [96, SEQ], BF16, ATT) for h in range(2)]
            KTw = [sb(f"KTw{h}", [64, SEQ], BF16, ATT) for h in range(2)]
            kcT = [sb(f"kcT{h}", [64, 64], BF16, ATT) for h in range(2)]
            Vaug = sb("Vaug", [128, NT, 3, 2, 65], BF16, ATT)
            vcaug = sb("vcaug", [64, 2, 65], BF16, ATT)
            gates = sb("gates", [128, NT, 24], F32, ATT)
            convyT = sb("convyT", [128, 4, SEQ], BF16, ATT)
            for h in range(2):
                A("pool", lambda e, h=h: e.memset(KTs[h][64:96, :], 1.0), w=[("KTsaug", h)])
                A("pool", lambda e, h=h: e.affine_select(out=KTs[h][64:96, :], in_=KTs[h][64:96, :], pattern=[[1, SEQ]],
                                                         compare_op=ALU.is_ge, fill=0.0, base=0, channel_multiplier=-64),
                  r=[("KTsaug", h)], w=[("KTsaug", h)])
                A("pool", lambda e, h=h: e.affine_select(out=KTs[h][64:96, :], in_=KTs[h][64:96, :], pattern=[[-1, SEQ]],
                                                         compare_op=ALU.is_ge, fill=0.0, base=63, channel_multiplier=64),
                  r=[("KTsaug", h)], w=[("KTsaug", h)])
            A("pool", lambda e: e.memset(Vaug[:], 1.0), w=["Vaug_init"])
            A("pool", lambda e: e.memset(vcaug[:], 1.0), w=["vcaug_init"])

            with contextlib.ExitStack() as P1:
                build_p1(nc, S, P1, sb, pst, locals())
                S.barrier()
            RS1.close()
            RS2 = contextlib.ExitStack()
            h_acc = sb("h_acc", [128, NT, DM], F32, RS2, side="right")
            if "p2" in stages:
                with contextlib.ExitStack() as P2:
                    build_p2(nc, S, P2, sb, pst, locals())
                    S.barrier()
        if "p3" in stages:
            with contextlib.ExitStack() as P3:
                build_p3(nc, S, P3, sb, pst, locals())
        RS2.close()
        S.emit()
    return nc, dbg


def build_p1(nc, S, P1, sb, pst, L):
    A = S.add
    (QTs, KTs, KTw, kcT, Vaug, vcaug, gates, convyT, cw, wckb, PmB, w_in_bf, ident, identf, xp, g_attn, kv_p, win_p,
     conv_p) = (L[k] for k in ("QTs", "KTs", "KTw", "kcT", "Vaug", "vcaug", "gates", "convyT", "cw", "wckb", "PmB",
                               "w_in_bf", "ident", "identf", "xp", "g_attn", "kv_p", "win_p", "conv_p"))
    debug, dbg, dout = L["debug"], L["dbg"], L["dout"]
    onesf = sb("onesf", [128, 128], F32, P1)
    A("pool", lambda e: e.memset(onesf[:], 1.0 / 512.0), w=["onesf"])
    gbc_attn = sb("gbc_attn", [128, DM], F32, P1)
    A("sp", lambda e: e.dma_start(out=gbc_attn[:], in_=g_attn.partition_broadcast(128)), w=["gbc_attn"], dma=True)
    xt = [sb(f"xt{i}", [128, DM], F32, P1) for i in range(2)]
    ssall = sb("ssall", [128, NT], F32, P1)
    xn = [sb("xn0", [128, DM], BF16, P1)] * 2
    A("pool", lambda e: e.memset(ssall[:], 0.0), w=["ssall"])
    for t in range(NT):
        A("sp", lambda e, t=t: e.dma_start(out=xt[t % 2][:], in_=xp[t * 128:(t + 1) * 128, :]), w=[("xt", t % 2)], dma=True)
        A("act", lambda e, t=t: e.activation(out=xn[t % 2][:], in_=xt[t % 2][:], func=AF.Square, accum_out=ssall[:, t:t + 1]),
          r=[("xt", t % 2), "ssall"], w=["xn", "ssall"])
    A("dve", lambda e: e.tensor_scalar(out=ssall[:], in0=ssall[:], scalar1=1.0 / DM, scalar2=1e-6, op0=ALU.mult, op1=ALU.add),
      r=["ssall"], w=["ssall"])
    A("act", lambda e: e.activation(out=ssall[:], in_=ssall[:], func=AF.Sqrt), r=["ssall"], w=["ssall"])
    A("dve", lambda e: e.reciprocal(out=ssall[:], in_=ssall[:]), r=["ssall"], w=["ssall"])
    xnT = [sb("xnT0", [128, 8, 512], BF16, P1)] * 2
    glu = [sb(f"glu{i}", [128, 4, 542], BF16, P1) for i in range(2)]
    glutail = sb("glutail", [128, 4, 30], F32, P1)
    glo = [sb(f"glo{i}", [128, 4, 542], BF16, P1) for i in range(2)]
    Dg = [sb(f"Dg{i}", [128, 16, 128], BF16, P1) for i in range(2)]
    sgt = sb("sgt", [128, 512], F32, P1)
    ych = sb("ych", [128, 4, 512], F32, P1)
    mean_sb = sb("mean_sb", [128, 512], F32, P1)
    rstd_sb = sb("rstd_sb", [128, 512], F32, P1)
    ysq = rstd_sb
    kcp = mean_sb[0:64, :].rearrange("p (c j) -> p c j", j=32)
    zt = sb("zt", [128, 792], F32, P1)
    kcf = sb("kcf", [64, 16], F32, P1)
    psF = [pst(f"psF{i}", [128, 512], F32, P1) for i in range(2)]
    psTM = pst("psTM", [128, 1024], F32, P1)
    psT = pst("psT", [128, 8, 128], BF16, P1)
    psVC = pst("psVC", [128, 512], F32, P1)
    psMean = pst("psMean", [128, 512], F32, P1)
    psMsq = pst("psMsq", [128, 512], F32, P1)

    A("pool", lambda e: e.memset(glu[0][:, :, 0:30], 0.0), w=[("gluhead", 0)])
    A("pool", lambda e: e.memset(glo[0][:, :, 0:30], 0.0), w=[("gluhead", 0)])
    fcnt = [0]

    def fm_mm(xT, c0, M, Gk):
        b = fcnt[0] % 2
        fcnt[0] += 1
        for k in range(8):
            A("pe", lambda e, k=k, b=b: e.matmul(psF[b][0:M, :], lhsT=w_in_bf[:, k, c0:c0 + M], rhs=xT[:, k, :],
                                                 start=(k == 0), stop=(k == 7)),
              r=[("w_in_bf", k), Gk], w=[("psF", b)])
        return b

    def p1_front(G):
        xTg = xnT[G % 2]
        Gk = "xnT"
        gl = glu[G % 2]
        gln = glu[(G + 1) % 2]
        tok = slice(G * 512, (G + 1) * 512)
        for tt in range(4):
            t = 4 * G + tt
            xb = t % 2
            A("sp", lambda e, t=t, xb=xb: e.dma_start(out=xt[xb][:], in_=xp[t * 128:(t + 1) * 128, :]), w=[("xt", xb)], dma=True)
            A("dve", lambda e, t=t, xb=xb: e.scalar_tensor_tensor(out=xn[xb][:], in0=xt[xb][:], scalar=ssall[:, t:t + 1],
                                                                  in1=gbc_attn[:], op0=ALU.mult, op1=ALU.mult),
              r=[("xt", xb), "ssall", "gbc_attn"], w=["xn"])
            for k in range(8):
                A("pe", lambda e, k=k, xb=xb: e.transpose(out=psT[:, k, :], in_=xn[xb][:, k * 128:(k + 1) * 128], identity=ident[:]),
                  r=["xn", "ident"], w=["psT"])
            A("act", lambda e, tt=tt, xTg=xTg: e.copy(out=xTg[:, :, tt * 128:(tt + 1) * 128], in_=psT[:]),
              r=["psT"], w=[Gk])
            yield
        for c4 in range(4):
            ba = fm_mm(xTg, c4 * 128, 128, Gk)
            bb = fm_mm(xTg, 512 + c4 * 128, 128, Gk)
            A("act", lambda e, bb=bb: e.activation(out=sgt[:], in_=psF[bb][:], func=AF.Sigmoid), r=[("psF", bb)], w=["sgt"])
            A("dve", lambda e, ba=ba, c4=c4, gl=gl: e.tensor_tensor(out=gl[:, c4, 30:542], in0=psF[ba][:], in1=sgt[:], op=ALU.mult),
              r=[("psF", ba), "sgt"], w=[("glu", G % 2, c4)])
            A("dve", lambda e, ba=ba, c4=c4, G=G: e.tensor_tensor(out=glo[G % 2][:, c4, 29:541], in0=psF[ba][:], in1=sgt[:], op=ALU.mult),
              r=[("psF", ba), "sgt", ("gluhead", G % 2)], w=[("glu", G % 2, c4)])
            if G == 3:
                A("dve", lambda e, ba=ba, c4=c4: e.tensor_tensor(out=glutail[:, c4, :], in0=psF[ba][:, 482:512], in1=sgt[:, 482:512],
                                                                 op=ALU.mult), r=[("psF", ba), "sgt"], w=["glutail"])
            yield
        for hd in range(8):
            h, g = hd // 4, hd % 4
            b = fm_mm(xTg, 1024 + hd * 64, 64, Gk)
            A("act", lambda e, b=b, h=h, g=g, tok=tok: e.copy(out=QTs[h][0:64, g, tok], in_=psF[b][0:64, :]),
              r=[("psF", b)], w=[("QT", h, G)])
            if hd % 4 == 3:
                yield
        for h in range(2):
            b = fm_mm(xTg, 1536 + h * 64, 64, Gk)
            A("act", lambda e, b=b: e.copy(out=mean_sb[0:64, :], in_=psF[b][0:64, :]), r=[("psF", b)], w=["mean_sb"])
            A("pool", lambda e, h=h: e.tensor_tensor(
                out=kcp, in0=kcp, in1=wckb[0:64, h, :].unsqueeze(1).to_broadcast([64, 16, 32]), op=ALU.mult),
              r=["mean_sb", "wckb"], w=["mean_sb"])
            A("dve", lambda e: e.tensor_reduce(out=kcf[:], in_=kcp, axis=AX.X, op=ALU.add), r=["mean_sb"], w=["kcf"])
            A("pool", lambda e, h=h, G=G: e.tensor_copy(out=kcT[h][:, G * 16:(G + 1) * 16], in_=kcf[:]),
              r=["kcf"], w=[("kcT", h, G)])
            b = fm_mm(xTg, 1536 + 256 + h * 64, 64, Gk)
            A("act", lambda e, b=b, h=h, tok=tok: e.copy(out=KTs[h][0:64, tok], in_=psF[b][0:64, :]),
              r=[("psF", b)], w=[("KTs", h, G)])
            b = fm_mm(xTg, 1536 + 512 + h * 64, 64, Gk)
            A("act", lambda e, b=b, h=h, tok=tok: e.copy(out=KTw[h][:, tok], in_=psF[b][0:64, :]),
              r=[("psF", b)], w=[("KTw", h, G)])
            yield
        for tt in range(4):
            t = 4 * G + tt
            for (c0, n, o0) in ((1536, 512, 0), (2048, 280, 512)):
                for k in range(8):
                    A("pe", lambda e, k=k, tt=tt, c0=c0, n=n, o0=o0, xTg=xTg: e.matmul(
                        psTM[:, o0:o0 + n], lhsT=xTg[:, k, tt * 128:(tt + 1) * 128], rhs=w_in_bf[:, k, c0:c0 + n],
                        start=(k == 0), stop=(k == 7)),
                      r=[("w_in_bf", k), Gk], w=["psTM"])
            A("act", lambda e: e.copy(out=zt[:], in_=psTM[:, 0:792]), r=["psTM"], w=["zt"])
            A("sp", lambda e, t=t: e.dma_start(out=kv_p[t * 128:(t + 1) * 128, :], in_=zt[:, 0:512]), r=["zt"], dma=True)
            if t >= 12:
                A("sp", lambda e, t=t: e.dma_start(out=win_p[(t - 12) * 128:(t - 11) * 128, :], in_=zt[:, 512:768]),
                  r=["zt"], dma=True)
            A("act", lambda e, t=t: e.activation(out=gates[:, t, :], in_=zt[:, 768:792], func=AF.Sigmoid),
              r=["zt"], w=[("gates", t)])
            for s3 in range(3):
                A("pool", lambda e, t=t, s3=s3: e.tensor_copy(
                    out=Vaug[:, t, s3, :, 0:64],
                    in_=zt[:, 128 + 256 * s3:256 + 256 * s3].rearrange("p (h d) -> p h d", d=64)),
                  r=["zt", "Vaug_init"], w=[("Vaug", t)])
            for h in range(2):
                A("pe", lambda e, t=t, h=h: e.matmul(psVC[0:64, h * 64:(h + 1) * 64], lhsT=PmB[h][:, 60 - 4 * t:124 - 4 * t],
                                                     rhs=Vaug[:, t, 0, h, 0:64], start=(t == 0 and h == 0), stop=(t == NT - 1),
                                                     skip_group_check=True),
                  r=[("PmB", h), ("Vaug", t)], w=["psVC"])
            yield

    def p1_conv(G):
        xTg = xnT[G % 2]
        Gk = "xnT"
        gl = glu[G % 2]
        gln = glu[(G + 1) % 2]
        tok = slice(G * 512, (G + 1) * 512)
        for c4 in range(4):
            rk = [("glu", G % 2, c4), ("gluhead", G % 2)]
            for di, (j0, nj) in enumerate(((0, 16), (16, 15))):
                A("pool", lambda e, c4=c4, di=di, j0=j0, nj=nj: e.tensor_tensor(
                    out=Dg[di][:, 0:nj, :], in0=ident[:].unsqueeze(1).to_broadcast([128, nj, 128]),
                    in1=cw[:, c4, j0:j0 + nj].unsqueeze(2).to_broadcast([128, nj, 128]), op=ALU.mult), r=["ident", "cw"], w=[("Dg", di)])
            for j in range(31):
                di, jj = (0, j) if j < 16 else (1, j - 16)
                src = gl[:, c4, j:j + 512] if j % 2 == 0 else glo[G % 2][:, c4, j - 1:j - 1 + 512]
                A("pe", lambda e, j=j, src=src, di=di, jj=jj: e.matmul(psMsq[:], lhsT=Dg[di][:, jj, :], rhs=src,
                                                                       start=(j == 0), stop=(j == 30)),
                  r=rk + [("Dg", di)], w=["psMsq"])
            A("act", lambda e, c4=c4: e.activation(out=ych[:, c4, :], in_=psMsq[:], func=AF.Identity, bias=cw[:, c4, 31:32]),
              r=["psMsq", "cw"], w=[("ych", c4)])
            yield

    def p1_ln(G):
        xTg = xnT[G % 2]
        Gk = "xnT"
        gl = glu[G % 2]
        gln = glu[(G + 1) % 2]
        tok = slice(G * 512, (G + 1) * 512)
        for c4 in range(4):
            A("act", lambda e, c4=c4: e.activation(out=ysq[:], in_=ych[:, c4, :], func=AF.Square),
              r=[("ych", c4)], w=["rstd_sb"])
            A("pe", lambda e, c4=c4: e.matmul(psMean[:], lhsT=onesf[:], rhs=ych[:, c4, :], start=(c4 == 0), stop=(c4 == 3)),
              r=["onesf", ("ych", c4)], w=["psMean"])
            A("pe", lambda e, c4=c4: e.matmul(psMsq[:], lhsT=onesf[:], rhs=ysq[:], start=(c4 == 0), stop=(c4 == 3)),
              r=["onesf", "rstd_sb"], w=["psMsq"])
        A("pool", lambda e, gl=gl, gln=gln: e.tensor_copy(out=gln[:, :, 0:30], in_=gl[:, :, 512:542]),
          r=[("glu", G % 2, c4) for c4 in range(4)], w=[("gluhead", (G + 1) % 2)])
        A("pool", lambda e, gl=gl, G=G: e.tensor_copy(out=glo[(G + 1) % 2][:, :, 0:29], in_=gl[:, :, 513:542]),
          r=[("glu", G % 2, c4) for c4 in range(4)], w=[("gluhead", (G + 1) % 2)])
        A("act", lambda e: e.copy(out=mean_sb[:], in_=psMean[:]), r=["psMean"], w=["mean_sb"])
        A("pool", lambda e: e.tensor_tensor(out=rstd_sb[:], in0=mean_sb[:], in1=mean_sb[:], op=ALU.mult),
          r=["mean_sb"], w=["rstd_sb"])
        A("dve", lambda e: e.tensor_tensor(out=rstd_sb[:], in0=psMsq[:], in1=rstd_sb[:], op=ALU.subtract),
          r=["psMsq", "rstd_sb"], w=["rstd_sb"])
        A("dve", lambda e: e.tensor_scalar(out=rstd_sb[:], in0=rstd_sb[:], scalar1=1e-5, scalar2=None,
                                           op0=ALU.add), r=["rstd_sb"], w=["rstd_sb"])
        A("act", lambda e: e.activation(out=rstd_sb[:], in_=rstd_sb[:], func=AF.Sqrt), r=["rstd_sb"], w=["rstd_sb"])
        A("dve", lambda e: e.reciprocal(out=rstd_sb[:], in_=rstd_sb[:]), r=["rstd_sb"], w=["rstd_sb"])
        for c4 in range(4):
            eng = "dve" if c4 % 2 == 0 else "pool"
            A(eng, lambda e, c4=c4: e.tensor_tensor(out=ych[:, c4, :], in0=ych[:, c4, :], in1=mean_sb[:], op=ALU.subtract),
              r=[("ych", c4), "mean_sb"], w=[("ych", c4)])
            A(eng, lambda e, c4=c4: e.tensor_tensor(out=ych[:, c4, :], in0=ych[:, c4, :], in1=rstd_sb[:], op=ALU.mult),
              r=[("ych", c4), "rstd_sb"], w=[("ych", c4)])
            A("act", lambda e, c4=c4, tok=tok: e.activation(out=convyT[:, c4, tok], in_=ych[:, c4, :], func=AF.Silu,
                                                            bias=cw[:, c4, 33:34], scale=cw[:, c4, 32:33]),
              r=[("ych", c4), "cw"], w=[("convyT", G)])
            yield
    def run_interleaved(ga, gb):
        sentinel = object()
        for _ in range(4):
            if next(ga, sentinel) is sentinel:
                break
        a_alive, b_alive = True, True
        while a_alive or b_alive:
            for _ in range(2):
                if a_alive and next(ga, sentinel) is sentinel:
                    a_alive = False
            if b_alive and next(gb, sentinel) is sentinel:
                b_alive = False

    for _ in p1_front(0):
        pass
    for G in range(4):
        run_interleaved(p1_front(G + 1) if G + 1 < 4 else iter(()), p1_conv(G))
        for _ in p1_ln(G):
            pass
    A("act", lambda e: e.copy(out=vcaug[:, :, 0:64], in_=psVC[0:64, 0:128].rearrange("p (h d) -> p h d", d=64)),
      r=["psVC", "vcaug_init"], w=["vcaug"])
    for c4 in range(4):
        A("pe", lambda e, c4=c4: e.transpose(out=psF[0][0:30, c4 * 128:(c4 + 1) * 128], in_=glutail[:, c4, :], identity=identf[:]),
          r=["glutail", "identf"], w=[("psF", 0)])
    cps = ych[0:30, 0, :]
    A("act", lambda e: e.copy(out=cps, in_=psF[0][0:30, :]), r=[("psF", 0), ("ych", 0)], w=[("ych", 0)])
    A("sp", lambda e: e.dma_start(out=conv_p, in_=cps), r=[("ych", 0)], dma=True)
    if debug:
        dbg["QT0"] = dout("d_QT0", [96, 4 * SEQ], BF16)
        A("sp", lambda e: e.dma_start(out=dbg["QT0"], in_=QTs[0][:]), r=[("QT", 0, G) for G in range(4)], dma=True)
        dbg["convyT"] = dout("d_convyT", [128, 4 * SEQ], BF16)
        A("sp", lambda e: e.dma_start(out=dbg["convyT"], in_=convyT[:]), r=[("convyT", G) for G in range(4)], dma=True)
        dbg["kcT0"] = dout("d_kcT0", [64, 64], BF16)
        A("sp", lambda e: e.dma_start(out=dbg["kcT0"], in_=kcT[0][:]), r=[("kcT", 0, G) for G in range(4)], dma=True)
        dbg["vcaug"] = dout("d_vcaug", [64, 130], BF16)
        A("sp", lambda e: e.dma_start(out=dbg["vcaug"], in_=vcaug[:]), r=["vcaug"], dma=True)


def build_p2(nc, S, P2, sb, pst, L):
    A = S.add
    QTs, KTs, KTw, kcT, Vaug, vcaug, gates, convyT = (L[k] for k in
                                                       ("QTs", "KTs", "KTw", "kcT", "Vaug", "vcaug", "gates", "convyT"))
    ident, identf, w_out_bf, h_acc, gbc_mlp, xp = (L[k] for k in
                                                   ("ident", "identf", "w_out_bf", "h_acc", "gbc_mlp", "xp"))
    debug, dbg, dout = L["debug"], L["dbg"], L["dout"]
    sbias = sb("sbias", [128, NT, 32], F32, P2)
    A("pool", lambda e: e.memset(sbias[:], 0.0), w=["sbias"])
    for qt in range(NT):
        for half in range(2):
            qb = 2 * qt + half
            ps_ = slice(64 * half, 64 * half + 64)
            if qb + 1 < 32:
                A("pool", lambda e, qt=qt, ps_=ps_, qb=qb: e.memset(sbias[ps_, qt, qb + 1:32], -1e30), r=["sbias"], w=["sbias"])
            A("pool", lambda e, qt=qt, ps_=ps_: e.memset(sbias[ps_, qt, 0:1], 5.0), r=["sbias"], w=["sbias"])
            A("pool", lambda e, qt=qt, ps_=ps_, qb=qb: e.memset(sbias[ps_, qt, max(qb - 1, 0):qb + 1], 5.0),
              r=["sbias"], w=["sbias"])
    pS = [pst(f"pS{i}", [128, 512], F32, P2) for i in range(2)]
    pOb = [pst(f"pO{i}", [128, 512], F32, P2) for i in range(3)]
    pO = [p[:, 0:260].rearrange("p (g d) -> p g d", d=65) for p in pOb]
    pM = pst("pM", [128, 512], F32, P2)
    pH = pst("pH", [128, 512], F32, P2)
    pTb = pst("pTb", [128, 8, 128], BF16, P2)
    PT = [sb(f"PT{i}", [128, 512], BF16, P2) for i in range(3)]
    NSEL = 3
    Ecmp = [sb(f"Ecmp{i}", [128, 8, 64], F32, P2) for i in range(NSEL)]
    zc = [sb(f"zc{i}", [128, 16], F32, P2) for i in range(NSEL)]
    pblk = [sb(f"pblk{i}", [128, 2, 32], F32, P2) for i in range(NSEL)]
    pg4 = [sb(f"pg4{i}", [128, 2, 64], F32, P2) for i in range(NSEL)]
    m8 = [sb(f"m8{i}", [128, 2, 8], F32, P2) for i in range(NSEL)]
    wk32 = [sb(f"wk32{i}", [128, 2, 32], F32, P2) for i in range(NSEL)]
    selT_in = [sb(f"selT_in{i}", [128, 2, 96], BF16, P2) for i in range(NSEL)]
    for i in range(NSEL):
        A("pool", lambda e, i=i: e.memset(selT_in[i][:], 0.0), w=[("selT_in", i)])
    coef = sb("coef", [128, 3, 4], F32, P2)
    zr = sb("zr", [128, 3, 4], F32, P2)
    osb = sb("osb", [128, 4, 64], F32, P2)
    otmp = sb("otmp", [128, 4, 64], F32, P2)
    attn = sb("attn", [128, 512], BF16, P2)
    attnT = sb("attnT", [128, 4, 128], BF16, P2)
    xt2 = [sb("xt2_0", [128, DM], F32, P2)] * 2
    pcnt = [0]
    scnt = [0]

    def score_exp(h, qt, lhsT, K, rhs_rows, masks, rkeys):
        sbuf_i = scnt[0] % 2
        scnt[0] += 1
        pb = pcnt[0] % 3
        pcnt[0] += 1
        M = lhsT.shape[1]
        A("pe", lambda e: e.matmul(pS[sbuf_i][0:M, :], lhsT=lhsT, rhs=QTs[h][0:K, :, qt * 128:(qt + 1) * 128],
                                   start=True, stop=True),
          r=rkeys + [("QT", h, qt // 4)] + ([("QTaug", h, qt)] if K == 96 else []), w=[("pS", sbuf_i)])
        A("act", lambda e: e.activation(out=PT[pb][0:M, :], in_=pS[sbuf_i][0:M, :], func=AF.Exp, scale=SCALE),
          r=[("pS", sbuf_i)], w=[("PT", pb)])
        for (cm, qs, base) in masks:
            A("pool", lambda e, cm=cm, qs=qs, base=base: e.affine_select(
                out=PT[pb][0:M, :].rearrange("p (g q) -> p g q", g=4), in_=PT[pb][0:M, :].rearrange("p (g q) -> p g q", g=4),
                pattern=[[0, 4], [qs, 128]], compare_op=ALU.is_ge, fill=0.0, base=base, channel_multiplier=cm),
              r=[("PT", pb)], w=[("PT", pb)])
        return pb

    def emit_sel(qt):
        r = qt % NSEL
        pm = pH
        pmk = "pH"
        for hd in range(8):
            h, g = hd // 4, hd % 4
            A("pe", lambda e, h=h, g=g, hd=hd, qt=qt, pm=pm: e.matmul(pm[:, hd * 64:(hd + 1) * 64],
                                                                      lhsT=QTs[h][0:64, g, qt * 128:(qt + 1) * 128], rhs=kcT[h][:],
                                                                      start=True, stop=True),
              r=[("QT", h, qt // 4)] + [("kcT", h, G) for G in range(4)], w=[pmk])
        E = Ecmp[r]
        A("act", lambda e, E=E, pm=pm: e.activation(out=E[:], in_=pm[:].rearrange("p (g c) -> p g c", g=8), func=AF.Exp, scale=SCALE),
          r=[pmk], w=[("Ecmp", r)])
        A("pool", lambda e, qt=qt, E=E: e.affine_select(out=E[:], in_=E[:], pattern=[[0, 8], [-32, 64]], compare_op=ALU.is_ge,
                                                        fill=0.0, base=qt * 128 - 31, channel_multiplier=1),
          r=[("Ecmp", r)], w=[("Ecmp", r)])
        Z = zc[r]
        A("dve", lambda e, E=E, Z=Z: e.tensor_reduce(out=Z[:, 0:8], in_=E[:], axis=AX.X, op=ALU.add), r=[("Ecmp", r)], w=[("zc", r)])
        A("dve", lambda e, Z=Z: e.tensor_scalar(out=Z[:, 0:8], in0=Z[:, 0:8], scalar1=1e-30, scalar2=None, op0=ALU.add),
          r=[("zc", r)], w=[("zc", r)])
        A("dve", lambda e, Z=Z: e.reciprocal(out=Z[:, 8:16], in_=Z[:, 0:8]), r=[("zc", r)], w=[("zc", r)])
        A("dve", lambda e, E=E, Z=Z: e.tensor_tensor(out=E[:], in0=E[:], in1=Z[:, 8:16].unsqueeze(2).to_broadcast([128, 8, 64]),
                                                     op=ALU.mult), r=[("Ecmp", r), ("zc", r)], w=[("Ecmp", r)])
        for h in range(2):
            A("dve", lambda e, E=E, h=h, r=r: e.tensor_reduce(out=pg4[r][:, h, :], in_=E[:, h * 4:(h + 1) * 4, :].rearrange("p g c -> p c g"),
                                                              axis=AX.X, op=ALU.add), r=[("Ecmp", r)], w=[("pg4", r)])
        A("dve", lambda e, r=r: e.tensor_reduce(out=pblk[r][:], in_=pg4[r][:].rearrange("p h (b t) -> p h b t", t=2),
                                                axis=AX.X, op=ALU.add), r=[("pg4", r)], w=[("pblk", r)])
        A("dve", lambda e, r=r, qt=qt: e.tensor_tensor(out=pblk[r][:], in0=pblk[r][:],
                                                       in1=sbias[:, qt, :].unsqueeze(1).to_broadcast([128, 2, 32]), op=ALU.add),
          r=[("pblk", r), "sbias"], w=[("pblk", r)])
        for h in range(2):
            A("dve", lambda e, h=h, r=r: e.max(out=m8[r][:, h, :], in_=pblk[r][:, h, :]), r=[("pblk", r)], w=[("m8", r, h)])
            A("dve", lambda e, h=h, r=r: e.match_replace(out=wk32[r][:, h, :], in_to_replace=m8[r][:, h, :],
                                                         in_values=pblk[r][:, h, :], imm_value=-3e38),
              r=[("m8", r, h), ("pblk", r)], w=[("wk32", r, h)])
            A("dve", lambda e, h=h, r=r: e.max(out=m8[r][:, h, :], in_=wk32[r][:, h, :]), r=[("wk32", r, h)], w=[("m8", r, h)])
            A("dve", lambda e, h=h, r=r: e.tensor_scalar(out=wk32[r][:, h, :], in0=pblk[r][:, h, :], scalar1=m8[r][:, h, 7:8],
                                                         scalar2=-1.0, op0=ALU.is_ge, op1=ALU.add),
              r=[("pblk", r), ("m8", r, h)], w=[("wk32", r, h)])
        A("dve", lambda e, r=r: e.tensor_scalar(out=selT_in[r][:, :, 64:96], in0=wk32[r][:], scalar1=BIG, scalar2=None, op0=ALU.mult),
          r=[("wk32", r, 0), ("wk32", r, 1), ("selT_in", r)], w=[("selT_in", r)])
        for h in range(2):
            A("pe", lambda e, h=h, r=r: e.transpose(out=pTb[0:96, 4 + h, :], in_=selT_in[r][:, h, :], identity=ident[:]),
              r=[("selT_in", r), "ident"], w=[("pTbs", h)])
            A("act", lambda e, h=h, qt=qt: e.copy(out=QTs[h][64:96, :, qt * 128:(qt + 1) * 128],
                                                  in_=pTb[64:96, 4 + h, :].unsqueeze(1).to_broadcast([32, 4, 128])),
              r=[("pTbs", h)], w=[("QTaug", h, qt)])
    LOOK = 2
    pSx = [pS[0], pS[1], pM]
    Osb = sb("Osb", [128, 3, 4, 65], F32, P2)
    tiles = []
    for qt in range(NT):
        for h in range(2):
            tl = [dict(br=0, kt=0, lhsT=kcT[h][:], K=64, masks=[(-32, 1, qt * 128 - 31)],
                       rk=[("kcT", h, G) for G in range(4)], rhs=vcaug[:, h, :], rhsk="vcaug", first=True, last=True)]
            for kt in range(qt + 1):
                tl.append(dict(br=1, kt=kt, lhsT=KTs[h][:, kt * 128:(kt + 1) * 128], K=96, masks=[(-1, 1, 0)] if kt == qt else [],
                               rk=[("KTs", h, kt // 4), ("KTsaug", h)], rhs=Vaug[:, kt, 1, h, :], rhsk=("Vaug", kt),
                               first=(kt == 0), last=(kt == qt)))
            k0 = max(0, qt - 4)
            for kt in range(k0, qt + 1):
                masks = []
                if kt == qt:
                    masks.append((-1, 1, 0))
                if kt == qt - 4:
                    masks.append((1, -1, 0))
                tl.append(dict(br=2, kt=kt, lhsT=KTw[h][:, kt * 128:(kt + 1) * 128], K=64, masks=masks,
                               rk=[("KTw", h, kt // 4)], rhs=Vaug[:, kt, 2, h, :], rhsk=("Vaug", kt),
                               first=(kt == k0), last=(kt == qt)))
            for t_ in tl:
                t_["qt"], t_["h"] = qt, h
            tl[-1]["end"] = True
            tiles += tl

    def emit_qk(t_, i):
        si = i % 3
        pb = i % 3
        t_["pb"] = pb
        h, qt, K, lhsT = t_["h"], t_["qt"], t_["K"], t_["lhsT"]
        M = lhsT.shape[1]
        A("pe", lambda e: e.matmul(pSx[si][0:M, :], lhsT=lhsT, rhs=QTs[h][0:K, :, qt * 128:(qt + 1) * 128], start=True, stop=True),
          r=t_["rk"] + [("QT", h, qt // 4)] + ([("QTaug", h, qt)] if K == 96 else []), w=[("pSx", si)])
        A("act", lambda e: e.activation(out=PT[pb][0:M, :], in_=pSx[si][0:M, :], func=AF.Exp, scale=SCALE),
          r=[("pSx", si)], w=[("PT", pb)])
        for (cm, qs, base) in t_["masks"]:
            A("pool", lambda e, cm=cm, qs=qs, base=base: e.affine_select(
                out=PT[pb][0:M, :].rearrange("p (g q) -> p g q", g=4), in_=PT[pb][0:M, :].rearrange("p (g q) -> p g q", g=4),
                pattern=[[0, 4], [qs, 128]], compare_op=ALU.is_ge, fill=0.0, base=base, channel_multiplier=cm),
              r=[("PT", pb)], w=[("PT", pb)])

    def emit_pv(t_):
        pb, br = t_["pb"], t_["br"]
        M = t_["lhsT"].shape[1]
        for g in range(4):
            A("pe", lambda e, g=g: e.matmul(pO[br][:, g, :], lhsT=PT[pb][0:M, g * 128:(g + 1) * 128], rhs=t_["rhs"],
                                            start=(t_["first"] and g == 0), stop=t_["last"], skip_group_check=True),
              r=[("PT", pb), t_["rhsk"]], w=[("pO", br)])
        if t_["last"]:
            A("dve", lambda e: e.tensor_copy(out=Osb[:, br, :, :], in_=pO[br][:]), r=[("pO", br)], w=[("Osb", br)])

    def emit_end(qt, h):
        A("dve", lambda e: e.tensor_scalar(out=zr[:], in0=Osb[:, :, :, 64], scalar1=1e-30, scalar2=None, op0=ALU.add),
          r=[("Osb", br) for br in range(3)], w=["zr"])
        A("dve", lambda e: e.reciprocal(out=zr[:], in_=zr[:]), r=["zr"], w=["zr"])
        A("dve", lambda e: e.tensor_tensor(
            out=coef[:], in0=zr[:], in1=gates[:, qt, h * 12:(h + 1) * 12].rearrange("p (g b) -> p b g", b=3), op=ALU.mult),
          r=["zr", ("gates", qt)], w=["coef"])
        for br in range(3):
            dst = osb if br == 0 else otmp
            A("dve", lambda e, br=br, dst=dst: e.tensor_tensor(
                out=dst[:], in0=Osb[:, br, :, 0:64], in1=coef[:, br, :].unsqueeze(2).to_broadcast([128, 4, 64]), op=ALU.mult),
              r=[("Osb", br), "coef"], w=["osb" if br == 0 else "otmp"])
            if br > 0:
                A("dve", lambda e: e.tensor_tensor(out=osb[:], in0=osb[:], in1=otmp[:], op=ALU.add),
                  r=["osb", "otmp"], w=["osb"])
        A("pool", lambda e: e.tensor_copy(out=attn[:, h * 256:(h + 1) * 256], in_=osb[:].rearrange("p g d -> p (g d)")),
          r=["osb"], w=[("attn", h)])
        if h == 0:
            return
        for c4 in range(4):
            A("pe", lambda e, c4=c4: e.transpose(out=pTb[:, c4, :], in_=attn[:, c4 * 128:(c4 + 1) * 128], identity=ident[:]),
              r=[("attn", 0), ("attn", 1), "ident"], w=["pTb"])
        A("dve", lambda e: e.tensor_copy(out=attnT[:], in_=pTb[:, 0:4, :]), r=["pTb"], w=["attnT"])
        A("sp", lambda e: e.dma_start(out=xt2[0][:], in_=xp[qt * 128:(qt + 1) * 128, :]), w=["xt2"], dma=True)
        for half in range(2):
            for k in range(8):
                lhs = (convyT[:, k, qt * 128:(qt + 1) * 128] if k < 4 else attnT[:, k - 4, :])
                A("pe", lambda e, k=k, half=half, lhs=lhs: e.matmul(pH[:], lhsT=lhs, rhs=w_out_bf[:, k, half * 512:(half + 1) * 512],
                                                                    start=(k == 0), stop=(k == 7)),
                  r=[("w_out_bf", k), ("convyT", qt // 4), "attnT"], w=["pH"])
            A("dve", lambda e, half=half: e.tensor_tensor(
                out=h_acc[:, qt, half * 512:(half + 1) * 512], in0=pH[:], in1=xt2[0][:, half * 512:(half + 1) * 512], op=ALU.add),
              r=["pH", "xt2"], w=[("h_acc", qt)])
        A("act", lambda e: e.activation(out=xt2[0][:], in_=h_acc[:, qt, :], func=AF.Square, accum_out=L["hss"][:, qt:qt + 1]),
          r=[("h_acc", qt), "hss", "xt2"], w=["xt2", "hss"])
        if qt + 2 < NT:
            emit_sel(qt + 2)

    emit_sel(0)
    emit_sel(1)
    for i in range(len(tiles) + LOOK):
        if i < len(tiles):
            emit_qk(tiles[i], i)
        j = i - LOOK
        if j >= 0:
            emit_pv(tiles[j])
            if tiles[j].get("end"):
                emit_end(tiles[j]["qt"], tiles[j]["h"])
    if debug:
        dbg["attn"] = dout("d_attn", [128, 512], BF16)
        A("sp", lambda e: e.dma_start(out=dbg["attn"], in_=attn[:]), r=[("attn", 0), ("attn", 1)], dma=True)
        dbg["h"] = dout("d_h", [128, NT * DM], F32)
        A("sp", lambda e: e.dma_start(out=dbg["h"], in_=h_acc[:]), r=[("h_acc", t) for t in range(NT)], dma=True)


def emit_norm_T(S, src, srckey, junk, ss, hn, gbc, gkey, ident, psT8, dst_k, dst_all, dstkey):
    A = S.add
    P = src.shape[0]
    A("pool", lambda e: e.memset(ss[0:P, 0:1], 0.0), w=[("ss", id(ss))])
    A("act", lambda e: e.activation(out=junk[0:P, :], in_=src, func=AF.Square, accum_out=ss[0:P, 0:1]),
      r=[srckey, ("ss", id(ss))], w=[("junk", id(junk)), ("ss", id(ss))])
    A("dve", lambda e: e.tensor_scalar(out=ss[0:P, 1:2], in0=ss[0:P, 0:1], scalar1=1.0 / DM, scalar2=1e-6,
                                       op0=ALU.mult, op1=ALU.add), r=[("ss", id(ss))], w=[("rs", id(ss))])
    A("act", lambda e: e.activation(out=ss[0:P, 1:2], in_=ss[0:P, 1:2], func=AF.Sqrt), r=[("rs", id(ss))], w=[("rs", id(ss))])
    A("dve", lambda e: e.reciprocal(out=ss[0:P, 1:2], in_=ss[0:P, 1:2]), r=[("rs", id(ss))], w=[("rs", id(ss))])
    A("dve", lambda e: e.scalar_tensor_tensor(out=hn[0:P, :], in0=src, scalar=ss[0:P, 1:2], in1=gbc[0:P, :],
                                              op0=ALU.mult, op1=ALU.mult), r=[srckey, ("rs", id(ss)), gkey], w=[("hn", id(hn))])
    for k in range(8):
        A("pe", lambda e, k=k: e.transpose(out=psT8[:, k, 0:P], in_=hn[0:P, k * 128:(k + 1) * 128], identity=ident[0:P, 0:P]),
          r=[("hn", id(hn)), "ident"], w=["pTb"])
    A("act", lambda e: e.copy(out=dst_all, in_=psT8[:, :, 0:P]), r=["pTb"], w=[dstkey])


def build_p3(nc, S, P3, sb, pst, L):
    A = S.add
    h_acc, hs_acc, hnTs, w_up, w_down, y_p, y_s, gbc_mlp, ident = (L[k] for k in (
        "h_acc", "hs_acc", "hnTs", "w_up", "w_down", "y_p", "y_s", "gbc_mlp", "ident"))
    with_s = "samp" in L["stages"]
    hnT = sb("hnT", [128, 8, SEQ], BF16, P3)
    junk3 = sb("junk3", [128, DM], BF16, P3)
    ss3 = sb("ss3", [128, 2], F32, P3)
    hnb = [junk3, sb("hn", [128, DM], BF16, P3)]
    pTb = pst("pTb3", [128, 8, 128], BF16, P3)
    hss = L["hss"]
    A("dve", lambda e: e.tensor_scalar(out=hss[:], in0=hss[:], scalar1=1.0 / DM, scalar2=1e-6, op0=ALU.mult, op1=ALU.add),
      r=["hss"], w=["hss"])
    A("act", lambda e: e.activation(out=hss[:], in_=hss[:], func=AF.Sqrt), r=["hss"], w=["hss"])
    A("dve", lambda e: e.reciprocal(out=hss[:], in_=hss[:]), r=["hss"], w=["hss"])
    for qt in range(NT):
        hb = qt % 2
        A("dve", lambda e, qt=qt, hb=hb: e.scalar_tensor_tensor(out=hnb[hb][:], in0=h_acc[:, qt, :], scalar=hss[:, qt:qt + 1],
                                                                in1=gbc_mlp[:], op0=ALU.mult, op1=ALU.mult),
          r=[("h_acc", qt), "hss", "gbc_mlp"], w=[("hnb", hb)])
        for k in range(8):
            A("pe", lambda e, k=k, hb=hb: e.transpose(out=pTb[:, k, :], in_=hnb[hb][:, k * 128:(k + 1) * 128], identity=ident[:]),
              r=[("hnb", hb), "ident"], w=["pTb3"])
        A("act", lambda e, qt=qt: e.copy(out=hnT[:, :, qt * 128:(qt + 1) * 128], in_=pTb[:]), r=["pTb3"], w=[("hnT", qt)])
    stgU = [sb("stgU0", [128, 8, 512], F32, P3)] * 2
    stgD = [sb("stgD0", [128, 4, DM], F32, P3)] * 2
    gbc_fin = sb("gbc_fin", [128, DM], F32, P3)
    A("sp", lambda e: e.dma_start(out=gbc_fin[:], in_=L["g_fin"].partition_broadcast(128)), w=["gbc_fin"], dma=True)
    wu = [sb(f"wu{i}", [128, 8, 512], BF16, P3) for i in range(2)]
    wd = [sb(f"wd{i}", [128, 4, DM], BF16, P3) for i in range(2)]
    aT = [sb(f"aT{i}", [128, 4, 512], BF16, P3) for i in range(2)]
    aTs = sb("aTs", [128, 4, NSB], BF16, P3)
    rl = [sb(f"rl{i}", [128, 512], F32, P3) for i in range(2)]
    pU = [pst(f"pU{i}", [128, 512], F32, P3) for i in range(2)]
    pD = [pst(f"pD{i}", [128, 512], F32, P3) for i in range(2)]
    yo = [stgD[0][:, 0, :]] * 2
    ucnt = [0]
    dcnt = [0]
    acnt = [0]
    groups = list(range(4)) + (["s"] if with_s else [])

    def load_w(fg):
        b = fg % 2
        A("sp", lambda e, fg=fg, b=b: e.dma_start(out=stgU[b][:], in_=w_up[:, fg * 512:(fg + 1) * 512].rearrange(
            "(k p) f -> p k f", p=128)), w=["stgU"], dma=True)
        A("sp", lambda e, fg=fg, b=b: e.dma_start(out=stgD[b][:], in_=w_down[fg * 512:(fg + 1) * 512, :].rearrange(
            "(c p) d -> p c d", p=128)), w=["stgD"], dma=True)
        A("act", lambda e, b=b: e.copy(out=wu[b][:], in_=stgU[b][:]), r=["stgU"], w=[("wu", b)])
        A("pool", lambda e, b=b: e.tensor_copy(out=wd[b][:], in_=stgD[b][:]), r=["stgD"], w=[("wd", b)])

    def emit_up(fg, TG):
        b = fg % 2
        ntok = 512 if TG != "s" else NSB
        if TG == "s":
            rhs_of = lambda k: hnTs[:, k, :]
            rk = ["hnTs"]
            adst = aTs
            akey = "aTs"
        else:
            rhs_of = lambda k, TG=TG: hnT[:, k, TG * 512:(TG + 1) * 512]
            rk = [("hnT", t) for t in range(4 * TG, 4 * TG + 4)]
            ai = acnt[0] % 2
            acnt[0] += 1
            adst = aT[ai]
            akey = ("aT", ai)
        for fc in range(4):
            ui = ucnt[0] % 2
            ucnt[0] += 1
            for k in range(8):
                A("pe", lambda e, k=k, fc=fc, ui=ui, b=b, rhs_of=rhs_of, ntok=ntok: e.matmul(
                    pU[ui][:, 0:ntok], lhsT=wu[b][:, k, fc * 128:(fc + 1) * 128], rhs=rhs_of(k),
                    start=(k == 0), stop=(k == 7)), r=[("wu", b)] + rk, w=[("pU", ui)])
            A("act", lambda e, ui=ui, ntok=ntok: e.activation(out=rl[ui][:, 0:ntok], in_=pU[ui][:, 0:ntok], func=AF.Relu),
              r=[("pU", ui)], w=[("rl", ui)])
            A("pool" if fc % 2 else "dve", lambda e, ui=ui, fc=fc, adst=adst, ntok=ntok: e.tensor_tensor(
                out=adst[:, fc, 0:ntok], in0=rl[ui][:, 0:ntok], in1=rl[ui][:, 0:ntok], op=ALU.mult),
              r=[("rl", ui)], w=[(akey, fc)])
        return adst, akey

    def emit_down(fg, TG, adst, akey):
        b = fg % 2
        tiles = range(4) if TG != "s" else [0]
        for tt in tiles:
            for half in range(2):
                di = dcnt[0] % 2
                dcnt[0] += 1
                mrows = 128 if TG != "s" else NSB
                for fc in range(4):
                    lhs = adst[:, fc, tt * 128:(tt + 1) * 128] if TG != "s" else adst[:, fc, :]
                    A("pe", lambda e, fc=fc, half=half, di=di, lhs=lhs, b=b, mrows=mrows: e.matmul(
                        pD[di][0:mrows, :], lhsT=lhs, rhs=wd[b][:, fc, half * 512:(half + 1) * 512],
                        start=(fc == 0), stop=(fc == 3)), r=[(akey, fc), ("wd", b)], w=[("pD", di)])
                if TG != "s":
                    t = 4 * TG + tt
                    A("dve", lambda e, t=t, half=half, di=di: e.tensor_tensor(
                        out=h_acc[:, t, half * 512:(half + 1) * 512], in0=pD[di][:], in1=h_acc[:, t, half * 512:(half + 1) * 512],
                        op=ALU.add), r=[("pD", di), ("h_acc", t)], w=[("h_acc", t)])
                else:
                    A("dve", lambda e, half=half, di=di: e.tensor_tensor(
                        out=hs_acc[:, half * 512:(half + 1) * 512], in0=pD[di][0:NSB, :],
                        in1=hs_acc[:, half * 512:(half + 1) * 512], op=ALU.add), r=[("pD", di), "hs_acc"], w=["hs_acc"])

    load_w(0)
    pending = None
    for fg in range(8):
        for gi, TG in enumerate(groups):
            cur = emit_up(fg, TG)
            if gi == 1 and fg + 1 < 8:
                load_w(fg + 1)
            if pending is not None:
                emit_down(*pending)
            pending = (fg, TG) + cur
    emit_down(*pending)
    outs = [(h_acc[:, t, :], ("h_acc", t), y_p[t * 128:(t + 1) * 128, :], 128) for t in range(NT)]
    if with_s:
        outs.append((hs_acc[:], "hs_acc", y_s, NSB))
    for i, (src, skey, dst, P) in enumerate(outs):
        yb = i % 2
        A("pool", lambda e, P=P: e.memset(ss3[0:P, 0:1], 0.0), w=["ss3"])
        A("act", lambda e, src=src, P=P: e.activation(out=junk3[0:P, :], in_=src, func=AF.Square, accum_out=ss3[0:P, 0:1]),
          r=[skey, "ss3", ("hnb", 0)], w=[("hnb", 0), "ss3"])
        A("dve", lambda e, P=P: e.tensor_scalar(out=ss3[0:P, 1:2], in0=ss3[0:P, 0:1], scalar1=1.0 / DM, scalar2=1e-6,
                                                op0=ALU.mult, op1=ALU.add), r=["ss3"], w=["rs3"])
        A("act", lambda e, P=P: e.activation(out=ss3[0:P, 1:2], in_=ss3[0:P, 1:2], func=AF.Sqrt), r=["rs3"], w=["rs3"])
        A("dve", lambda e, P=P: e.reciprocal(out=ss3[0:P, 1:2], in_=ss3[0:P, 1:2]), r=["rs3"], w=["rs3"])
        A("dve", lambda e, src=src, P=P, yb=yb: e.scalar_tensor_tensor(
            out=yo[yb][0:P], in0=src, scalar=ss3[0:P, 1:2], in1=gbc_fin[0:P, :], op0=ALU.mult, op1=ALU.mult),
          r=[skey, "rs3", "gbc_fin"], w=["stgD"])
        A("sp", lambda e, dst=dst, P=P, yb=yb: e.dma_start(out=dst, in_=yo[yb][0:P]), r=["stgD"], dma=True)


_NC_CACHE = {}


def _get_nc():
    if "nc" not in _NC_CACHE:
        _NC_CACHE["nc"] = build()[0]
    return _NC_CACHE["nc"]


def make_in_maps(inp, cores):
    f = lambda a: np.ascontiguousarray(np.asarray(a, dtype=np.float32))
    cache = f(inp["cache_kv"]).reshape(2560 * 128 * 4, 128)
    wdall = np.ascontiguousarray(np.concatenate(
        [f(inp["w_dw"])[0], f(inp["b_dw"]), f(inp["conv_ln_g"]), f(inp["conv_ln_b"])], axis=0))
    shared = dict(
        cache=cache, g_attn=f(inp["g_attn_norm"]), w_in=f(inp["w_in"])[0], wdall=wdall,
        w_ck=f(inp["w_cmp_k"])[0], w_cv=f(inp["w_cmp_v"])[0], w_out=f(inp["w_out"])[0], g_mlp=f(inp["g_mlp_norm"]),
        w_up=f(inp["w_up"])[0], w_down=f(inp["w_down"])[0], g_fin=f(inp["g_final"]).reshape(1, DM))
    maps = []
    for c in cores:
        sl = slice(c * NSB, (c + 1) * NSB)
        m = dict(shared)
        m["xp"] = f(inp["x_prompt"])[c]
        m["xs"] = f(inp["x_sample"])[sl, 0]
        m["cwin"] = f(inp["cache_win"])[0, sl].reshape(NSB, 512, 256)
        m["sconv"] = f(inp["state_conv"])[0, sl].reshape(NSB * 30, 512)
        m["ptab"] = np.ascontiguousarray(np.asarray(inp["page_table"], dtype=np.int32)[sl].reshape(1, NSB * 16))
        maps.append(m)
    return maps


def kernel(**inp):
    nc = _get_nc()
    cores = list(range(N_CORES))
    res = run_bass_kernel_spmd(nc, make_in_maps(inp, cores), core_ids=cores)
    R = res.results
    cat = lambda k: np.stack([np.asarray(r[k], dtype=np.float32) for r in R], axis=0)
    y_p = cat("y_p")
    y_s = cat("y_s").reshape(128, 1, DM)
    kv_p = cat("kv_p").reshape(1, 8, SEQ, 4, 2, 64)
    win_p = cat("win_p").reshape(1, 8, 512, 2, 2, 64)
    conv_p = cat("conv_p").reshape(1, 8, 30, 512)
    kv_s = cat("kv_s").reshape(1, 128, 1, 4, 2, 64)
    win_s = cat("win_s").reshape(1, 128, 512, 2, 2, 64)
    conv_s = cat("conv_s").reshape(1, 128, 30, 512)
    return (y_p, y_s, kv_p, win_p, conv_p, kv_s, win_s, conv_s)


def _dap(ap, offset, dims):
    return bass.AP(ap.tensor, offset, [list(d) for d in dims])


def build_samp(nc, S, SP, sb, pst, L):
    A = S.add
    (xs, cache, cwin, sconv, ptab, wdall, w_cv, kv_s, win_s, conv_s, w_in_bf, w_out_bf, ident, identf, gbc_mlp, hs_acc,
     hnTs, wckb, g_attn) = (L[k] for k in ("xs", "cache", "cwin", "sconv", "ptab", "wdall", "w_cv", "kv_s", "win_s",
                                            "conv_s", "w_in_bf", "w_out_bf", "ident", "identf", "gbc_mlp", "hs_acc",
                                            "hnTs", "wckb", "g_attn"))
    debug, dbg, dout = L["debug"], L["dbg"], L["dout"]
    zs_d = nc.dram_tensor("zs_d", [NSB, INC], F32, kind="Internal").ap()
    ptb = sb("ptb", [128, NSB * 16], I32, SP)
    iop = sb("iop", [128, 1], I32, SP)
    idx = sb("idx", [128, NSB * 8], I32, SP)
    A("sp", lambda e: e.dma_start(out=ptb[:], in_=ptab.partition_broadcast(128)), w=["ptb"], dma=True)
    A("pool", lambda e: e.iota(iop[:], pattern=[[0, 1]], base=0, channel_multiplier=1), w=["iop"])
    A("dve", lambda e: e.tensor_scalar(out=iop[64:128, :], in0=iop[64:128, :], scalar1=-64, scalar2=None, op0=ALU.add),
      r=["iop"], w=["iop"])
    for hf in range(2):
        ps_ = slice(64 * hf, 64 * hf + 64)
        A("dve", lambda e, hf=hf, ps_=ps_: e.tensor_scalar(
            out=idx[ps_, :], in0=ptb[ps_, :].rearrange("p (c t) -> p c t", t=2)[:, :, hf], scalar1=64, scalar2=iop[ps_, 0:1],
            op0=ALU.mult, op1=ALU.add), r=["ptb", "iop"], w=["idx"])
    cacheR = cache.rearrange("(r s) c -> r (s c)", s=8)

    xs_sb = sb("xs_sb", [NSB, DM], F32, SP)
    gbc_a = sb("gbc_a", [NSB, DM], F32, SP)
    xnTs = sb("xnTs", [128, 8, NSB], BF16, SP)
    junk = sb("s_junk", [NSB, DM], BF16, SP)
    ssx = sb("s_ss", [128, 2], F32, SP)
    hn = sb("s_hn", [NSB, DM], BF16, SP)
    zs = sb("zs", [NSB, INC], F32, SP)
    pA = pst("s_pA", [128, 512], F32, SP)
    pB = pst("s_pB", [128, 512], F32, SP)
    pTb = pst("s_pTb", [128, 8, 128], BF16, SP)
    pKT = pst("s_pKT", [128, 8, 128], BF16, SP)
    pS = [pst(f"s_pS{i}", [128, 512], F32, SP) for i in range(3)]
    A("sp", lambda e: e.dma_start(out=xs_sb[:], in_=xs), w=["xs_sb"], dma=True)
    A("sp", lambda e: e.dma_start(out=gbc_a[:], in_=g_attn.partition_broadcast(NSB)), w=["gbc_a"], dma=True)
    emit_norm_T(S, xs_sb[:], "xs_sb", junk, ssx, hn, gbc_a, "gbc_a", ident, pTb, None, xnTs[:], "xnTs")
    for ci, c0 in enumerate(range(0, INC, 512)):
        n = min(512, INC - c0)
        for k in range(8):
            A("pe", lambda e, k=k, c0=c0, n=n: e.matmul(pA[0:NSB, 0:n], lhsT=xnTs[:, k, :], rhs=w_in_bf[:, k, c0:c0 + n],
                                                        start=(k == 0), stop=(k == 7)), r=["xnTs", ("w_in_bf", k)], w=["s_pA"])
        A("act", lambda e, c0=c0, n=n: e.copy(out=zs[:, c0:c0 + n], in_=pA[0:NSB, 0:n]), r=["s_pA"], w=["zs"])
    A("sp", lambda e: e.dma_start(out=zs_d, in_=zs[:]), r=["zs"], w=["zs_d"], dma=True)
    A("sp", lambda e: e.dma_start(out=kv_s, in_=zs[:, 1536:2048]), r=["zs"], dma=True)
    A("sp", lambda e: e.dma_start(out=win_s[:, 511, :], in_=zs[:, 2048:2304]), r=["zs"], dma=True)
    for b in range(NSB):
        A("sp", lambda e, b=b: e.dma_start(out=win_s[b, 0:511, :], in_=cwin[b, 1:512, :]), dma=True)
    A("sp", lambda e: e.dma_start(out=conv_s.rearrange("(b j) c -> b (j c)", j=30)[:, 0:29 * 512],
                                  in_=sconv.rearrange("(b j) c -> b (j c)", j=30)[:, 512:30 * 512]), dma=True)
    Qrows = sb("Qrows", [128, 64], F32, SP)
    Kn = sb("Kn", [128, 2, 64], F32, SP)
    Vn = sb("Vn", [128, 2, 64], F32, SP)
    Grows = sb("Grows", [128, 3], F32, SP)
    for b in range(NSB):
        rows = slice(8 * b, 8 * b + 8)
        A("sp", lambda e, b=b, rows=rows: e.dma_start(out=Qrows[rows, :], in_=_dap(zs_d, b * INC + 1024, [[64, 8], [1, 64]])),
          r=["zs_d"], w=["Qrows"], dma=True)
        for j, col in enumerate((1536 + 256, 1536 + 512)):
            A("sp", lambda e, b=b, rows=rows, j=j, col=col: e.dma_start(
                out=Kn[rows, j, :], in_=_dap(zs_d, b * INC + col, [[64, 2], [0, 4], [1, 64]])), r=["zs_d"], w=["Kn"], dma=True)
        for j, col in enumerate((1536 + 384, 1536 + 640)):
            A("sp", lambda e, b=b, rows=rows, j=j, col=col: e.dma_start(
                out=Vn[rows, j, :], in_=_dap(zs_d, b * INC + col, [[64, 2], [0, 4], [1, 64]])), r=["zs_d"], w=["Vn"], dma=True)
        A("sp", lambda e, b=b, rows=rows: e.dma_start(out=Grows[rows, :], in_=_dap(zs_d, b * INC + 2304, [[3, 8], [1, 3]])),
          r=["zs_d"], w=["Grows"], dma=True)
    QTpad = sb("QTpad", [128, NSB, 128], BF16, SP)
    A("pool", lambda e: e.memset(QTpad[:], 0.0), w=["QTpad"])
    qsrc = sb("qsrc", [NSB, 4, 2, 64], F32, SP)
    A("dve", lambda e: e.tensor_copy(out=qsrc[:], in_=zs[:, 1024:1536].rearrange("p (h g d) -> p g h d", h=2, g=4)),
      r=["zs"], w=["qsrc"])
    for g in range(4):
        A("pe", lambda e, g=g: e.transpose(out=pB[:, g * 16:(g + 1) * 16], in_=qsrc[:, g, :, :].rearrange("p h d -> p (h d)"),
                                           identity=identf[0:NSB, 0:NSB]), r=["qsrc", "identf"], w=["s_pB"])
    QTflat = QTpad[:].rearrange("p b c -> p (b c)")
    for h in range(2):
        for g in range(4):
            c0 = 4 * h + g
            A("act", lambda e, h=h, g=g, c0=c0: e.copy(out=QTflat[64 * h:64 * h + 64, c0:c0 + 136 * 15 + 1:136],
                                                       in_=pB[64 * h:64 * h + 64, g * 16:(g + 1) * 16]),
              r=["s_pB", "QTpad"], w=["QTpad"])
    pidx = sb("pidx", [128, 4], I32, SP)
    pf = sb("pf", [128, 4], F32, SP)
    A("pool", lambda e: e.iota(pidx[:, 0:1], pattern=[[0, 1]], base=0, channel_multiplier=1), w=["pidx"])
    A("dve", lambda e: e.tensor_single_scalar(out=pidx[:, 1:2], in_=pidx[:, 0:1], scalar=2, op=ALU.arith_shift_right),
      r=["pidx"], w=["pidx"])
    A("dve", lambda e: e.tensor_single_scalar(out=pidx[:, 2:3], in_=pidx[:, 1:2], scalar=1, op=ALU.bitwise_and),
      r=["pidx"], w=["pidx"])
    A("dve", lambda e: e.tensor_copy(out=pf[:, 0:3], in_=pidx[:, 0:3]), r=["pidx"], w=["pf"])
    coli = sb("coli", [128, 128], I32, SP)
    colf = sb("colf", [128, 128], F32, SP)
    GG = sb("GG", [128, 128], F32, SP)
    A("pool", lambda e: e.iota(coli[:], pattern=[[1, 128]], base=0, channel_multiplier=0), w=["coli"])
    A("dve", lambda e: e.tensor_single_scalar(out=coli[:], in_=coli[:], scalar=2, op=ALU.arith_shift_right), r=["coli"], w=["coli"])
    A("dve", lambda e: e.tensor_copy(out=colf[:], in_=coli[:]), r=["coli"], w=["colf"])
    A("dve", lambda e: e.tensor_scalar(out=GG[:], in0=colf[:], scalar1=pf[:, 1:2], scalar2=None, op0=ALU.is_equal),
      r=["colf", "pf"], w=["GG"])
    Mh = sb("Mh", [128, 2, 16], F32, SP)
    for half in range(2):
        A("dve", lambda e, half=half: e.tensor_scalar(out=Mh[:, half, :], in0=colf[:, 0:64:4], scalar1=float(16 * half),
                                                      scalar2=pf[:, 1:2], op0=ALU.add, op1=ALU.is_equal),
          r=["colf", "pf"], w=["Mh"])
    wcvb = L["wcvb"]
    Wk = sb("Wk", [128, 32], F32, SP)
    Wv = sb("Wv", [128, 32], F32, SP)
    wtmp = sb("wtmp", [128, 32], F32, SP)
    for (src, dst, key) in ((wckb, Wk, "Wk"), (wcvb, Wv, "Wv")):
        A("dve", lambda e, src=src: e.tensor_tensor(out=wtmp[:], in0=src[:, 1, :], in1=src[:, 0, :], op=ALU.subtract),
          r=["wckb", "wcvb"], w=["wtmp"])
        A("dve", lambda e, src=src, dst=dst: e.scalar_tensor_tensor(out=dst[:], in0=wtmp[:], scalar=pf[:, 2:3], in1=src[:, 0, :],
                                                                    op0=ALU.mult, op1=ALU.add), r=["wtmp", "pf", "wckb", "wcvb"], w=[key])
    SS_ = sb("SS_", [128, 2, SEQ], F32, SP)
    Scmp = SS_[:, 0, :]
    Ssel = SS_[:, 1, :]
    Swin = sb("Swin", [128, 512], F32, SP)
    NKB = 3
    kst = [sb(f"kst{i}", [128, 4, 4, 128], F32, SP) for i in range(NKB)]
    kbf = [sb(f"kbf{i}", [128, 4, 2, 128], BF16, SP) for i in range(2)]
    KTsb = [sb(f"KTsb{i}", [128, 2, 512], BF16, SP) for i in range(2)]
    it = 0
    for pg in range(5):
        for b in range(NSB):
            bi = it % NKB
            b2 = it % 2
            it += 1
            if pg < 4:
                for mm in range(2):
                    col = b * 8 + pg * 2 + mm
                    A("pool", lambda e, bi=bi, mm=mm, col=col: e.indirect_dma_start(
                        out=kst[bi][:, 2 * mm:2 * mm + 2, :, :].rearrange("p e s c -> p (e s c)"), out_offset=None, in_=cacheR,
                        in_offset=bass.IndirectOffsetOnAxis(ap=idx[:, col:col + 1], axis=0)),
                      r=["idx"], w=[("kst", bi, 2 * mm), ("kst", bi, 2 * mm + 1)], dma=True)
                A("act", lambda e, bi=bi, b2=b2: e.copy(out=kbf[b2][:], in_=kst[bi][:, :, 0:4:2, :]),
                  r=[("kst", bi, i) for i in range(4)], w=[("kbf", b2)])
                for i in range(4):
                    for si in range(2):
                        A("pe", lambda e, b2=b2, i=i, si=si: e.transpose(out=pKT[:, si * 4 + i, :], in_=kbf[b2][:, i, si, :],
                                                                         identity=ident[:]), r=[("kbf", b2), "ident"], w=["s_pKT"])
                A("dve", lambda e, b2=b2: e.tensor_copy(out=KTsb[b2][:], in_=pKT[:].rearrange("p (s i) t -> p s (i t)", s=2)),
                  r=["s_pKT"], w=[("KTsb", b2)])
                for si in range(2):
                    A("pe", lambda e, b2=b2, si=si, b=b: e.matmul(pS[si][:], lhsT=QTpad[:, b, :], rhs=KTsb[b2][:, si, :],
                                                                  start=(b == 0), stop=(b == NSB - 1)),
                      r=["QTpad", ("KTsb", b2)], w=[("s_pS", si)])
            else:
                A("sp", lambda e, bi=bi, b=b: e.dma_start(
                    out=kst[bi][:, :, 0, :], in_=cwin[b].rearrange("(i p) (s c) -> p i s c", p=128, c=128)[:, :, 0, :]),
                  w=[("kst", bi, i) for i in range(4)], dma=True)
                A("act", lambda e, bi=bi, b2=b2: e.copy(out=kbf[b2][:, :, 0, :], in_=kst[bi][:, :, 0, :]),
                  r=[("kst", bi, i) for i in range(4)], w=[("kbf", b2)])
                for i in range(4):
                    A("pe", lambda e, b2=b2, i=i: e.transpose(out=pKT[:, i, :], in_=kbf[b2][:, i, 0, :], identity=ident[:]),
                      r=[("kbf", b2), "ident"], w=["s_pKT"])
                A("dve", lambda e, b2=b2: e.tensor_copy(out=KTsb[b2][:, 0, :], in_=pKT[:, 0:4, :].rearrange("p i t -> p (i t)")),
                  r=["s_pKT"], w=[("KTsb", b2)])
                A("pe", lambda e, b2=b2, b=b: e.matmul(pS[2][:], lhsT=QTpad[:, b, :], rhs=KTsb[b2][:, 0, :],
                                                       start=(b == 0), stop=(b == NSB - 1)), r=["QTpad", ("KTsb", b2)], w=[("s_pS", 2)])
        if pg < 4:
            A("act", lambda e, pg=pg: e.copy(out=SS_[:, 0, pg * 512:(pg + 1) * 512], in_=pS[0][:]), r=[("s_pS", 0)], w=["Scmp"])
            A("dve", lambda e, pg=pg: e.tensor_copy(out=SS_[:, 1, pg * 512:(pg + 1) * 512], in_=pS[1][:]), r=[("s_pS", 1)], w=["Ssel"])
        else:
            A("act", lambda e: e.copy(out=Swin[:], in_=pS[2][:]), r=[("s_pS", 2)], w=["Swin"])
    big = sb("s_big", [128, SEQ], F32, SP)
    sc = sb("s_sc", [128, 64], F32, SP)
    st = sb("s_st", [128, 16], F32, SP)
    pb32 = sb("s_pb32", [128, 32], F32, SP)
    m8 = sb("s_m8", [128, 8], F32, SP)
    wk32 = sb("s_wk32", [128, 32], F32, SP)
    selm = sb("s_selm", [128, 32], F32, SP)
    Pc = sb("Pc", [128, SEQ], BF16, SP)
    Ps = sb("Ps", [128, SEQ], BF16, SP)
    Pw = sb("Pw", [128, 512], BF16, SP)
    Wk2 = sb("Wk2", [128, 2, 16], F32, SP)
    Wv2 = sb("Wv2", [128, 2, 16], F32, SP)
    sc2 = sb("s_sc2", [128, 128], F32, SP)
    A("dve", lambda e: e.tensor_copy(out=Wk2[:], in_=Wk[:].rearrange("p (q e) -> p e q", e=2)), r=["Wk"], w=["Wk2"])
    A("dve", lambda e: e.tensor_copy(out=Wv2[:], in_=Wv[:].rearrange("p (q e) -> p e q", e=2)), r=["Wv"], w=["Wv2"])

    def v_meg(ap, e_):
        return ap.rearrange("p (m e g q) -> p m e g q", m=8, e=2, g=8)[:, :, e_, :, :]
    for e_ in range(2):
        A("dve", lambda e, e_=e_: e.tensor_tensor(
            out=v_meg(big[:], e_), in0=v_meg(Scmp, e_),
            in1=Wk2[:, e_, :].unsqueeze(1).unsqueeze(1).to_broadcast([128, 8, 8, 16]), op=ALU.mult), r=["Scmp", "Wk2"], w=["s_big"])
    A("dve", lambda e: e.tensor_reduce(out=sc2[:], in_=big[:].rearrange("p (c q) -> p c q", q=16), axis=AX.X, op=ALU.add),
      r=["s_big"], w=["s_sc2"])
    A("dve", lambda e: e.tensor_reduce(out=sc[:].rearrange("p (m g) -> p m g", g=8),
                                       in_=sc2[:].rearrange("p (m e g) -> p m g e", m=8, e=2), axis=AX.X, op=ALU.add),
      r=["s_sc2"], w=["s_sc"])
    A("dve", lambda e: e.memset(st[:], 0.0), w=["s_st"])
    A("act", lambda e: e.activation(out=sc[:], in_=sc[:], func=AF.Exp, scale=SCALE, accum_out=st[:, 0:1]), r=["s_sc", "s_st"], w=["s_sc", "s_st"])
    A("dve", lambda e: e.reciprocal(out=st[:, 1:2], in_=st[:, 0:1]), r=["s_st"], w=["s_st"])
    A("dve", lambda e: e.tensor_scalar(out=sc[:], in0=sc[:], scalar1=st[:, 1:2], scalar2=None, op0=ALU.mult), r=["s_sc", "s_st"], w=["s_sc"])
    A("pe", lambda e: e.matmul(pA[:, 0:64], lhsT=GG[:], rhs=sc[:], start=True, stop=True), r=["GG", "s_sc"], w=["s_pA"])
    A("dve", lambda e: e.tensor_reduce(out=pb32[:], in_=pA[:, 0:64].rearrange("p (b t) -> p b t", t=2), axis=AX.X, op=ALU.add),
      r=["s_pA"], w=["s_pb32"])
    A("dve", lambda e: e.tensor_scalar(out=pb32[:, 0:1], in0=pb32[:, 0:1], scalar1=5.0, scalar2=None, op0=ALU.add), r=["s_pb32"], w=["s_pb32"])
    A("dve", lambda e: e.tensor_scalar(out=pb32[:, 31:32], in0=pb32[:, 31:32], scalar1=5.0, scalar2=None, op0=ALU.add), r=["s_pb32"], w=["s_pb32"])
    A("dve", lambda e: e.max(out=m8[:], in_=pb32[:]), r=["s_pb32"], w=["s_m8"])
    A("dve", lambda e: e.match_replace(out=wk32[:], in_to_replace=m8[:], in_values=pb32[:], imm_value=-3e38), r=["s_m8", "s_pb32"], w=["s_wk32"])
    A("dve", lambda e: e.max(out=m8[:], in_=wk32[:]), r=["s_wk32"], w=["s_m8"])
    A("dve", lambda e: e.tensor_scalar(out=selm[:], in0=pb32[:], scalar1=m8[:, 6:7], scalar2=None, op0=ALU.is_ge), r=["s_pb32", "s_m8"], w=["s_selm"])
    for e_ in range(2):
        A("dve", lambda e, e_=e_: e.tensor_tensor(
            out=v_meg(big[:], e_), in0=sc[:].rearrange("p (m g) -> p m g", g=8).unsqueeze(3).to_broadcast([128, 8, 8, 16]),
            in1=Wv2[:, e_, :].unsqueeze(1).unsqueeze(1).to_broadcast([128, 8, 8, 16]), op=ALU.mult),
          r=["s_sc", "Wv2", "s_big"], w=["s_big"])
    A("act", lambda e: e.copy(out=Pc[:], in_=big[:]), r=["s_big"], w=["Pc"])
    for j in range(2):
        A("dve", lambda e, j=j: e.tensor_tensor(out=big[:, 0:64], in0=Qrows[:], in1=Kn[:, j, :], op=ALU.mult), r=["Qrows", "Kn", "s_big", "Pc"], w=["s_big"])
        A("dve", lambda e, j=j: e.tensor_reduce(out=st[:, 2 + 2 * j:3 + 2 * j], in_=big[:, 0:64], axis=AX.X, op=ALU.add), r=["s_big"], w=["s_st"])
        A("act", lambda e, j=j: e.activation(out=st[:, 3 + 2 * j:4 + 2 * j], in_=st[:, 2 + 2 * j:3 + 2 * j], func=AF.Exp, scale=SCALE),
          r=["s_st"], w=["s_st"])
    A("act", lambda e: e.activation(out=big[:], in_=Ssel, func=AF.Exp, scale=SCALE), r=["Ssel", "s_big"], w=["s_big"])
    def v_mek(ap, e_):
        return ap.rearrange("p (m e k t) -> p m e k t", m=8, e=2, k=4)[:, :, e_, :, :]
    for e_ in range(2):
        A("dve", lambda e, e_=e_: e.tensor_tensor(
            out=v_mek(big[:], e_), in0=v_mek(big[:], e_),
            in1=selm[:].rearrange("p (m k) -> p m k", k=4).unsqueeze(3).to_broadcast([128, 8, 4, 32]), op=ALU.mult),
          r=["s_big", "s_selm"], w=["s_big"])
    A("dve", lambda e: e.tensor_reduce(out=st[:, 6:7], in_=big[:], axis=AX.X, op=ALU.add), r=["s_big"], w=["s_st"])
    A("act", lambda e: e.copy(out=Ps[:], in_=big[:]), r=["s_big"], w=["Ps"])
    A("act", lambda e: e.activation(out=big[:, 0:512], in_=Swin[:], func=AF.Exp, scale=SCALE), r=["Swin", "s_big", "Ps"], w=["s_big"])
    A("dve", lambda e: e.tensor_reduce(out=st[:, 7:8], in_=big[:, 0:512], axis=AX.X, op=ALU.add), r=["s_big"], w=["s_st"])
    A("act", lambda e: e.copy(out=Pw[:], in_=big[:, 0:512]), r=["s_big"], w=["Pw"])
    A("dve", lambda e: e.tensor_tensor(out=st[:, 8:9], in0=st[:, 6:7], in1=st[:, 3:4], op=ALU.add), r=["s_st"], w=["s_st"])
    A("dve", lambda e: e.tensor_tensor(out=st[:, 9:10], in0=st[:, 7:8], in1=st[:, 5:6], op=ALU.add), r=["s_st"], w=["s_st"])
    A("dve", lambda e: e.reciprocal(out=st[:, 8:10], in_=st[:, 8:10]), r=["s_st"], w=["s_st"])
    PTc = sb("PTc", [128, 16, 128], BF16, SP)
    PTs = sb("PTs", [128, 16, 128], BF16, SP)
    PTw = sb("PTw", [128, 4, 128], BF16, SP)
    for (src, dst, key, n) in ((Pc, PTc, "PTc", 16), (Ps, PTs, "PTs", 16), (Pw, PTw, "PTw", 4)):
        for i0 in range(0, n, 8):
            m = min(8, n - i0)
            for i in range(m):
                A("pe", lambda e, src=src, i=i, i0=i0: e.transpose(out=pKT[:, i, :], in_=src[:, (i0 + i) * 128:(i0 + i + 1) * 128],
                                                                   identity=ident[:]), r=["ident", "Pc", "Ps", "Pw"], w=["s_pKT"])
            A("dve", lambda e, dst=dst, i0=i0, m=m: e.tensor_copy(out=dst[:, i0:i0 + m, :], in_=pKT[:, 0:m, :]), r=["s_pKT"], w=[key])
    vst = kst
    vbf = [sb(f"vbf{i}", [128, 4, 2, 128], BF16, SP) for i in range(2)]
    vwst = [sb(f"vwst{i}", [128, 4, 128], F32, SP) for i in range(2)]
    vwbf = [sb(f"vwbf{i}", [128, 4, 128], BF16, SP) for i in range(2)]
    Oacc = sb("Oacc", [128, 3, 64], F32, SP)
    otmp = big[:, 0:1024].rearrange("p (b h d) -> p b h d", b=8, h=2)
    ored = sb("s_ored", [128, 64], F32, SP)
    A("dve", lambda e: e.memset(Oacc[:], 0.0), w=["Oacc"])
    pW2 = pst("s_pW2", [128, 512], F32, SP)
    pO = [[pA, pB], [pS[0], pS[1]], [pS[2], pW2]]
    pkeys = [["s_pA", "s_pB"], [("s_pS", 0), ("s_pS", 1)], [("s_pS", 2), "s_pW2"]]
    vcnt = 0
    for half in range(2):
        for bl in range(8):
            b = half * 8 + bl
            bank, cb = bl // 4, (bl % 4) * 128
            wi = b % 2
            A("sp", lambda e, wi=wi, b=b: e.dma_start(
                out=vwst[wi][:], in_=cwin[b].rearrange("(i p) (s c) -> p i s c", p=128, c=128)[:, :, 1, :]), w=[("vwst", wi)], dma=True)
            A("act", lambda e, wi=wi: e.copy(out=vwbf[wi][:], in_=vwst[wi][:]), r=[("vwst", wi)], w=[("vwbf", wi)])
            for q4 in range(4):
                vi = vcnt % NKB
                vb = vcnt % 2
                vcnt += 1
                for mm in range(2):
                    col = b * 8 + q4 * 2 + mm
                    A("pool", lambda e, vi=vi, mm=mm, col=col: e.indirect_dma_start(
                        out=vst[vi][:, 2 * mm:2 * mm + 2, :, :].rearrange("p e s c -> p (e s c)"), out_offset=None, in_=cacheR,
                        in_offset=bass.IndirectOffsetOnAxis(ap=idx[:, col:col + 1], axis=0)), r=["idx"],
                      w=[("kst", vi, 2 * mm), ("kst", vi, 2 * mm + 1)], dma=True)
                A("act", lambda e, vi=vi, vb=vb: e.copy(out=vbf[vb][:], in_=vst[vi][:, :, 1:4:2, :]),
                  r=[("kst", vi, i) for i in range(4)], w=[("vbf", vb)])
                for br, (PT_, ptk) in enumerate(((PTc, "PTc"), (PTs, "PTs"))):
                    for i in range(4):
                        pgi = q4 * 4 + i
                        A("pe", lambda e, br=br, bank=bank, cb=cb, PT_=PT_, i=i, pgi=pgi, vb=vb: e.matmul(
                            pO[br][bank][:, cb:cb + 128], lhsT=PT_[:, pgi, :], rhs=vbf[vb][:, i, br, :],
                            start=(pgi == 0), stop=(pgi == 15)), r=[ptk, ("vbf", vb)], w=[pkeys[br][bank]])
            for i in range(4):
                A("pe", lambda e, bank=bank, cb=cb, i=i, wi=wi: e.matmul(
                    pO[2][bank][:, cb:cb + 128], lhsT=PTw[:, i, :], rhs=vwbf[wi][:, i, :], start=(i == 0), stop=(i == 3)),
                  r=["PTw", ("vwbf", wi)], w=[pkeys[2][bank]])
        for br in range(3):
            for bank in range(2):
                A("dve", lambda e, br=br, bank=bank, half=half: e.tensor_tensor(
                    out=otmp[:, bank * 4:(bank + 1) * 4, :, :].rearrange("p b h d -> p (b h) d"),
                    in0=pO[br][bank][:].rearrange("p (j d) -> p j d", d=64),
                    in1=Mh[:, half, bank * 8:(bank + 1) * 8].unsqueeze(2).to_broadcast([128, 8, 64]), op=ALU.mult),
                  r=[pkeys[br][bank], "Mh"], w=["s_big"])
            A("dve", lambda e: e.tensor_reduce(out=ored[:], in_=otmp.rearrange("p b h d -> p d (b h)"), axis=AX.X, op=ALU.add),
              r=["s_big"], w=["s_ored"])
            A("dve", lambda e, br=br: e.tensor_tensor(out=Oacc[:, br, :], in0=Oacc[:, br, :], in1=ored[:], op=ALU.add),
              r=["s_ored", "Oacc"], w=["Oacc"])
    G3 = sb("G3", [128, 3], F32, SP)
    A("act", lambda e: e.activation(out=G3[:], in_=Grows[:], func=AF.Sigmoid), r=["Grows"], w=["G3"])
    arow = sb("arow", [128, 128], F32, SP)
    tmp64 = sb("tmp64", [128, 64], F32, SP)
    A("dve", lambda e: e.tensor_scalar(out=arow[:, 0:64], in0=Oacc[:, 0, :], scalar1=G3[:, 0:1], scalar2=None, op0=ALU.mult),
      r=["Oacc", "G3"], w=["arow"])
    for j, br in ((0, 1), (1, 2)):
        A("dve", lambda e, j=j, br=br: e.scalar_tensor_tensor(out=tmp64[:], in0=Vn[:, j, :], scalar=st[:, 3 + 2 * j:4 + 2 * j],
                                                              in1=Oacc[:, br, :], op0=ALU.mult, op1=ALU.add),
          r=["Vn", "s_st", "Oacc"], w=["tmp64"])
        A("dve", lambda e, j=j, br=br: e.tensor_scalar(out=tmp64[:], in0=tmp64[:], scalar1=st[:, 8 + j:9 + j], scalar2=G3[:, br:br + 1],
                                                       op0=ALU.mult, op1=ALU.mult), r=["tmp64", "s_st", "G3"], w=["tmp64"])
        A("dve", lambda e: e.tensor_tensor(out=arow[:, 0:64], in0=arow[:, 0:64], in1=tmp64[:], op=ALU.add), r=["arow", "tmp64"], w=["arow"])
    glus = sb("glus", [NSB, 512], F32, SP)
    cvb = sb("cvb", [NSB, 4, 512], F32, SP)
    A("sp", lambda e: e.dma_start(out=cvb[:], in_=wdall[30:34, :].partition_broadcast(NSB)), w=["cvb"], dma=True)
    A("act", lambda e: e.activation(out=glus[:], in_=zs[:, 512:1024], func=AF.Sigmoid), r=["zs"], w=["glus"])
    A("dve", lambda e: e.tensor_tensor(out=glus[:], in0=glus[:], in1=zs[:, 0:512], op=ALU.mult), r=["glus", "zs"], w=["glus"])
    A("sp", lambda e: e.dma_start(out=conv_s.rearrange("(b j) c -> b j c", j=30)[:, 29, :], in_=glus[:]), r=["glus"], dma=True)
    Xc = sb("Xc", [120, 512], F32, SP)
    Wrep = sb("Wrep", [120, 512], F32, SP)
    sel4 = sb("sel4", [120, 4, NSB], F32, SP)
    for r4 in range(4):
        A("sp", lambda e, r4=r4: e.dma_start(out=Wrep[r4 * 30:(r4 + 1) * 30, :], in_=wdall[0:30, :]), w=["Wrep"], dma=True)
    A("pool", lambda e: e.memset(sel4[:], 1.0), w=["sel4"])
    for i4 in range(4):
        A("pool", lambda e, i4=i4: e.affine_select(out=sel4[:, i4, :], in_=sel4[:, i4, :], pattern=[[-30, NSB]], compare_op=ALU.is_ge,
                                                   fill=0.0, base=120 * i4, channel_multiplier=1), r=["sel4"], w=["sel4"])
        A("pool", lambda e, i4=i4: e.affine_select(out=sel4[:, i4, :], in_=sel4[:, i4, :], pattern=[[30, NSB]], compare_op=ALU.is_ge,
                                                   fill=0.0, base=29 - 120 * i4, channel_multiplier=-1), r=["sel4"], w=["sel4"])
    for i4 in range(4):
        A("sp", lambda e, i4=i4: e.dma_start(out=Xc[:], in_=sconv[i4 * 120:(i4 + 1) * 120, :]), w=["Xc"], dma=True)
        A("dve", lambda e: e.tensor_tensor(out=Xc[:], in0=Xc[:], in1=Wrep[:], op=ALU.mult), r=["Xc", "Wrep"], w=["Xc"])
        A("pe", lambda e, i4=i4: e.matmul(pB[0:NSB, :], lhsT=sel4[:, i4, :], rhs=Xc[:], start=(i4 == 0), stop=(i4 == 3)),
          r=["sel4", "Xc"], w=["s_pB"])
    yc = sb("yc", [NSB, 512], F32, SP)
    ycs = sb("ycs", [NSB, 4], F32, SP)
    A("dve", lambda e: e.tensor_tensor(out=yc[:], in0=glus[:], in1=cvb[:, 0, :], op=ALU.mult), r=["glus", "cvb"], w=["yc"])
    A("dve", lambda e: e.tensor_tensor(out=yc[:], in0=yc[:], in1=pB[0:NSB, :], op=ALU.add), r=["yc", "s_pB"], w=["yc"])
    A("dve", lambda e: e.tensor_tensor(out=yc[:], in0=yc[:], in1=cvb[:, 1, :], op=ALU.add), r=["yc", "cvb"], w=["yc"])
    A("dve", lambda e: e.tensor_reduce(out=ycs[:, 0:1], in_=yc[:], axis=AX.X, op=ALU.add), r=["yc"], w=["ycs"])
    A("dve", lambda e: e.tensor_scalar(out=ycs[:, 0:1], in0=ycs[:, 0:1], scalar1=1.0 / 512.0, scalar2=None, op0=ALU.mult), r=["ycs"], w=["ycs"])
    A("dve", lambda e: e.tensor_scalar(out=yc[:], in0=yc[:], scalar1=ycs[:, 0:1], scalar2=None, op0=ALU.subtract), r=["yc", "ycs"], w=["yc"])
    ysq_ = attn_s_early = sb("ysq_", [NSB, 512], F32, SP)
    A("dve", lambda e: e.tensor_tensor(out=ysq_[:], in0=yc[:], in1=yc[:], op=ALU.mult), r=["yc"], w=["ysq_"])
    A("dve", lambda e: e.tensor_reduce(out=ycs[:, 1:2], in_=ysq_[:], axis=AX.X, op=ALU.add), r=["ysq_"], w=["ycs"])
    A("dve", lambda e: e.tensor_scalar(out=ycs[:, 1:2], in0=ycs[:, 1:2], scalar1=1.0 / 512.0, scalar2=1e-5, op0=ALU.mult, op1=ALU.add),
      r=["ycs"], w=["ycs"])
    A("act", lambda e: e.activation(out=ycs[:, 1:2], in_=ycs[:, 1:2], func=AF.Sqrt), r=["ycs"], w=["ycs"])
    A("dve", lambda e: e.reciprocal(out=ycs[:, 1:2], in_=ycs[:, 1:2]), r=["ycs"], w=["ycs"])
    A("dve", lambda e: e.scalar_tensor_tensor(out=yc[:], in0=yc[:], scalar=ycs[:, 1:2], in1=cvb[:, 2, :], op0=ALU.mult, op1=ALU.mult),
      r=["yc", "ycs", "cvb"], w=["yc"])
    A("dve", lambda e: e.tensor_tensor(out=yc[:], in0=yc[:], in1=cvb[:, 3, :], op=ALU.add), r=["yc", "cvb"], w=["yc"])
    ycb = sb("ycb", [NSB, 512], BF16, SP)
    A("act", lambda e: e.activation(out=ycb[:], in_=yc[:], func=AF.Silu), r=["yc"], w=["ycb"])
    cyT = sb("cyT", [128, 4, NSB], BF16, SP)
    for c4 in range(4):
        A("pe", lambda e, c4=c4: e.transpose(out=pTb[:, c4, 0:NSB], in_=ycb[:, c4 * 128:(c4 + 1) * 128], identity=ident[0:NSB, 0:NSB]),
          r=["ycb", "ident"], w=["pTb"])
    A("act", lambda e: e.copy(out=cyT[:], in_=pTb[:, 0:4, 0:NSB]), r=["pTb"], w=["cyT"])
    as_d = nc.dram_tensor("as_d", [128, 64], F32, kind="Internal").ap()
    attn_s = ysq_
    attn_sb = sb("attn_sb", [NSB, 512], BF16, SP)
    aT2 = sb("aT2", [128, 4, NSB], BF16, SP)
    A("sp", lambda e: e.dma_start(out=as_d, in_=arow[:, 0:64]), r=["arow"], w=["as_d"], dma=True)
    A("sp", lambda e: e.dma_start(out=attn_s[:], in_=as_d.rearrange("(b h) d -> b (h d)", h=8)), r=["as_d"], w=["ysq_"], dma=True)
    A("act", lambda e: e.copy(out=attn_sb[:], in_=attn_s[:]), r=["ysq_"], w=["attn_sb"])
    for c4 in range(4):
        A("pe", lambda e, c4=c4: e.transpose(out=pTb[:, 4 + c4, 0:NSB], in_=attn_sb[:, c4 * 128:(c4 + 1) * 128], identity=ident[0:NSB, 0:NSB]),
          r=["attn_sb", "ident"], w=["pTb"])
    A("act", lambda e: e.copy(out=aT2[:], in_=pTb[:, 4:8, 0:NSB]), r=["pTb"], w=["aT2"])
    for half in range(2):
        cs = slice(half * 512, (half + 1) * 512)
        for k in range(8):
            lhs = cyT[:, k, :] if k < 4 else aT2[:, k - 4, :]
            A("pe", lambda e, k=k, cs=cs, lhs=lhs: e.matmul(pA[0:NSB, :], lhsT=lhs, rhs=w_out_bf[:, k, cs], start=(k == 0), stop=(k == 7)),
              r=["cyT", "aT2", ("w_out_bf", k)], w=["s_pA"])
        A("dve", lambda e, cs=cs: e.tensor_tensor(out=hs_acc[:, cs], in0=pA[0:NSB, :], in1=xs_sb[:, cs], op=ALU.add),
          r=["s_pA", "xs_sb"], w=["hs_acc"])
    emit_norm_T(S, hs_acc[:], "hs_acc", junk, ssx, hn, gbc_mlp, "gbc_mlp", ident, pTb, None, hnTs[:], "hnTs")
    if debug:
        dbg["arow"] = dout("d_arow", [128, 128], F32)
        A("sp", lambda e: e.dma_start(out=dbg["arow"], in_=arow[:]), r=["arow"], dma=True)
        dbg["hs"] = dout("d_hs", [NSB, DM], F32)
        A("sp", lambda e: e.dma_start(out=dbg["hs"], in_=hs_acc[:]), r=["hs_acc"], dma=True)
        dbg["ycb"] = dout("d_ycb", [NSB, 512], BF16)
        A("sp", lambda e: e.dma_start(out=dbg["ycb"], in_=ycb[:]), r=["ycb"], dma=True)
        dbg["Oacc"] = dout("d_Oacc", [128, 192], F32)
        A("sp", lambda e: e.dma_start(out=dbg["Oacc"], in_=Oacc[:]), r=["Oacc"], dma=True)
        dbg["st"] = dout("d_st", [128, 16], F32)
        A("sp", lambda e: e.dma_start(out=dbg["st"], in_=st[:]), r=["s_st"], dma=True)
        dbg["selm"] = dout("d_selm", [128, 32], F32)
        A("sp", lambda e: e.dma_start(out=dbg["selm"], in_=selm[:]), r=["s_selm"], dma=True)
```

```python
import contextlib
import os
import numpy as np
import concourse.bass as bass
import concourse.mybir as mybir
from concourse.bass_utils import run_bass_kernel_spmd

F32 = mybir.dt.float32
BF16 = mybir.dt.bfloat16
I32 = mybir.dt.int32
AF = mybir.ActivationFunctionType
ALU = mybir.AluOpType
AX = mybir.AxisListType

EPOCH = 4000
N_DMA_SEMS = 8

SEQ = 2048
DM = 1024
NT = 16
INC = 2328
SCALE = 0.125
BIG = 20000.0
NSB = 16
N_CORES = 8


class Sched:
    def __init__(self, nc):
        self.nc = nc
        self.ops = []
        self.last_writer = {}
        self.readers = {}
        self.cur_barrier = 0

    def add(self, eng, fn, r=(), w=(), dma=False):
        deps = set()
        for k in r:
            if k in self.last_writer:
                deps.add(self.last_writer[k])
        for k in w:
            if k in self.last_writer:
                deps.add(self.last_writer[k])
            deps.update(self.readers.get(k, ()))
        idx = len(self.ops)
        self.ops.append(dict(eng=eng, fn=fn, deps=sorted(deps), dma=dma, barrier=self.cur_barrier))
        for k in r:
            self.readers.setdefault(k, []).append(idx)
        for k in w:
            self.last_writer[k] = idx
            self.readers[k] = []
        return idx

    def barrier(self):
        self.cur_barrier = len(self.ops)

    def emit(self, final_wait_engine="sp"):
        nc = self.nc
        engs = ["pe", "act", "dve", "pool", "sp"]
        ops = self.ops
        cnt = {e: 0 for e in engs}
        dcnt = {e: 0 for e in engs}
        need = set()
        for op in ops:
            e = op["eng"]
            if op["dma"]:
                k = dcnt[e] % N_DMA_SEMS
                n = dcnt[e] // N_DMA_SEMS
                dcnt[e] += 1
                op["ticket"] = (("d", e, k), 16 * (n + 1))
                op["prev"] = (("d", e, k), 16 * n) if n > 0 else None
            else:
                ep = cnt[e] // EPOCH
                v = cnt[e] % EPOCH + 1
                cnt[e] += 1
                op["ticket"] = (("c", e, ep), v)
            need.add(op["ticket"][0])
        bounds = sorted(set(op["barrier"] for op in ops))
        btk = {}
        run = {}
        bi = 0
        for i, op in enumerate(ops):
            while bi < len(bounds) and bounds[bi] <= i:
                btk[bounds[bi]] = dict(run)
                bi += 1
            sn, v = op["ticket"]
            run[sn] = max(run.get(sn, 0), v)
        while bi < len(bounds):
            btk[bounds[bi]] = dict(run)
            bi += 1
        with contextlib.ExitStack() as st:
            sems = {}
            for sn in sorted(need):
                sems[sn] = st.enter_context(nc.semaphore("s_" + "_".join(map(str, sn))))
            block = st.enter_context(nc.Block())

            def make(e):
                def body(engine):
                    known = {}
                    seen_barrier = [0]

                    def wait(t):
                        sn, v = t
                        if sn[0] == "c":
                            for sn2 in known:
                                if sn2[0] == "c" and sn2[1] == sn[1] and sn2[2] > sn[2]:
                                    return
                        if known.get(sn, 0) >= v:
                            return
                        engine.wait_ge(sems[sn], v)
                        known[sn] = v

                    for op in ops:
                        if op["eng"] != e:
                            continue
                        if op["barrier"] > seen_barrier[0]:
                            seen_barrier[0] = op["barrier"]
                            for sn, v in sorted(btk[op["barrier"]].items()):
                                if sn == ("c", e, sn[2]) and e == "pe":
                                    continue
                                wait((sn, v))
                        for d in op["deps"]:
                            dop = ops[d]
                            if dop["eng"] == e and e == "pe" and not dop["dma"]:
                                continue
                            wait(dop["ticket"])
                        if op["dma"] and op["prev"] is not None:
                            wait(op["prev"])
                        ins = op["fn"](engine)
                        sn, v = op["ticket"]
                        ins.then_inc(sems[sn], 16 if op["dma"] else 1)
                    if e == final_wait_engine:
                        fin = {}
                        for op in ops:
                            sn, v = op["ticket"]
                            fin[sn] = max(fin.get(sn, 0), v)
                        for sn, v in sorted(fin.items()):
                            wait((sn, v))
                return body

            block.tensor(make("pe"))
            block.scalar(make("act"))
            block.vector(make("dve"))
            block.gpsimd(make("pool"))
            block.sync(make("sp"))


def build(stages=("p1", "p2", "p3", "samp"), debug=False, cache_rows=2560 * 128 * 4):
    nc = bass.Bass("TRN2", target_bir_lowering=False)

    def din(name, shape, dt=F32):
        return nc.dram_tensor(name, shape, dt, kind="ExternalInput").ap()

    def dout(name, shape, dt=F32):
        return nc.dram_tensor(name, shape, dt, kind="ExternalOutput").ap()

    xp = din("xp", [SEQ, DM])
    xs = din("xs", [NSB, DM])
    cache = din("cache", [cache_rows, 128])
    cwin = din("cwin", [NSB, 512, 256])
    sconv = din("sconv", [NSB * 30, 512])
    ptab = din("ptab", [1, NSB * 16], I32)
    g_attn = din("g_attn", [1, DM])
    w_in = din("w_in", [DM, INC])
    wdall = din("wdall", [34, 512])
    w_ck = din("w_ck", [32, 2])
    w_cv = din("w_cv", [32, 2])
    w_out = din("w_out", [DM, DM])
    g_mlp = din("g_mlp", [1, DM])
    w_up = din("w_up", [DM, 4096])
    w_down = din("w_down", [4096, DM])
    g_fin = din("g_fin", [1, DM])

    y_p = dout("y_p", [SEQ, DM])
    y_s = dout("y_s", [NSB, DM])
    kv_p = dout("kv_p", [SEQ, 512])
    win_p = dout("win_p", [512, 256])
    conv_p = dout("conv_p", [30, 512])
    kv_s = dout("kv_s", [NSB, 512])
    win_s = dout("win_s", [NSB, 512, 256])
    conv_s = dout("conv_s", [NSB * 30, 512])
    dbg = {}

    S = Sched(nc)
    A = S.add
    ES = contextlib.ExitStack()

    def sb(name, shape, dt, stack=None, side=None):
        return (stack or ES).enter_context(nc.sbuf_tensor(name, shape, dt, side=side))

    def pst(name, shape, dt, stack):
        return stack.enter_context(nc.psum_tensor(name, shape, dt))

    with ES:
        identf = sb("identf", [128, 128], F32)
        ident = sb("ident", [128, 128], BF16)
        A("pool", lambda e: e.memset(identf[:], 0.0), w=["identf"])
        A("pool", lambda e: e.affine_select(out=identf[:], in_=identf[:], pattern=[[-1, 128]],
                                            compare_op=ALU.not_equal, fill=1.0, base=0, channel_multiplier=1),
          r=["identf"], w=["identf"])
        A("dve", lambda e: e.tensor_copy(out=ident[:], in_=identf[:]), r=["identf"], w=["ident"])
        gbc_mlp = sb("gbc_mlp", [128, DM], F32)
        A("sp", lambda e: e.dma_start(out=gbc_mlp[:], in_=g_mlp.partition_broadcast(128)), w=["gbc_mlp"], dma=True)
        hs_acc = sb("hs_acc", [NSB, DM], F32)
        hss = sb("hss", [128, NT], F32)
        A("pool", lambda e: e.memset(hss[:], 0.0), w=["hss"])
        hnTs = sb("hnTs", [128, 8, NSB], BF16)
        w_out_bf = sb("w_out_bf", [128, 8, DM], BF16)
        cw = sb("cw", [128, 4, 34], F32)
        wckb = sb("wckb", [128, 2, 32], F32)
        wcvb = sb("wcvb", [128, 2, 32], F32)
        PmB = [sb(f"PmB{h}", [128, 124], BF16) for h in range(2)]

        RS1 = contextlib.ExitStack()
        w_in_bf = sb("w_in_bf", [128, 8, INC], BF16, RS1, side="right")

        with contextlib.ExitStack() as W0:
            stg = [sb(f"stg{i}", [128, INC], F32, W0) for i in range(2)]
            ci = 0
            for (wsrc, wdst, wkey, ncol) in ((w_in, w_in_bf, "w_in_bf", INC), (w_out, w_out_bf, "w_out_bf", DM)):
                for k in range(8):
                    b = ci % 2
                    A("sp", lambda e, k=k, b=b, wsrc=wsrc, ncol=ncol: e.dma_start(out=stg[b][:, 0:ncol], in_=wsrc[k * 128:(k + 1) * 128, :]),
                      w=[("stg", b)], dma=True)
                    if ci % 2 == 0:
                        A("act", lambda e, k=k, b=b, wdst=wdst, ncol=ncol: e.copy(out=wdst[:, k, :], in_=stg[b][:, 0:ncol]),
                          r=[("stg", b)], w=[(wkey, k)])
                    else:
                        A("dve", lambda e, k=k, b=b, wdst=wdst, ncol=ncol: e.tensor_copy(out=wdst[:, k, :], in_=stg[b][:, 0:ncol]),
                          r=[("stg", b)], w=[(wkey, k)])
                    ci += 1
            wd_sb = sb("wd_sb", [34, 512], F32, W0)
            A("sp", lambda e: e.dma_start(out=wd_sb[:], in_=wdall), w=["wd_sb"], dma=True)
            with contextlib.ExitStack() as PW:
                pcw = pst("pcw", [128, 4, 34], F32, PW)
                for c4 in range(4):
                    A("pe", lambda e, c4=c4: e.transpose(out=pcw[:, c4, :], in_=wd_sb[:, c4 * 128:(c4 + 1) * 128],
                                                         identity=identf[0:34, 0:34]),
                      r=["wd_sb", "identf"], w=[("pcw", c4)])
                A("dve", lambda e: e.tensor_copy(out=cw[:], in_=pcw[:]), r=[("pcw", c4) for c4 in range(4)], w=["cw"])
                S.barrier()
            wraw = sb("wraw", [128, 2, 64], F32, W0)
            A("sp", lambda e: e.dma_start(out=wraw[:, 0, :], in_=w_ck.rearrange("j h -> (j h)").partition_broadcast(128)), w=["wraw"], dma=True)
            A("sp", lambda e: e.dma_start(out=wraw[:, 1, :], in_=w_cv.rearrange("j h -> (j h)").partition_broadcast(128)), w=["wraw"], dma=True)
            A("dve", lambda e: e.tensor_copy(out=wckb[:], in_=wraw[:, 0, :].rearrange("p (j h) -> p h j", h=2)), r=["wraw"], w=["wckb"])
            A("dve", lambda e: e.tensor_copy(out=wcvb[:], in_=wraw[:, 1, :].rearrange("p (j h) -> p h j", h=2)), r=["wraw"], w=["wcvb"])
            wcol = sb("wcol", [128, 2], F32, W0)
            for rr in range(4):
                A("sp", lambda e, rr=rr: e.dma_start(out=wcol[rr * 32:(rr + 1) * 32, :], in_=w_cv), w=["wcol"], dma=True)
            pmf = sb("pmf", [128, 4], F32, W0)
            A("pool", lambda e: e.memset(pmf[:], 1.0), w=["pmf"])
            A("pool", lambda e: e.affine_select(out=pmf[:], in_=pmf[:], pattern=[[-32, 4]], compare_op=ALU.is_ge,
                                                fill=0.0, base=0, channel_multiplier=1), r=["pmf"], w=["pmf"])
            A("pool", lambda e: e.affine_select(out=pmf[:], in_=pmf[:], pattern=[[32, 4]], compare_op=ALU.is_ge,
                                                fill=0.0, base=31, channel_multiplier=-1), r=["pmf"], w=["pmf"])
            for h in range(2):
                A("pool", lambda e, h=h: e.memset(PmB[h][:], 0.0), w=[("PmB", h)])
                A("dve", lambda e, h=h: e.tensor_scalar(out=PmB[h][:, 60:64], in0=pmf[:], scalar1=wcol[:, h:h + 1],
                                                        scalar2=None, op0=ALU.mult),
                  r=["pmf", "wcol", ("PmB", h)], w=[("PmB", h)])
            S.barrier()

        if "samp" in stages:
            with contextlib.ExitStack() as SP:
                build_samp(nc, S, SP, sb, pst, locals())
                S.barrier()

        ATT = contextlib.ExitStack()
        with ATT:
            QTs = [sb(f"QTs{h}", [96, 4, SEQ], BF16, ATT) for h in range(2)]
            KTs = [sb(f"KTs{h}", [96, SEQ], BF16, ATT) for h in range(2)]
            KTw = [sb(f"KTw{h}", [64, SEQ], BF16, ATT) for h in range(2)]
            kcT = [sb(f"kcT{h}", [64, 64], BF16, ATT) for h in range(2)]
            Vaug = sb("Vaug", [128, NT, 3, 2, 65], BF16, ATT)
            vcaug = sb("vcaug", [64, 2, 65], BF16, ATT)
            gates = sb("gates", [128, NT, 24], F32, ATT)
            convyT = sb("convyT", [128, 4, SEQ], BF16, ATT)
            for h in range(2):
                A("pool", lambda e, h=h: e.memset(KTs[h][64:96, :], 1.0), w=[("KTsaug", h)])
                A("pool", lambda e, h=h: e.affine_select(out=KTs[h][64:96, :], in_=KTs[h][64:96, :], pattern=[[1, SEQ]],
                                                         compare_op=ALU.is_ge, fill=0.0, base=0, channel_multiplier=-64),
                  r=[("KTsaug", h)], w=[("KTsaug", h)])
                A("pool", lambda e, h=h: e.affine_select(out=KTs[h][64:96, :], in_=KTs[h][64:96, :], pattern=[[-1, SEQ]],
                                                         compare_op=ALU.is_ge, fill=0.0, base=63, channel_multiplier=64),
                  r=[("KTsaug", h)], w=[("KTsaug", h)])
            A("pool", lambda e: e.memset(Vaug[:], 1.0), w=["Vaug_init"])
            A("pool", lambda e: e.memset(vcaug[:], 1.0), w=["vcaug_init"])

            with contextlib.ExitStack() as P1:
                build_p1(nc, S, P1, sb, pst, locals())
                S.barrier()
            RS1.close()
            RS2 = contextlib.ExitStack()
            h_acc = sb("h_acc", [128, NT, DM], F32, RS2, side="right")
            if "p2" in stages:
                with contextlib.ExitStack() as P2:
                    build_p2(nc, S, P2, sb, pst, locals())
                    S.barrier()
        if "p3" in stages:
            with contextlib.ExitStack() as P3:
                build_p3(nc, S, P3, sb, pst, locals())
        RS2.close()
        S.emit()
    return nc, dbg


def build_p1(nc, S, P1, sb, pst, L):
    A = S.add
    (QTs, KTs, KTw, kcT, Vaug, vcaug, gates, convyT, cw, wckb, PmB, w_in_bf, ident, identf, xp, g_attn, kv_p, win_p,
     conv_p) = (L[k] for k in ("QTs", "KTs", "KTw", "kcT", "Vaug", "vcaug", "gates", "convyT", "cw", "wckb", "PmB",
                               "w_in_bf", "ident", "identf", "xp", "g_attn", "kv_p", "win_p", "conv_p"))
    debug, dbg, dout = L["debug"], L["dbg"], L["dout"]
    onesf = sb("onesf", [128, 128], F32, P1)
    A("pool", lambda e: e.memset(onesf[:], 1.0 / 512.0), w=["onesf"])
    gbc_attn = sb("gbc_attn", [128, DM], F32, P1)
    A("sp", lambda e: e.dma_start(out=gbc_attn[:], in_=g_attn.partition_broadcast(128)), w=["gbc_attn"], dma=True)
    xt = [sb(f"xt{i}", [128, DM], F32, P1) for i in range(2)]
    ssall = sb("ssall", [128, NT], F32, P1)
    xn = [sb("xn0", [128, DM], BF16, P1)] * 2
    A("pool", lambda e: e.memset(ssall[:], 0.0), w=["ssall"])
    for t in range(NT):
        A("sp", lambda e, t=t: e.dma_start(out=xt[t % 2][:], in_=xp[t * 128:(t + 1) * 128, :]), w=[("xt", t % 2)], dma=True)
        A("act", lambda e, t=t: e.activation(out=xn[t % 2][:], in_=xt[t % 2][:], func=AF.Square, accum_out=ssall[:, t:t + 1]),
          r=[("xt", t % 2), "ssall"], w=["xn", "ssall"])
    A("dve", lambda e: e.tensor_scalar(out=ssall[:], in0=ssall[:], scalar1=1.0 / DM, scalar2=1e-6, op0=ALU.mult, op1=ALU.add),
      r=["ssall"], w=["ssall"])
    A("act", lambda e: e.activation(out=ssall[:], in_=ssall[:], func=AF.Sqrt), r=["ssall"], w=["ssall"])
    A("dve", lambda e: e.reciprocal(out=ssall[:], in_=ssall[:]), r=["ssall"], w=["ssall"])
    xnT = [sb("xnT0", [128, 8, 512], BF16, P1)] * 2
    glu = [sb(f"glu{i}", [128, 4, 542], BF16, P1) for i in range(2)]
    glutail = sb("glutail", [128, 4, 30], F32, P1)
    glo = [sb(f"glo{i}", [128, 4, 542], BF16, P1) for i in range(2)]
    Dg = [sb(f"Dg{i}", [128, 16, 128], BF16, P1) for i in range(2)]
    sgt = sb("sgt", [128, 512], F32, P1)
    ych = sb("ych", [128, 4, 512], F32, P1)
    mean_sb = sb("mean_sb", [128, 512], F32, P1)
    rstd_sb = sb("rstd_sb", [128, 512], F32, P1)
    ysq = rstd_sb
    kcp = mean_sb[0:64, :].rearrange("p (c j) -> p c j", j=32)
    zt = sb("zt", [128, 792], F32, P1)
    kcf = sb("kcf", [64, 16], F32, P1)
    psF = [pst(f"psF{i}", [128, 512], F32, P1) for i in range(2)]
    psTM = pst("psTM", [128, 1024], F32, P1)
    psT = pst("psT", [128, 8, 128], BF16, P1)
    psVC = pst("psVC", [128, 512], F32, P1)
    psMean = pst("psMean", [128, 512], F32, P1)
    psMsq = pst("psMsq", [128, 512], F32, P1)

    A("pool", lambda e: e.memset(glu[0][:, :, 0:30], 0.0), w=[("gluhead", 0)])
    A("pool", lambda e: e.memset(glo[0][:, :, 0:30], 0.0), w=[("gluhead", 0)])
    fcnt = [0]

    def fm_mm(xT, c0, M, Gk):
        b = fcnt[0] % 2
        fcnt[0] += 1
        for k in range(8):
            A("pe", lambda e, k=k, b=b: e.matmul(psF[b][0:M, :], lhsT=w_in_bf[:, k, c0:c0 + M], rhs=xT[:, k, :],
                                                 start=(k == 0), stop=(k == 7)),
              r=[("w_in_bf", k), Gk], w=[("psF", b)])
        return b

    def p1_front(G):
        xTg = xnT[G % 2]
        Gk = "xnT"
        gl = glu[G % 2]
        gln = glu[(G + 1) % 2]
        tok = slice(G * 512, (G + 1) * 512)
        for tt in range(4):
            t = 4 * G + tt
            xb = t % 2
            A("sp", lambda e, t=t, xb=xb: e.dma_start(out=xt[xb][:], in_=xp[t * 128:(t + 1) * 128, :]), w=[("xt", xb)], dma=True)
            A("dve", lambda e, t=t, xb=xb: e.scalar_tensor_tensor(out=xn[xb][:], in0=xt[xb][:], scalar=ssall[:, t:t + 1],
                                                                  in1=gbc_attn[:], op0=ALU.mult, op1=ALU.mult),
              r=[("xt", xb), "ssall", "gbc_attn"], w=["xn"])
            for k in range(8):
                A("pe", lambda e, k=k, xb=xb: e.transpose(out=psT[:, k, :], in_=xn[xb][:, k * 128:(k + 1) * 128], identity=ident[:]),
                  r=["xn", "ident"], w=["psT"])
            A("act", lambda e, tt=tt, xTg=xTg: e.copy(out=xTg[:, :, tt * 128:(tt + 1) * 128], in_=psT[:]),
              r=["psT"], w=[Gk])
            yield
        for c4 in range(4):
            ba = fm_mm(xTg, c4 * 128, 128, Gk)
            bb = fm_mm(xTg, 512 + c4 * 128, 128, Gk)
            A("act", lambda e, bb=bb: e.activation(out=sgt[:], in_=psF[bb][:], func=AF.Sigmoid), r=[("psF", bb)], w=["sgt"])
            A("dve", lambda e, ba=ba, c4=c4, gl=gl: e.tensor_tensor(out=gl[:, c4, 30:542], in0=psF[ba][:], in1=sgt[:], op=ALU.mult),
              r=[("psF", ba), "sgt"], w=[("glu", G % 2, c4)])
            A("dve", lambda e, ba=ba, c4=c4, G=G: e.tensor_tensor(out=glo[G % 2][:, c4, 29:541], in0=psF[ba][:], in1=sgt[:], op=ALU.mult),
              r=[("psF", ba), "sgt", ("gluhead", G % 2)], w=[("glu", G % 2, c4)])
            if G == 3:
                A("dve", lambda e, ba=ba, c4=c4: e.tensor_tensor(out=glutail[:, c4, :], in0=psF[ba][:, 482:512], in1=sgt[:, 482:512],
                                                                 op=ALU.mult), r=[("psF", ba), "sgt"], w=["glutail"])
            yield
        for hd in range(8):
            h, g = hd // 4, hd % 4
            b = fm_mm(xTg, 1024 + hd * 64, 64, Gk)
            A("act", lambda e, b=b, h=h, g=g, tok=tok: e.copy(out=QTs[h][0:64, g, tok], in_=psF[b][0:64, :]),
              r=[("psF", b)], w=[("QT", h, G)])
            if hd % 4 == 3:
                yield
        for h in range(2):
            b = fm_mm(xTg, 1536 + h * 64, 64, Gk)
            A("act", lambda e, b=b: e.copy(out=mean_sb[0:64, :], in_=psF[b][0:64, :]), r=[("psF", b)], w=["mean_sb"])
            A("pool", lambda e, h=h: e.tensor_tensor(
                out=kcp, in0=kcp, in1=wckb[0:64, h, :].unsqueeze(1).to_broadcast([64, 16, 32]), op=ALU.mult),
              r=["mean_sb", "wckb"], w=["mean_sb"])
            A("dve", lambda e: e.tensor_reduce(out=kcf[:], in_=kcp, axis=AX.X, op=ALU.add), r=["mean_sb"], w=["kcf"])
            A("pool", lambda e, h=h, G=G: e.tensor_copy(out=kcT[h][:, G * 16:(G + 1) * 16], in_=kcf[:]),
              r=["kcf"], w=[("kcT", h, G)])
            b = fm_mm(xTg, 1536 + 256 + h * 64, 64, Gk)
            A("act", lambda e, b=b, h=h, tok=tok: e.copy(out=KTs[h][0:64, tok], in_=psF[b][0:64, :]),
              r=[("psF", b)], w=[("KTs", h, G)])
            b = fm_mm(xTg, 1536 + 512 + h * 64, 64, Gk)
            A("act", lambda e, b=b, h=h, tok=tok: e.copy(out=KTw[h][:, tok], in_=psF[b][0:64, :]),
              r=[("psF", b)], w=[("KTw", h, G)])
            yield
        for tt in range(4):
            t = 4 * G + tt
            for (c0, n, o0) in ((1536, 512, 0), (2048, 280, 512)):
                for k in range(8):
                    A("pe", lambda e, k=k, tt=tt, c0=c0, n=n, o0=o0, xTg=xTg: e.matmul(
                        psTM[:, o0:o0 + n], lhsT=xTg[:, k, tt * 128:(tt + 1) * 128], rhs=w_in_bf[:, k, c0:c0 + n],
                        start=(k == 0), stop=(k == 7)),
                      r=[("w_in_bf", k), Gk], w=["psTM"])
            A("act", lambda e: e.copy(out=zt[:], in_=psTM[:, 0:792]), r=["psTM"], w=["zt"])
            A("sp", lambda e, t=t: e.dma_start(out=kv_p[t * 128:(t + 1) * 128, :], in_=zt[:, 0:512]), r=["zt"], dma=True)
            if t >= 12:
                A("sp", lambda e, t=t: e.dma_start(out=win_p[(t - 12) * 128:(t - 11) * 128, :], in_=zt[:, 512:768]),
                  r=["zt"], dma=True)
            A("act", lambda e, t=t: e.activation(out=gates[:, t, :], in_=zt[:, 768:792], func=AF.Sigmoid),
              r=["zt"], w=[("gates", t)])
            for s3 in range(3):
                A("pool", lambda e, t=t, s3=s3: e.tensor_copy(
                    out=Vaug[:, t, s3, :, 0:64],
                    in_=zt[:, 128 + 256 * s3:256 + 256 * s3].rearrange("p (h d) -> p h d", d=64)),
                  r=["zt", "Vaug_init"], w=[("Vaug", t)])
            for h in range(2):
                A("pe", lambda e, t=t, h=h: e.matmul(psVC[0:64, h * 64:(h + 1) * 64], lhsT=PmB[h][:, 60 - 4 * t:124 - 4 * t],
                                                     rhs=Vaug[:, t, 0, h, 0:64], start=(t == 0 and h == 0), stop=(t == NT - 1),
                                                     skip_group_check=True),
                  r=[("PmB", h), ("Vaug", t)], w=["psVC"])
            yield

    def p1_conv(G):
        xTg = xnT[G % 2]
        Gk = "xnT"
        gl = glu[G % 2]
        gln = glu[(G + 1) % 2]
        tok = slice(G * 512, (G + 1) * 512)
        for c4 in range(4):
            rk = [("glu", G % 2, c4), ("gluhead", G % 2)]
            for di, (j0, nj) in enumerate(((0, 16), (16, 15))):
                A("pool", lambda e, c4=c4, di=di, j0=j0, nj=nj: e.tensor_tensor(
                    out=Dg[di][:, 0:nj, :], in0=ident[:].unsqueeze(1).to_broadcast([128, nj, 128]),
                    in1=cw[:, c4, j0:j0 + nj].unsqueeze(2).to_broadcast([128, nj, 128]), op=ALU.mult), r=["ident", "cw"], w=[("Dg", di)])
            for j in range(31):
                di, jj = (0, j) if j < 16 else (1, j - 16)
                src = gl[:, c4, j:j + 512] if j % 2 == 0 else glo[G % 2][:, c4, j - 1:j - 1 + 512]
                A("pe", lambda e, j=j, src=src, di=di, jj=jj: e.matmul(psMsq[:], lhsT=Dg[di][:, jj, :], rhs=src,
                                                                       start=(j == 0), stop=(j == 30)),
                  r=rk + [("Dg", di)], w=["psMsq"])
            A("act", lambda e, c4=c4: e.activation(out=ych[:, c4, :], in_=psMsq[:], func=AF.Identity, bias=cw[:, c4, 31:32]),
              r=["psMsq", "cw"], w=[("ych", c4)])
            yield

    def p1_ln(G):
        xTg = xnT[G % 2]
        Gk = "xnT"
        gl = glu[G % 2]
        gln = glu[(G + 1) % 2]
        tok = slice(G * 512, (G + 1) * 512)
        for c4 in range(4):
            A("act", lambda e, c4=c4: e.activation(out=ysq[:], in_=ych[:, c4, :], func=AF.Square),
              r=[("ych", c4)], w=["rstd_sb"])
            A("pe", lambda e, c4=c4: e.matmul(psMean[:], lhsT=onesf[:], rhs=ych[:, c4, :], start=(c4 == 0), stop=(c4 == 3)),
              r=["onesf", ("ych", c4)], w=["psMean"])
            A("pe", lambda e, c4=c4: e.matmul(psMsq[:], lhsT=onesf[:], rhs=ysq[:], start=(c4 == 0), stop=(c4 == 3)),
              r=["onesf", "rstd_sb"], w=["psMsq"])
        A("pool", lambda e, gl=gl, gln=gln: e.tensor_copy(out=gln[:, :, 0:30], in_=gl[:, :, 512:542]),
          r=[("glu", G % 2, c4) for c4 in range(4)], w=[("gluhead", (G + 1) % 2)])
        A("pool", lambda e, gl=gl, G=G: e.tensor_copy(out=glo[(G + 1) % 2][:, :, 0:29], in_=gl[:, :, 513:542]),
          r=[("glu", G % 2, c4) for c4 in range(4)], w=[("gluhead", (G + 1) % 2)])
        A("act", lambda e: e.copy(out=mean_sb[:], in_=psMean[:]), r=["psMean"], w=["mean_sb"])
        A("pool", lambda e: e.tensor_tensor(out=rstd_sb[:], in0=mean_sb[:], in1=mean_sb[:], op=ALU.mult),
          r=["mean_sb"], w=["rstd_sb"])
        A("dve", lambda e: e.tensor_tensor(out=rstd_sb[:], in0=psMsq[:], in1=rstd_sb[:], op=ALU.subtract),
          r=["psMsq", "rstd_sb"], w=["rstd_sb"])
        A("dve", lambda e: e.tensor_scalar(out=rstd_sb[:], in0=rstd_sb[:], scalar1=1e-5, scalar2=None,
                                           op0=ALU.add), r=["rstd_sb"], w=["rstd_sb"])
        A("act", lambda e: e.activation(out=rstd_sb[:], in_=rstd_sb[:], func=AF.Sqrt), r=["rstd_sb"], w=["rstd_sb"])
        A("dve", lambda e: e.reciprocal(out=rstd_sb[:], in_=rstd_sb[:]), r=["rstd_sb"], w=["rstd_sb"])
        for c4 in range(4):
            eng = "dve" if c4 % 2 == 0 else "pool"
            A(eng, lambda e, c4=c4: e.tensor_tensor(out=ych[:, c4, :], in0=ych[:, c4, :], in1=mean_sb[:], op=ALU.subtract),
              r=[("ych", c4), "mean_sb"], w=[("ych", c4)])
            A(eng, lambda e, c4=c4: e.tensor_tensor(out=ych[:, c4, :], in0=ych[:, c4, :], in1=rstd_sb[:], op=ALU.mult),
              r=[("ych", c4), "rstd_sb"], w=[("ych", c4)])
            A("act", lambda e, c4=c4, tok=tok: e.activation(out=convyT[:, c4, tok], in_=ych[:, c4, :], func=AF.Silu,
                                                            bias=cw[:, c4, 33:34], scale=cw[:, c4, 32:33]),
              r=[("ych", c4), "cw"], w=[("convyT", G)])
            yield
    def run_interleaved(ga, gb):
        sentinel = object()
        for _ in range(4):
            if next(ga, sentinel) is sentinel:
                break
        a_alive, b_alive = True, True
        while a_alive or b_alive:
            for _ in range(2):
                if a_alive and next(ga, sentinel) is sentinel:
                    a_alive = False
            if b_alive and next(gb, sentinel) is sentinel:
                b_alive = False

    for _ in p1_front(0):
        pass
    for G in range(4):
        run_interleaved(p1_front(G + 1) if G + 1 < 4 else iter(()), p1_conv(G))
        for _ in p1_ln(G):
            pass
    A("act", lambda e: e.copy(out=vcaug[:, :, 0:64], in_=psVC[0:64, 0:128].rearrange("p (h d) -> p h d", d=64)),
      r=["psVC", "vcaug_init"], w=["vcaug"])
    for c4 in range(4):
        A("pe", lambda e, c4=c4: e.transpose(out=psF[0][0:30, c4 * 128:(c4 + 1) * 128], in_=glutail[:, c4, :], identity=identf[:]),
          r=["glutail", "identf"], w=[("psF", 0)])
    cps = ych[0:30, 0, :]
    A("act", lambda e: e.copy(out=cps, in_=psF[0][0:30, :]), r=[("psF", 0), ("ych", 0)], w=[("ych", 0)])
    A("sp", lambda e: e.dma_start(out=conv_p, in_=cps), r=[("ych", 0)], dma=True)
    if debug:
        dbg["QT0"] = dout("d_QT0", [96, 4 * SEQ], BF16)
        A("sp", lambda e: e.dma_start(out=dbg["QT0"], in_=QTs[0][:]), r=[("QT", 0, G) for G in range(4)], dma=True)
        dbg["convyT"] = dout("d_convyT", [128, 4 * SEQ], BF16)
        A("sp", lambda e: e.dma_start(out=dbg["convyT"], in_=convyT[:]), r=[("convyT", G) for G in range(4)], dma=True)
        dbg["kcT0"] = dout("d_kcT0", [64, 64], BF16)
        A("sp", lambda e: e.dma_start(out=dbg["kcT0"], in_=kcT[0][:]), r=[("kcT", 0, G) for G in range(4)], dma=True)
        dbg["vcaug"] = dout("d_vcaug", [64, 130], BF16)
        A("sp", lambda e: e.dma_start(out=dbg["vcaug"], in_=vcaug[:]), r=["vcaug"], dma=True)


def build_p2(nc, S, P2, sb, pst, L):
    A = S.add
    QTs, KTs, KTw, kcT, Vaug, vcaug, gates, convyT = (L[k] for k in
                                                       ("QTs", "KTs", "KTw", "kcT", "Vaug", "vcaug", "gates", "convyT"))
    ident, identf, w_out_bf, h_acc, gbc_mlp, xp = (L[k] for k in
                                                   ("ident", "identf", "w_out_bf", "h_acc", "gbc_mlp", "xp"))
    debug, dbg, dout = L["debug"], L["dbg"], L["dout"]
    sbias = sb("sbias", [128, NT, 32], F32, P2)
    A("pool", lambda e: e.memset(sbias[:], 0.0), w=["sbias"])
    for qt in range(NT):
        for half in range(2):
            qb = 2 * qt + half
            ps_ = slice(64 * half, 64 * half + 64)
            if qb + 1 < 32:
                A("pool", lambda e, qt=qt, ps_=ps_, qb=qb: e.memset(sbias[ps_, qt, qb + 1:32], -1e30), r=["sbias"], w=["sbias"])
            A("pool", lambda e, qt=qt, ps_=ps_: e.memset(sbias[ps_, qt, 0:1], 5.0), r=["sbias"], w=["sbias"])
            A("pool", lambda e, qt=qt, ps_=ps_, qb=qb: e.memset(sbias[ps_, qt, max(qb - 1, 0):qb + 1], 5.0),
              r=["sbias"], w=["sbias"])
    pS = [pst(f"pS{i}", [128, 512], F32, P2) for i in range(2)]
    pOb = [pst(f"pO{i}", [128, 512], F32, P2) for i in range(3)]
    pO = [p[:, 0:260].rearrange("p (g d) -> p g d", d=65) for p in pOb]
    pM = pst("pM", [128, 512], F32, P2)
    pH = pst("pH", [128, 512], F32, P2)
    pTb = pst("pTb", [128, 8, 128], BF16, P2)
    PT = [sb(f"PT{i}", [128, 512], BF16, P2) for i in range(3)]
    NSEL = 3
    Ecmp = [sb(f"Ecmp{i}", [128, 8, 64], F32, P2) for i in range(NSEL)]
    zc = [sb(f"zc{i}", [128, 16], F32, P2) for i in range(NSEL)]
    pblk = [sb(f"pblk{i}", [128, 2, 32], F32, P2) for i in range(NSEL)]
    pg4 = [sb(f"pg4{i}", [128, 2, 64], F32, P2) for i in range(NSEL)]
    m8 = [sb(f"m8{i}", [128, 2, 8], F32, P2) for i in range(NSEL)]
    wk32 = [sb(f"wk32{i}", [128, 2, 32], F32, P2) for i in range(NSEL)]
    selT_in = [sb(f"selT_in{i}", [128, 2, 96], BF16, P2) for i in range(NSEL)]
    for i in range(NSEL):
        A("pool", lambda e, i=i: e.memset(selT_in[i][:], 0.0), w=[("selT_in", i)])
    coef = sb("coef", [128, 3, 4], F32, P2)
    zr = sb("zr", [128, 3, 4], F32, P2)
    osb = sb("osb", [128, 4, 64], F32, P2)
    otmp = sb("otmp", [128, 4, 64], F32, P2)
    attn = sb("attn", [128, 512], BF16, P2)
    attnT = sb("attnT", [128, 4, 128], BF16, P2)
    xt2 = [sb("xt2_0", [128, DM], F32, P2)] * 2
    pcnt = [0]
    scnt = [0]

    def score_exp(h, qt, lhsT, K, rhs_rows, masks, rkeys):
        sbuf_i = scnt[0] % 2
        scnt[0] += 1
        pb = pcnt[0] % 3
        pcnt[0] += 1
        M = lhsT.shape[1]
        A("pe", lambda e: e.matmul(pS[sbuf_i][0:M, :], lhsT=lhsT, rhs=QTs[h][0:K, :, qt * 128:(qt + 1) * 128],
                                   start=True, stop=True),
          r=rkeys + [("QT", h, qt // 4)] + ([("QTaug", h, qt)] if K == 96 else []), w=[("pS", sbuf_i)])
        A("act", lambda e: e.activation(out=PT[pb][0:M, :], in_=pS[sbuf_i][0:M, :], func=AF.Exp, scale=SCALE),
          r=[("pS", sbuf_i)], w=[("PT", pb)])
        for (cm, qs, base) in masks:
            A("pool", lambda e, cm=cm, qs=qs, base=base: e.affine_select(
                out=PT[pb][0:M, :].rearrange("p (g q) -> p g q", g=4), in_=PT[pb][0:M, :].rearrange("p (g q) -> p g q", g=4),
                pattern=[[0, 4], [qs, 128]], compare_op=ALU.is_ge, fill=0.0, base=base, channel_multiplier=cm),
              r=[("PT", pb)], w=[("PT", pb)])
        return pb

    def emit_sel(qt):
        r = qt % NSEL
        pm = pH
        pmk = "pH"
        for hd in range(8):
            h, g = hd // 4, hd % 4
            A("pe", lambda e, h=h, g=g, hd=hd, qt=qt, pm=pm: e.matmul(pm[:, hd * 64:(hd + 1) * 64],
                                                                      lhsT=QTs[h][0:64, g, qt * 128:(qt + 1) * 128], rhs=kcT[h][:],
                                                                      start=True, stop=True),
              r=[("QT", h, qt // 4)] + [("kcT", h, G) for G in range(4)], w=[pmk])
        E = Ecmp[r]
        A("act", lambda e, E=E, pm=pm: e.activation(out=E[:], in_=pm[:].rearrange("p (g c) -> p g c", g=8), func=AF.Exp, scale=SCALE),
          r=[pmk], w=[("Ecmp", r)])
        A("pool", lambda e, qt=qt, E=E: e.affine_select(out=E[:], in_=E[:], pattern=[[0, 8], [-32, 64]], compare_op=ALU.is_ge,
                                                        fill=0.0, base=qt * 128 - 31, channel_multiplier=1),
          r=[("Ecmp", r)], w=[("Ecmp", r)])
        Z = zc[r]
        A("dve", lambda e, E=E, Z=Z: e.tensor_reduce(out=Z[:, 0:8], in_=E[:], axis=AX.X, op=ALU.add), r=[("Ecmp", r)], w=[("zc", r)])
        A("dve", lambda e, Z=Z: e.tensor_scalar(out=Z[:, 0:8], in0=Z[:, 0:8], scalar1=1e-30, scalar2=None, op0=ALU.add),
          r=[("zc", r)], w=[("zc", r)])
        A("dve", lambda e, Z=Z: e.reciprocal(out=Z[:, 8:16], in_=Z[:, 0:8]), r=[("zc", r)], w=[("zc", r)])
        A("dve", lambda e, E=E, Z=Z: e.tensor_tensor(out=E[:], in0=E[:], in1=Z[:, 8:16].unsqueeze(2).to_broadcast([128, 8, 64]),
                                                     op=ALU.mult), r=[("Ecmp", r), ("zc", r)], w=[("Ecmp", r)])
        for h in range(2):
            A("dve", lambda e, E=E, h=h, r=r: e.tensor_reduce(out=pg4[r][:, h, :], in_=E[:, h * 4:(h + 1) * 4, :].rearrange("p g c -> p c g"),
                                                              axis=AX.X, op=ALU.add), r=[("Ecmp", r)], w=[("pg4", r)])
        A("dve", lambda e, r=r: e.tensor_reduce(out=pblk[r][:], in_=pg4[r][:].rearrange("p h (b t) -> p h b t", t=2),
                                                axis=AX.X, op=ALU.add), r=[("pg4", r)], w=[("pblk", r)])
        A("dve", lambda e, r=r, qt=qt: e.tensor_tensor(out=pblk[r][:], in0=pblk[r][:],
                                                       in1=sbias[:, qt, :].unsqueeze(1).to_broadcast([128, 2, 32]), op=ALU.add),
          r=[("pblk", r), "sbias"], w=[("pblk", r)])
        for h in range(2):
            A("dve", lambda e, h=h, r=r: e.max(out=m8[r][:, h, :], in_=pblk[r][:, h, :]), r=[("pblk", r)], w=[("m8", r, h)])
            A("dve", lambda e, h=h, r=r: e.match_replace(out=wk32[r][:, h, :], in_to_replace=m8[r][:, h, :],
                                                         in_values=pblk[r][:, h, :], imm_value=-3e38),
              r=[("m8", r, h), ("pblk", r)], w=[("wk32", r, h)])
            A("dve", lambda e, h=h, r=r: e.max(out=m8[r][:, h, :], in_=wk32[r][:, h, :]), r=[("wk32", r, h)], w=[("m8", r, h)])
            A("dve", lambda e, h=h, r=r: e.tensor_scalar(out=wk32[r][:, h, :], in0=pblk[r][:, h, :], scalar1=m8[r][:, h, 7:8],
                                                         scalar2=-1.0, op0=ALU.is_ge, op1=ALU.add),
              r=[("pblk", r), ("m8", r, h)], w=[("wk32", r, h)])
        A("dve", lambda e, r=r: e.tensor_scalar(out=selT_in[r][:, :, 64:96], in0=wk32[r][:], scalar1=BIG, scalar2=None, op0=ALU.mult),
          r=[("wk32", r, 0), ("wk32", r, 1), ("selT_in", r)], w=[("selT_in", r)])
        for h in range(2):
            A("pe", lambda e, h=h, r=r: e.transpose(out=pTb[0:96, 4 + h, :], in_=selT_in[r][:, h, :], identity=ident[:]),
              r=[("selT_in", r), "ident"], w=[("pTbs", h)])
            A("act", lambda e, h=h, qt=qt: e.copy(out=QTs[h][64:96, :, qt * 128:(qt + 1) * 128],
                                                  in_=pTb[64:96, 4 + h, :].unsqueeze(1).to_broadcast([32, 4, 128])),
              r=[("pTbs", h)], w=[("QTaug", h, qt)])
    LOOK = 2
    pSx = [pS[0], pS[1], pM]
    Osb = sb("Osb", [128, 3, 4, 65], F32, P2)
    tiles = []
    for qt in range(NT):
        for h in range(2):
            tl = [dict(br=0, kt=0, lhsT=kcT[h][:], K=64, masks=[(-32, 1, qt * 128 - 31)],
                       rk=[("kcT", h, G) for G in range(4)], rhs=vcaug[:, h, :], rhsk="vcaug", first=True, last=True)]
            for kt in range(qt + 1):
                tl.append(dict(br=1, kt=kt, lhsT=KTs[h][:, kt * 128:(kt + 1) * 128], K=96, masks=[(-1, 1, 0)] if kt == qt else [],
                               rk=[("KTs", h, kt // 4), ("KTsaug", h)], rhs=Vaug[:, kt, 1, h, :], rhsk=("Vaug", kt),
                               first=(kt == 0), last=(kt == qt)))
            k0 = max(0, qt - 4)
            for kt in range(k0, qt + 1):
                masks = []
                if kt == qt:
                    masks.append((-1, 1, 0))
                if kt == qt - 4:
                    masks.append((1, -1, 0))
                tl.append(dict(br=2, kt=kt, lhsT=KTw[h][:, kt * 128:(kt + 1) * 128], K=64, masks=masks,
                               rk=[("KTw", h, kt // 4)], rhs=Vaug[:, kt, 2, h, :], rhsk=("Vaug", kt),
                               first=(kt == k0), last=(kt == qt)))
            for t_ in tl:
                t_["qt"], t_["h"] = qt, h
            tl[-1]["end"] = True
            tiles += tl

    def emit_qk(t_, i):
        si = i % 3
        pb = i % 3
        t_["pb"] = pb
        h, qt, K, lhsT = t_["h"], t_["qt"], t_["K"], t_["lhsT"]
        M = lhsT.shape[1]
        A("pe", lambda e: e.matmul(pSx[si][0:M, :], lhsT=lhsT, rhs=QTs[h][0:K, :, qt * 128:(qt + 1) * 128], start=True, stop=True),
          r=t_["rk"] + [("QT", h, qt // 4)] + ([("QTaug", h, qt)] if K == 96 else []), w=[("pSx", si)])
        A("act", lambda e: e.activation(out=PT[pb][0:M, :], in_=pSx[si][0:M, :], func=AF.Exp, scale=SCALE),
          r=[("pSx", si)], w=[("PT", pb)])
        for (cm, qs, base) in t_["masks"]:
            A("pool", lambda e, cm=cm, qs=qs, base=base: e.affine_select(
                out=PT[pb][0:M, :].rearrange("p (g q) -> p g q", g=4), in_=PT[pb][0:M, :].rearrange("p (g q) -> p g q", g=4),
                pattern=[[0, 4], [qs, 128]], compare_op=ALU.is_ge, fill=0.0, base=base, channel_multiplier=cm),
              r=[("PT", pb)], w=[("PT", pb)])

    def emit_pv(t_):
        pb, br = t_["pb"], t_["br"]
        M = t_["lhsT"].shape[1]
        for g in range(4):
            A("pe", lambda e, g=g: e.matmul(pO[br][:, g, :], lhsT=PT[pb][0:M, g * 128:(g + 1) * 128], rhs=t_["rhs"],
                                            start=(t_["first"] and g == 0), stop=t_["last"], skip_group_check=True),
              r=[("PT", pb), t_["rhsk"]], w=[("pO", br)])
        if t_["last"]:
            A("dve", lambda e: e.tensor_copy(out=Osb[:, br, :, :], in_=pO[br][:]), r=[("pO", br)], w=[("Osb", br)])

    def emit_end(qt, h):
        A("dve", lambda e: e.tensor_scalar(out=zr[:], in0=Osb[:, :, :, 64], scalar1=1e-30, scalar2=None, op0=ALU.add),
          r=[("Osb", br) for br in range(3)], w=["zr"])
        A("dve", lambda e: e.reciprocal(out=zr[:], in_=zr[:]), r=["zr"], w=["zr"])
        A("dve", lambda e: e.tensor_tensor(
            out=coef[:], in0=zr[:], in1=gates[:, qt, h * 12:(h + 1) * 12].rearrange("p (g b) -> p b g", b=3), op=ALU.mult),
          r=["zr", ("gates", qt)], w=["coef"])
        for br in range(3):
            dst = osb if br == 0 else otmp
            A("dve", lambda e, br=br, dst=dst: e.tensor_tensor(
                out=dst[:], in0=Osb[:, br, :, 0:64], in1=coef[:, br, :].unsqueeze(2).to_broadcast([128, 4, 64]), op=ALU.mult),
              r=[("Osb", br), "coef"], w=["osb" if br == 0 else "otmp"])
            if br > 0:
                A("dve", lambda e: e.tensor_tensor(out=osb[:], in0=osb[:], in1=otmp[:], op=ALU.add),
                  r=["osb", "otmp"], w=["osb"])
        A("pool", lambda e: e.tensor_copy(out=attn[:, h * 256:(h + 1) * 256], in_=osb[:].rearrange("p g d -> p (g d)")),
          r=["osb"], w=[("attn", h)])
        if h == 0:
            return
        for c4 in range(4):
            A("pe", lambda e, c4=c4: e.transpose(out=pTb[:, c4, :], in_=attn[:, c4 * 128:(c4 + 1) * 128], identity=ident[:]),
              r=[("attn", 0), ("attn", 1), "ident"], w=["pTb"])
        A("dve", lambda e: e.tensor_copy(out=attnT[:], in_=pTb[:, 0:4, :]), r=["pTb"], w=["attnT"])
        A("sp", lambda e: e.dma_start(out=xt2[0][:], in_=xp[qt * 128:(qt + 1) * 128, :]), w=["xt2"], dma=True)
        for half in range(2):
            for k in range(8):
                lhs = (convyT[:, k, qt * 128:(qt + 1) * 128] if k < 4 else attnT[:, k - 4, :])
                A("pe", lambda e, k=k, half=half, lhs=lhs: e.matmul(pH[:], lhsT=lhs, rhs=w_out_bf[:, k, half * 512:(half + 1) * 512],
                                                                    start=(k == 0), stop=(k == 7)),
                  r=[("w_out_bf", k), ("convyT", qt // 4), "attnT"], w=["pH"])
            A("dve", lambda e, half=half: e.tensor_tensor(
                out=h_acc[:, qt, half * 512:(half + 1) * 512], in0=pH[:], in1=xt2[0][:, half * 512:(half + 1) * 512], op=ALU.add),
              r=["pH", "xt2"], w=[("h_acc", qt)])
        A("act", lambda e: e.activation(out=xt2[0][:], in_=h_acc[:, qt, :], func=AF.Square, accum_out=L["hss"][:, qt:qt + 1]),
          r=[("h_acc", qt), "hss", "xt2"], w=["xt2", "hss"])
        if qt + 2 < NT:
            emit_sel(qt + 2)

    emit_sel(0)
    emit_sel(1)
    for i in range(len(tiles) + LOOK):
        if i < len(tiles):
            emit_qk(tiles[i], i)
        j = i - LOOK
        if j >= 0:
            emit_pv(tiles[j])
            if tiles[j].get("end"):
                emit_end(tiles[j]["qt"], tiles[j]["h"])
    if debug:
        dbg["attn"] = dout("d_attn", [128, 512], BF16)
        A("sp", lambda e: e.dma_start(out=dbg["attn"], in_=attn[:]), r=[("attn", 0), ("attn", 1)], dma=True)
        dbg["h"] = dout("d_h", [128, NT * DM], F32)
        A("sp", lambda e: e.dma_start(out=dbg["h"], in_=h_acc[:]), r=[("h_acc", t) for t in range(NT)], dma=True)


def emit_norm_T(S, src, srckey, junk, ss, hn, gbc, gkey, ident, psT8, dst_k, dst_all, dstkey):
    A = S.add
    P = src.shape[0]
    A("pool", lambda e: e.memset(ss[0:P, 0:1], 0.0), w=[("ss", id(ss))])
    A("act", lambda e: e.activation(out=junk[0:P, :], in_=src, func=AF.Square, accum_out=ss[0:P, 0:1]),
      r=[srckey, ("ss", id(ss))], w=[("junk", id(junk)), ("ss", id(ss))])
    A("dve", lambda e: e.tensor_scalar(out=ss[0:P, 1:2], in0=ss[0:P, 0:1], scalar1=1.0 / DM, scalar2=1e-6,
                                       op0=ALU.mult, op1=ALU.add), r=[("ss", id(ss))], w=[("rs", id(ss))])
    A("act", lambda e: e.activation(out=ss[0:P, 1:2], in_=ss[0:P, 1:2], func=AF.Sqrt), r=[("rs", id(ss))], w=[("rs", id(ss))])
    A("dve", lambda e: e.reciprocal(out=ss[0:P, 1:2], in_=ss[0:P, 1:2]), r=[("rs", id(ss))], w=[("rs", id(ss))])
    A("dve", lambda e: e.scalar_tensor_tensor(out=hn[0:P, :], in0=src, scalar=ss[0:P, 1:2], in1=gbc[0:P, :],
                                              op0=ALU.mult, op1=ALU.mult), r=[srckey, ("rs", id(ss)), gkey], w=[("hn", id(hn))])
    for k in range(8):
        A("pe", lambda e, k=k: e.transpose(out=psT8[:, k, 0:P], in_=hn[0:P, k * 128:(k + 1) * 128], identity=ident[0:P, 0:P]),
          r=[("hn", id(hn)), "ident"], w=["pTb"])
    A("act", lambda e: e.copy(out=dst_all, in_=psT8[:, :, 0:P]), r=["pTb"], w=[dstkey])


def build_p3(nc, S, P3, sb, pst, L):
    A = S.add
    h_acc, hs_acc, hnTs, w_up, w_down, y_p, y_s, gbc_mlp, ident = (L[k] for k in (
        "h_acc", "hs_acc", "hnTs", "w_up", "w_down", "y_p", "y_s", "gbc_mlp", "ident"))
    with_s = "samp" in L["stages"]
    hnT = sb("hnT", [128, 8, SEQ], BF16, P3)
    junk3 = sb("junk3", [128, DM], BF16, P3)
    ss3 = sb("ss3", [128, 2], F32, P3)
    hnb = [junk3, sb("hn", [128, DM], BF16, P3)]
    pTb = pst("pTb3", [128, 8, 128], BF16, P3)
    hss = L["hss"]
    A("dve", lambda e: e.tensor_scalar(out=hss[:], in0=hss[:], scalar1=1.0 / DM, scalar2=1e-6, op0=ALU.mult, op1=ALU.add),
      r=["hss"], w=["hss"])
    A("act", lambda e: e.activation(out=hss[:], in_=hss[:], func=AF.Sqrt), r=["hss"], w=["hss"])
    A("dve", lambda e: e.reciprocal(out=hss[:], in_=hss[:]), r=["hss"], w=["hss"])
    for qt in range(NT):
        hb = qt % 2
        A("dve", lambda e, qt=qt, hb=hb: e.scalar_tensor_tensor(out=hnb[hb][:], in0=h_acc[:, qt, :], scalar=hss[:, qt:qt + 1],
                                                                in1=gbc_mlp[:], op0=ALU.mult, op1=ALU.mult),
          r=[("h_acc", qt), "hss", "gbc_mlp"], w=[("hnb", hb)])
        for k in range(8):
            A("pe", lambda e, k=k, hb=hb: e.transpose(out=pTb[:, k, :], in_=hnb[hb][:, k * 128:(k + 1) * 128], identity=ident[:]),
              r=[("hnb", hb), "ident"], w=["pTb3"])
        A("act", lambda e, qt=qt: e.copy(out=hnT[:, :, qt * 128:(qt + 1) * 128], in_=pTb[:]), r=["pTb3"], w=[("hnT", qt)])
    stgU = [sb("stgU0", [128, 8, 512], F32, P3)] * 2
    stgD = [sb("stgD0", [128, 4, DM], F32, P3)] * 2
    gbc_fin = sb("gbc_fin", [128, DM], F32, P3)
    A("sp", lambda e: e.dma_start(out=gbc_fin[:], in_=L["g_fin"].partition_broadcast(128)), w=["gbc_fin"], dma=True)
    wu = [sb(f"wu{i}", [128, 8, 512], BF16, P3) for i in range(2)]
    wd = [sb(f"wd{i}", [128, 4, DM], BF16, P3) for i in range(2)]
    aT = [sb(f"aT{i}", [128, 4, 512], BF16, P3) for i in range(2)]
    aTs = sb("aTs", [128, 4, NSB], BF16, P3)
    rl = [sb(f"rl{i}", [128, 512], F32, P3) for i in range(2)]
    pU = [pst(f"pU{i}", [128, 512], F32, P3) for i in range(2)]
    pD = [pst(f"pD{i}", [128, 512], F32, P3) for i in range(2)]
    yo = [stgD[0][:, 0, :]] * 2
    ucnt = [0]
    dcnt = [0]
    acnt = [0]
    groups = list(range(4)) + (["s"] if with_s else [])

    def load_w(fg):
        b = fg % 2
        A("sp", lambda e, fg=fg, b=b: e.dma_start(out=stgU[b][:], in_=w_up[:, fg * 512:(fg + 1) * 512].rearrange(
            "(k p) f -> p k f", p=128)), w=["stgU"], dma=True)
        A("sp", lambda e, fg=fg, b=b: e.dma_start(out=stgD[b][:], in_=w_down[fg * 512:(fg + 1) * 512, :].rearrange(
            "(c p) d -> p c d", p=128)), w=["stgD"], dma=True)
        A("act", lambda e, b=b: e.copy(out=wu[b][:], in_=stgU[b][:]), r=["stgU"], w=[("wu", b)])
        A("pool", lambda e, b=b: e.tensor_copy(out=wd[b][:], in_=stgD[b][:]), r=["stgD"], w=[("wd", b)])

    def emit_up(fg, TG):
        b = fg % 2
        ntok = 512 if TG != "s" else NSB
        if TG == "s":
            rhs_of = lambda k: hnTs[:, k, :]
            rk = ["hnTs"]
            adst = aTs
            akey = "aTs"
        else:
            rhs_of = lambda k, TG=TG: hnT[:, k, TG * 512:(TG + 1) * 512]
            rk = [("hnT", t) for t in range(4 * TG, 4 * TG + 4)]
            ai = acnt[0] % 2
            acnt[0] += 1
            adst = aT[ai]
            akey = ("aT", ai)
        for fc in range(4):
            ui = ucnt[0] % 2
            ucnt[0] += 1
            for k in range(8):
                A("pe", lambda e, k=k, fc=fc, ui=ui, b=b, rhs_of=rhs_of, ntok=ntok: e.matmul(
                    pU[ui][:, 0:ntok], lhsT=wu[b][:, k, fc * 128:(fc + 1) * 128], rhs=rhs_of(k),
                    start=(k == 0), stop=(k == 7)), r=[("wu", b)] + rk, w=[("pU", ui)])
            A("act", lambda e, ui=ui, ntok=ntok: e.activation(out=rl[ui][:, 0:ntok], in_=pU[ui][:, 0:ntok], func=AF.Relu),
              r=[("pU", ui)], w=[("rl", ui)])
            A("pool" if fc % 2 else "dve", lambda e, ui=ui, fc=fc, adst=adst, ntok=ntok: e.tensor_tensor(
                out=adst[:, fc, 0:ntok], in0=rl[ui][:, 0:ntok], in1=rl[ui][:, 0:ntok], op=ALU.mult),
              r=[("rl", ui)], w=[(akey, fc)])
        return adst, akey

    def emit_down(fg, TG, adst, akey):
        b = fg % 2
        tiles = range(4) if TG != "s" else [0]
        for tt in tiles:
            for half in range(2):
                di = dcnt[0] % 2
                dcnt[0] += 1
                mrows = 128 if TG != "s" else NSB
                for fc in range(4):
                    lhs = adst[:, fc, tt * 128:(tt + 1) * 128] if TG != "s" else adst[:, fc, :]
                    A("pe", lambda e, fc=fc, half=half, di=di, lhs=lhs, b=b, mrows=mrows: e.matmul(
                        pD[di][0:mrows, :], lhsT=lhs, rhs=wd[b][:, fc, half * 512:(half + 1) * 512],
                        start=(fc == 0), stop=(fc == 3)), r=[(akey, fc), ("wd", b)], w=[("pD", di)])
                if TG != "s":
                    t = 4 * TG + tt
                    A("dve", lambda e, t=t, half=half, di=di: e.tensor_tensor(
                        out=h_acc[:, t, half * 512:(half + 1) * 512], in0=pD[di][:], in1=h_acc[:, t, half * 512:(half + 1) * 512],
                        op=ALU.add), r=[("pD", di), ("h_acc", t)], w=[("h_acc", t)])
                else:
                    A("dve", lambda e, half=half, di=di: e.tensor_tensor(
                        out=hs_acc[:, half * 512:(half + 1) * 512], in0=pD[di][0:NSB, :],
                        in1=hs_acc[:, half * 512:(half + 1) * 512], op=ALU.add), r=[("pD", di), "hs_acc"], w=["hs_acc"])

    load_w(0)
    pending = None
    for fg in range(8):
        for gi, TG in enumerate(groups):
            cur = emit_up(fg, TG)
            if gi == 1 and fg + 1 < 8:
                load_w(fg + 1)
            if pending is not None:
                emit_down(*pending)
            pending = (fg, TG) + cur
    emit_down(*pending)
    outs = [(h_acc[:, t, :], ("h_acc", t), y_p[t * 128:(t + 1) * 128, :], 128) for t in range(NT)]
    if with_s:
        outs.append((hs_acc[:], "hs_acc", y_s, NSB))
    ssf = sb("ssf", [128, NT + 1], F32, P3)
    A("dve", lambda e: e.memset(ssf[:], 0.0), w=["ssf"])
    for i, (src, skey, dst, P) in enumerate(outs):
        A("act", lambda e, src=src, P=P, i=i: e.activation(out=junk3[0:P, :], in_=src, func=AF.Square, accum_out=ssf[0:P, i:i + 1]),
          r=[skey, "ssf", ("hnb", 0)], w=[("hnb", 0), "ssf"])
    A("dve", lambda e: e.tensor_scalar(out=ssf[:], in0=ssf[:], scalar1=1.0 / DM, scalar2=1e-6, op0=ALU.mult, op1=ALU.add),
      r=["ssf"], w=["ssf"])
    A("act", lambda e: e.activation(out=ssf[:], in_=ssf[:], func=AF.Sqrt), r=["ssf"], w=["ssf"])
    A("dve", lambda e: e.reciprocal(out=ssf[:], in_=ssf[:]), r=["ssf"], w=["ssf"])
    for i, (src, skey, dst, P) in enumerate(outs):
        yb = i % 4
        A("dve", lambda e, src=src, P=P, yb=yb, i=i: e.scalar_tensor_tensor(
            out=stgD[0][0:P, yb, :], in0=src, scalar=ssf[0:P, i:i + 1], in1=gbc_fin[0:P, :], op0=ALU.mult, op1=ALU.mult),
          r=[skey, "ssf", "gbc_fin"], w=[("yo", yb)] + (["stgD"] if i < 4 else []))
        A("sp", lambda e, dst=dst, P=P, yb=yb: e.dma_start(out=dst, in_=stgD[0][0:P, yb, :]), r=[("yo", yb)], dma=True)


_NC_CACHE = {}


def _get_nc():
    if "nc" not in _NC_CACHE:
        _NC_CACHE["nc"] = build()[0]
    return _NC_CACHE["nc"]


def make_in_maps(inp, cores):
    f = lambda a: np.ascontiguousarray(np.asarray(a, dtype=np.float32))
    cache = f(inp["cache_kv"]).reshape(2560 * 128 * 4, 128)
    wdall = np.ascontiguousarray(np.concatenate(
        [f(inp["w_dw"])[0], f(inp["b_dw"]), f(inp["conv_ln_g"]), f(inp["conv_ln_b"])], axis=0))
    shared = dict(
        cache=cache, g_attn=f(inp["g_attn_norm"]), w_in=f(inp["w_in"])[0], wdall=wdall,
        w_ck=f(inp["w_cmp_k"])[0], w_cv=f(inp["w_cmp_v"])[0], w_out=f(inp["w_out"])[0], g_mlp=f(inp["g_mlp_norm"]),
        w_up=f(inp["w_up"])[0], w_down=f(inp["w_down"])[0], g_fin=f(inp["g_final"]).reshape(1, DM))
    maps = []
    for c in cores:
        sl = slice(c * NSB, (c + 1) * NSB)
        m = dict(shared)
        m["xp"] = f(inp["x_prompt"])[c]
        m["xs"] = f(inp["x_sample"])[sl, 0]
        m["cwin"] = f(inp["cache_win"])[0, sl].reshape(NSB, 512, 256)
        m["sconv"] = f(inp["state_conv"])[0, sl].reshape(NSB * 30, 512)
        m["ptab"] = np.ascontiguousarray(np.asarray(inp["page_table"], dtype=np.int32)[sl].reshape(1, NSB * 16))
        maps.append(m)
    return maps


def kernel(**inp):
    nc = _get_nc()
    cores = list(range(N_CORES))
    res = run_bass_kernel_spmd(nc, make_in_maps(inp, cores), core_ids=cores)
    R = res.results
    cat = lambda k: np.stack([np.asarray(r[k], dtype=np.float32) for r in R], axis=0)
    y_p = cat("y_p")
    y_s = cat("y_s").reshape(128, 1, DM)
    kv_p = cat("kv_p").reshape(1, 8, SEQ, 4, 2, 64)
    win_p = cat("win_p").reshape(1, 8, 512, 2, 2, 64)
    conv_p = cat("conv_p").reshape(1, 8, 30, 512)
    kv_s = cat("kv_s").reshape(1, 128, 1, 4, 2, 64)
    win_s = cat("win_s").reshape(1, 128, 512, 2, 2, 64)
    conv_s = cat("conv_s").reshape(1, 128, 30, 512)
    return (y_p, y_s, kv_p, win_p, conv_p, kv_s, win_s, conv_s)


def _dap(ap, offset, dims):
    return bass.AP(ap.tensor, offset, [list(d) for d in dims])


def build_samp(nc, S, SP, sb, pst, L):
    A = S.add
    (xs, cache, cwin, sconv, ptab, wdall, w_cv, kv_s, win_s, conv_s, w_in_bf, w_out_bf, ident, identf, gbc_mlp, hs_acc,
     hnTs, wckb, g_attn) = (L[k] for k in ("xs", "cache", "cwin", "sconv", "ptab", "wdall", "w_cv", "kv_s", "win_s",
                                            "conv_s", "w_in_bf", "w_out_bf", "ident", "identf", "gbc_mlp", "hs_acc",
                                            "hnTs", "wckb", "g_attn"))
    debug, dbg, dout = L["debug"], L["dbg"], L["dout"]
    zs_d = nc.dram_tensor("zs_d", [NSB, INC], F32, kind="Internal").ap()
    ptb = sb("ptb", [128, NSB * 16], I32, SP)
    iop = sb("iop", [128, 1], I32, SP)
    idx = sb("idx", [128, NSB * 8], I32, SP)
    A("sp", lambda e: e.dma_start(out=ptb[:], in_=ptab.partition_broadcast(128)), w=["ptb"], dma=True)
    A("pool", lambda e: e.iota(iop[:], pattern=[[0, 1]], base=0, channel_multiplier=1), w=["iop"])
    A("dve", lambda e: e.tensor_scalar(out=iop[64:128, :], in0=iop[64:128, :], scalar1=-64, scalar2=None, op0=ALU.add),
      r=["iop"], w=["iop"])
    for hf in range(2):
        ps_ = slice(64 * hf, 64 * hf + 64)
        A("dve", lambda e, hf=hf, ps_=ps_: e.tensor_scalar(
            out=idx[ps_, :], in0=ptb[ps_, :].rearrange("p (c t) -> p c t", t=2)[:, :, hf], scalar1=64, scalar2=iop[ps_, 0:1],
            op0=ALU.mult, op1=ALU.add), r=["ptb", "iop"], w=["idx"])
    cacheR = cache.rearrange("(r s) c -> r (s c)", s=8)

    xs_sb = sb("xs_sb", [NSB, DM], F32, SP)
    gbc_a = sb("gbc_a", [NSB, DM], F32, SP)
    xnTs = sb("xnTs", [128, 8, NSB], BF16, SP)
    junk = sb("s_junk", [NSB, DM], BF16, SP)
    ssx = sb("s_ss", [128, 2], F32, SP)
    hn = sb("s_hn", [NSB, DM], BF16, SP)
    zs = sb("zs", [NSB, INC], F32, SP)
    pA = pst("s_pA", [128, 512], F32, SP)
    pB = pst("s_pB", [128, 512], F32, SP)
    pTb = pst("s_pTb", [128, 8, 128], BF16, SP)
    pKT = pst("s_pKT", [128, 8, 128], BF16, SP)
    pS = [pst(f"s_pS{i}", [128, 512], F32, SP) for i in range(3)]
    A("sp", lambda e: e.dma_start(out=xs_sb[:], in_=xs), w=["xs_sb"], dma=True)
    A("sp", lambda e: e.dma_start(out=gbc_a[:], in_=g_attn.partition_broadcast(NSB)), w=["gbc_a"], dma=True)
    emit_norm_T(S, xs_sb[:], "xs_sb", junk, ssx, hn, gbc_a, "gbc_a", ident, pTb, None, xnTs[:], "xnTs")
    for ci, c0 in enumerate(range(0, INC, 512)):
        n = min(512, INC - c0)
        for k in range(8):
            A("pe", lambda e, k=k, c0=c0, n=n: e.matmul(pA[0:NSB, 0:n], lhsT=xnTs[:, k, :], rhs=w_in_bf[:, k, c0:c0 + n],
                                                        start=(k == 0), stop=(k == 7)), r=["xnTs", ("w_in_bf", k)], w=["s_pA"])
        A("act", lambda e, c0=c0, n=n: e.copy(out=zs[:, c0:c0 + n], in_=pA[0:NSB, 0:n]), r=["s_pA"], w=["zs"])
    A("sp", lambda e: e.dma_start(out=zs_d, in_=zs[:]), r=["zs"], w=["zs_d"], dma=True)
    A("sp", lambda e: e.dma_start(out=kv_s, in_=zs[:, 1536:2048]), r=["zs"], dma=True)
    A("sp", lambda e: e.dma_start(out=win_s[:, 511, :], in_=zs[:, 2048:2304]), r=["zs"], dma=True)
    for b in range(NSB):
        A("sp", lambda e, b=b: e.dma_start(out=win_s[b, 0:511, :], in_=cwin[b, 1:512, :]), dma=True)
    A("sp", lambda e: e.dma_start(out=conv_s.rearrange("(b j) c -> b (j c)", j=30)[:, 0:29 * 512],
                                  in_=sconv.rearrange("(b j) c -> b (j c)", j=30)[:, 512:30 * 512]), dma=True)
    Qrows = sb("Qrows", [128, 64], F32, SP)
    Kn = sb("Kn", [128, 2, 64], F32, SP)
    Vn = sb("Vn", [128, 2, 64], F32, SP)
    Grows = sb("Grows", [128, 3], F32, SP)
    for b in range(NSB):
        rows = slice(8 * b, 8 * b + 8)
        A("sp", lambda e, b=b, rows=rows: e.dma_start(out=Qrows[rows, :], in_=_dap(zs_d, b * INC + 1024, [[64, 8], [1, 64]])),
          r=["zs_d"], w=["Qrows"], dma=True)
        for j, col in enumerate((1536 + 256, 1536 + 512)):
            A("sp", lambda e, b=b, rows=rows, j=j, col=col: e.dma_start(
                out=Kn[rows, j, :], in_=_dap(zs_d, b * INC + col, [[64, 2], [0, 4], [1, 64]])), r=["zs_d"], w=["Kn"], dma=True)
        for j, col in enumerate((1536 + 384, 1536 + 640)):
            A("sp", lambda e, b=b, rows=rows, j=j, col=col: e.dma_start(
                out=Vn[rows, j, :], in_=_dap(zs_d, b * INC + col, [[64, 2], [0, 4], [1, 64]])), r=["zs_d"], w=["Vn"], dma=True)
        A("sp", lambda e, b=b, rows=rows: e.dma_start(out=Grows[rows, :], in_=_dap(zs_d, b * INC + 2304, [[3, 8], [1, 3]])),
          r=["zs_d"], w=["Grows"], dma=True)
    QTpad = sb("QTpad", [128, NSB, 128], BF16, SP)
    A("pool", lambda e: e.memset(QTpad[:], 0.0), w=["QTpad"])
    qsrc = sb("qsrc", [NSB, 4, 2, 64], F32, SP)
    A("dve", lambda e: e.tensor_copy(out=qsrc[:], in_=zs[:, 1024:1536].rearrange("p (h g d) -> p g h d", h=2, g=4)),
      r=["zs"], w=["qsrc"])
    for g in range(4):
        A("pe", lambda e, g=g: e.transpose(out=pB[:, g * 16:(g + 1) * 16], in_=qsrc[:, g, :, :].rearrange("p h d -> p (h d)"),
                                           identity=identf[0:NSB, 0:NSB]), r=["qsrc", "identf"], w=["s_pB"])
    QTflat = QTpad[:].rearrange("p b c -> p (b c)")
    for h in range(2):
        for g in range(4):
            c0 = 4 * h + g
            A("act", lambda e, h=h, g=g, c0=c0: e.copy(out=QTflat[64 * h:64 * h + 64, c0:c0 + 136 * 15 + 1:136],
                                                       in_=pB[64 * h:64 * h + 64, g * 16:(g + 1) * 16]),
              r=["s_pB", "QTpad"], w=["QTpad"])
    pidx = sb("pidx", [128, 4], I32, SP)
    pf = sb("pf", [128, 4], F32, SP)
    A("pool", lambda e: e.iota(pidx[:, 0:1], pattern=[[0, 1]], base=0, channel_multiplier=1), w=["pidx"])
    A("dve", lambda e: e.tensor_single_scalar(out=pidx[:, 1:2], in_=pidx[:, 0:1], scalar=2, op=ALU.arith_shift_right),
      r=["pidx"], w=["pidx"])
    A("dve", lambda e: e.tensor_single_scalar(out=pidx[:, 2:3], in_=pidx[:, 1:2], scalar=1, op=ALU.bitwise_and),
      r=["pidx"], w=["pidx"])
    A("dve", lambda e: e.tensor_copy(out=pf[:, 0:3], in_=pidx[:, 0:3]), r=["pidx"], w=["pf"])
    coli = sb("coli", [128, 128], I32, SP)
    colf = sb("colf", [128, 128], F32, SP)
    GG = sb("GG", [128, 128], F32, SP)
    A("pool", lambda e: e.iota(coli[:], pattern=[[1, 128]], base=0, channel_multiplier=0), w=["coli"])
    A("dve", lambda e: e.tensor_single_scalar(out=coli[:], in_=coli[:], scalar=2, op=ALU.arith_shift_right), r=["coli"], w=["coli"])
    A("dve", lambda e: e.tensor_copy(out=colf[:], in_=coli[:]), r=["coli"], w=["colf"])
    A("dve", lambda e: e.tensor_scalar(out=GG[:], in0=colf[:], scalar1=pf[:, 1:2], scalar2=None, op0=ALU.is_equal),
      r=["colf", "pf"], w=["GG"])
    Mh = sb("Mh", [128, 2, 16], F32, SP)
    for half in range(2):
        A("dve", lambda e, half=half: e.tensor_scalar(out=Mh[:, half, :], in0=colf[:, 0:64:4], scalar1=float(16 * half),
                                                      scalar2=pf[:, 1:2], op0=ALU.add, op1=ALU.is_equal),
          r=["colf", "pf"], w=["Mh"])
    wcvb = L["wcvb"]
    Wk = sb("Wk", [128, 32], F32, SP)
    Wv = sb("Wv", [128, 32], F32, SP)
    wtmp = sb("wtmp", [128, 32], F32, SP)
    for (src, dst, key) in ((wckb, Wk, "Wk"), (wcvb, Wv, "Wv")):
        A("dve", lambda e, src=src: e.tensor_tensor(out=wtmp[:], in0=src[:, 1, :], in1=src[:, 0, :], op=ALU.subtract),
          r=["wckb", "wcvb"], w=["wtmp"])
        A("dve", lambda e, src=src, dst=dst: e.scalar_tensor_tensor(out=dst[:], in0=wtmp[:], scalar=pf[:, 2:3], in1=src[:, 0, :],
                                                                    op0=ALU.mult, op1=ALU.add), r=["wtmp", "pf", "wckb", "wcvb"], w=[key])
    SS_ = sb("SS_", [128, 2, SEQ], F32, SP)
    Scmp = SS_[:, 0, :]
    Ssel = SS_[:, 1, :]
    Swin = sb("Swin", [128, 512], F32, SP)
    NKB = 3
    kst = [sb(f"kst{i}", [128, 4, 4, 128], F32, SP) for i in range(NKB)]
    kbf = [sb(f"kbf{i}", [128, 4, 2, 128], BF16, SP) for i in range(2)]
    KTsb = [sb(f"KTsb{i}", [128, 2, 512], BF16, SP) for i in range(2)]
    it = 0
    for pg in range(5):
        for b in range(NSB):
            bi = it % NKB
            b2 = it % 2
            it += 1
            if pg < 4:
                for mm in range(2):
                    col = b * 8 + pg * 2 + mm
                    A("pool", lambda e, bi=bi, mm=mm, col=col: e.indirect_dma_start(
                        out=kst[bi][:, 2 * mm:2 * mm + 2, :, :].rearrange("p e s c -> p (e s c)"), out_offset=None, in_=cacheR,
                        in_offset=bass.IndirectOffsetOnAxis(ap=idx[:, col:col + 1], axis=0)),
                      r=["idx"], w=[("kst", bi, 2 * mm), ("kst", bi, 2 * mm + 1)], dma=True)
                A("act", lambda e, bi=bi, b2=b2: e.copy(out=kbf[b2][:], in_=kst[bi][:, :, 0:4:2, :]),
                  r=[("kst", bi, i) for i in range(4)], w=[("kbf", b2)])
                for i in range(4):
                    for si in range(2):
                        A("pe", lambda e, b2=b2, i=i, si=si: e.transpose(out=pKT[:, si * 4 + i, :], in_=kbf[b2][:, i, si, :],
                                                                         identity=ident[:]), r=[("kbf", b2), "ident"], w=["s_pKT"])
                A("dve", lambda e, b2=b2: e.tensor_copy(out=KTsb[b2][:], in_=pKT[:].rearrange("p (s i) t -> p s (i t)", s=2)),
                  r=["s_pKT"], w=[("KTsb", b2)])
                for si in range(2):
                    A("pe", lambda e, b2=b2, si=si, b=b: e.matmul(pS[si][:], lhsT=QTpad[:, b, :], rhs=KTsb[b2][:, si, :],
                                                                  start=(b == 0), stop=(b == NSB - 1)),
                      r=["QTpad", ("KTsb", b2)], w=[("s_pS", si)])
            else:
                A("sp", lambda e, bi=bi, b=b: e.dma_start(
                    out=kst[bi][:, :, 0, :], in_=cwin[b].rearrange("(i p) (s c) -> p i s c", p=128, c=128)[:, :, 0, :]),
                  w=[("kst", bi, i) for i in range(4)], dma=True)
                A("act", lambda e, bi=bi, b2=b2: e.copy(out=kbf[b2][:, :, 0, :], in_=kst[bi][:, :, 0, :]),
                  r=[("kst", bi, i) for i in range(4)], w=[("kbf", b2)])
                for i in range(4):
                    A("pe", lambda e, b2=b2, i=i: e.transpose(out=pKT[:, i, :], in_=kbf[b2][:, i, 0, :], identity=ident[:]),
                      r=[("kbf", b2), "ident"], w=["s_pKT"])
                A("dve", lambda e, b2=b2: e.tensor_copy(out=KTsb[b2][:, 0, :], in_=pKT[:, 0:4, :].rearrange("p i t -> p (i t)")),
                  r=["s_pKT"], w=[("KTsb", b2)])
                A("pe", lambda e, b2=b2, b=b: e.matmul(pS[2][:], lhsT=QTpad[:, b, :], rhs=KTsb[b2][:, 0, :],
                                                       start=(b == 0), stop=(b == NSB - 1)), r=["QTpad", ("KTsb", b2)], w=[("s_pS", 2)])
        if pg < 4:
            A("act", lambda e, pg=pg: e.copy(out=SS_[:, 0, pg * 512:(pg + 1) * 512], in_=pS[0][:]), r=[("s_pS", 0)], w=["Scmp"])
            A("dve", lambda e, pg=pg: e.tensor_copy(out=SS_[:, 1, pg * 512:(pg + 1) * 512], in_=pS[1][:]), r=[("s_pS", 1)], w=["Ssel"])
        else:
            A("act", lambda e: e.copy(out=Swin[:], in_=pS[2][:]), r=[("s_pS", 2)], w=["Swin"])
    big = sb("s_big", [128, SEQ], F32, SP)
    sc = sb("s_sc", [128, 64], F32, SP)
    st = sb("s_st", [128, 16], F32, SP)
    pb32 = sb("s_pb32", [128, 32], F32, SP)
    m8 = sb("s_m8", [128, 8], F32, SP)
    wk32 = sb("s_wk32", [128, 32], F32, SP)
    selm = sb("s_selm", [128, 32], F32, SP)
    Pc = sb("Pc", [128, SEQ], BF16, SP)
    Ps = sb("Ps", [128, SEQ], BF16, SP)
    Pw = sb("Pw", [128, 512], BF16, SP)
    Wk2 = sb("Wk2", [128, 2, 16], F32, SP)
    Wv2 = sb("Wv2", [128, 2, 16], F32, SP)
    sc2 = sb("s_sc2", [128, 128], F32, SP)
    A("dve", lambda e: e.tensor_copy(out=Wk2[:], in_=Wk[:].rearrange("p (q e) -> p e q", e=2)), r=["Wk"], w=["Wk2"])
    A("dve", lambda e: e.tensor_copy(out=Wv2[:], in_=Wv[:].rearrange("p (q e) -> p e q", e=2)), r=["Wv"], w=["Wv2"])

    def v_meg(ap, e_):
        return ap.rearrange("p (m e g q) -> p m e g q", m=8, e=2, g=8)[:, :, e_, :, :]
    for e_ in range(2):
        A("dve", lambda e, e_=e_: e.tensor_tensor(
            out=v_meg(big[:], e_), in0=v_meg(Scmp, e_),
            in1=Wk2[:, e_, :].unsqueeze(1).unsqueeze(1).to_broadcast([128, 8, 8, 16]), op=ALU.mult), r=["Scmp", "Wk2"], w=["s_big"])
    A("dve", lambda e: e.tensor_reduce(out=sc2[:], in_=big[:].rearrange("p (c q) -> p c q", q=16), axis=AX.X, op=ALU.add),
      r=["s_big"], w=["s_sc2"])
    A("dve", lambda e: e.tensor_reduce(out=sc[:].rearrange("p (m g) -> p m g", g=8),
                                       in_=sc2[:].rearrange("p (m e g) -> p m g e", m=8, e=2), axis=AX.X, op=ALU.add),
      r=["s_sc2"], w=["s_sc"])
    A("dve", lambda e: e.memset(st[:], 0.0), w=["s_st"])
    A("act", lambda e: e.activation(out=sc[:], in_=sc[:], func=AF.Exp, scale=SCALE, accum_out=st[:, 0:1]), r=["s_sc", "s_st"], w=["s_sc", "s_st"])
    A("dve", lambda e: e.reciprocal(out=st[:, 1:2], in_=st[:, 0:1]), r=["s_st"], w=["s_st"])
    A("dve", lambda e: e.tensor_scalar(out=sc[:], in0=sc[:], scalar1=st[:, 1:2], scalar2=None, op0=ALU.mult), r=["s_sc", "s_st"], w=["s_sc"])
    A("pe", lambda e: e.matmul(pA[:, 0:64], lhsT=GG[:], rhs=sc[:], start=True, stop=True), r=["GG", "s_sc"], w=["s_pA"])
    A("dve", lambda e: e.tensor_reduce(out=pb32[:], in_=pA[:, 0:64].rearrange("p (b t) -> p b t", t=2), axis=AX.X, op=ALU.add),
      r=["s_pA"], w=["s_pb32"])
    A("dve", lambda e: e.tensor_scalar(out=pb32[:, 0:1], in0=pb32[:, 0:1], scalar1=5.0, scalar2=None, op0=ALU.add), r=["s_pb32"], w=["s_pb32"])
    A("dve", lambda e: e.tensor_scalar(out=pb32[:, 31:32], in0=pb32[:, 31:32], scalar1=5.0, scalar2=None, op0=ALU.add), r=["s_pb32"], w=["s_pb32"])
    A("dve", lambda e: e.max(out=m8[:], in_=pb32[:]), r=["s_pb32"], w=["s_m8"])
    A("dve", lambda e: e.match_replace(out=wk32[:], in_to_replace=m8[:], in_values=pb32[:], imm_value=-3e38), r=["s_m8", "s_pb32"], w=["s_wk32"])
    A("dve", lambda e: e.max(out=m8[:], in_=wk32[:]), r=["s_wk32"], w=["s_m8"])
    A("dve", lambda e: e.tensor_scalar(out=selm[:], in0=pb32[:], scalar1=m8[:, 6:7], scalar2=None, op0=ALU.is_ge), r=["s_pb32", "s_m8"], w=["s_selm"])
    for e_ in range(2):
        A("dve", lambda e, e_=e_: e.tensor_tensor(
            out=v_meg(big[:], e_), in0=sc[:].rearrange("p (m g) -> p m g", g=8).unsqueeze(3).to_broadcast([128, 8, 8, 16]),
            in1=Wv2[:, e_, :].unsqueeze(1).unsqueeze(1).to_broadcast([128, 8, 8, 16]), op=ALU.mult),
          r=["s_sc", "Wv2", "s_big"], w=["s_big"])
    A("act", lambda e: e.copy(out=Pc[:], in_=big[:]), r=["s_big"], w=["Pc"])
    for j in range(2):
        A("dve", lambda e, j=j: e.tensor_tensor(out=big[:, 0:64], in0=Qrows[:], in1=Kn[:, j, :], op=ALU.mult), r=["Qrows", "Kn", "s_big", "Pc"], w=["s_big"])
        A("dve", lambda e, j=j: e.tensor_reduce(out=st[:, 2 + 2 * j:3 + 2 * j], in_=big[:, 0:64], axis=AX.X, op=ALU.add), r=["s_big"], w=["s_st"])
        A("act", lambda e, j=j: e.activation(out=st[:, 3 + 2 * j:4 + 2 * j], in_=st[:, 2 + 2 * j:3 + 2 * j], func=AF.Exp, scale=SCALE),
          r=["s_st"], w=["s_st"])
    A("act", lambda e: e.activation(out=big[:], in_=Ssel, func=AF.Exp, scale=SCALE), r=["Ssel", "s_big"], w=["s_big"])
    def v_mek(ap, e_):
        return ap.rearrange("p (m e k t) -> p m e k t", m=8, e=2, k=4)[:, :, e_, :, :]
    for e_ in range(2):
        A("dve", lambda e, e_=e_: e.tensor_tensor(
            out=v_mek(big[:], e_), in0=v_mek(big[:], e_),
            in1=selm[:].rearrange("p (m k) -> p m k", k=4).unsqueeze(3).to_broadcast([128, 8, 4, 32]), op=ALU.mult),
          r=["s_big", "s_selm"], w=["s_big"])
    A("dve", lambda e: e.tensor_reduce(out=st[:, 6:7], in_=big[:], axis=AX.X, op=ALU.add), r=["s_big"], w=["s_st"])
    A("act", lambda e: e.copy(out=Ps[:], in_=big[:]), r=["s_big"], w=["Ps"])
    A("act", lambda e: e.activation(out=big[:, 0:512], in_=Swin[:], func=AF.Exp, scale=SCALE), r=["Swin", "s_big", "Ps"], w=["s_big"])
    A("dve", lambda e: e.tensor_reduce(out=st[:, 7:8], in_=big[:, 0:512], axis=AX.X, op=ALU.add), r=["s_big"], w=["s_st"])
    A("act", lambda e: e.copy(out=Pw[:], in_=big[:, 0:512]), r=["s_big"], w=["Pw"])
    A("dve", lambda e: e.tensor_tensor(out=st[:, 8:9], in0=st[:, 6:7], in1=st[:, 3:4], op=ALU.add), r=["s_st"], w=["s_st"])
    A("dve", lambda e: e.tensor_tensor(out=st[:, 9:10], in0=st[:, 7:8], in1=st[:, 5:6], op=ALU.add), r=["s_st"], w=["s_st"])
    A("dve", lambda e: e.reciprocal(out=st[:, 8:10], in_=st[:, 8:10]), r=["s_st"], w=["s_st"])
    PTc = sb("PTc", [128, 16, 128], BF16, SP)
    PTs = sb("PTs", [128, 16, 128], BF16, SP)
    PTw = sb("PTw", [128, 4, 128], BF16, SP)
    for (src, dst, key, n) in ((Pc, PTc, "PTc", 16), (Ps, PTs, "PTs", 16), (Pw, PTw, "PTw", 4)):
        for i0 in range(0, n, 8):
            m = min(8, n - i0)
            for i in range(m):
                A("pe", lambda e, src=src, i=i, i0=i0: e.transpose(out=pKT[:, i, :], in_=src[:, (i0 + i) * 128:(i0 + i + 1) * 128],
                                                                   identity=ident[:]), r=["ident", "Pc", "Ps", "Pw"], w=["s_pKT"])
            A("dve", lambda e, dst=dst, i0=i0, m=m: e.tensor_copy(out=dst[:, i0:i0 + m, :], in_=pKT[:, 0:m, :]), r=["s_pKT"], w=[key])
    vst = kst
    vbf = [sb(f"vbf{i}", [128, 4, 2, 128], BF16, SP) for i in range(2)]
    vwst = [sb(f"vwst{i}", [128, 4, 128], F32, SP) for i in range(2)]
    vwbf = [sb(f"vwbf{i}", [128, 4, 128], BF16, SP) for i in range(2)]
    Oacc = sb("Oacc", [128, 3, 64], F32, SP)
    otmp = big[:, 0:1024].rearrange("p (b h d) -> p b h d", b=8, h=2)
    ored = sb("s_ored", [128, 64], F32, SP)
    A("dve", lambda e: e.memset(Oacc[:], 0.0), w=["Oacc"])
    pW2 = pst("s_pW2", [128, 512], F32, SP)
    pO = [[pA, pB], [pS[0], pS[1]], [pS[2], pW2]]
    pkeys = [["s_pA", "s_pB"], [("s_pS", 0), ("s_pS", 1)], [("s_pS", 2), "s_pW2"]]
    vcnt = 0
    for half in range(2):
        for bl in range(8):
            b = half * 8 + bl
            bank, cb = bl // 4, (bl % 4) * 128
            wi = b % 2
            A("sp", lambda e, wi=wi, b=b: e.dma_start(
                out=vwst[wi][:], in_=cwin[b].rearrange("(i p) (s c) -> p i s c", p=128, c=128)[:, :, 1, :]), w=[("vwst", wi)], dma=True)
            A("act", lambda e, wi=wi: e.copy(out=vwbf[wi][:], in_=vwst[wi][:]), r=[("vwst", wi)], w=[("vwbf", wi)])
            for q4 in range(4):
                vi = vcnt % NKB
                vb = vcnt % 2
                vcnt += 1
                for mm in range(2):
                    col = b * 8 + q4 * 2 + mm
                    A("pool", lambda e, vi=vi, mm=mm, col=col: e.indirect_dma_start(
                        out=vst[vi][:, 2 * mm:2 * mm + 2, :, :].rearrange("p e s c -> p (e s c)"), out_offset=None, in_=cacheR,
                        in_offset=bass.IndirectOffsetOnAxis(ap=idx[:, col:col + 1], axis=0)), r=["idx"],
                      w=[("kst", vi, 2 * mm), ("kst", vi, 2 * mm + 1)], dma=True)
                A("act", lambda e, vi=vi, vb=vb: e.copy(out=vbf[vb][:], in_=vst[vi][:, :, 1:4:2, :]),
                  r=[("kst", vi, i) for i in range(4)], w=[("vbf", vb)])
                for br, (PT_, ptk) in enumerate(((PTc, "PTc"), (PTs, "PTs"))):
                    for i in range(4):
                        pgi = q4 * 4 + i
                        A("pe", lambda e, br=br, bank=bank, cb=cb, PT_=PT_, i=i, pgi=pgi, vb=vb: e.matmul(
                            pO[br][bank][:, cb:cb + 128], lhsT=PT_[:, pgi, :], rhs=vbf[vb][:, i, br, :],
                            start=(pgi == 0), stop=(pgi == 15)), r=[ptk, ("vbf", vb)], w=[pkeys[br][bank]])
            for i in range(4):
                A("pe", lambda e, bank=bank, cb=cb, i=i, wi=wi: e.matmul(
                    pO[2][bank][:, cb:cb + 128], lhsT=PTw[:, i, :], rhs=vwbf[wi][:, i, :], start=(i == 0), stop=(i == 3)),
                  r=["PTw", ("vwbf", wi)], w=[pkeys[2][bank]])
        for br in range(3):
            for bank in range(2):
                A("dve", lambda e, br=br, bank=bank, half=half: e.tensor_tensor(
                    out=otmp[:, bank * 4:(bank + 1) * 4, :, :].rearrange("p b h d -> p (b h) d"),
                    in0=pO[br][bank][:].rearrange("p (j d) -> p j d", d=64),
                    in1=Mh[:, half, bank * 8:(bank + 1) * 8].unsqueeze(2).to_broadcast([128, 8, 64]), op=ALU.mult),
                  r=[pkeys[br][bank], "Mh"], w=["s_big"])
            A("dve", lambda e: e.tensor_reduce(out=ored[:], in_=otmp.rearrange("p b h d -> p d (b h)"), axis=AX.X, op=ALU.add),
              r=["s_big"], w=["s_ored"])
            A("dve", lambda e, br=br: e.tensor_tensor(out=Oacc[:, br, :], in0=Oacc[:, br, :], in1=ored[:], op=ALU.add),
              r=["s_ored", "Oacc"], w=["Oacc"])
    G3 = sb("G3", [128, 3], F32, SP)
    A("act", lambda e: e.activation(out=G3[:], in_=Grows[:], func=AF.Sigmoid), r=["Grows"], w=["G3"])
    arow = sb("arow", [128, 128], F32, SP)
    tmp64 = sb("tmp64", [128, 64], F32, SP)
    A("dve", lambda e: e.tensor_scalar(out=arow[:, 0:64], in0=Oacc[:, 0, :], scalar1=G3[:, 0:1], scalar2=None, op0=ALU.mult),
      r=["Oacc", "G3"], w=["arow"])
    for j, br in ((0, 1), (1, 2)):
        A("dve", lambda e, j=j, br=br: e.scalar_tensor_tensor(out=tmp64[:], in0=Vn[:, j, :], scalar=st[:, 3 + 2 * j:4 + 2 * j],
                                                              in1=Oacc[:, br, :], op0=ALU.mult, op1=ALU.add),
          r=["Vn", "s_st", "Oacc"], w=["tmp64"])
        A("dve", lambda e, j=j, br=br: e.tensor_scalar(out=tmp64[:], in0=tmp64[:], scalar1=st[:, 8 + j:9 + j], scalar2=G3[:, br:br + 1],
                                                       op0=ALU.mult, op1=ALU.mult), r=["tmp64", "s_st", "G3"], w=["tmp64"])
        A("dve", lambda e: e.tensor_tensor(out=arow[:, 0:64], in0=arow[:, 0:64], in1=tmp64[:], op=ALU.add), r=["arow", "tmp64"], w=["arow"])
    glus = sb("glus", [NSB, 512], F32, SP)
    cvb = sb("cvb", [NSB, 4, 512], F32, SP)
    A("sp", lambda e: e.dma_start(out=cvb[:], in_=wdall[30:34, :].partition_broadcast(NSB)), w=["cvb"], dma=True)
    A("act", lambda e: e.activation(out=glus[:], in_=zs[:, 512:1024], func=AF.Sigmoid), r=["zs"], w=["glus"])
    A("dve", lambda e: e.tensor_tensor(out=glus[:], in0=glus[:], in1=zs[:, 0:512], op=ALU.mult), r=["glus", "zs"], w=["glus"])
    A("sp", lambda e: e.dma_start(out=conv_s.rearrange("(b j) c -> b j c", j=30)[:, 29, :], in_=glus[:]), r=["glus"], dma=True)
    Xc = sb("Xc", [120, 512], F32, SP)
    Wrep = sb("Wrep", [120, 512], F32, SP)
    sel4 = sb("sel4", [120, 4, NSB], F32, SP)
    for r4 in range(4):
        A("sp", lambda e, r4=r4: e.dma_start(out=Wrep[r4 * 30:(r4 + 1) * 30, :], in_=wdall[0:30, :]), w=["Wrep"], dma=True)
    A("pool", lambda e: e.memset(sel4[:], 1.0), w=["sel4"])
    for i4 in range(4):
        A("pool", lambda e, i4=i4: e.affine_select(out=sel4[:, i4, :], in_=sel4[:, i4, :], pattern=[[-30, NSB]], compare_op=ALU.is_ge,
                                                   fill=0.0, base=120 * i4, channel_multiplier=1), r=["sel4"], w=["sel4"])
        A("pool", lambda e, i4=i4: e.affine_select(out=sel4[:, i4, :], in_=sel4[:, i4, :], pattern=[[30, NSB]], compare_op=ALU.is_ge,
                                                   fill=0.0, base=29 - 120 * i4, channel_multiplier=-1), r=["sel4"], w=["sel4"])
    for i4 in range(4):
        A("sp", lambda e, i4=i4: e.dma_start(out=Xc[:], in_=sconv[i4 * 120:(i4 + 1) * 120, :]), w=["Xc"], dma=True)
        A("dve", lambda e: e.tensor_tensor(out=Xc[:], in0=Xc[:], in1=Wrep[:], op=ALU.mult), r=["Xc", "Wrep"], w=["Xc"])
        A("pe", lambda e, i4=i4: e.matmul(pB[0:NSB, :], lhsT=sel4[:, i4, :], rhs=Xc[:], start=(i4 == 0), stop=(i4 == 3)),
          r=["sel4", "Xc"], w=["s_pB"])
    yc = sb("yc", [NSB, 512], F32, SP)
    ycs = sb("ycs", [NSB, 4], F32, SP)
    A("dve", lambda e: e.tensor_tensor(out=yc[:], in0=glus[:], in1=cvb[:, 0, :], op=ALU.mult), r=["glus", "cvb"], w=["yc"])
    A("dve", lambda e: e.tensor_tensor(out=yc[:], in0=yc[:], in1=pB[0:NSB, :], op=ALU.add), r=["yc", "s_pB"], w=["yc"])
    A("dve", lambda e: e.tensor_tensor(out=yc[:], in0=yc[:], in1=cvb[:, 1, :], op=ALU.add), r=["yc", "cvb"], w=["yc"])
    A("dve", lambda e: e.tensor_reduce(out=ycs[:, 0:1], in_=yc[:], axis=AX.X, op=ALU.add), r=["yc"], w=["ycs"])
    A("dve", lambda e: e.tensor_scalar(out=ycs[:, 0:1], in0=ycs[:, 0:1], scalar1=1.0 / 512.0, scalar2=None, op0=ALU.mult), r=["ycs"], w=["ycs"])
    A("dve", lambda e: e.tensor_scalar(out=yc[:], in0=yc[:], scalar1=ycs[:, 0:1], scalar2=None, op0=ALU.subtract), r=["yc", "ycs"], w=["yc"])
    ysq_ = attn_s_early = sb("ysq_", [NSB, 512], F32, SP)
    A("dve", lambda e: e.tensor_tensor(out=ysq_[:], in0=yc[:], in1=yc[:], op=ALU.mult), r=["yc"], w=["ysq_"])
    A("dve", lambda e: e.tensor_reduce(out=ycs[:, 1:2], in_=ysq_[:], axis=AX.X, op=ALU.add), r=["ysq_"], w=["ycs"])
    A("dve", lambda e: e.tensor_scalar(out=ycs[:, 1:2], in0=ycs[:, 1:2], scalar1=1.0 / 512.0, scalar2=1e-5, op0=ALU.mult, op1=ALU.add),
      r=["ycs"], w=["ycs"])
    A("act", lambda e: e.activation(out=ycs[:, 1:2], in_=ycs[:, 1:2], func=AF.Sqrt), r=["ycs"], w=["ycs"])
    A("dve", lambda e: e.reciprocal(out=ycs[:, 1:2], in_=ycs[:, 1:2]), r=["ycs"], w=["ycs"])
    A("dve", lambda e: e.scalar_tensor_tensor(out=yc[:], in0=yc[:], scalar=ycs[:, 1:2], in1=cvb[:, 2, :], op0=ALU.mult, op1=ALU.mult),
      r=["yc", "ycs", "cvb"], w=["yc"])
    A("dve", lambda e: e.tensor_tensor(out=yc[:], in0=yc[:], in1=cvb[:, 3, :], op=ALU.add), r=["yc", "cvb"], w=["yc"])
    ycb = sb("ycb", [NSB, 512], BF16, SP)
    A("act", lambda e: e.activation(out=ycb[:], in_=yc[:], func=AF.Silu), r=["yc"], w=["ycb"])
    cyT = sb("cyT", [128, 4, NSB], BF16, SP)
    for c4 in range(4):
        A("pe", lambda e, c4=c4: e.transpose(out=pTb[:, c4, 0:NSB], in_=ycb[:, c4 * 128:(c4 + 1) * 128], identity=ident[0:NSB, 0:NSB]),
          r=["ycb", "ident"], w=["pTb"])
    A("act", lambda e: e.copy(out=cyT[:], in_=pTb[:, 0:4, 0:NSB]), r=["pTb"], w=["cyT"])
    as_d = nc.dram_tensor("as_d", [128, 64], F32, kind="Internal").ap()
    attn_s = ysq_
    attn_sb = sb("attn_sb", [NSB, 512], BF16, SP)
    aT2 = sb("aT2", [128, 4, NSB], BF16, SP)
    A("sp", lambda e: e.dma_start(out=as_d, in_=arow[:, 0:64]), r=["arow"], w=["as_d"], dma=True)
    A("sp", lambda e: e.dma_start(out=attn_s[:], in_=as_d.rearrange("(b h) d -> b (h d)", h=8)), r=["as_d"], w=["ysq_"], dma=True)
    A("act", lambda e: e.copy(out=attn_sb[:], in_=attn_s[:]), r=["ysq_"], w=["attn_sb"])
    for c4 in range(4):
        A("pe", lambda e, c4=c4: e.transpose(out=pTb[:, 4 + c4, 0:NSB], in_=attn_sb[:, c4 * 128:(c4 + 1) * 128], identity=ident[0:NSB, 0:NSB]),
          r=["attn_sb", "ident"], w=["pTb"])
    A("act", lambda e: e.copy(out=aT2[:], in_=pTb[:, 4:8, 0:NSB]), r=["pTb"], w=["aT2"])
    for half in range(2):
        cs = slice(half * 512, (half + 1) * 512)
        for k in range(8):
            lhs = cyT[:, k, :] if k < 4 else aT2[:, k - 4, :]
            A("pe", lambda e, k=k, cs=cs, lhs=lhs: e.matmul(pA[0:NSB, :], lhsT=lhs, rhs=w_out_bf[:, k, cs], start=(k == 0), stop=(k == 7)),
              r=["cyT", "aT2", ("w_out_bf", k)], w=["s_pA"])
        A("dve", lambda e, cs=cs: e.tensor_tensor(out=hs_acc[:, cs], in0=pA[0:NSB, :], in1=xs_sb[:, cs], op=ALU.add),
          r=["s_pA", "xs_sb"], w=["hs_acc"])
    emit_norm_T(S, hs_acc[:], "hs_acc", junk, ssx, hn, gbc_mlp, "gbc_mlp", ident, pTb, None, hnTs[:], "hnTs")
    if debug:
        dbg["arow"] = dout("d_arow", [128, 128], F32)
        A("sp", lambda e: e.dma_start(out=dbg["arow"], in_=arow[:]), r=["arow"], dma=True)
        dbg["hs"] = dout("d_hs", [NSB, DM], F32)
        A("sp", lambda e: e.dma_start(out=dbg["hs"], in_=hs_acc[:]), r=["hs_acc"], dma=True)
        dbg["ycb"] = dout("d_ycb", [NSB, 512], BF16)
        A("sp", lambda e: e.dma_start(out=dbg["ycb"], in_=ycb[:]), r=["ycb"], dma=True)
        dbg["Oacc"] = dout("d_Oacc", [128, 192], F32)
        A("sp", lambda e: e.dma_start(out=dbg["Oacc"], in_=Oacc[:]), r=["Oacc"], dma=True)
        dbg["st"] = dout("d_st", [128, 16], F32)
        A("sp", lambda e: e.dma_start(out=dbg["st"], in_=st[:]), r=["s_st"], dma=True)
        dbg["selm"] = dout("d_selm", [128, 32], F32)
        A("sp", lambda e: e.dma_start(out=dbg["selm"], in_=selm[:]), r=["s_selm"], dma=True)
```
